# Optimizing a Trainium2 kernel written in Bass

```python
import math
import jax, jax.numpy as jnp
from jax import lax
import numpy as np


D_MODEL = 1024
BATCH = 16
SEQ = 4096
DEPTH = 4

N_META = 16
D_SSM = D_MODEL // 2
SSM_GROUP = 16
SSM_GROUPS = D_SSM // SSM_GROUP
SSM_STATE = 64
DT_MIN = 1e-3
DT_MAX = 1e-1
D_CONV = D_MODEL // 2
CONV_WIDTH = 31
N_HEADS = 8
HEAD_DIM = 64
D_ATTN = N_HEADS * HEAD_DIM
Q_BLOCK = 128
D_FF = 2816
N_BRANCH = 3
SPLITS = [D_SSM,
          D_SSM + 2 * D_CONV,
          D_SSM + 2 * D_CONV + D_ATTN,
          D_SSM + 2 * D_CONV + 2 * D_ATTN,
          D_SSM + 2 * D_CONV + 3 * D_ATTN]
D_IN = D_SSM + 2 * D_CONV + 3 * D_ATTN + N_BRANCH * D_MODEL
RMS_EPS = 1e-6
LN_EPS = 1e-5
F32 = jnp.float32

kernel_name = 'hybrid_s5_conformer_stickbreaking_block'


def rms_norm(x, g):
    xf = x.astype(F32)
    y = xf * lax.rsqrt(jnp.mean(xf * xf, axis=-1, keepdims=True) + RMS_EPS)
    return (y * g.astype(F32)).astype(x.dtype)


def layer_norm(x, g, b):
    xf = x.astype(F32)
    mu = jnp.mean(xf, axis=-1, keepdims=True)
    xc = xf - mu
    y = xc * lax.rsqrt(jnp.mean(xc * xc, axis=-1, keepdims=True) + LN_EPS)
    return (y * g.astype(F32) + b.astype(F32)).astype(x.dtype)


def swiglu_ffn(x, w13, w2):
    a, b = jnp.split(x @ w13, 2, axis=-1)
    return (jax.nn.silu(a) * b) @ w2


def _complex_affine_combine(e1, e2):
    a1r, a1i, b1r, b1i = e1
    a2r, a2i, b2r, b2i = e2
    ar = a2r * a1r - a2i * a1i
    ai = a2r * a1i + a2i * a1r
    br = a2r * b1r - a2i * b1i + b2r
    bi = a2r * b1i + a2i * b1r + b2i
    return (ar, ai, br, bi)


def s5_mixer(u, lam_re, lam_im, log_dt, b_re, b_im, c_re, c_im, d_skip, w_glu):
    Bsz, L, _ = u.shape
    uf = u.astype(F32).reshape(Bsz, L, SSM_GROUPS, SSM_GROUP)
    dt = jnp.exp(log_dt.astype(F32))[:, None]
    lr = lam_re.astype(F32)
    li = lam_im.astype(F32)
    mag = jnp.exp(lr * dt)
    ab_re = mag * jnp.cos(li * dt)
    ab_im = mag * jnp.sin(li * dt)
    den = lr * lr + li * li
    nr = ab_re - 1.0
    ni = ab_im
    coef_re = (nr * lr + ni * li) / den
    coef_im = (ni * lr - nr * li) / den
    br = b_re.astype(F32)
    bi = b_im.astype(F32)
    bb_re = coef_re[..., None] * br - coef_im[..., None] * bi
    bb_im = coef_re[..., None] * bi + coef_im[..., None] * br
    bu_re = jnp.einsum('gnc,blgc->lbgn', bb_re, uf)
    bu_im = jnp.einsum('gnc,blgc->lbgn', bb_im, uf)
    a_re = jnp.broadcast_to(ab_re, (L, 1, SSM_GROUPS, SSM_STATE))
    a_im = jnp.broadcast_to(ab_im, (L, 1, SSM_GROUPS, SSM_STATE))
    _, _, s_re, s_im = lax.associative_scan(_complex_affine_combine,
                                            (a_re, a_im, bu_re, bu_im), axis=0)
    y = (jnp.einsum('gcn,lbgn->blgc', c_re.astype(F32), s_re)
         - jnp.einsum('gcn,lbgn->blgc', c_im.astype(F32), s_im))
    y = y.reshape(Bsz, L, D_SSM) + d_skip.astype(F32) * u.astype(F32)
    y = jax.nn.gelu(y).astype(u.dtype)
    a, g = jnp.split(y @ w_glu, 2, axis=-1)
    return a * jax.nn.sigmoid(g)


def conformer_conv(xc, conv_w, conv_b, ln_g, ln_b, w_pw):
    a, g = jnp.split(xc, 2, axis=-1)
    h = a * jax.nn.sigmoid(g)
    h = lax.conv_general_dilated(
        h, conv_w[:, None, :].astype(h.dtype), window_strides=(1,),
        padding=((CONV_WIDTH - 1, 0),),
        dimension_numbers=('NWC', 'WIO', 'NWC'),
        feature_group_count=D_CONV) + conv_b
    h = jax.nn.silu(layer_norm(h, ln_g, ln_b))
    return h @ w_pw


def stick_breaking_attention(q, k, v, w_o):
    Bsz, L = q.shape[0], q.shape[1]
    scale = 1.0 / math.sqrt(HEAD_DIM)
    qf = q.astype(F32).transpose(0, 2, 1, 3)
    kf = k.astype(F32).transpose(0, 2, 1, 3)
    vf = v.astype(F32).transpose(0, 2, 1, 3)
    key_pos = jnp.arange(L)

    def attend(q_blk, q_pos):
        z = jnp.einsum('bhqd,bhkd->bhqk', q_blk, kf) * scale
        mask = key_pos[None, :] < q_pos[:, None]
        log_keep = jnp.where(mask, jax.nn.log_sigmoid(-z), 0.0)
        later = lax.cumsum(log_keep, axis=3, reverse=True) - log_keep
        w = jnp.where(mask, jnp.exp(jax.nn.log_sigmoid(z) + later), 0.0)
        return jnp.einsum('bhqk,bhkd->bhqd', w, vf)

    meta_out = attend(qf[:, :, :N_META], jnp.arange(N_META))
    n_blk = (L - N_META) // Q_BLOCK
    q_real = qf[:, :, N_META:].reshape(Bsz, N_HEADS, n_blk, Q_BLOCK, HEAD_DIM)
    q_real = q_real.transpose(2, 0, 1, 3, 4)
    pos = N_META + jnp.arange(n_blk * Q_BLOCK).reshape(n_blk, Q_BLOCK)
    real_out = lax.map(lambda a: attend(a[0], a[1]), (q_real, pos))
    real_out = real_out.transpose(1, 2, 0, 3, 4).reshape(Bsz, N_HEADS, n_blk * Q_BLOCK, HEAD_DIM)
    o = jnp.concatenate([meta_out, real_out], axis=2)
    o = o.transpose(0, 2, 1, 3).reshape(Bsz, L, D_ATTN).astype(q.dtype)
    return o @ w_o


def hybrid_mixer(xn, w_in, lam_re, lam_im, log_dt, b_re, b_im, c_re, c_im, d_skip, w_glu,
                 conv_w, conv_b, conv_ln_g, conv_ln_b, conv_w_out, attn_w_o, w_out):
    Bsz, L, _ = xn.shape
    proj = xn @ w_in
    u, xc, q, k, v, gates = jnp.split(proj, SPLITS, axis=-1)
    o_ssm = s5_mixer(u, lam_re, lam_im, log_dt, b_re, b_im, c_re, c_im, d_skip, w_glu)
    o_conv = conformer_conv(xc, conv_w, conv_b, conv_ln_g, conv_ln_b, conv_w_out)
    o_attn = stick_breaking_attention(q.reshape(Bsz, L, N_HEADS, HEAD_DIM),
                                      k.reshape(Bsz, L, N_HEADS, HEAD_DIM),
                                      v.reshape(Bsz, L, N_HEADS, HEAD_DIM), attn_w_o)
    g = jax.nn.sigmoid(gates.astype(F32)).reshape(Bsz, L, N_BRANCH, D_MODEL)
    merged = (g[:, :, 0] * o_ssm.astype(F32) + g[:, :, 1] * o_conv.astype(F32)
              + g[:, :, 2] * o_attn.astype(F32))
    return merged.astype(xn.dtype) @ w_out


def setup_inputs(seed: int = 0) -> dict:
    key = jax.random.key(seed)
    ks = jax.random.split(key, 32)

    def nrm(k, shape, scale):
        return scale * jax.random.normal(k, shape, F32)

    G, N = SSM_GROUPS, SSM_STATE
    lam_im_init = math.pi * jnp.arange(N, dtype=F32)
    return {
        'x': nrm(ks[0], (BATCH, SEQ, D_MODEL), 1.0),
        'meta_tokens': nrm(ks[1], (N_META, D_MODEL), 1.0),
        'ffn1_norm': 1.0 + nrm(ks[2], (DEPTH, D_MODEL), 0.05),
        'ffn1_w13': nrm(ks[3], (DEPTH, D_MODEL, 2 * D_FF), D_MODEL ** -0.5),
        'ffn1_w2': nrm(ks[4], (DEPTH, D_FF, D_MODEL), D_FF ** -0.5),
        'mix_norm': 1.0 + nrm(ks[5], (DEPTH, D_MODEL), 0.05),
        'w_in': nrm(ks[6], (DEPTH, D_MODEL, D_IN), D_MODEL ** -0.5),
        'ssm_lam_re': -0.5 + nrm(ks[7], (DEPTH, G, N), 0.01),
        'ssm_lam_im': lam_im_init + nrm(ks[8], (DEPTH, G, N), 0.01),
        'ssm_log_dt': jax.random.uniform(ks[9], (DEPTH, G), F32,
                                         math.log(DT_MIN), math.log(DT_MAX)),
        'ssm_b_re': nrm(ks[10], (DEPTH, G, N, SSM_GROUP), (2 * SSM_GROUP) ** -0.5),
        'ssm_b_im': nrm(ks[11], (DEPTH, G, N, SSM_GROUP), (2 * SSM_GROUP) ** -0.5),
        'ssm_c_re': nrm(ks[12], (DEPTH, G, SSM_GROUP, N), (2 * N) ** -0.5),
        'ssm_c_im': nrm(ks[13], (DEPTH, G, SSM_GROUP, N), (2 * N) ** -0.5),
        'ssm_d': nrm(ks[14], (DEPTH, D_SSM), 1.0),
        'ssm_w_glu': nrm(ks[15], (DEPTH, D_SSM, 2 * D_MODEL), D_SSM ** -0.5),
        'conv_w': nrm(ks[16], (DEPTH, CONV_WIDTH, D_CONV), CONV_WIDTH ** -0.5),
        'conv_b': nrm(ks[17], (DEPTH, D_CONV), 0.01),
        'conv_ln_g': 1.0 + nrm(ks[18], (DEPTH, D_CONV), 0.05),
        'conv_ln_b': nrm(ks[19], (DEPTH, D_CONV), 0.01),
        'conv_w_out': nrm(ks[20], (DEPTH, D_CONV, D_MODEL), D_CONV ** -0.5),
        'attn_w_o': nrm(ks[21], (DEPTH, D_ATTN, D_MODEL), D_ATTN ** -0.5),
        'w_out': nrm(ks[22], (DEPTH, D_MODEL, D_MODEL), D_MODEL ** -0.5),
        'ffn2_norm': 1.0 + nrm(ks[23], (DEPTH, D_MODEL), 0.05),
        'ffn2_w13': nrm(ks[24], (DEPTH, D_MODEL, 2 * D_FF), D_MODEL ** -0.5),
        'ffn2_w2': nrm(ks[25], (DEPTH, D_FF, D_MODEL), D_FF ** -0.5),
        'final_norm': 1.0 + nrm(ks[26], (D_MODEL,), 0.05),
    }


def reference(x, meta_tokens, ffn1_norm, ffn1_w13, ffn1_w2, mix_norm, w_in,
              ssm_lam_re, ssm_lam_im, ssm_log_dt, ssm_b_re, ssm_b_im, ssm_c_re, ssm_c_im,
              ssm_d, ssm_w_glu, conv_w, conv_b, conv_ln_g, conv_ln_b, conv_w_out,
              attn_w_o, w_out, ffn2_norm, ffn2_w13, ffn2_w2, final_norm):
    Bsz = x.shape[0]
    meta = jnp.broadcast_to(meta_tokens[None].astype(x.dtype), (Bsz, N_META, D_MODEL))
    h = jnp.concatenate([meta, x], axis=1)
    for i in range(DEPTH):
        h = h + 0.5 * swiglu_ffn(rms_norm(h, ffn1_norm[i]), ffn1_w13[i], ffn1_w2[i])
        h = h + hybrid_mixer(rms_norm(h, mix_norm[i]), w_in[i],
                             ssm_lam_re[i], ssm_lam_im[i], ssm_log_dt[i],
                             ssm_b_re[i], ssm_b_im[i], ssm_c_re[i], ssm_c_im[i],
                             ssm_d[i], ssm_w_glu[i],
                             conv_w[i], conv_b[i], conv_ln_g[i], conv_ln_b[i], conv_w_out[i],
                             attn_w_o[i], w_out[i])
        h = h + 0.5 * swiglu_ffn(rms_norm(h, ffn2_norm[i]), ffn2_w13[i], ffn2_w2[i])
    return rms_norm(h, final_norm)[:, N_META:]
```

```python
import math
import os
from contextlib import ExitStack

import numpy as np
import concourse.bass as bass
import concourse.mybir as mybir
from concourse.bass_utils import run_bass_kernel_spmd

F32 = mybir.dt.float32
BF16 = mybir.dt.bfloat16
I32 = mybir.dt.int32
AF = mybir.ActivationFunctionType
ALU = mybir.AluOpType

D = 1024
DT = 8
DFF = 2816
FT = 22
NMETA = 16
DIN = 6144
NG = 32
NST = 64
CW = 31
RMS_EPS = 1e-6
LN_EPS = 1e-5

ENGS = ["pe", "act", "dve", "pool", "sp"]


class Sched:
    def __init__(self, nc, es, n_dma_sems=24):
        self.nc = nc
        self.ops = {e: [] for e in ENGS}
        self.sem = {e: es.enter_context(nc.semaphore("sem_" + e)) for e in ENGS}
        self.cnt = {e: 0 for e in ENGS}
        self.dsem = [es.enter_context(nc.semaphore("semd%d" % i)) for i in range(n_dma_sems)]
        self.duse = [0] * n_dma_sems
        self.dnext = 0
        self.waited = {}
        self.lastw = {}
        self.readers = {}

    def _semof(self, key):
        if isinstance(key, str):
            return self.sem[key]
        return self.dsem[key[1]]

    def _deps(self, eng, reads, writes):
        toks = {}
        def add(t):
            k, v = t
            if toks.get(k, 0) < v:
                toks[k] = v
        for r in reads:
            if r in self.lastw:
                add(self.lastw[r])
        for w in writes:
            if w in self.lastw:
                add(self.lastw[w])
            for k, v in self.readers.get(w, {}).items():
                add((k, v))
        waits = []
        for k, v in toks.items():
            if k == "pe" and eng == "pe":
                continue
            if self.waited.get((eng, k), 0) >= v:
                continue
            self.waited[(eng, k)] = v
            waits.append((k, v))
        return waits

    def _record(self, tok, reads, writes):
        k, v = tok
        for r in reads:
            d = self.readers.setdefault(r, {})
            if d.get(k, 0) < v:
                d[k] = v
        for w in writes:
            self.lastw[w] = tok
            self.readers[w] = {}

    def op(self, eng, meth, *args, reads=(), writes=(), excl=(), **kw):
        fn = (meth, args, kw)
        if excl:
            reads = list(reads) + list(excl)
            writes = list(writes) + list(excl)
        waits = self._deps(eng, reads, writes)
        self.cnt[eng] += 1
        tok = (eng, self.cnt[eng])
        self.ops[eng].append((waits, fn, eng, 1))
        self._record(tok, reads, writes)

    def dma(self, eng, reads=(), writes=(), **kw):
        fn = ("dma_start", (), kw)
        waits = self._deps(eng, reads, writes)
        slot = self.dnext
        self.dnext = (self.dnext + 1) % len(self.dsem)
        key = ("d", slot)
        if self.duse[slot] > 0:
            v = 16 * self.duse[slot]
            if self.waited.get((eng, key), 0) < v:
                self.waited[(eng, key)] = v
                waits.append((key, v))
        self.duse[slot] += 1
        tok = (key, 16 * self.duse[slot])
        self.ops[eng].append((waits, fn, key, 16))
        self._record(tok, reads, writes)

    def barrier(self):
        for eng in ENGS:
            waits = []
            for k in ENGS:
                v = self.cnt[k]
                if v > 0 and k != eng and self.waited.get((eng, k), 0) < v:
                    self.waited[(eng, k)] = v
                    waits.append((k, v))
            if eng != "pe":
                v = self.cnt[eng]
                if v > 0 and self.waited.get((eng, eng), 0) < v:
                    self.waited[(eng, eng)] = v
                    waits.append((eng, v))
            for i, u in enumerate(self.duse):
                key = ("d", i)
                if u > 0 and self.waited.get((eng, key), 0) < 16 * u:
                    self.waited[(eng, key)] = 16 * u
                    waits.append((key, 16 * u))
            if waits:
                self.ops[eng].append((waits, None, None, 0))
        self.lastw = {}
        self.readers = {}

    def emit(self, block):
        def replay(name):
            def run(e):
                for waits, fn, inc_key, inc in self.ops[name]:
                    for k, v in waits:
                        e.wait_ge(self._semof(k), v)
                    if fn is not None:
                        meth, args, kw = fn
                        getattr(e, meth)(*args, **kw).then_inc(self._semof(inc_key), inc)
            return run
        block.tensor(replay("pe"))
        block.scalar(replay("act"))
        block.vector(replay("dve"))
        block.gpsimd(replay("pool"))
        block.sync(replay("sp"))


class Cfg:
    def __init__(self, seq=4096, nseq=2, depth=4, phases=None, dump=()):
        self.dump = dump
        self.seq = seq
        self.nseq = nseq
        self.depth = depth
        self.L = seq + NMETA
        tiles = []
        t = 0
        while t + 512 <= self.L:
            tiles.append((t, 512))
            t += 512
        if t < self.L:
            tiles.append((t, self.L - t))
        self.tiles = tiles
        self.nkb = (self.L + 127) // 128
        self.LP = self.nkb * 128
        self.NK = self.L // 8
        self.phases = phases


def build(cfg):
    nc = bass.Bass("TRN2", target_bir_lowering=False)
    L, NS, DEPTH = cfg.L, cfg.nseq, cfg.depth

    def din(name, shape):
        return nc.dram_tensor(name, list(shape), F32, kind="ExternalInput").ap()

    x = din("x", (NS, cfg.seq, D))
    meta = din("meta_tokens", (NMETA, D))
    P = {}
    for name, shape in [
        ("ffn1_norm", (DEPTH, D)), ("ffn1_w13", (DEPTH, D, 2 * DFF)), ("ffn1_w2", (DEPTH, DFF, D)),
        ("mix_norm", (DEPTH, D)), ("w_in", (DEPTH, D, DIN)),
        ("ssm_lam_re", (DEPTH, NG, NST)), ("ssm_lam_im", (DEPTH, NG, NST)), ("ssm_log_dt", (DEPTH, NG)),
        ("ssm_b_re", (DEPTH, NG, NST, 16)), ("ssm_b_im", (DEPTH, NG, NST, 16)),
        ("ssm_c_re", (DEPTH, NG, 16, NST)), ("ssm_c_im", (DEPTH, NG, 16, NST)),
        ("ssm_d", (DEPTH, 512)), ("ssm_w_glu", (DEPTH, 512, 2048)),
        ("conv_w", (DEPTH, CW, 512)), ("conv_b", (DEPTH, 512)),
        ("conv_ln_g", (DEPTH, 512)), ("conv_ln_b", (DEPTH, 512)),
        ("conv_w_out", (DEPTH, 512, D)), ("attn_w_o", (DEPTH, 512, D)), ("w_out", (DEPTH, D, D)),
        ("ffn2_norm", (DEPTH, D)), ("ffn2_w13", (DEPTH, D, 2 * DFF)), ("ffn2_w2", (DEPTH, DFF, D)),
        ("final_norm", (D,)),
    ]:
        P[name] = din(name, shape)
    out = nc.dram_tensor("out", [NS, cfg.seq, D], F32, kind="ExternalOutput").ap()

    def scratch(name, shape, dt):
        kind = "ExternalOutput" if name in cfg.dump else "Internal"
        return nc.dram_tensor(name, list(shape), dt, kind=kind).ap()

    hT = scratch("hT", (NS, D, L), F32)
    NK = cfg.NK
    Usc = scratch("Usc", (NS, NG, 8, 16, NK), BF16)
    qT = scratch("qT", (NS, 512, L), BF16)
    kT = scratch("kT", (NS, 512, L), BF16)
    vS = scratch("vS", (NS, L, 512), BF16)
    oattT = scratch("oattT", (NS, 512, L), BF16)
    ysS = scratch("ysS", (NS, NG, 8, 16, NK), F32)

    with ExitStack() as es:
        S = Sched(nc, es)

        uid = [0]

        def sb(name, shape, dt, stack=es):
            uid[0] += 1
            return stack.enter_context(nc.sbuf_tensor("%s_%d" % (name, uid[0]), list(shape), dt))

        def ps(name, shape, dt=F32, stack=es):
            uid[0] += 1
            return stack.enter_context(nc.psum_tensor("%s_%d" % (name, uid[0]), list(shape), dt))

        ones_bf = sb("ones_bf", [128, 128], BF16)
        ident_f = sb("ident_f", [128, 128], F32)
        gcols = sb("gcols", [128, (3 * DEPTH + 1) * DT], F32)
        eps_rms = sb("eps_rms", [128, 1], F32)
        S.op("pool", "memset", ones_bf[:], 1.0, writes=["ones_bf"])
        S.op("pool", "memset", ident_f[:], 1.0, writes=["ident_f"])
        S.op("pool", "affine_select", out=ident_f[:], in_=ident_f[:], pattern=[[-1, 128]],
                                               compare_op=ALU.is_equal, fill=0.0, base=0,
                                               channel_multiplier=1,
             reads=["ident_f"], writes=["ident_f"])
        S.op("pool", "memset", eps_rms[:], RMS_EPS, writes=["eps_rms"])
        for wi, nm in enumerate(["ffn1_norm", "mix_norm", "ffn2_norm"]):
            for l in range(DEPTH):
                c0 = (wi * DEPTH + l) * DT
                S.dma("sp",
                    out=gcols[:, c0:c0 + DT], in_=P[nm][l].rearrange("(t p) -> p t", p=128),
                    allow_slow_non_contiguous=True, writes=["gcols"])
        cF = 3 * DEPTH * DT
        S.dma("sp", out=gcols[:, cF:cF + DT],
                                          in_=P["final_norm"].rearrange("(t p) -> p t", p=128),
                                          allow_slow_non_contiguous=True, writes=["gcols"])
        S.barrier()

        def gcol(which, l):
            c0 = (which * DEPTH + l) * DT if which < 3 else cF
            return c0

        def load_h(hs, s, t0, n):
            S.dma("sp",
                out=hs[:, :, 0:n], in_=hT[s, :, t0:t0 + n].rearrange("(t p) n -> p t n", p=128),
                writes=["hs"])

        def store_h(hs, s, t0, n):
            S.dma("sp",
                out=hT[s, :, t0:t0 + n].rearrange("(t p) n -> p t n", p=128), in_=hs[:, :, 0:n],
                reads=["hs"])

        def rmsnorm(hs, xn, sqb, pss, rstd, c0, n, xn_name="xn"):
            for dt in range(DT):
                b = dt % 2
                S.op("act", "activation", out=sqb[b][:, 0:n], in_=hs[:, dt, 0:n],
                                                                 func=AF.Square,
                     reads=["hs"], writes=[("sqb", b)])
                S.op("pe", "matmul", pss[:, 0:n], lhsT=ones_bf[:], rhs=sqb[b][:, 0:n],
                                                           start=(dt == 0), stop=(dt == DT - 1),
                     reads=[("sqb", b), "ones_bf"], writes=["pss"])
            S.op("act", "activation", out=rstd[:, 0:n], in_=pss[:, 0:n], func=AF.Sqrt,
                                               scale=1.0 / D, bias=eps_rms[:],
                 reads=["pss"], writes=["rstd"])
            S.op("dve", "reciprocal", out=rstd[:, 0:n], in_=rstd[:, 0:n],
                 reads=["rstd"], writes=["rstd"])
            for dt in range(DT):
                S.op("dve", "scalar_tensor_tensor",
                    out=xn[:, dt, 0:n], in0=hs[:, dt, 0:n], scalar=gcols[:, c0 + dt:c0 + dt + 1],
                    in1=rstd[:, 0:n], op0=ALU.mult, op1=ALU.mult,
                    reads=["hs", "rstd", "gcols"], writes=[xn_name])

        def load_w_cast(dst_ap, src_ap, res):
            S.dma("pool", out=dst_ap, in_=src_ap, max_dma_last_dim=8192,
                  writes=[res])

        def phase_in():
            with ExitStack() as st:
                hs = sb("in_hs", [128, DT, 512], F32, st)
                xt = [sb("in_xt%d" % i, [128, D], F32, st) for i in range(2)]
                pt = [ps("in_pt%d" % i, [128, 512], F32, st) for i in range(2)]
                blk = 0
                for s in range(NS):
                    for (t0, n) in cfg.tiles:
                        for j in range((n + 127) // 128):
                            tb = t0 + j * 128
                            nb = min(128, t0 + n - tb)
                            xb = xt[blk % 2]
                            xr = ("xt", blk % 2)
                            if tb == 0:
                                S.dma("sp", out=xb[0:NMETA, :], in_=meta[:, :],
                                      writes=[xr])
                                S.dma("sp",
                                    out=xb[NMETA:nb, :], in_=x[s, 0:nb - NMETA, :], writes=[(xr, 1)],
                                    reads=[])
                                rd = [xr, (xr, 1)]
                            else:
                                S.dma("sp",
                                    out=xb[0:nb, :], in_=x[s, tb - NMETA:tb - NMETA + nb, :], writes=[xr, (xr, 1)])
                                rd = [xr, (xr, 1)]
                            for half in range(2):
                                pp = pt[half]
                                for q in range(4):
                                    dt = half * 4 + q
                                    S.op("pe", "transpose",
                                        out=pp[:, q * 128:q * 128 + nb], in_=xb[0:nb, dt * 128:(dt + 1) * 128],
                                        identity=ident_f[0:nb, 0:nb],
                                        reads=rd + ["ident_f"], writes=[("pt", half)])
                                eng = "act" if half == 0 else "dve"
                                if eng == "act":
                                    S.op("act", "activation",
                                        out=hs[:, half * 4:half * 4 + 4, j * 128:j * 128 + nb],
                                        in_=pp[:].rearrange("p (q c) -> p q c", q=4)[:, :, 0:nb], func=AF.Copy,
                                        reads=[("pt", half)], writes=["hs"])
                                else:
                                    S.op("dve", "tensor_copy",
                                        out=hs[:, half * 4:half * 4 + 4, j * 128:j * 128 + nb],
                                        in_=pp[:].rearrange("p (q c) -> p q c", q=4)[:, :, 0:nb],
                                        reads=[("pt", half)], writes=["hs"])
                            blk += 1
                        store_h(hs, s, t0, n)
                S.barrier()

        def phase_out():
            with ExitStack() as st:
                hs = sb("o_hs", [128, DT, 512], F32, st)
                xn = sb("o_xn", [128, DT, 512], F32, st)
                sqb = [sb("o_sq%d" % i, [128, 512], BF16, st) for i in range(2)]
                rstd = sb("o_rstd", [128, 512], F32, st)
                ot = [sb("o_ot%d" % i, [128, D], F32, st) for i in range(2)]
                pss = ps("o_pss", [128, 512], F32, st)
                pt = [ps("o_pt%d" % i, [128, 512], F32, st) for i in range(2)]
                blk = 0
                for s in range(NS):
                    for (t0, n) in cfg.tiles:
                        load_h(hs, s, t0, n)
                        rmsnorm(hs, xn, sqb, pss, rstd, cF, n)
                        for j in range((n + 127) // 128):
                            tb = t0 + j * 128
                            nb = min(128, t0 + n - tb)
                            ob = ot[blk % 2]
                            orr = ("ot", blk % 2)
                            for half in range(2):
                                pp = pt[half]
                                for q in range(4):
                                    dt = half * 4 + q
                                    S.op("pe", "transpose",
                                        out=pp[0:nb, q * 128:(q + 1) * 128], in_=xn[:, dt, j * 128:j * 128 + nb],
                                        identity=ident_f[:, :],
                                        reads=["xn", "ident_f"], writes=[("pt", half)])
                                if half == 0:
                                    S.op("act", "activation",
                                        out=ob[0:nb, 0:512], in_=pp[0:nb, :], func=AF.Copy,
                                        reads=[("pt", half)], writes=[orr])
                                else:
                                    S.op("dve", "tensor_copy",
                                        out=ob[0:nb, 512:1024], in_=pp[0:nb, :],
                                        reads=[("pt", half)], writes=[(orr, 1)])
                            lo = NMETA if tb == 0 else 0
                            S.dma("sp",
                                out=out[s, tb + lo - NMETA:tb + nb - NMETA, :], in_=ob[lo:nb, :],
                                reads=[orr, (orr, 1)])
                            blk += 1
                S.barrier()

        def phase_ffn(l, which):
            pre = "ffn1" if which == 0 else "ffn2"
            w13 = P[pre + "_w13"][l]
            w2 = P[pre + "_w2"][l]
            c0 = gcol(0 if which == 0 else 2, l)
            with ExitStack() as st:
                w13s = sb("f_w13", [128, DT, 2 * DFF], BF16, st)
                w2s = sb("f_w2", [128, FT, D], BF16, st)
                hs = sb("f_hs", [128, DT, 512], F32, st)
                xn = sb("f_xn", [128, DT, 512], BF16, st)
                sqb = [sb("f_sq%d" % i, [128, 512], BF16, st) for i in range(2)]
                rstd = sb("f_rstd", [128, 512], F32, st)
                gh = sb("f_g", [128, FT, 512], BF16, st)
                sa = [sb("f_sa%d" % i, [128, 512], F32, st) for i in range(2)]
                pss = ps("f_pss", [128, 512], F32, st)
                pa = [ps("f_pa%d" % i, [128, 512], F32, st) for i in range(2)]
                pb = [ps("f_pb%d" % i, [128, 512], F32, st) for i in range(2)]
                po = [ps("f_po%d" % i, [128, 512], F32, st) for i in range(2)]
                for dt in range(DT):
                    load_w_cast(w13s[:, dt, :], w13[dt * 128:(dt + 1) * 128, :], ("w13", dt))
                for f in range(FT):
                    load_w_cast(w2s[:, f, :], w2[f * 128:(f + 1) * 128, :], ("w2", f))
                for s in range(NS):
                    for (t0, n) in cfg.tiles:
                        load_h(hs, s, t0, n)
                        rmsnorm(hs, xn, sqb, pss, rstd, c0, n)
                        for f in range(FT):
                            b = f % 2
                            for dt in range(DT):
                                S.op("pe", "matmul",
                                    pa[b][:, 0:n], lhsT=w13s[:, dt, f * 128:(f + 1) * 128], rhs=xn[:, dt, 0:n],
                                    start=(dt == 0), stop=(dt == DT - 1),
                                    reads=[("w13", dt), "xn"], writes=[("pa", b)])
                            for dt in range(DT):
                                S.op("pe", "matmul",
                                    pb[b][:, 0:n], lhsT=w13s[:, dt, DFF + f * 128:DFF + (f + 1) * 128],
                                    rhs=xn[:, dt, 0:n], start=(dt == 0), stop=(dt == DT - 1),
                                    reads=[("w13", dt), "xn"], writes=[("pb", b)])
                            S.op("act", "activation", out=sa[b][:, 0:n], in_=pa[b][:, 0:n],
                                                                    func=AF.Silu,
                                 reads=[("pa", b)], writes=[("sa", b)])
                            S.op("dve", "tensor_tensor",
                                out=gh[:, f, 0:n], in0=pb[b][:, 0:n], in1=sa[b][:, 0:n], op=ALU.mult,
                                reads=[("pb", b), ("sa", b)], writes=[("gh", f)])
                        for o in range(DT):
                            b = o % 2
                            for f in range(FT):
                                S.op("pe", "matmul",
                                    po[b][:, 0:n], lhsT=w2s[:, f, o * 128:(o + 1) * 128], rhs=gh[:, f, 0:n],
                                    start=(f == 0), stop=(f == FT - 1),
                                    reads=[("w2", f), ("gh", f)], writes=[("po", b)])
                            import os
                            if os.environ.get("DBG") == "po":
                                S.op("dve", "tensor_copy", out=hs[:, o, 0:n], in_=po[b][:, 0:n],
                                     reads=[("po", b), "hs"], writes=["hs"])
                            elif os.environ.get("DBG") == "xn":
                                S.op("dve", "tensor_copy", out=hs[:, o, 0:n], in_=xn[:, o, 0:n],
                                     reads=[("po", b), "hs", "xn"], writes=["hs"])
                            elif os.environ.get("DBG") == "gh":
                                S.op("dve", "tensor_copy", out=hs[:, o, 0:n], in_=gh[:, o, 0:n],
                                     reads=[("po", b), "hs", "xn", ("gh", o)], writes=["hs"])
                            else:
                              S.op("dve", "scalar_tensor_tensor",
                                out=hs[:, o, 0:n], in0=po[b][:, 0:n], scalar=0.5, in1=hs[:, o, 0:n],
                                op0=ALU.mult, op1=ALU.add,
                                reads=[("po", b), "hs"], writes=["hs"])
                        store_h(hs, s, t0, n)
                S.barrier()


        def phase_mixA(l):
            w_in = P["w_in"][l]
            c0 = gcol(1, l)
            with ExitStack() as st:
                wA = sb("a_w", [128, DT, 2048], BF16, st)
                hs = sb("a_hs", [128, DT, 512], F32, st)
                xn = sb("a_xn", [128, DT, 512], BF16, st)
                sqb = [sb("a_sq%d" % i, [128, 512], BF16, st) for i in range(2)]
                rstd = sb("a_rstd", [128, 512], F32, st)
                stg = [sb("a_stg%d" % i, [128, 512], BF16, st) for i in range(4)]
                pss = ps("a_pss", [128, 512], F32, st)
                pp = [ps("a_pp%d" % i, [128, 512], F32, st) for i in range(4)]
                for dt in range(DT):
                    load_w_cast(wA[:, dt, 0:512], w_in[dt * 128:(dt + 1) * 128, 0:512], ("wA", dt))
                    load_w_cast(wA[:, dt, 512:2048], w_in[dt * 128:(dt + 1) * 128, 1536:3072], ("wA", dt, 1))
                cnt = 0
                for s in range(NS):
                    for (t0, n) in cfg.tiles:
                        load_h(hs, s, t0, n)
                        rmsnorm(hs, xn, sqb, pss, rstd, c0, n)
                        nk = n // 8
                        k0 = t0 // 8
                        for f in range(12):
                            b = cnt % 4
                            cnt += 1
                            pb = pp[b]
                            sg = stg[b]
                            for dt in range(DT):
                                S.op("pe", "matmul", pb[:, 0:n], lhsT=wA[:, dt, f * 128:(f + 1) * 128],
                                     rhs=xn[:, dt, 0:n], start=(dt == 0), stop=(dt == DT - 1),
                                     reads=[("wA", dt), ("wA", dt, 1), "xn"], writes=[("pp", b)])
                            if f < 4:
                                S.op("act", "activation",
                                     out=sg[:, 0:n].rearrange("p (s k) -> p s k", s=8),
                                     in_=pb[:, 0:n].rearrange("p (k s) -> p s k", s=8), func=AF.Copy,
                                     reads=[("pp", b)], writes=[("stg", b)])
                                for gl in range(8):
                                    S.dma("sp", out=Usc[s, f * 8 + gl, :, :, k0:k0 + nk].rearrange("s c k -> c s k"),
                                          in_=sg[gl * 16:(gl + 1) * 16, 0:n].rearrange("p (s k) -> p s k", s=8),
                                          reads=[("stg", b)])
                            else:
                                if f % 2 == 0:
                                    S.op("act", "activation", out=sg[:, 0:n], in_=pb[:, 0:n], func=AF.Copy,
                                         reads=[("pp", b)], writes=[("stg", b)])
                                else:
                                    S.op("dve", "tensor_copy", out=sg[:, 0:n], in_=pb[:, 0:n],
                                         reads=[("pp", b)], writes=[("stg", b)])
                                dst = qT if f < 8 else kT
                                r0 = (f % 4) * 128
                                S.dma("sp", out=dst[s, r0:r0 + 128, t0:t0 + n], in_=sg[:, 0:n],
                                      reads=[("stg", b)])
                        for j in range((n + 127) // 128):
                            nb = min(128, n - j * 128)
                            b = cnt % 4
                            cnt += 1
                            pb = pp[b]
                            sg = stg[b]
                            for dt in range(DT):
                                S.op("pe", "matmul", pb[0:nb, 0:512], lhsT=xn[:, dt, j * 128:j * 128 + nb],
                                     rhs=wA[:, dt, 1536:2048], start=(dt == 0), stop=(dt == DT - 1),
                                     reads=[("wA", dt), ("wA", dt, 1), "xn"], writes=[("pp", b)])
                            S.op("dve", "tensor_copy", out=sg[0:nb, :], in_=pb[0:nb, :],
                                 reads=[("pp", b)], writes=[("stg", b)])
                            S.dma("sp", out=vS[s, t0 + j * 128:t0 + j * 128 + nb, :], in_=sg[0:nb, :],
                                  reads=[("stg", b)])
                S.barrier()

        def phase_att(l):
            nkb, LP = cfg.nkb, cfg.LP
            tail = L - (nkb - 1) * 128
            with ExitStack() as st:
                triT = sb("t_tri", [128, 128], BF16, st)
                sel0 = sb("t_sel0", [128, 128], BF16, st)
                onec = sb("t_onec", [128, 1], F32, st)
                masks = [sb("t_mask%d" % i, [128, 512], F32, st) for i in range(4)]
                kT2 = [sb("t_k%d" % i, [128, LP], BF16, st) for i in range(2)]
                qz = [[sb("t_q%d_%d" % (i, h), [128, L], BF16, st) for h in range(2)] for i in range(2)]
                vz = [[sb("t_v%d_%d" % (i, h), [128, nkb, 128], BF16, st) for h in range(2)] for i in range(2)]
                NB3 = 3
                e_t = [sb("t_e%d" % i, [128, 512], F32, st) for i in range(NB3)]
                sp_t = [sb("t_sp%d" % i, [128, 512], BF16, st) for i in range(NB3)]
                g_t = [sb("t_g%d" % i, [128, 512], F32, st) for i in range(NB3)]
                w_t = [sb("t_w%d" % i, [128, 512], BF16, st) for i in range(NB3)]
                chi = [sb("t_chi%d" % i, [128, 512], BF16, st) for i in range(2)]
                clo = [sb("t_clo%d" % i, [128, 512], BF16, st) for i in range(2)]
                ob = [sb("t_ob%d" % i, [128, 512], BF16, st) for i in range(2)]
                pz = [ps("t_pz%d" % i, [128, 512], F32, st) for i in range(2)]
                pcs = [ps("t_pcs%d" % i, [128, 512], F32, st) for i in range(2)]
                po = [ps("t_po%d" % i, [128, 512], F32, st) for i in range(2)]
                S.op("pool", "memset", triT[:], 1.0, writes=["triT"])
                S.op("pool", "affine_select", out=triT[:], in_=triT[:], pattern=[[-1, 128]],
                     compare_op=ALU.is_ge, fill=0.0, base=0, channel_multiplier=1,
                     reads=["triT"], writes=["triT"])
                S.op("pool", "memset", sel0[:], 1.0, writes=["sel0"])
                S.op("pool", "affine_select", out=sel0[:], in_=sel0[:], pattern=[[0, 128]],
                     compare_op=ALU.is_equal, fill=0.0, base=0, channel_multiplier=1,
                     reads=["sel0"], writes=["sel0"])
                S.op("pool", "memset", onec[:], 1.0, writes=["onec"])
                for i in range(4):
                    S.op("pool", "memset", masks[i][:], 1.0, writes=[("mask", i)])
                    S.op("pool", "affine_select", out=masks[i][:], in_=masks[i][:], pattern=[[1, 512]],
                         compare_op=ALU.is_gt, fill=0.0, base=-128 * i, channel_multiplier=-1,
                         reads=[("mask", i)], writes=[("mask", i)])
                for i in range(2):
                    S.op("pool", "memset", kT2[i][:], 0.0, writes=[("kT2", i)])
                    S.op("pool", "memset", chi[i][:], 0.0, writes=[("chi", i)])
                    S.op("pool", "memset", clo[i][:], 0.0, writes=[("clo", i)])
                    for h in range(2):
                        S.op("pool", "memset", qz[i][h][:], 0.0, writes=[("qz", i, h)])
                        S.op("pool", "memset", vz[i][h][:], 0.0, writes=[("vz", i, h), ("vz", i, h, 1)])
                it = 0
                blk = 0
                qcnt = 0
                for s in range(NS):
                    for hp in range(4):
                        bb = it % 2
                        it += 1
                        S.dma("sp", out=kT2[bb][:, 0:L], in_=kT[s, hp * 128:(hp + 1) * 128, :],
                              reads=[], writes=[("kT2", bb)])
                        for h in range(2):
                            r0 = hp * 128 + h * 64
                            S.dma("sp", out=qz[bb][h][h * 64:(h + 1) * 64, :], in_=qT[s, r0:r0 + 64, :],
                                  writes=[("qz", bb, h)])
                            if nkb > 1:
                                S.dma("sp", out=vz[bb][h][:, 0:nkb - 1, h * 64:(h + 1) * 64],
                                      in_=vS[s, 0:(nkb - 1) * 128, r0:r0 + 64].rearrange("(b p) d -> p b d", p=128),
                                      writes=[("vz", bb, h)])
                            S.dma("sp", out=vz[bb][h][0:tail, nkb - 1, h * 64:(h + 1) * 64],
                                  in_=vS[s, (nkb - 1) * 128:L, r0:r0 + 64],
                                  writes=[("vz", bb, h, 1)])
                        for (q0, nq) in cfg.tiles:
                            qb = qcnt % 2
                            qcnt += 1
                            nblk = (q0 + nq - 1 + 127) // 128
                            nblk = max(nblk, 1)
                            first_pv = True
                            for h in range(2):
                                for bi, kb in enumerate(reversed(range(nblk))):
                                    i3 = blk % NB3
                                    i2 = blk % 2
                                    blk += 1
                                    diag = (kb * 128 + 128 > q0)
                                    S.op("pe", "matmul", pz[i2][:, 0:nq], lhsT=kT2[bb][:, kb * 128:(kb + 1) * 128],
                                         rhs=qz[bb][h][:, q0:q0 + nq], start=True, stop=True,
                                         reads=[("kT2", bb), ("qz", bb, h)], writes=[("pz", i2)])
                                    S.op("act", "activation", out=e_t[i3][:, 0:nq], in_=pz[i2][:, 0:nq],
                                         func=AF.Exp, scale=0.125,
                                         reads=[("pz", i2)], writes=[("e", i3)])
                                    if diag:
                                        mi = (kb * 128 - q0) // 128
                                        assert 0 <= mi < 4
                                        S.op("pool", "tensor_tensor", out=e_t[i3][:, 0:nq], in0=e_t[i3][:, 0:nq],
                                             in1=masks[mi][:, 0:nq], op=ALU.mult,
                                             reads=[("e", i3), ("mask", mi)], writes=[("e", i3)])
                                    S.op("act", "activation", out=sp_t[i3][:, 0:nq], in_=e_t[i3][:, 0:nq],
                                         func=AF.Ln, bias=onec[:], scale=1.0,
                                         reads=[("e", i3), "onec"], writes=[("sp", i3)])
                                    S.op("pe", "matmul", pcs[i2][:, 0:nq], lhsT=triT[:], rhs=sp_t[i3][:, 0:nq],
                                         start=True, stop=(bi == 0),
                                         reads=["triT", ("sp", i3)], writes=[("pcs", i2)])
                                    if bi > 0:
                                        S.op("pe", "matmul", pcs[i2][:, 0:nq], lhsT=sel0[:], rhs=chi[i2][:, 0:nq],
                                             start=False, stop=False,
                                             reads=["sel0", ("chi", i2)], writes=[("pcs", i2)])
                                        S.op("pe", "matmul", pcs[i2][:, 0:nq], lhsT=sel0[:], rhs=clo[i2][:, 0:nq],
                                             start=False, stop=True,
                                             reads=["sel0", ("clo", i2)], writes=[("pcs", i2)])
                                    if bi < nblk - 1:
                                        n2 = (i2 + 1) % 2
                                        S.op("dve", "tensor_copy", out=chi[n2][0:1, 0:nq], in_=pcs[i2][0:1, 0:nq],
                                             excl=[("pcs", i2)], writes=[("chi", n2)])
                                        S.op("dve", "tensor_tensor", out=clo[n2][0:1, 0:nq], in0=pcs[i2][0:1, 0:nq],
                                             in1=chi[n2][0:1, 0:nq], op=ALU.subtract,
                                             excl=[("pcs", i2)], reads=[("chi", n2)], writes=[("clo", n2)])
                                    S.op("act", "activation", out=g_t[i3][:, 0:nq], in_=pcs[i2][:, 0:nq],
                                         func=AF.Exp, scale=-1.0,
                                         excl=[("pcs", i2)], writes=[("g", i3)])
                                    S.op("dve", "tensor_tensor", out=w_t[i3][:, 0:nq], in0=e_t[i3][:, 0:nq],
                                         in1=g_t[i3][:, 0:nq], op=ALU.mult,
                                         reads=[("e", i3), ("g", i3)], writes=[("w", i3)])
                                    last_pv = (h == 1 and bi == nblk - 1)
                                    S.op("pe", "matmul", po[qb][:, 0:nq], lhsT=vz[bb][h][:, kb, :],
                                         rhs=w_t[i3][:, 0:nq], start=first_pv, stop=last_pv,
                                         reads=[("vz", bb, h), ("vz", bb, h, 1), ("w", i3)], writes=[("po", qb)])
                                    first_pv = False
                            S.op("act", "activation", out=ob[qb][:, 0:nq], in_=po[qb][:, 0:nq], func=AF.Copy,
                                 reads=[("po", qb)], writes=[("ob", qb)])
                            S.dma("sp", out=oattT[s, hp * 128:(hp + 1) * 128, q0:q0 + nq], in_=ob[qb][:, 0:nq],
                                  reads=[("ob", qb)])
                S.barrier()

        def phase_ssm(l):
            NKc = cfg.NK
            if NKc <= 512:
                halves = [(0, NKc)]
            else:
                halves = [(0, NKc // 2), (NKc // 2, NKc - NKc // 2)]
            TWO_PI = 2.0 * math.pi
            with ExitStack() as st:
                def t32(name, shape=(128, 32), dt=F32):
                    return sb("s_" + name, list(shape), dt, st)
                lr, li, dtb, ar, ft = t32("lr"), t32("li"), t32("dtb"), t32("ar"), t32("ft")
                yp, ti, tf, tt, sn, cs, mag = (t32("yp"), t32("ti", dt=I32), t32("tf"), t32("tt"),
                                               t32("sn"), t32("cs"), t32("mag"))
                pwr = {p: t32("pwr%d" % (p + 7)) for p in range(-7, 9)}
                pwi = {p: t32("pwi%d" % (p + 7)) for p in range(-7, 9)}
                rho8, f8, cre, cim, t_a, t_b, dcol = (t32("rho8"), t32("f8"), t32("cre"), t32("cim"),
                                                      t32("ta"), t32("tb"), t32("dcol"))
                Bre, Bim = t32("Bre", (128, 32, 16)), t32("Bim", (128, 32, 16))
                bbr, bbi = t32("bbr", (128, 32, 16)), t32("bbi", (128, 32, 16))
                Craw = [t32("Craw%d" % i, (128, 4, 128)) for i in range(2)]
                Cn = [t32("Cn%d" % i, (128, 32, 16)) for i in range(2)]
                q1, q2, q3, q4 = (t32("q%d" % i, (128, 32, 16)) for i in range(4))
                X1, X1s, X2, X2s, X1p, X2p = (t32("X%d" % i, (128, 32, 8, 16)) for i in range(6))
                W1, W1s, W2, W2s, W3 = (t32("W%d" % i, (128, 32, 128), BF16) for i in range(5))
                maskBL = t32("maskBL", (128, 128))
                w3t = t32("w3t", (128, 128))
                w3r = [t32("w3r%d" % i, (128, 128)) for i in range(2)]
                kidx = t32("kidx", (128, NKc))
                cT = [t32("cT%d" % i, (128, NKc)) for i in range(2)]
                sT = [t32("sT%d" % i, (128, NKc)) for i in range(2)]
                rtab = [t32("rtab%d" % i, (128, NKc)) for i in range(2)]
                yk = t32("yk", (128, NKc))
                ki = t32("ki", (128, NKc), I32)
                kidx_i = ki
                kf = t32("kf", (128, NKc))
                kt = t32("kt", (128, NKc))
                Ut = [t32("U%d" % i, (128, NKc), BF16) for i in range(2)]
                m1, m2, Tt = t32("m1", (128, NKc)), t32("m2", (128, NKc)), t32("Tt", (128, NKc))
                Pm = [t32("Pm%d" % i, (128, NKc + 2), BF16) for i in range(2)]
                Qm = [t32("Qm%d" % i, (128, NKc + 2), BF16) for i in range(2)]
                yo1 = t32("yo", (128, NKc))
                yo = [yo1, yo1]
                pw = [ps("s_pw%d" % i, [128, 512], F32, st) for i in range(2)]
                pL = ps("s_pL", [128, 2, 512], F32, st)
                pLs = ps("s_pLs", [128, 2, 512], F32, st)
                py = ps("s_py", [128, 2, 512], F32, st)

                for hf in (0, 64):
                    S.dma("sp", out=lr[hf:hf + 64, :], in_=P["ssm_lam_re"][l].rearrange("g n -> n g"),
                          allow_slow_non_contiguous=True, writes=["lr"])
                    S.dma("sp", out=li[hf:hf + 64, :], in_=P["ssm_lam_im"][l].rearrange("g n -> n g"),
                          allow_slow_non_contiguous=True, writes=["li"])
                    S.dma("sp", out=dtb[hf:hf + 64, :], in_=P["ssm_log_dt"][l].partition_broadcast(64),
                          writes=["dtb"])
                    S.dma("sp", out=Bre[hf:hf + 64, :, :], in_=P["ssm_b_re"][l].rearrange("g n c -> n g c"),
                          writes=["Bre"])
                    S.dma("sp", out=Bim[hf:hf + 64, :, :], in_=P["ssm_b_im"][l].rearrange("g n c -> n g c"),
                          writes=["Bim"])
                for i, nm in enumerate(["ssm_c_re", "ssm_c_im"]):
                    for dup in range(2):
                        S.dma("sp", out=Craw[i][:, :, dup * 64:(dup + 1) * 64],
                              in_=P[nm][l].rearrange("g c n -> (g c) n").rearrange("(t p) n -> p t n", p=128),
                              writes=[("Craw", i)])
                for s8 in range(8):
                    S.dma("sp", out=dcol[s8 * 16:(s8 + 1) * 16, :], in_=P["ssm_d"][l].rearrange("(g c) -> c g", c=16),
                          allow_slow_non_contiguous=True, writes=["dcol"])
                S.op("pool", "memset", maskBL[:], 1.0, writes=["maskBL"])
                S.op("pool", "affine_select", out=maskBL[:].rearrange("p (j c) -> p j c", c=16),
                     in_=maskBL[:].rearrange("p (j c) -> p j c", c=16), pattern=[[16, 8], [0, 16]],
                     compare_op=ALU.is_ge, fill=0.0, base=15, channel_multiplier=-1,
                     reads=["maskBL"], writes=["maskBL"])
                S.op("pool", "iota", kidx_i[:], pattern=[[1, NKc]], base=0, channel_multiplier=0,
                     writes=["kti"])
                S.op("pool", "tensor_copy", out=kidx[:], in_=kidx_i[:], reads=["kti"], writes=["kidx"])
                for i in range(2):
                    S.op("pool", "memset", Pm[i][:], 0.0, writes=[("Pm", i)])
                    S.op("pool", "memset", Qm[i][:], 0.0, writes=[("Qm", i)])

                if int(os.environ.get("SSMSTOP", "9")) <= 1:
                    S.barrier()
                    return
                def V(eng, meth, *a, r=(), w=(), **kw):
                    S.op(eng, meth, *a, reads=list(r), writes=list(w), **kw)

                def sin_turns(dst, dname, y, yname, ti_, tf_, tt_, pre):
                    V("dve", "tensor_copy", out=ti_, in_=y, r=[yname], w=[pre + "ti"])
                    V("dve", "tensor_copy", out=tf_, in_=ti_, r=[pre + "ti"], w=[pre + "tf"])
                    V("dve", "tensor_tensor", out=tf_, in0=y, in1=tf_, op=ALU.subtract, r=[yname, pre + "tf"], w=[pre + "tf"])
                    V("dve", "tensor_single_scalar", out=tt_, in_=tf_, scalar=0.5, op=ALU.is_gt, r=[pre + "tf"], w=[pre + "tt"])
                    V("dve", "tensor_tensor", out=tf_, in0=tf_, in1=tt_, op=ALU.subtract, r=[pre + "tf", pre + "tt"], w=[pre + "tf"])
                    V("dve", "tensor_single_scalar", out=tt_, in_=tf_, scalar=-0.5, op=ALU.is_lt, r=[pre + "tf"], w=[pre + "tt"])
                    V("dve", "tensor_tensor", out=tf_, in0=tf_, in1=tt_, op=ALU.add, r=[pre + "tf", pre + "tt"], w=[pre + "tf"])
                    if dst is not None:
                        V("act", "activation", out=dst, in_=tf_, func=AF.Sin, scale=6.283185, r=[pre + "tf"], w=[dname])

                V("act", "activation", out=dtb[:], in_=dtb[:], func=AF.Exp, r=["dtb"], w=["dtb"])
                V("dve", "tensor_tensor", out=ar[:], in0=lr[:], in1=dtb[:], op=ALU.mult, r=["lr", "dtb"], w=["ar"])
                V("dve", "tensor_tensor", out=ft[:], in0=li[:], in1=dtb[:], op=ALU.mult, r=["li", "dtb"], w=["ft"])
                V("dve", "tensor_scalar_mul", out=ft[:], in0=ft[:], scalar1=1.0 / TWO_PI, r=["ft"], w=["ft"])
                for p in range(-7, 9):
                    V("act", "activation", out=mag[:], in_=ar[:], func=AF.Exp, scale=float(p), r=["ar"], w=["mag"])
                    if p == 8:
                        V("dve", "tensor_copy", out=rho8[:], in_=mag[:], r=["mag"], w=["rho8"])
                    V("dve", "tensor_scalar_mul", out=yp[:], in0=ft[:], scalar1=float(p), r=["ft"], w=["yp"])
                    sin_turns(sn[:], "sn", yp[:], "yp", ti[:], tf[:], tt[:], "a")
                    if p == 8:
                        V("dve", "tensor_copy", out=f8[:], in_=tf[:], r=["atf"], w=["f8"])
                    V("dve", "tensor_scalar_add", out=yp[:], in0=yp[:], scalar1=0.25, r=["yp"], w=["yp"])
                    sin_turns(cs[:], "cs", yp[:], "yp", ti[:], tf[:], tt[:], "a")
                    V("dve", "tensor_tensor", out=pwr[p][:], in0=mag[:], in1=cs[:], op=ALU.mult, r=["mag", "cs"], w=[("pwr", p)])
                    V("dve", "tensor_tensor", out=pwi[p][:], in0=mag[:], in1=sn[:], op=ALU.mult, r=["mag", "sn"], w=[("pwi", p)])
                if int(os.environ.get("SSMSTOP", "9")) <= 2:
                    S.barrier()
                    return
                V("dve", "tensor_scalar_add", out=t_a[:], in0=pwr[1][:], scalar1=-1.0, r=[("pwr", 1)], w=["ta"])
                V("dve", "tensor_tensor", out=cre[:], in0=t_a[:], in1=lr[:], op=ALU.mult, r=["ta", "lr"], w=["cre"])
                V("dve", "tensor_tensor", out=t_b[:], in0=pwi[1][:], in1=li[:], op=ALU.mult, r=[("pwi", 1), "li"], w=["tb"])
                V("dve", "tensor_tensor", out=cre[:], in0=cre[:], in1=t_b[:], op=ALU.add, r=["cre", "tb"], w=["cre"])
                V("dve", "tensor_tensor", out=cim[:], in0=pwi[1][:], in1=lr[:], op=ALU.mult, r=[("pwi", 1), "lr"], w=["cim"])
                V("dve", "tensor_tensor", out=t_b[:], in0=t_a[:], in1=li[:], op=ALU.mult, r=["ta", "li"], w=["tb"])
                V("dve", "tensor_tensor", out=cim[:], in0=cim[:], in1=t_b[:], op=ALU.subtract, r=["cim", "tb"], w=["cim"])
                V("dve", "tensor_tensor", out=t_a[:], in0=lr[:], in1=lr[:], op=ALU.mult, r=["lr"], w=["ta"])
                V("dve", "tensor_tensor", out=t_b[:], in0=li[:], in1=li[:], op=ALU.mult, r=["li"], w=["tb"])
                V("dve", "tensor_tensor", out=t_a[:], in0=t_a[:], in1=t_b[:], op=ALU.add, r=["ta", "tb"], w=["ta"])
                V("dve", "reciprocal", out=t_a[:], in_=t_a[:], r=["ta"], w=["ta"])
                V("dve", "tensor_tensor", out=cre[:], in0=cre[:], in1=t_a[:], op=ALU.mult, r=["cre", "ta"], w=["cre"])
                V("dve", "tensor_tensor", out=cim[:], in0=cim[:], in1=t_a[:], op=ALU.mult, r=["cim", "ta"], w=["cim"])

                def bc(t):
                    return t[:, :].unsqueeze(2).broadcast_to([128, 32, 16])

                def cplx(fr, fi, frn, fin, xr, xi, xrn, xin):
                    V("dve", "tensor_tensor", out=q1[:], in0=xr[:], in1=bc(fr), op=ALU.mult, r=[xrn, frn, "q1"], w=["q1"])
                    V("dve", "tensor_tensor", out=q3[:], in0=xi[:], in1=bc(fi), op=ALU.mult, r=[xin, fin, "q3"], w=["q3"])
                    V("dve", "tensor_tensor", out=q1[:], in0=q1[:], in1=q3[:], op=ALU.subtract, r=["q1", "q3"], w=["q1"])
                    V("dve", "tensor_tensor", out=q2[:], in0=xi[:], in1=bc(fr), op=ALU.mult, r=[xin, frn, "q2"], w=["q2"])
                    V("dve", "tensor_tensor", out=q4[:], in0=xr[:], in1=bc(fi), op=ALU.mult, r=[xrn, fin, "q4"], w=["q4"])
                    V("dve", "tensor_tensor", out=q2[:], in0=q2[:], in1=q4[:], op=ALU.add, r=["q2", "q4"], w=["q2"])

                def put(dst, dname, blk, lo_src, lo_sign, up_src, up_sign):
                    for (rows, src, sign, eng) in ((slice(0, 64), lo_src, lo_sign, "act"),
                                                   (slice(64, 128), up_src, up_sign, "pool")):
                        srcn = "q1" if src is q1 else "q2"
                        if eng == "act":
                            V("act", "activation", out=dst[rows, :, blk, :], in_=src[rows, :, :], func=AF.Copy,
                              scale=float(sign), r=[srcn], w=[dname])
                        else:
                            V("pool", "tensor_scalar", out=dst[rows, :, blk, :], in0=src[rows, :, :],
                              scalar1=float(sign), scalar2=None, op0=ALU.mult, r=[srcn], w=[dname])

                if int(os.environ.get("SSMSTOP", "9")) <= 3:
                    S.barrier()
                    return
                cplx(cre, cim, "cre", "cim", Bre, Bim, "Bre", "Bim")
                V("dve", "tensor_copy", out=bbr[:], in_=q1[:], r=["q1"], w=["bbr"])
                V("dve", "tensor_copy", out=bbi[:], in_=q2[:], r=["q2"], w=["bbi"])
                for i in range(2):
                    for t4 in range(4):
                        pb = pw[(i * 4 + t4) % 2]
                        V("pe", "transpose", out=pb[:, 0:128], in_=Craw[i][:, t4, :], identity=ident_f[:, :],
                          r=[("Craw", i), "ident_f"], w=[("pw", (i * 4 + t4) % 2)])
                        V("act", "activation", out=Cn[i][:, t4 * 8:(t4 + 1) * 8, :],
                          in_=pb[:, 0:128].rearrange("p (g c) -> p g c", c=16), func=AF.Copy,
                          r=[("pw", (i * 4 + t4) % 2)], w=[("Cn", i)])
                for s8 in range(8):
                    p = 7 - s8
                    cplx(pwr[p], pwi[p], ("pwr", p), ("pwi", p), bbr, bbi, "bbr", "bbi")
                    put(X1, "X1", s8, q1, 1, q2, 1)
                    put(X1s, "X1s", s8, q2, 1, q1, -1)
                    p = -s8
                    cplx(pwr[p], pwi[p], ("pwr", p), ("pwi", p), bbr, bbi, "bbr", "bbi")
                    put(X1p, "X1p", s8, q1, 1, q2, 1)
                    p = s8 + 1
                    cplx(pwr[p], pwi[p], ("pwr", p), ("pwi", p), Cn[0], Cn[1], ("Cn", 0), ("Cn", 1))
                    put(X2, "X2", s8, q1, 1, q2, -1)
                    put(X2s, "X2s", s8, q2, -1, q1, -1)
                    p = s8
                    cplx(pwr[p], pwi[p], ("pwr", p), ("pwi", p), Cn[0], Cn[1], ("Cn", 0), ("Cn", 1))
                    put(X2p, "X2p", s8, q1, 1, q2, -1)
                if int(os.environ.get("SSMSTOP", "9")) <= 4:
                    S.barrier()
                    return
                V("act", "activation", out=W2[:].rearrange("p g x -> p (g x)"),
                  in_=X2[:].rearrange("p g s c -> p (g s c)"), func=AF.Copy, r=["X2"], w=["W2"])
                V("dve", "tensor_copy", out=W2s[:].rearrange("p g x -> p (g x)"),
                  in_=X2s[:].rearrange("p g s c -> p (g s c)"), r=["X2s"], w=["W2s"])
                for g in range(NG):
                    b = g % 2
                    V("pe", "transpose", out=pw[b][:, 0:128], in_=X1[:, g].rearrange("p s c -> p (s c)"),
                      identity=ident_f[:, :], r=["X1", "ident_f"], w=[("pw", b)])
                    V("pe", "transpose", out=pw[b][:, 128:256], in_=X1s[:, g].rearrange("p s c -> p (s c)"),
                      identity=ident_f[:, :], r=["X1s", "ident_f"], w=[("pw", b)])
                    V("pe", "matmul", pw[b][:, 256:384], lhsT=X1p[:, g].rearrange("p s c -> p (s c)"),
                      rhs=X2p[:, g].rearrange("p s c -> p (s c)"), start=True, stop=True,
                      r=["X1p", "X2p"], w=[("pw", b)])
                    V("act", "activation", out=W1[:, g, :], in_=pw[b][:, 0:128], func=AF.Copy,
                      r=[("pw", b)], w=["W1"])
                    V("act", "activation", out=W1s[:, g, :], in_=pw[b][:, 128:256], func=AF.Copy,
                      r=[("pw", b)], w=["W1s"])
                    V("act", "activation", out=w3r[b][:], in_=pw[b][:, 256:384], func=AF.Copy,
                      r=[("pw", b)], w=[("w3r", b)])
                    V("dve", "tensor_tensor", out=w3t[:], in0=w3r[b][:], in1=maskBL[:], op=ALU.mult,
                      r=[("w3r", b), "maskBL"], w=["w3t"])
                    V("dve", "scalar_tensor_tensor", out=W3[:, g, :], in0=ident_f[:], scalar=dcol[:, g:g + 1],
                      in1=w3t[:], op0=ALU.mult, op1=ALU.add, r=["ident_f", "dcol", "w3t"], w=["W3"])

                if int(os.environ.get("SSMSTOP", "9")) <= 5:
                    S.barrier()
                    return
                it = 0
                for g in range(NG):
                    gb = g % 2
                    V("dve", "tensor_scalar_mul", out=yk[:], in0=kidx[:], scalar1=f8[:, g:g + 1],
                      r=["kidx", "f8"], w=["yk"])
                    sin_turns(sT[gb][:], ("sT", gb), yk[:], "yk", ki[:], kf[:], kt[:], "k")
                    V("dve", "tensor_scalar_add", out=yk[:], in0=yk[:], scalar1=0.25, r=["yk"], w=["yk"])
                    sin_turns(cT[gb][:], ("cT", gb), yk[:], "yk", ki[:], kf[:], kt[:], "k")
                    V("act", "activation", out=rtab[gb][:], in_=kidx[:], func=AF.Identity, scale=0.0,
                      bias=rho8[:, g:g + 1], r=["kidx", "rho8"], w=[("rtab", gb)])
                    RL = int(os.environ.get("RUNLVL", "9"))
                    for s in range(NS):
                        if RL < 2:
                            continue
                        ub = it % 2
                        it += 1
                        S.dma("sp", out=Ut[ub][:], in_=Usc[s, g].rearrange("s c k -> (s c) k"),
                              writes=[("U", ub)])
                        for hi, (lo, wd) in enumerate(halves):
                            V("pe", "matmul", pL[:, hi, 0:wd], lhsT=W1[:, g, :], rhs=Ut[ub][:, lo:lo + wd],
                              start=True, stop=True, r=["W1", ("U", ub)], w=["pL"])
                            V("pe", "matmul", pLs[:, hi, 0:wd], lhsT=W1s[:, g, :], rhs=Ut[ub][:, lo:lo + wd],
                              start=True, stop=True, r=["W1s", ("U", ub)], w=["pLs"])
                        if RL < 3:
                            continue
                        for hi, (lo, wd) in enumerate(halves):
                            V("dve", "tensor_tensor", out=m1[:, lo:lo + wd], in0=pL[:, hi, 0:wd],
                              in1=cT[gb][:, lo:lo + wd], op=ALU.mult, r=["pL", ("cT", gb)], w=["m1"])
                            V("dve", "tensor_tensor", out=m2[:, lo:lo + wd], in0=pLs[:, hi, 0:wd],
                              in1=sT[gb][:, lo:lo + wd], op=ALU.mult, r=["pLs", ("sT", gb)], w=["m2"])
                        V("dve", "tensor_tensor", out=m1[:], in0=m1[:], in1=m2[:], op=ALU.add,
                          r=["m1", "m2"], w=["m1"])
                        V("dve", "tensor_tensor_scan", out=Tt[:], data0=rtab[gb][:], data1=m1[:], initial=0.0,
                          op0=ALU.mult, op1=ALU.add, r=[("rtab", gb), "m1"], w=["Tt"])
                        V("dve", "tensor_tensor", out=Pm[ub][:, 1:NKc + 1], in0=Tt[:], in1=cT[gb][:], op=ALU.mult,
                          r=["Tt", ("cT", gb)], w=[("Pm", ub)])
                        V("pool", "tensor_tensor", out=Qm[ub][:, 1:NKc + 1], in0=Tt[:], in1=sT[gb][:], op=ALU.mult,
                          r=["Tt", ("sT", gb)], w=[("Qm", ub)])
                        if RL < 4:
                            continue
                        for hi, (lo, wd) in enumerate(halves):
                            V("pe", "matmul", py[:, hi, 0:wd], lhsT=W3[:, g, :], rhs=Ut[ub][:, lo:lo + wd],
                              start=True, stop=False, r=["W3", ("U", ub)], w=["py"])
                            V("pe", "matmul", py[:, hi, 0:wd], lhsT=W2[:, g, :],
                              rhs=Pm[ub][:, lo:lo + wd], start=False, stop=False,
                              r=["W2", ("Pm", ub)], w=["py"])
                            V("pe", "matmul", py[:, hi, 0:wd], lhsT=W2s[:, g, :],
                              rhs=Qm[ub][:, lo:lo + wd], start=False, stop=True,
                              r=["W2s", ("Qm", ub)], w=["py"])
                        if RL < 5:
                            continue
                        for hi, (lo, wd) in enumerate(halves):
                            V("act", "activation", out=yo[ub][:, lo:lo + wd], in_=py[:, hi, 0:wd], func=AF.Copy,
                              r=["py"], w=["yo"])
                        S.dma("sp", out=ysS[s, g].rearrange("s c k -> (s c) k"), in_=yo[ub][:],
                              reads=["yo"])
                S.barrier()

        def phase_mixB(l):
            w_in = P["w_in"][l]
            c0 = gcol(1, l)
            HALO = CW - 1
            with ExitStack() as st:
                wB = sb("b_w", [128, DT, 4096], BF16, st)
                wglu = sb("b_wglu", [128, 4, 2048], BF16, st)
                wpw = sb("b_wpw", [128, 4, D], BF16, st)
                wo = sb("b_wo", [128, 4, D], BF16, st)
                wout = sb("b_wout", [128, DT, D], BF16, st)
                cw = sb("b_cw", [128, 4, CW], F32, st)
                cb = sb("b_cb", [128, 4], F32, st)
                lng = sb("b_lng", [128, 4], F32, st)
                lnb = sb("b_lnb", [128, 4], F32, st)
                eps_ln = sb("b_epsln", [128, 1], F32, st)
                hs = sb("b_hs", [128, DT, 512], F32, st)
                xn = sb("b_xn", [128, DT, 512], BF16, st)
                sqb = [sb("b_sq%d" % i, [128, 512], BF16, st) for i in range(2)]
                rstd = sb("b_rstd", [128, 512], F32, st)
                hc = sb("b_hc", [128, 4, HALO + 512], F32, st)
                cacc = sb("b_cacc", [128, 4, 512], F32, st)
                hcv = sb("b_hcv", [128, 4, 512], BF16, st)
                yst = [sb("b_yst%d" % i, [128, 8, 64], F32, st) for i in range(2)]
                gy = sb("b_gy", [128, 4, 512], BF16, st)
                oat = sb("b_oat", [128, 4, 512], BF16, st)
                mrg = sb("b_mrg", [128, DT, 512], BF16, st)
                sg = [sb("b_sg%d" % i, [128, 512], F32, st) for i in range(2)]
                tm = [sb("b_tm%d" % i, [128, 512], F32, st) for i in range(3)]
                macc = sb("b_macc", [128, 512], F32, st)
                mu = sb("b_mu", [128, 512], F32, st)
                lrs = sb("b_lrs", [128, 512], F32, st)
                pss = ps("b_pss", [128, 512], F32, st)
                NPB = 6
                pp = [ps("b_pp%d" % i, [128, 512], F32, st) for i in range(NPB)]
                pcnt = [0]

                def nxt():
                    b = pcnt[0] % NPB
                    pcnt[0] += 1
                    return b

                for dt in range(DT):
                    load_w_cast(wB[:, dt, 0:1024], w_in[dt * 128:(dt + 1) * 128, 512:1536], ("wB", dt))
                    load_w_cast(wB[:, dt, 1024:4096], w_in[dt * 128:(dt + 1) * 128, 3072:6144], ("wB", dt, 1))
                    load_w_cast(wout[:, dt, :], P["w_out"][l][dt * 128:(dt + 1) * 128, :], ("wout", dt))
                for ct in range(4):
                    load_w_cast(wglu[:, ct, :], P["ssm_w_glu"][l][ct * 128:(ct + 1) * 128, :], ("wglu", ct))
                    load_w_cast(wpw[:, ct, :], P["conv_w_out"][l][ct * 128:(ct + 1) * 128, :], ("wpw", ct))
                    load_w_cast(wo[:, ct, :], P["attn_w_o"][l][ct * 128:(ct + 1) * 128, :], ("wo", ct))
                wBr = [("wB", dt) for dt in range(DT)] + [("wB", dt, 1) for dt in range(DT)]
                for ct in range(4):
                    S.dma("sp", out=cw[:, ct, :], in_=P["conv_w"][l][:, ct * 128:(ct + 1) * 128].rearrange("j p -> p j"),
                          allow_slow_non_contiguous=True, writes=["cw"])
                for nm, tl in (("conv_b", cb), ("conv_ln_g", lng), ("conv_ln_b", lnb)):
                    S.dma("sp", out=tl[:], in_=P[nm][l].rearrange("(t p) -> p t", p=128),
                          allow_slow_non_contiguous=True, writes=[nm])
                S.op("pool", "memset", eps_ln[:], LN_EPS, writes=["eps_ln"])

                ycnt = 0
                for s in range(NS):
                    for (t0, n) in cfg.tiles:
                        nk = n // 8
                        k0 = t0 // 8
                        load_h(hs, s, t0, n)
                        S.dma("sp", out=oat[:, :, 0:n],
                              in_=oattT[s, :, t0:t0 + n].rearrange("(t p) n -> p t n", p=128), writes=["oat"])
                        rmsnorm(hs, xn, sqb, pss, rstd, c0, n)
                        if t0 == 0:
                            S.op("pool", "memset", hc[:, :, 0:HALO], 0.0, writes=["hc"])
                        for ct in range(4):
                            ba, bg = nxt(), nxt()
                            for dt in range(DT):
                                S.op("pe", "matmul", pp[ba][:, 0:n], lhsT=wB[:, dt, ct * 128:(ct + 1) * 128],
                                     rhs=xn[:, dt, 0:n], start=(dt == 0), stop=(dt == DT - 1),
                                     reads=wBr + ["xn"], writes=[("pp", ba)])
                            for dt in range(DT):
                                S.op("pe", "matmul", pp[bg][:, 0:n], lhsT=wB[:, dt, 512 + ct * 128:512 + (ct + 1) * 128],
                                     rhs=xn[:, dt, 0:n], start=(dt == 0), stop=(dt == DT - 1),
                                     reads=wBr + ["xn"], writes=[("pp", bg)])
                            S.op("act", "activation", out=sg[ct % 2][:, 0:n], in_=pp[bg][:, 0:n], func=AF.Sigmoid,
                                 reads=[("pp", bg)], writes=[("sg", ct % 2)])
                            S.op("dve", "tensor_tensor", out=hc[:, ct, HALO:HALO + n], in0=pp[ba][:, 0:n],
                                 in1=sg[ct % 2][:, 0:n], op=ALU.mult,
                                 reads=[("pp", ba), ("sg", ct % 2)], writes=["hc"])
                        for ct in range(4):
                            cres = ("cacc", ct)
                            if ct % 2 == 0:
                                S.op("dve", "tensor_scalar", out=cacc[:, ct, 0:n], in0=hc[:, ct, 0:n],
                                     scalar1=cw[:, ct, 0:1], scalar2=cb[:, ct:ct + 1], op0=ALU.mult, op1=ALU.add,
                                     reads=["hc", "cw", "conv_b"], writes=[cres])
                                for j in range(1, CW):
                                    S.op("dve", "scalar_tensor_tensor", out=cacc[:, ct, 0:n], in0=hc[:, ct, j:j + n],
                                         scalar=cw[:, ct, j:j + 1], in1=cacc[:, ct, 0:n], op0=ALU.mult, op1=ALU.add,
                                         reads=["hc", "cw", cres], writes=[cres])
                            else:
                                pt_ = tm[2]
                                S.op("pool", "tensor_scalar", out=cacc[:, ct, 0:n], in0=hc[:, ct, 0:n],
                                     scalar1=cw[:, ct, 0:1], scalar2=cb[:, ct:ct + 1], op0=ALU.mult, op1=ALU.add,
                                     reads=["hc", "cw", "conv_b"], writes=[cres])
                                for j in range(1, CW):
                                    S.op("pool", "tensor_scalar", out=pt_[:, 0:n], in0=hc[:, ct, j:j + n],
                                         scalar1=cw[:, ct, j:j + 1], scalar2=None, op0=ALU.mult,
                                         reads=["hc", "cw"], writes=["ptmp"])
                                    S.op("pool", "tensor_tensor", out=cacc[:, ct, 0:n], in0=cacc[:, ct, 0:n],
                                         in1=pt_[:, 0:n], op=ALU.add, reads=["ptmp", cres], writes=[cres])
                        if n >= HALO:
                            S.op("act", "activation", out=hc[:, :, 0:HALO], in_=hc[:, :, n:n + HALO], func=AF.Copy,
                                 reads=["hc"] + [("cacc", c) for c in range(4)], writes=["hc"])
                        bmu, bvar = nxt(), nxt()
                        for ct in range(4):
                            b2 = ct % 2
                            S.op("act", "activation", out=sqb[b2][:, 0:n], in_=cacc[:, ct, 0:n], func=AF.Copy,
                                 reads=[("cacc", ct)], writes=[("sqb", b2)])
                            S.op("pe", "matmul", pp[bmu][:, 0:n], lhsT=ones_bf[:], rhs=sqb[b2][:, 0:n],
                                 start=(ct == 0), stop=(ct == 3), reads=[("sqb", b2), "ones_bf"], writes=[("pp", bmu)])
                        for ct in range(4):
                            b2 = ct % 2
                            S.op("act", "activation", out=sqb[b2][:, 0:n], in_=cacc[:, ct, 0:n], func=AF.Square,
                                 reads=[("cacc", ct)], writes=[("sqb", b2)])
                            S.op("pe", "matmul", pp[bvar][:, 0:n], lhsT=ones_bf[:], rhs=sqb[b2][:, 0:n],
                                 start=(ct == 0), stop=(ct == 3), reads=[("sqb", b2), "ones_bf"], writes=[("pp", bvar)])
                        S.op("act", "activation", out=mu[:, 0:n], in_=pp[bmu][:, 0:n], func=AF.Copy, scale=1.0 / 512,
                             reads=[("pp", bmu)], writes=["mu"])
                        S.op("dve", "tensor_tensor", out=tm[0][:, 0:n], in0=mu[:, 0:n], in1=mu[:, 0:n], op=ALU.mult,
                             reads=["mu"], writes=[("tm", 0)])
                        S.op("dve", "scalar_tensor_tensor", out=lrs[:, 0:n], in0=pp[bvar][:, 0:n], scalar=1.0 / 512,
                             in1=tm[0][:, 0:n], op0=ALU.mult, op1=ALU.subtract,
                             reads=[("pp", bvar), ("tm", 0)], writes=["lrs"])
                        S.op("act", "activation", out=lrs[:, 0:n], in_=lrs[:, 0:n], func=AF.Sqrt, bias=eps_ln[:], scale=1.0,
                             reads=["lrs", "eps_ln"], writes=["lrs"])
                        S.op("dve", "reciprocal", out=lrs[:, 0:n], in_=lrs[:, 0:n], reads=["lrs"], writes=["lrs"])
                        for ct in range(4):
                            S.op("dve", "tensor_tensor", out=tm[0][:, 0:n], in0=cacc[:, ct, 0:n], in1=mu[:, 0:n],
                                 op=ALU.subtract, reads=[("cacc", ct), "mu"], writes=[("tm", 0)])
                            S.op("dve", "tensor_tensor", out=tm[0][:, 0:n], in0=tm[0][:, 0:n], in1=lrs[:, 0:n],
                                 op=ALU.mult, reads=[("tm", 0), "lrs"], writes=[("tm", 0)])
                            S.op("act", "activation", out=hcv[:, ct, 0:n], in_=tm[0][:, 0:n], func=AF.Silu,
                                 scale=lng[:, ct:ct + 1], bias=lnb[:, ct:ct + 1],
                                 reads=[("tm", 0), "conv_ln_g", "conv_ln_b"], writes=["hcv"])
                        for ct in range(4):
                            yb = ycnt % 2
                            ycnt += 1
                            yt = yst[yb]
                            for gl in range(8):
                                S.dma("sp", out=yt[gl * 16:(gl + 1) * 16, :, 0:nk],
                                      in_=ysS[s, ct * 8 + gl, :, :, k0:k0 + nk].rearrange("j c k -> c j k"),
                                      writes=[("yst", yb)])
                            yv = yt[:, :, 0:nk]
                            t1 = tm[0][:, 0:n].rearrange("p (j k) -> p j k", j=8)
                            t2 = tm[1][:, 0:n].rearrange("p (j k) -> p j k", j=8)
                            S.op("act", "activation", out=t1, in_=yv, func=AF.Square,
                                 reads=[("yst", yb)], writes=[("tm", 0)])
                            S.op("dve", "tensor_scalar", out=t1, in0=t1, scalar1=0.044715, scalar2=1.0,
                                 op0=ALU.mult, op1=ALU.add, reads=[("tm", 0)], writes=[("tm", 0)])
                            S.op("dve", "tensor_tensor", out=t1, in0=t1, in1=yv, op=ALU.mult,
                                 reads=[("tm", 0), ("yst", yb)], writes=[("tm", 0)])
                            S.op("act", "activation", out=t2, in_=t1, func=AF.Tanh, scale=0.7978845608,
                                 reads=[("tm", 0)], writes=[("tm", 1)])
                            S.op("dve", "tensor_scalar", out=t2, in0=t2, scalar1=0.5, scalar2=0.5,
                                 op0=ALU.mult, op1=ALU.add, reads=[("tm", 1)], writes=[("tm", 1)])
                            S.op("dve", "tensor_tensor", out=gy[:, ct, 0:n].rearrange("p (k j) -> p j k", j=8),
                                 in0=t2, in1=yv, op=ALU.mult,
                                 reads=[("tm", 1), ("yst", yb)], writes=["gy"])
                        for f in range(DT):
                            def gate(br):
                                bgt = nxt()
                                c1 = 1024 + br * 1024 + f * 128
                                for dt in range(DT):
                                    S.op("pe", "matmul", pp[bgt][:, 0:n], lhsT=wB[:, dt, c1:c1 + 128],
                                         rhs=xn[:, dt, 0:n], start=(dt == 0), stop=(dt == DT - 1),
                                         reads=wBr + ["xn"], writes=[("pp", bgt)])
                                S.op("act", "activation", out=sg[br % 2][:, 0:n], in_=pp[bgt][:, 0:n], func=AF.Sigmoid,
                                     reads=[("pp", bgt)], writes=[("sg", br % 2)])
                                return sg[br % 2], ("sg", br % 2)
                            ba, bg = nxt(), nxt()
                            for ct in range(4):
                                S.op("pe", "matmul", pp[ba][:, 0:n], lhsT=wglu[:, ct, f * 128:(f + 1) * 128],
                                     rhs=gy[:, ct, 0:n], start=(ct == 0), stop=(ct == 3),
                                     reads=[("wglu", c) for c in range(4)] + ["gy"], writes=[("pp", ba)])
                            for ct in range(4):
                                S.op("pe", "matmul", pp[bg][:, 0:n], lhsT=wglu[:, ct, 1024 + f * 128:1024 + (f + 1) * 128],
                                     rhs=gy[:, ct, 0:n], start=(ct == 0), stop=(ct == 3),
                                     reads=[("wglu", c) for c in range(4)] + ["gy"], writes=[("pp", bg)])
                            S.op("act", "activation", out=tm[1][:, 0:n], in_=pp[bg][:, 0:n], func=AF.Sigmoid,
                                 reads=[("pp", bg)], writes=[("tm", 1)])
                            S.op("dve", "tensor_tensor", out=tm[0][:, 0:n], in0=pp[ba][:, 0:n], in1=tm[1][:, 0:n],
                                 op=ALU.mult, reads=[("pp", ba), ("tm", 1)], writes=[("tm", 0)])
                            g0, g0r = gate(0)
                            S.op("dve", "tensor_tensor", out=macc[:, 0:n], in0=tm[0][:, 0:n], in1=g0[:, 0:n],
                                 op=ALU.mult, reads=[("tm", 0), g0r], writes=["macc"])
                            bc_ = nxt()
                            for ct in range(4):
                                S.op("pe", "matmul", pp[bc_][:, 0:n], lhsT=wpw[:, ct, f * 128:(f + 1) * 128],
                                     rhs=hcv[:, ct, 0:n], start=(ct == 0), stop=(ct == 3),
                                     reads=[("wpw", c) for c in range(4)] + ["hcv"], writes=[("pp", bc_)])
                            g1, g1r = gate(1)
                            S.op("dve", "tensor_tensor", out=tm[0][:, 0:n], in0=pp[bc_][:, 0:n], in1=g1[:, 0:n],
                                 op=ALU.mult, reads=[("pp", bc_), g1r], writes=[("tm", 0)])
                            S.op("dve", "tensor_tensor", out=macc[:, 0:n], in0=macc[:, 0:n], in1=tm[0][:, 0:n],
                                 op=ALU.add, reads=["macc", ("tm", 0)], writes=["macc"])
                            bo_ = nxt()
                            for ct in range(4):
                                S.op("pe", "matmul", pp[bo_][:, 0:n], lhsT=wo[:, ct, f * 128:(f + 1) * 128],
                                     rhs=oat[:, ct, 0:n], start=(ct == 0), stop=(ct == 3),
                                     reads=[("wo", c) for c in range(4)] + ["oat"], writes=[("pp", bo_)])
                            g2, g2r = gate(2)
                            S.op("dve", "tensor_tensor", out=tm[0][:, 0:n], in0=pp[bo_][:, 0:n], in1=g2[:, 0:n],
                                 op=ALU.mult, reads=[("pp", bo_), g2r], writes=[("tm", 0)])
                            S.op("dve", "tensor_tensor", out=mrg[:, f, 0:n], in0=macc[:, 0:n], in1=tm[0][:, 0:n],
                                 op=ALU.add, reads=["macc", ("tm", 0)], writes=[("mrg", f)])
                        for o in range(DT):
                            bo_ = nxt()
                            for f in range(DT):
                                S.op("pe", "matmul", pp[bo_][:, 0:n], lhsT=wout[:, f, o * 128:(o + 1) * 128],
                                     rhs=mrg[:, f, 0:n], start=(f == 0), stop=(f == DT - 1),
                                     reads=[("wout", f), ("mrg", f)], writes=[("pp", bo_)])
                            S.op("dve", "tensor_tensor", out=hs[:, o, 0:n], in0=pp[bo_][:, 0:n], in1=hs[:, o, 0:n],
                                 op=ALU.add, reads=[("pp", bo_), "hs"], writes=["hs"])
                        store_h(hs, s, t0, n)
                S.barrier()

        phases = cfg.phases
        phase_in()
        for l in range(DEPTH):
            if phases is None or "noffn1" not in phases:
                phase_ffn(l, 0)
            if phases is None or "mix" in phases:
                phase_mixA(l)
                if phases is None or "noatt" not in phases:
                    phase_att(l)
                phase_ssm(l)
                phase_mixB(l)
            if phases is None or "ffn2" in phases:
                phase_ffn(l, 1)
        phase_out()

        with nc.Block() as block:
            S.emit(block)
    return nc


def kernel(**inputs):
    cfg = Cfg()
    nc = build(cfg)
    n = 8
    in_maps = []
    for c in range(n):
        m = {}
        for k, v in inputs.items():
            a = np.asarray(v)
            if k == "x":
                a = a[c * cfg.nseq:(c + 1) * cfg.nseq]
            m[k] = np.ascontiguousarray(a, dtype=np.float32)
        in_maps.append(m)
    res = run_bass_kernel_spmd(nc, in_maps, core_ids=list(range(n)))
    return np.concatenate([np.asarray(r["out"]) for r in res.results], axis=0).astype(np.float32)
```

```python
import math
import os
from contextlib import ExitStack

import numpy as np
import concourse.bass as bass
import concourse.mybir as mybir
from concourse.bass_utils import run_bass_kernel_spmd

F32 = mybir.dt.float32
BF16 = mybir.dt.bfloat16
I32 = mybir.dt.int32
AF = mybir.ActivationFunctionType
ALU = mybir.AluOpType

D = 1024
DT = 8
DFF = 2816
FT = 22
NMETA = 16
DIN = 6144
NG = 32
NST = 64
CW = 31
RMS_EPS = 1e-6
LN_EPS = 1e-5

ENGS = ["pe", "act", "dve", "pool", "sp"]


class Sched:
    def __init__(self, nc, es, n_dma_sems=24):
        self.nc = nc
        self.ops = {e: [] for e in ENGS}
        self.sem = {e: es.enter_context(nc.semaphore("sem_" + e)) for e in ENGS}
        self.cnt = {e: 0 for e in ENGS}
        self.dsem = [es.enter_context(nc.semaphore("semd%d" % i)) for i in range(n_dma_sems)]
        self.duse = [0] * n_dma_sems
        self.dnext = 0
        self.waited = {}
        self.lastw = {}
        self.readers = {}

    def _semof(self, key):
        if isinstance(key, str):
            return self.sem[key]
        return self.dsem[key[1]]

    def _deps(self, eng, reads, writes):
        toks = {}
        def add(t):
            k, v = t
            if toks.get(k, 0) < v:
                toks[k] = v
        for r in reads:
            if r in self.lastw:
                add(self.lastw[r])
        for w in writes:
            if w in self.lastw:
                add(self.lastw[w])
            for k, v in self.readers.get(w, {}).items():
                add((k, v))
        waits = []
        for k, v in toks.items():
            if k == "pe" and eng == "pe":
                continue
            if self.waited.get((eng, k), 0) >= v:
                continue
            self.waited[(eng, k)] = v
            waits.append((k, v))
        return waits

    def _record(self, tok, reads, writes):
        k, v = tok
        for r in reads:
            d = self.readers.setdefault(r, {})
            if d.get(k, 0) < v:
                d[k] = v
        for w in writes:
            self.lastw[w] = tok
            self.readers[w] = {}

    def op(self, eng, meth, *args, reads=(), writes=(), excl=(), **kw):
        fn = (meth, args, kw)
        if excl:
            reads = list(reads) + list(excl)
            writes = list(writes) + list(excl)
        waits = self._deps(eng, reads, writes)
        self.cnt[eng] += 1
        tok = (eng, self.cnt[eng])
        self.ops[eng].append((waits, fn, eng, 1))
        self._record(tok, reads, writes)

    def dma(self, eng, reads=(), writes=(), **kw):
        fn = ("dma_start", (), kw)
        waits = self._deps(eng, reads, writes)
        slot = self.dnext
        self.dnext = (self.dnext + 1) % len(self.dsem)
        key = ("d", slot)
        if self.duse[slot] > 0:
            v = 16 * self.duse[slot]
            if self.waited.get((eng, key), 0) < v:
                self.waited[(eng, key)] = v
                waits.append((key, v))
        self.duse[slot] += 1
        tok = (key, 16 * self.duse[slot])
        self.ops[eng].append((waits, fn, key, 16))
        self._record(tok, reads, writes)

    def barrier(self):
        for eng in ENGS:
            waits = []
            for k in ENGS:
                v = self.cnt[k]
                if v > 0 and k != eng and self.waited.get((eng, k), 0) < v:
                    self.waited[(eng, k)] = v
                    waits.append((k, v))
            if eng != "pe":
                v = self.cnt[eng]
                if v > 0 and self.waited.get((eng, eng), 0) < v:
                    self.waited[(eng, eng)] = v
                    waits.append((eng, v))
            for i, u in enumerate(self.duse):
                key = ("d", i)
                if u > 0 and self.waited.get((eng, key), 0) < 16 * u:
                    self.waited[(eng, key)] = 16 * u
                    waits.append((key, 16 * u))
            if waits:
                self.ops[eng].append((waits, None, None, 0))
        self.lastw = {}
        self.readers = {}

    def emit(self, block):
        def replay(name):
            def run(e):
                for waits, fn, inc_key, inc in self.ops[name]:
                    for k, v in waits:
                        e.wait_ge(self._semof(k), v)
                    if fn is not None:
                        meth, args, kw = fn
                        getattr(e, meth)(*args, **kw).then_inc(self._semof(inc_key), inc)
            return run
        block.tensor(replay("pe"))
        block.scalar(replay("act"))
        block.vector(replay("dve"))
        block.gpsimd(replay("pool"))
        block.sync(replay("sp"))


class Cfg:
    def __init__(self, seq=4096, nseq=2, depth=4, phases=None, dump=()):
        self.dump = dump
        self.seq = seq
        self.nseq = nseq
        self.depth = depth
        self.L = seq + NMETA
        tiles = []
        t = 0
        while t + 512 <= self.L:
            tiles.append((t, 512))
            t += 512
        if t < self.L:
            tiles.append((t, self.L - t))
        self.tiles = tiles
        self.nkb = (self.L + 127) // 128
        self.LP = self.nkb * 128
        self.NK = self.L // 8
        self.phases = phases


def build(cfg):
    nc = bass.Bass("TRN2", target_bir_lowering=False)
    L, NS, DEPTH = cfg.L, cfg.nseq, cfg.depth

    def din(name, shape):
        return nc.dram_tensor(name, list(shape), F32, kind="ExternalInput").ap()

    x = din("x", (NS, cfg.seq, D))
    meta = din("meta_tokens", (NMETA, D))
    P = {}
    for name, shape in [
        ("ffn1_norm", (DEPTH, D)), ("ffn1_w13", (DEPTH, D, 2 * DFF)), ("ffn1_w2", (DEPTH, DFF, D)),
        ("mix_norm", (DEPTH, D)), ("w_in", (DEPTH, D, DIN)),
        ("ssm_lam_re", (DEPTH, NG, NST)), ("ssm_lam_im", (DEPTH, NG, NST)), ("ssm_log_dt", (DEPTH, NG)),
        ("ssm_b_re", (DEPTH, NG, NST, 16)), ("ssm_b_im", (DEPTH, NG, NST, 16)),
        ("ssm_c_re", (DEPTH, NG, 16, NST)), ("ssm_c_im", (DEPTH, NG, 16, NST)),
        ("ssm_d", (DEPTH, 512)), ("ssm_w_glu", (DEPTH, 512, 2048)),
        ("conv_w", (DEPTH, CW, 512)), ("conv_b", (DEPTH, 512)),
        ("conv_ln_g", (DEPTH, 512)), ("conv_ln_b", (DEPTH, 512)),
        ("conv_w_out", (DEPTH, 512, D)), ("attn_w_o", (DEPTH, 512, D)), ("w_out", (DEPTH, D, D)),
        ("ffn2_norm", (DEPTH, D)), ("ffn2_w13", (DEPTH, D, 2 * DFF)), ("ffn2_w2", (DEPTH, DFF, D)),
        ("final_norm", (D,)),
    ]:
        P[name] = din(name, shape)
    out = nc.dram_tensor("out", [NS, cfg.seq, D], F32, kind="ExternalOutput").ap()

    def scratch(name, shape, dt):
        kind = "ExternalOutput" if name in cfg.dump else "Internal"
        return nc.dram_tensor(name, list(shape), dt, kind=kind).ap()

    hT = scratch("hT", (NS, D, L), F32)
    NK = cfg.NK
    Usc = scratch("Usc", (NS, NG, 8, 16, NK), BF16)
    qT = scratch("qT", (NS, 512, L), BF16)
    kT = scratch("kT", (NS, 512, L), BF16)
    vS = scratch("vS", (NS, L, 512), BF16)
    oattT = scratch("oattT", (NS, 512, L), BF16)
    ysS = scratch("ysS", (NS, NG, 8, 16, NK), F32)

    with ExitStack() as es:
        S = Sched(nc, es)

        uid = [0]

        def sb(name, shape, dt, stack=es):
            uid[0] += 1
            return stack.enter_context(nc.sbuf_tensor("%s_%d" % (name, uid[0]), list(shape), dt))

        def ps(name, shape, dt=F32, stack=es):
            uid[0] += 1
            return stack.enter_context(nc.psum_tensor("%s_%d" % (name, uid[0]), list(shape), dt))

        ones_bf = sb("ones_bf", [128, 128], BF16)
        ident_f = sb("ident_f", [128, 128], F32)
        gcols = sb("gcols", [128, (3 * DEPTH + 1) * DT], F32)
        eps_rms = sb("eps_rms", [128, 1], F32)
        S.op("pool", "memset", ones_bf[:], 1.0, writes=["ones_bf"])
        S.op("pool", "memset", ident_f[:], 1.0, writes=["ident_f"])
        S.op("pool", "affine_select", out=ident_f[:], in_=ident_f[:], pattern=[[-1, 128]],
                                               compare_op=ALU.is_equal, fill=0.0, base=0,
                                               channel_multiplier=1,
             reads=["ident_f"], writes=["ident_f"])
        S.op("pool", "memset", eps_rms[:], RMS_EPS, writes=["eps_rms"])
        for wi, nm in enumerate(["ffn1_norm", "mix_norm", "ffn2_norm"]):
            for l in range(DEPTH):
                c0 = (wi * DEPTH + l) * DT
                S.dma("sp",
                    out=gcols[:, c0:c0 + DT], in_=P[nm][l].rearrange("(t p) -> p t", p=128),
                    allow_slow_non_contiguous=True, writes=["gcols"])
        cF = 3 * DEPTH * DT
        S.dma("sp", out=gcols[:, cF:cF + DT],
                                          in_=P["final_norm"].rearrange("(t p) -> p t", p=128),
                                          allow_slow_non_contiguous=True, writes=["gcols"])
        S.barrier()

        def gcol(which, l):
            c0 = (which * DEPTH + l) * DT if which < 3 else cF
            return c0

        def load_h(hs, s, t0, n):
            S.dma("sp",
                out=hs[:, :, 0:n], in_=hT[s, :, t0:t0 + n].rearrange("(t p) n -> p t n", p=128),
                writes=["hs"])

        def store_h(hs, s, t0, n):
            S.dma("sp",
                out=hT[s, :, t0:t0 + n].rearrange("(t p) n -> p t n", p=128), in_=hs[:, :, 0:n],
                reads=["hs"])

        def rmsnorm(hs, xn, sqb, pss, rstd, c0, n, xn_name="xn"):
            for dt in range(DT):
                b = dt % 2
                S.op("act", "activation", out=sqb[b][:, 0:n], in_=hs[:, dt, 0:n],
                                                                 func=AF.Square,
                     reads=["hs"], writes=[("sqb", b)])
                S.op("pe", "matmul", pss[:, 0:n], lhsT=ones_bf[:], rhs=sqb[b][:, 0:n],
                                                           start=(dt == 0), stop=(dt == DT - 1),
                     reads=[("sqb", b), "ones_bf"], writes=["pss"])
            S.op("act", "activation", out=rstd[:, 0:n], in_=pss[:, 0:n], func=AF.Sqrt,
                                               scale=1.0 / D, bias=eps_rms[:],
                 reads=["pss"], writes=["rstd"])
            S.op("dve", "reciprocal", out=rstd[:, 0:n], in_=rstd[:, 0:n],
                 reads=["rstd"], writes=["rstd"])
            for dt in range(DT):
                S.op("dve", "scalar_tensor_tensor",
                    out=xn[:, dt, 0:n], in0=hs[:, dt, 0:n], scalar=gcols[:, c0 + dt:c0 + dt + 1],
                    in1=rstd[:, 0:n], op0=ALU.mult, op1=ALU.mult,
                    reads=["hs", "rstd", "gcols"], writes=[xn_name])

        def load_w_cast(dst_ap, src_ap, res):
            S.dma("pool", out=dst_ap, in_=src_ap, max_dma_last_dim=8192,
                  writes=[res])

        def phase_in():
            with ExitStack() as st:
                hs = sb("in_hs", [128, DT, 512], F32, st)
                xt = [sb("in_xt%d" % i, [128, D], F32, st) for i in range(2)]
                pt = [ps("in_pt%d" % i, [128, 512], F32, st) for i in range(2)]
                blk = 0
                for s in range(NS):
                    for (t0, n) in cfg.tiles:
                        for j in range((n + 127) // 128):
                            tb = t0 + j * 128
                            nb = min(128, t0 + n - tb)
                            xb = xt[blk % 2]
                            xr = ("xt", blk % 2)
                            if tb == 0:
                                S.dma("sp", out=xb[0:NMETA, :], in_=meta[:, :],
                                      writes=[xr])
                                S.dma("sp",
                                    out=xb[NMETA:nb, :], in_=x[s, 0:nb - NMETA, :], writes=[(xr, 1)],
                                    reads=[])
                                rd = [xr, (xr, 1)]
                            else:
                                S.dma("sp",
                                    out=xb[0:nb, :], in_=x[s, tb - NMETA:tb - NMETA + nb, :], writes=[xr, (xr, 1)])
                                rd = [xr, (xr, 1)]
                            for half in range(2):
                                pp = pt[half]
                                for q in range(4):
                                    dt = half * 4 + q
                                    S.op("pe", "transpose",
                                        out=pp[:, q * 128:q * 128 + nb], in_=xb[0:nb, dt * 128:(dt + 1) * 128],
                                        identity=ident_f[0:nb, 0:nb],
                                        reads=rd + ["ident_f"], writes=[("pt", half)])
                                eng = "act" if half == 0 else "dve"
                                if eng == "act":
                                    S.op("act", "activation",
                                        out=hs[:, half * 4:half * 4 + 4, j * 128:j * 128 + nb],
                                        in_=pp[:].rearrange("p (q c) -> p q c", q=4)[:, :, 0:nb], func=AF.Copy,
                                        reads=[("pt", half)], writes=["hs"])
                                else:
                                    S.op("dve", "tensor_copy",
                                        out=hs[:, half * 4:half * 4 + 4, j * 128:j * 128 + nb],
                                        in_=pp[:].rearrange("p (q c) -> p q c", q=4)[:, :, 0:nb],
                                        reads=[("pt", half)], writes=["hs"])
                            blk += 1
                        store_h(hs, s, t0, n)
                S.barrier()

        def phase_out():
            with ExitStack() as st:
                hs = sb("o_hs", [128, DT, 512], F32, st)
                xn = sb("o_xn", [128, DT, 512], F32, st)
                sqb = [sb("o_sq%d" % i, [128, 512], BF16, st) for i in range(2)]
                rstd = sb("o_rstd", [128, 512], F32, st)
                ot = [sb("o_ot%d" % i, [128, D], F32, st) for i in range(2)]
                pss = ps("o_pss", [128, 512], F32, st)
                pt = [ps("o_pt%d" % i, [128, 512], F32, st) for i in range(2)]
                blk = 0
                for s in range(NS):
                    for (t0, n) in cfg.tiles:
                        load_h(hs, s, t0, n)
                        rmsnorm(hs, xn, sqb, pss, rstd, cF, n)
                        for j in range((n + 127) // 128):
                            tb = t0 + j * 128
                            nb = min(128, t0 + n - tb)
                            ob = ot[blk % 2]
                            orr = ("ot", blk % 2)
                            for half in range(2):
                                pp = pt[half]
                                for q in range(4):
                                    dt = half * 4 + q
                                    S.op("pe", "transpose",
                                        out=pp[0:nb, q * 128:(q + 1) * 128], in_=xn[:, dt, j * 128:j * 128 + nb],
                                        identity=ident_f[:, :],
                                        reads=["xn", "ident_f"], writes=[("pt", half)])
                                if half == 0:
                                    S.op("act", "activation",
                                        out=ob[0:nb, 0:512], in_=pp[0:nb, :], func=AF.Copy,
                                        reads=[("pt", half)], writes=[orr])
                                else:
                                    S.op("dve", "tensor_copy",
                                        out=ob[0:nb, 512:1024], in_=pp[0:nb, :],
                                        reads=[("pt", half)], writes=[(orr, 1)])
                            lo = NMETA if tb == 0 else 0
                            S.dma("sp",
                                out=out[s, tb + lo - NMETA:tb + nb - NMETA, :], in_=ob[lo:nb, :],
                                reads=[orr, (orr, 1)])
                            blk += 1
                S.barrier()

        def phase_ffn(l, which):
            pre = "ffn1" if which == 0 else "ffn2"
            w13 = P[pre + "_w13"][l]
            w2 = P[pre + "_w2"][l]
            c0 = gcol(0 if which == 0 else 2, l)
            with ExitStack() as st:
                w13s = sb("f_w13", [128, DT, 2 * DFF], BF16, st)
                w2s = sb("f_w2", [128, FT, D], BF16, st)
                hs = sb("f_hs", [128, DT, 512], F32, st)
                xn = sb("f_xn", [128, DT, 512], BF16, st)
                sqb = [sb("f_sq%d" % i, [128, 512], BF16, st) for i in range(2)]
                rstd = sb("f_rstd", [128, 512], F32, st)
                gh = sb("f_g", [128, FT, 512], BF16, st)
                sa = [sb("f_sa%d" % i, [128, 512], F32, st) for i in range(2)]
                pss = ps("f_pss", [128, 512], F32, st)
                pa = [ps("f_pa%d" % i, [128, 512], F32, st) for i in range(2)]
                pb = [ps("f_pb%d" % i, [128, 512], F32, st) for i in range(2)]
                po = [ps("f_po%d" % i, [128, 512], F32, st) for i in range(2)]
                for dt in range(DT):
                    load_w_cast(w13s[:, dt, :], w13[dt * 128:(dt + 1) * 128, :], ("w13", dt))
                for f in range(FT):
                    load_w_cast(w2s[:, f, :], w2[f * 128:(f + 1) * 128, :], ("w2", f))
                for s in range(NS):
                    for (t0, n) in cfg.tiles:
                        load_h(hs, s, t0, n)
                        rmsnorm(hs, xn, sqb, pss, rstd, c0, n)
                        for f in range(FT):
                            b = f % 2
                            for dt in range(DT):
                                S.op("pe", "matmul",
                                    pa[b][:, 0:n], lhsT=w13s[:, dt, f * 128:(f + 1) * 128], rhs=xn[:, dt, 0:n],
                                    start=(dt == 0), stop=(dt == DT - 1),
                                    reads=[("w13", dt), "xn"], writes=[("pa", b)])
                            for dt in range(DT):
                                S.op("pe", "matmul",
                                    pb[b][:, 0:n], lhsT=w13s[:, dt, DFF + f * 128:DFF + (f + 1) * 128],
                                    rhs=xn[:, dt, 0:n], start=(dt == 0), stop=(dt == DT - 1),
                                    reads=[("w13", dt), "xn"], writes=[("pb", b)])
                            S.op("act", "activation", out=sa[b][:, 0:n], in_=pa[b][:, 0:n],
                                                                    func=AF.Silu,
                                 reads=[("pa", b)], writes=[("sa", b)])
                            S.op("dve", "tensor_tensor",
                                out=gh[:, f, 0:n], in0=pb[b][:, 0:n], in1=sa[b][:, 0:n], op=ALU.mult,
                                reads=[("pb", b), ("sa", b)], writes=[("gh", f)])
                        for o in range(DT):
                            b = o % 2
                            for f in range(FT):
                                S.op("pe", "matmul",
                                    po[b][:, 0:n], lhsT=w2s[:, f, o * 128:(o + 1) * 128], rhs=gh[:, f, 0:n],
                                    start=(f == 0), stop=(f == FT - 1),
                                    reads=[("w2", f), ("gh", f)], writes=[("po", b)])
                            import os
                            if os.environ.get("DBG") == "po":
                                S.op("dve", "tensor_copy", out=hs[:, o, 0:n], in_=po[b][:, 0:n],
                                     reads=[("po", b), "hs"], writes=["hs"])
                            elif os.environ.get("DBG") == "xn":
                                S.op("dve", "tensor_copy", out=hs[:, o, 0:n], in_=xn[:, o, 0:n],
                                     reads=[("po", b), "hs", "xn"], writes=["hs"])
                            elif os.environ.get("DBG") == "gh":
                                S.op("dve", "tensor_copy", out=hs[:, o, 0:n], in_=gh[:, o, 0:n],
                                     reads=[("po", b), "hs", "xn", ("gh", o)], writes=["hs"])
                            else:
                              S.op("dve", "scalar_tensor_tensor",
                                out=hs[:, o, 0:n], in0=po[b][:, 0:n], scalar=0.5, in1=hs[:, o, 0:n],
                                op0=ALU.mult, op1=ALU.add,
                                reads=[("po", b), "hs"], writes=["hs"])
                        store_h(hs, s, t0, n)
                S.barrier()


        def phase_mixA(l):
            w_in = P["w_in"][l]
            c0 = gcol(1, l)
            with ExitStack() as st:
                wA = sb("a_w", [128, DT, 2048], BF16, st)
                hs = sb("a_hs", [128, DT, 512], F32, st)
                xn = sb("a_xn", [128, DT, 512], BF16, st)
                sqb = [sb("a_sq%d" % i, [128, 512], BF16, st) for i in range(2)]
                rstd = sb("a_rstd", [128, 512], F32, st)
                stg = [sb("a_stg%d" % i, [128, 512], BF16, st) for i in range(4)]
                pss = ps("a_pss", [128, 512], F32, st)
                pp = [ps("a_pp%d" % i, [128, 512], F32, st) for i in range(4)]
                for dt in range(DT):
                    load_w_cast(wA[:, dt, 0:512], w_in[dt * 128:(dt + 1) * 128, 0:512], ("wA", dt))
                    load_w_cast(wA[:, dt, 512:2048], w_in[dt * 128:(dt + 1) * 128, 1536:3072], ("wA", dt, 1))
                cnt = 0
                for s in range(NS):
                    for (t0, n) in cfg.tiles:
                        load_h(hs, s, t0, n)
                        rmsnorm(hs, xn, sqb, pss, rstd, c0, n)
                        nk = n // 8
                        k0 = t0 // 8
                        for f in range(12):
                            b = cnt % 4
                            cnt += 1
                            pb = pp[b]
                            sg = stg[b]
                            for dt in range(DT):
                                S.op("pe", "matmul", pb[:, 0:n], lhsT=wA[:, dt, f * 128:(f + 1) * 128],
                                     rhs=xn[:, dt, 0:n], start=(dt == 0), stop=(dt == DT - 1),
                                     reads=[("wA", dt), ("wA", dt, 1), "xn"], writes=[("pp", b)])
                            if f < 4:
                                S.op("act", "activation",
                                     out=sg[:, 0:n].rearrange("p (s k) -> p s k", s=8),
                                     in_=pb[:, 0:n].rearrange("p (k s) -> p s k", s=8), func=AF.Copy,
                                     reads=[("pp", b)], writes=[("stg", b)])
                                for gl in range(8):
                                    S.dma("sp", out=Usc[s, f * 8 + gl, :, :, k0:k0 + nk].rearrange("s c k -> c s k"),
                                          in_=sg[gl * 16:(gl + 1) * 16, 0:n].rearrange("p (s k) -> p s k", s=8),
                                          reads=[("stg", b)])
                            else:
                                if f % 2 == 0:
                                    S.op("act", "activation", out=sg[:, 0:n], in_=pb[:, 0:n], func=AF.Copy,
                                         reads=[("pp", b)], writes=[("stg", b)])
                                else:
                                    S.op("dve", "tensor_copy", out=sg[:, 0:n], in_=pb[:, 0:n],
                                         reads=[("pp", b)], writes=[("stg", b)])
                                dst = qT if f < 8 else kT
                                r0 = (f % 4) * 128
                                S.dma("sp", out=dst[s, r0:r0 + 128, t0:t0 + n], in_=sg[:, 0:n],
                                      reads=[("stg", b)])
                        for j in range((n + 127) // 128):
                            nb = min(128, n - j * 128)
                            b = cnt % 4
                            cnt += 1
                            pb = pp[b]
                            sg = stg[b]
                            for dt in range(DT):
                                S.op("pe", "matmul", pb[0:nb, 0:512], lhsT=xn[:, dt, j * 128:j * 128 + nb],
                                     rhs=wA[:, dt, 1536:2048], start=(dt == 0), stop=(dt == DT - 1),
                                     reads=[("wA", dt), ("wA", dt, 1), "xn"], writes=[("pp", b)])
                            S.op("dve", "tensor_copy", out=sg[0:nb, :], in_=pb[0:nb, :],
                                 reads=[("pp", b)], writes=[("stg", b)])
                            S.dma("sp", out=vS[s, t0 + j * 128:t0 + j * 128 + nb, :], in_=sg[0:nb, :],
                                  reads=[("stg", b)])
                S.barrier()

        def phase_att(l):
            nkb, LP = cfg.nkb, cfg.LP
            tail = L - (nkb - 1) * 128
            with ExitStack() as st:
                triT = sb("t_tri", [128, 128], BF16, st)
                sel0 = sb("t_sel0", [128, 128], BF16, st)
                onec = sb("t_onec", [128, 1], F32, st)
                masks = [sb("t_mask%d" % i, [128, 512], F32, st) for i in range(4)]
                kT2 = [sb("t_k%d" % i, [128, LP], BF16, st) for i in range(2)]
                qz = [[sb("t_q%d_%d" % (i, h), [128, L], BF16, st) for h in range(2)] for i in range(2)]
                vz = [[sb("t_v%d_%d" % (i, h), [128, nkb, 128], BF16, st) for h in range(2)] for i in range(2)]
                NB3 = 3
                e_t = [sb("t_e%d" % i, [128, 512], F32, st) for i in range(NB3)]
                sp_t = [sb("t_sp%d" % i, [128, 512], BF16, st) for i in range(NB3)]
                g_t = [sb("t_g%d" % i, [128, 512], F32, st) for i in range(NB3)]
                w_t = [sb("t_w%d" % i, [128, 512], BF16, st) for i in range(NB3)]
                chi = [sb("t_chi%d" % i, [128, 512], BF16, st) for i in range(2)]
                clo = [sb("t_clo%d" % i, [128, 512], BF16, st) for i in range(2)]
                ob = [sb("t_ob%d" % i, [128, 512], BF16, st) for i in range(2)]
                pz = [ps("t_pz%d" % i, [128, 512], F32, st) for i in range(2)]
                pcs = [ps("t_pcs%d" % i, [128, 512], F32, st) for i in range(2)]
                po = [ps("t_po%d" % i, [128, 512], F32, st) for i in range(2)]
                S.op("pool", "memset", triT[:], 1.0, writes=["triT"])
                S.op("pool", "affine_select", out=triT[:], in_=triT[:], pattern=[[-1, 128]],
                     compare_op=ALU.is_ge, fill=0.0, base=0, channel_multiplier=1,
                     reads=["triT"], writes=["triT"])
                S.op("pool", "memset", sel0[:], 1.0, writes=["sel0"])
                S.op("pool", "affine_select", out=sel0[:], in_=sel0[:], pattern=[[0, 128]],
                     compare_op=ALU.is_equal, fill=0.0, base=0, channel_multiplier=1,
                     reads=["sel0"], writes=["sel0"])
                S.op("pool", "memset", onec[:], 1.0, writes=["onec"])
                for i in range(4):
                    S.op("pool", "memset", masks[i][:], 1.0, writes=[("mask", i)])
                    S.op("pool", "affine_select", out=masks[i][:], in_=masks[i][:], pattern=[[1, 512]],
                         compare_op=ALU.is_gt, fill=0.0, base=-128 * i, channel_multiplier=-1,
                         reads=[("mask", i)], writes=[("mask", i)])
                for i in range(2):
                    S.op("pool", "memset", kT2[i][:], 0.0, writes=[("kT2", i)])
                    S.op("pool", "memset", chi[i][:], 0.0, writes=[("chi", i)])
                    S.op("pool", "memset", clo[i][:], 0.0, writes=[("clo", i)])
                    for h in range(2):
                        S.op("pool", "memset", qz[i][h][:], 0.0, writes=[("qz", i, h)])
                        S.op("pool", "memset", vz[i][h][:], 0.0, writes=[("vz", i, h), ("vz", i, h, 1)])
                blocks = []
                it = 0
                qcnt = 0
                for s in range(NS):
                    for hp in range(4):
                        bb = it % 2
                        it += 1
                        first_of_load = True
                        for (q0, nq) in cfg.tiles:
                            qb = qcnt % 2
                            qcnt += 1
                            nblk = max((q0 + nq - 1 + 127) // 128, 1)
                            for h in range(2):
                                for bi, kb in enumerate(reversed(range(nblk))):
                                    blocks.append(dict(
                                        s=s, hp=hp, bb=bb, q0=q0, nq=nq, qb=qb, h=h, bi=bi, kb=kb, nblk=nblk,
                                        load=first_of_load, first_pv=(h == 0 and bi == 0),
                                        last_pv=(h == 1 and bi == nblk - 1)))
                                    first_of_load = False
                for i, B_ in enumerate(blocks):
                    B_["i3"] = i % NB3
                    B_["i2"] = i % 2

                def load_inputs(B_):
                    s, hp, bb = B_["s"], B_["hp"], B_["bb"]
                    S.dma("sp", out=kT2[bb][:, 0:L], in_=kT[s, hp * 128:(hp + 1) * 128, :],
                          reads=[], writes=[("kT2", bb)])
                    for h in range(2):
                        r0 = hp * 128 + h * 64
                        S.dma("sp", out=qz[bb][h][h * 64:(h + 1) * 64, :], in_=qT[s, r0:r0 + 64, :],
                              writes=[("qz", bb, h)])
                        if nkb > 1:
                            S.dma("sp", out=vz[bb][h][:, 0:nkb - 1, h * 64:(h + 1) * 64],
                                  in_=vS[s, 0:(nkb - 1) * 128, r0:r0 + 64].rearrange("(b p) d -> p b d", p=128),
                                  writes=[("vz", bb, h)])
                        S.dma("sp", out=vz[bb][h][0:tail, nkb - 1, h * 64:(h + 1) * 64],
                              in_=vS[s, (nkb - 1) * 128:L, r0:r0 + 64],
                              writes=[("vz", bb, h, 1)])

                def stageA(B_):
                    bb, h, q0, nq, kb, i3, i2 = (B_[k] for k in ("bb", "h", "q0", "nq", "kb", "i3", "i2"))
                    if B_["load"]:
                        load_inputs(B_)
                    diag = (kb * 128 + 128 > q0)
                    S.op("pe", "matmul", pz[i2][:, 0:nq], lhsT=kT2[bb][:, kb * 128:(kb + 1) * 128],
                         rhs=qz[bb][h][:, q0:q0 + nq], start=True, stop=True,
                         reads=[("kT2", bb), ("qz", bb, h)], writes=[("pz", i2)])
                    S.op("act", "activation", out=e_t[i3][:, 0:nq], in_=pz[i2][:, 0:nq],
                         func=AF.Exp, scale=0.125,
                         reads=[("pz", i2)], writes=[("e", i3)])
                    if diag:
                        mi = (kb * 128 - q0) // 128
                        assert 0 <= mi < 4
                        S.op("pool", "tensor_tensor", out=e_t[i3][:, 0:nq], in0=e_t[i3][:, 0:nq],
                             in1=masks[mi][:, 0:nq], op=ALU.mult,
                             reads=[("e", i3), ("mask", mi)], writes=[("e", i3)])
                    S.op("act", "activation", out=sp_t[i3][:, 0:nq], in_=e_t[i3][:, 0:nq],
                         func=AF.Ln, bias=onec[:], scale=1.0,
                         reads=[("e", i3), "onec"], writes=[("sp", i3)])

                def stageB(B_):
                    nq, bi, nblk, i3, i2 = (B_[k] for k in ("nq", "bi", "nblk", "i3", "i2"))
                    S.op("pe", "matmul", pcs[i2][:, 0:nq], lhsT=triT[:], rhs=sp_t[i3][:, 0:nq],
                         start=True, stop=(bi == 0),
                         reads=["triT", ("sp", i3)], writes=[("pcs", i2)])
                    if bi > 0:
                        S.op("pe", "matmul", pcs[i2][:, 0:nq], lhsT=sel0[:], rhs=chi[i2][:, 0:nq],
                             start=False, stop=False,
                             reads=["sel0", ("chi", i2)], writes=[("pcs", i2)])
                        S.op("pe", "matmul", pcs[i2][:, 0:nq], lhsT=sel0[:], rhs=clo[i2][:, 0:nq],
                             start=False, stop=True,
                             reads=["sel0", ("clo", i2)], writes=[("pcs", i2)])
                    if bi < nblk - 1:
                        n2 = (i2 + 1) % 2
                        S.op("dve", "tensor_copy", out=chi[n2][0:1, 0:nq], in_=pcs[i2][0:1, 0:nq],
                             excl=[("pcs", i2)], writes=[("chi", n2)])
                        S.op("dve", "tensor_tensor", out=clo[n2][0:1, 0:nq], in0=pcs[i2][0:1, 0:nq],
                             in1=chi[n2][0:1, 0:nq], op=ALU.subtract,
                             excl=[("pcs", i2)], reads=[("chi", n2)], writes=[("clo", n2)])
                    S.op("act", "activation", out=g_t[i3][:, 0:nq], in_=pcs[i2][:, 0:nq],
                         func=AF.Exp, scale=-1.0,
                         excl=[("pcs", i2)], writes=[("g", i3)])
                    S.op("dve", "tensor_tensor", out=w_t[i3][:, 0:nq], in0=e_t[i3][:, 0:nq],
                         in1=g_t[i3][:, 0:nq], op=ALU.mult,
                         reads=[("e", i3), ("g", i3)], writes=[("w", i3)])

                def stageC(B_):
                    s, hp, bb, h, q0, nq, kb, i3, qb = (B_[k] for k in ("s", "hp", "bb", "h", "q0", "nq", "kb", "i3", "qb"))
                    S.op("pe", "matmul", po[qb][:, 0:nq], lhsT=vz[bb][h][:, kb, :],
                         rhs=w_t[i3][:, 0:nq], start=B_["first_pv"], stop=B_["last_pv"],
                         reads=[("vz", bb, h), ("vz", bb, h, 1), ("w", i3)], writes=[("po", qb)])
                    if B_["last_pv"]:
                        S.op("act", "activation", out=ob[qb][:, 0:nq], in_=po[qb][:, 0:nq], func=AF.Copy,
                             reads=[("po", qb)], writes=[("ob", qb)])
                        S.dma("sp", out=oattT[s, hp * 128:(hp + 1) * 128, q0:q0 + nq], in_=ob[qb][:, 0:nq],
                              reads=[("ob", qb)])

                NBLK = len(blocks)
                for i in range(NBLK + 2):
                    if i < NBLK:
                        stageA(blocks[i])
                    if 0 <= i - 1 < NBLK:
                        stageB(blocks[i - 1])
                    if 0 <= i - 2 < NBLK:
                        stageC(blocks[i - 2])
                S.barrier()

        def phase_ssm(l):
            NKc = cfg.NK
            if NKc <= 512:
                halves = [(0, NKc)]
            else:
                halves = [(0, NKc // 2), (NKc // 2, NKc - NKc // 2)]
            TWO_PI = 2.0 * math.pi
            with ExitStack() as st:
                def t32(name, shape=(128, 32), dt=F32):
                    return sb("s_" + name, list(shape), dt, st)
                lr, li, dtb, ar, ft = t32("lr"), t32("li"), t32("dtb"), t32("ar"), t32("ft")
                yp, ti, tf, tt, sn, cs, mag = (t32("yp"), t32("ti", dt=I32), t32("tf"), t32("tt"),
                                               t32("sn"), t32("cs"), t32("mag"))
                pwr = {p: t32("pwr%d" % (p + 7)) for p in range(-7, 9)}
                pwi = {p: t32("pwi%d" % (p + 7)) for p in range(-7, 9)}
                rho8, f8, cre, cim, t_a, t_b, dcol = (t32("rho8"), t32("f8"), t32("cre"), t32("cim"),
                                                      t32("ta"), t32("tb"), t32("dcol"))
                Bre, Bim = t32("Bre", (128, 32, 16)), t32("Bim", (128, 32, 16))
                bbr, bbi = t32("bbr", (128, 32, 16)), t32("bbi", (128, 32, 16))
                Craw = [t32("Craw%d" % i, (128, 4, 128)) for i in range(2)]
                Cn = [t32("Cn%d" % i, (128, 32, 16)) for i in range(2)]
                q1, q2, q3, q4 = (t32("q%d" % i, (128, 32, 16)) for i in range(4))
                X1, X1s, X2, X2s, X1p, X2p = (t32("X%d" % i, (128, 32, 8, 16)) for i in range(6))
                W1, W1s, W2, W2s, W3 = (t32("W%d" % i, (128, 32, 128), BF16) for i in range(5))
                maskBL = t32("maskBL", (128, 128))
                w3t = t32("w3t", (128, 128))
                w3r = [t32("w3r%d" % i, (128, 128)) for i in range(2)]
                kidx = t32("kidx", (128, NKc))
                cT = [t32("cT%d" % i, (128, NKc)) for i in range(2)]
                sT = [t32("sT%d" % i, (128, NKc)) for i in range(2)]
                rtab = [t32("rtab%d" % i, (128, NKc)) for i in range(2)]
                yk = t32("yk", (128, NKc))
                ki = t32("ki", (128, NKc), I32)
                kidx_i = ki
                kf = t32("kf", (128, NKc))
                kt = t32("kt", (128, NKc))
                Ut = [t32("U%d" % i, (128, NKc), BF16) for i in range(2)]
                m1, m2, Tt = t32("m1", (128, NKc)), t32("m2", (128, NKc)), t32("Tt", (128, NKc))
                Pm = [t32("Pm%d" % i, (128, NKc + 2), BF16) for i in range(2)]
                Qm = [t32("Qm%d" % i, (128, NKc + 2), BF16) for i in range(2)]
                yo1 = t32("yo", (128, NKc))
                yo = [yo1, yo1]
                pw = [ps("s_pw%d" % i, [128, 512], F32, st) for i in range(2)]
                pL = ps("s_pL", [128, 2, 512], F32, st)
                pLs = ps("s_pLs", [128, 2, 512], F32, st)
                py = ps("s_py", [128, 2, 512], F32, st)

                for hf in (0, 64):
                    S.dma("sp", out=lr[hf:hf + 64, :], in_=P["ssm_lam_re"][l].rearrange("g n -> n g"),
                          allow_slow_non_contiguous=True, writes=["lr"])
                    S.dma("sp", out=li[hf:hf + 64, :], in_=P["ssm_lam_im"][l].rearrange("g n -> n g"),
                          allow_slow_non_contiguous=True, writes=["li"])
                    S.dma("sp", out=dtb[hf:hf + 64, :], in_=P["ssm_log_dt"][l].partition_broadcast(64),
                          writes=["dtb"])
                    S.dma("sp", out=Bre[hf:hf + 64, :, :], in_=P["ssm_b_re"][l].rearrange("g n c -> n g c"),
                          writes=["Bre"])
                    S.dma("sp", out=Bim[hf:hf + 64, :, :], in_=P["ssm_b_im"][l].rearrange("g n c -> n g c"),
                          writes=["Bim"])
                for i, nm in enumerate(["ssm_c_re", "ssm_c_im"]):
                    for dup in range(2):
                        S.dma("sp", out=Craw[i][:, :, dup * 64:(dup + 1) * 64],
                              in_=P[nm][l].rearrange("g c n -> (g c) n").rearrange("(t p) n -> p t n", p=128),
                              writes=[("Craw", i)])
                for s8 in range(8):
                    S.dma("sp", out=dcol[s8 * 16:(s8 + 1) * 16, :], in_=P["ssm_d"][l].rearrange("(g c) -> c g", c=16),
                          allow_slow_non_contiguous=True, writes=["dcol"])
                S.op("pool", "memset", maskBL[:], 1.0, writes=["maskBL"])
                S.op("pool", "affine_select", out=maskBL[:].rearrange("p (j c) -> p j c", c=16),
                     in_=maskBL[:].rearrange("p (j c) -> p j c", c=16), pattern=[[16, 8], [0, 16]],
                     compare_op=ALU.is_ge, fill=0.0, base=15, channel_multiplier=-1,
                     reads=["maskBL"], writes=["maskBL"])
                S.op("pool", "iota", kidx_i[:], pattern=[[1, NKc]], base=0, channel_multiplier=0,
                     writes=["kti"])
                S.op("pool", "tensor_copy", out=kidx[:], in_=kidx_i[:], reads=["kti"], writes=["kidx"])
                for i in range(2):
                    S.op("pool", "memset", Pm[i][:], 0.0, writes=[("Pm", i)])
                    S.op("pool", "memset", Qm[i][:], 0.0, writes=[("Qm", i)])

                if int(os.environ.get("SSMSTOP", "9")) <= 1:
                    S.barrier()
                    return
                def V(eng, meth, *a, r=(), w=(), **kw):
                    S.op(eng, meth, *a, reads=list(r), writes=list(w), **kw)

                def sin_turns(dst, dname, y, yname, ti_, tf_, tt_, pre):
                    V("dve", "tensor_copy", out=ti_, in_=y, r=[yname], w=[pre + "ti"])
                    V("dve", "tensor_copy", out=tf_, in_=ti_, r=[pre + "ti"], w=[pre + "tf"])
                    V("dve", "tensor_tensor", out=tf_, in0=y, in1=tf_, op=ALU.subtract, r=[yname, pre + "tf"], w=[pre + "tf"])
                    V("dve", "tensor_single_scalar", out=tt_, in_=tf_, scalar=0.5, op=ALU.is_gt, r=[pre + "tf"], w=[pre + "tt"])
                    V("dve", "tensor_tensor", out=tf_, in0=tf_, in1=tt_, op=ALU.subtract, r=[pre + "tf", pre + "tt"], w=[pre + "tf"])
                    V("dve", "tensor_single_scalar", out=tt_, in_=tf_, scalar=-0.5, op=ALU.is_lt, r=[pre + "tf"], w=[pre + "tt"])
                    V("dve", "tensor_tensor", out=tf_, in0=tf_, in1=tt_, op=ALU.add, r=[pre + "tf", pre + "tt"], w=[pre + "tf"])
                    if dst is not None:
                        V("act", "activation", out=dst, in_=tf_, func=AF.Sin, scale=6.283185, r=[pre + "tf"], w=[dname])

                V("act", "activation", out=dtb[:], in_=dtb[:], func=AF.Exp, r=["dtb"], w=["dtb"])
                V("dve", "tensor_tensor", out=ar[:], in0=lr[:], in1=dtb[:], op=ALU.mult, r=["lr", "dtb"], w=["ar"])
                V("dve", "tensor_tensor", out=ft[:], in0=li[:], in1=dtb[:], op=ALU.mult, r=["li", "dtb"], w=["ft"])
                V("dve", "tensor_scalar_mul", out=ft[:], in0=ft[:], scalar1=1.0 / TWO_PI, r=["ft"], w=["ft"])
                for p in range(-7, 9):
                    V("act", "activation", out=mag[:], in_=ar[:], func=AF.Exp, scale=float(p), r=["ar"], w=["mag"])
                    if p == 8:
                        V("dve", "tensor_copy", out=rho8[:], in_=mag[:], r=["mag"], w=["rho8"])
                    V("dve", "tensor_scalar_mul", out=yp[:], in0=ft[:], scalar1=float(p), r=["ft"], w=["yp"])
                    sin_turns(sn[:], "sn", yp[:], "yp", ti[:], tf[:], tt[:], "a")
                    if p == 8:
                        V("dve", "tensor_copy", out=f8[:], in_=tf[:], r=["atf"], w=["f8"])
                    V("dve", "tensor_scalar_add", out=yp[:], in0=yp[:], scalar1=0.25, r=["yp"], w=["yp"])
                    sin_turns(cs[:], "cs", yp[:], "yp", ti[:], tf[:], tt[:], "a")
                    V("dve", "tensor_tensor", out=pwr[p][:], in0=mag[:], in1=cs[:], op=ALU.mult, r=["mag", "cs"], w=[("pwr", p)])
                    V("dve", "tensor_tensor", out=pwi[p][:], in0=mag[:], in1=sn[:], op=ALU.mult, r=["mag", "sn"], w=[("pwi", p)])
                if int(os.environ.get("SSMSTOP", "9")) <= 2:
                    S.barrier()
                    return
                V("dve", "tensor_scalar_add", out=t_a[:], in0=pwr[1][:], scalar1=-1.0, r=[("pwr", 1)], w=["ta"])
                V("dve", "tensor_tensor", out=cre[:], in0=t_a[:], in1=lr[:], op=ALU.mult, r=["ta", "lr"], w=["cre"])
                V("dve", "tensor_tensor", out=t_b[:], in0=pwi[1][:], in1=li[:], op=ALU.mult, r=[("pwi", 1), "li"], w=["tb"])
                V("dve", "tensor_tensor", out=cre[:], in0=cre[:], in1=t_b[:], op=ALU.add, r=["cre", "tb"], w=["cre"])
                V("dve", "tensor_tensor", out=cim[:], in0=pwi[1][:], in1=lr[:], op=ALU.mult, r=[("pwi", 1), "lr"], w=["cim"])
                V("dve", "tensor_tensor", out=t_b[:], in0=t_a[:], in1=li[:], op=ALU.mult, r=["ta", "li"], w=["tb"])
                V("dve", "tensor_tensor", out=cim[:], in0=cim[:], in1=t_b[:], op=ALU.subtract, r=["cim", "tb"], w=["cim"])
                V("dve", "tensor_tensor", out=t_a[:], in0=lr[:], in1=lr[:], op=ALU.mult, r=["lr"], w=["ta"])
                V("dve", "tensor_tensor", out=t_b[:], in0=li[:], in1=li[:], op=ALU.mult, r=["li"], w=["tb"])
                V("dve", "tensor_tensor", out=t_a[:], in0=t_a[:], in1=t_b[:], op=ALU.add, r=["ta", "tb"], w=["ta"])
                V("dve", "reciprocal", out=t_a[:], in_=t_a[:], r=["ta"], w=["ta"])
                V("dve", "tensor_tensor", out=cre[:], in0=cre[:], in1=t_a[:], op=ALU.mult, r=["cre", "ta"], w=["cre"])
                V("dve", "tensor_tensor", out=cim[:], in0=cim[:], in1=t_a[:], op=ALU.mult, r=["cim", "ta"], w=["cim"])

                def bc(t):
                    return t[:, :].unsqueeze(2).broadcast_to([128, 32, 16])

                def cplx(fr, fi, frn, fin, xr, xi, xrn, xin):
                    V("dve", "tensor_tensor", out=q1[:], in0=xr[:], in1=bc(fr), op=ALU.mult, r=[xrn, frn, "q1"], w=["q1"])
                    V("dve", "tensor_tensor", out=q3[:], in0=xi[:], in1=bc(fi), op=ALU.mult, r=[xin, fin, "q3"], w=["q3"])
                    V("dve", "tensor_tensor", out=q1[:], in0=q1[:], in1=q3[:], op=ALU.subtract, r=["q1", "q3"], w=["q1"])
                    V("dve", "tensor_tensor", out=q2[:], in0=xi[:], in1=bc(fr), op=ALU.mult, r=[xin, frn, "q2"], w=["q2"])
                    V("dve", "tensor_tensor", out=q4[:], in0=xr[:], in1=bc(fi), op=ALU.mult, r=[xrn, fin, "q4"], w=["q4"])
                    V("dve", "tensor_tensor", out=q2[:], in0=q2[:], in1=q4[:], op=ALU.add, r=["q2", "q4"], w=["q2"])

                def put(dst, dname, blk, lo_src, lo_sign, up_src, up_sign):
                    for (rows, src, sign, eng) in ((slice(0, 64), lo_src, lo_sign, "act"),
                                                   (slice(64, 128), up_src, up_sign, "pool")):
                        srcn = "q1" if src is q1 else "q2"
                        if eng == "act":
                            V("act", "activation", out=dst[rows, :, blk, :], in_=src[rows, :, :], func=AF.Copy,
                              scale=float(sign), r=[srcn], w=[dname])
                        else:
                            V("pool", "tensor_scalar", out=dst[rows, :, blk, :], in0=src[rows, :, :],
                              scalar1=float(sign), scalar2=None, op0=ALU.mult, r=[srcn], w=[dname])

                if int(os.environ.get("SSMSTOP", "9")) <= 3:
                    S.barrier()
                    return
                cplx(cre, cim, "cre", "cim", Bre, Bim, "Bre", "Bim")
                V("dve", "tensor_copy", out=bbr[:], in_=q1[:], r=["q1"], w=["bbr"])
                V("dve", "tensor_copy", out=bbi[:], in_=q2[:], r=["q2"], w=["bbi"])
                for i in range(2):
                    for t4 in range(4):
                        pb = pw[(i * 4 + t4) % 2]
                        V("pe", "transpose", out=pb[:, 0:128], in_=Craw[i][:, t4, :], identity=ident_f[:, :],
                          r=[("Craw", i), "ident_f"], w=[("pw", (i * 4 + t4) % 2)])
                        V("act", "activation", out=Cn[i][:, t4 * 8:(t4 + 1) * 8, :],
                          in_=pb[:, 0:128].rearrange("p (g c) -> p g c", c=16), func=AF.Copy,
                          r=[("pw", (i * 4 + t4) % 2)], w=[("Cn", i)])
                for s8 in range(8):
                    p = 7 - s8
                    cplx(pwr[p], pwi[p], ("pwr", p), ("pwi", p), bbr, bbi, "bbr", "bbi")
                    put(X1, "X1", s8, q1, 1, q2, 1)
                    put(X1s, "X1s", s8, q2, 1, q1, -1)
                    p = -s8
                    cplx(pwr[p], pwi[p], ("pwr", p), ("pwi", p), bbr, bbi, "bbr", "bbi")
                    put(X1p, "X1p", s8, q1, 1, q2, 1)
                    p = s8 + 1
                    cplx(pwr[p], pwi[p], ("pwr", p), ("pwi", p), Cn[0], Cn[1], ("Cn", 0), ("Cn", 1))
                    put(X2, "X2", s8, q1, 1, q2, -1)
                    put(X2s, "X2s", s8, q2, -1, q1, -1)
                    p = s8
                    cplx(pwr[p], pwi[p], ("pwr", p), ("pwi", p), Cn[0], Cn[1], ("Cn", 0), ("Cn", 1))
                    put(X2p, "X2p", s8, q1, 1, q2, -1)
                if int(os.environ.get("SSMSTOP", "9")) <= 4:
                    S.barrier()
                    return
                V("act", "activation", out=W2[:].rearrange("p g x -> p (g x)"),
                  in_=X2[:].rearrange("p g s c -> p (g s c)"), func=AF.Copy, r=["X2"], w=["W2"])
                V("dve", "tensor_copy", out=W2s[:].rearrange("p g x -> p (g x)"),
                  in_=X2s[:].rearrange("p g s c -> p (g s c)"), r=["X2s"], w=["W2s"])
                for g in range(NG):
                    b = g % 2
                    V("pe", "transpose", out=pw[b][:, 0:128], in_=X1[:, g].rearrange("p s c -> p (s c)"),
                      identity=ident_f[:, :], r=["X1", "ident_f"], w=[("pw", b)])
                    V("pe", "transpose", out=pw[b][:, 128:256], in_=X1s[:, g].rearrange("p s c -> p (s c)"),
                      identity=ident_f[:, :], r=["X1s", "ident_f"], w=[("pw", b)])
                    V("pe", "matmul", pw[b][:, 256:384], lhsT=X1p[:, g].rearrange("p s c -> p (s c)"),
                      rhs=X2p[:, g].rearrange("p s c -> p (s c)"), start=True, stop=True,
                      r=["X1p", "X2p"], w=[("pw", b)])
                    V("act", "activation", out=W1[:, g, :], in_=pw[b][:, 0:128], func=AF.Copy,
                      r=[("pw", b)], w=["W1"])
                    V("act", "activation", out=W1s[:, g, :], in_=pw[b][:, 128:256], func=AF.Copy,
                      r=[("pw", b)], w=["W1s"])
                    V("act", "activation", out=w3r[b][:], in_=pw[b][:, 256:384], func=AF.Copy,
                      r=[("pw", b)], w=[("w3r", b)])
                    V("dve", "tensor_tensor", out=w3t[:], in0=w3r[b][:], in1=maskBL[:], op=ALU.mult,
                      r=[("w3r", b), "maskBL"], w=["w3t"])
                    V("dve", "scalar_tensor_tensor", out=W3[:, g, :], in0=ident_f[:], scalar=dcol[:, g:g + 1],
                      in1=w3t[:], op0=ALU.mult, op1=ALU.add, r=["ident_f", "dcol", "w3t"], w=["W3"])

                if int(os.environ.get("SSMSTOP", "9")) <= 5:
                    S.barrier()
                    return
                it = 0
                for g in range(NG):
                    gb = g % 2
                    V("dve", "tensor_scalar_mul", out=yk[:], in0=kidx[:], scalar1=f8[:, g:g + 1],
                      r=["kidx", "f8"], w=["yk"])
                    sin_turns(sT[gb][:], ("sT", gb), yk[:], "yk", ki[:], kf[:], kt[:], "k")
                    V("dve", "tensor_scalar_add", out=yk[:], in0=yk[:], scalar1=0.25, r=["yk"], w=["yk"])
                    sin_turns(cT[gb][:], ("cT", gb), yk[:], "yk", ki[:], kf[:], kt[:], "k")
                    V("act", "activation", out=rtab[gb][:], in_=kidx[:], func=AF.Identity, scale=0.0,
                      bias=rho8[:, g:g + 1], r=["kidx", "rho8"], w=[("rtab", gb)])
                    RL = int(os.environ.get("RUNLVL", "9"))
                    for s in range(NS):
                        if RL < 2:
                            continue
                        ub = it % 2
                        it += 1
                        S.dma("sp", out=Ut[ub][:], in_=Usc[s, g].rearrange("s c k -> (s c) k"),
                              writes=[("U", ub)])
                        for hi, (lo, wd) in enumerate(halves):
                            V("pe", "matmul", pL[:, hi, 0:wd], lhsT=W1[:, g, :], rhs=Ut[ub][:, lo:lo + wd],
                              start=True, stop=True, r=["W1", ("U", ub)], w=["pL"])
                            V("pe", "matmul", pLs[:, hi, 0:wd], lhsT=W1s[:, g, :], rhs=Ut[ub][:, lo:lo + wd],
                              start=True, stop=True, r=["W1s", ("U", ub)], w=["pLs"])
                        if RL < 3:
                            continue
                        for hi, (lo, wd) in enumerate(halves):
                            V("dve", "tensor_tensor", out=m1[:, lo:lo + wd], in0=pL[:, hi, 0:wd],
                              in1=cT[gb][:, lo:lo + wd], op=ALU.mult, r=["pL", ("cT", gb)], w=["m1"])
                            V("dve", "tensor_tensor", out=m2[:, lo:lo + wd], in0=pLs[:, hi, 0:wd],
                              in1=sT[gb][:, lo:lo + wd], op=ALU.mult, r=["pLs", ("sT", gb)], w=["m2"])
                        V("dve", "tensor_tensor", out=m1[:], in0=m1[:], in1=m2[:], op=ALU.add,
                          r=["m1", "m2"], w=["m1"])
                        V("dve", "tensor_tensor_scan", out=Tt[:], data0=rtab[gb][:], data1=m1[:], initial=0.0,
                          op0=ALU.mult, op1=ALU.add, r=[("rtab", gb), "m1"], w=["Tt"])
                        V("dve", "tensor_tensor", out=Pm[ub][:, 1:NKc + 1], in0=Tt[:], in1=cT[gb][:], op=ALU.mult,
                          r=["Tt", ("cT", gb)], w=[("Pm", ub)])
                        V("pool", "tensor_tensor", out=Qm[ub][:, 1:NKc + 1], in0=Tt[:], in1=sT[gb][:], op=ALU.mult,
                          r=["Tt", ("sT", gb)], w=[("Qm", ub)])
                        if RL < 4:
                            continue
                        for hi, (lo, wd) in enumerate(halves):
                            V("pe", "matmul", py[:, hi, 0:wd], lhsT=W3[:, g, :], rhs=Ut[ub][:, lo:lo + wd],
                              start=True, stop=False, r=["W3", ("U", ub)], w=["py"])
                            V("pe", "matmul", py[:, hi, 0:wd], lhsT=W2[:, g, :],
                              rhs=Pm[ub][:, lo:lo + wd], start=False, stop=False,
                              r=["W2", ("Pm", ub)], w=["py"])
                            V("pe", "matmul", py[:, hi, 0:wd], lhsT=W2s[:, g, :],
                              rhs=Qm[ub][:, lo:lo + wd], start=False, stop=True,
                              r=["W2s", ("Qm", ub)], w=["py"])
                        if RL < 5:
                            continue
                        for hi, (lo, wd) in enumerate(halves):
                            V("act", "activation", out=yo[ub][:, lo:lo + wd], in_=py[:, hi, 0:wd], func=AF.Copy,
                              r=["py"], w=["yo"])
                        S.dma("sp", out=ysS[s, g].rearrange("s c k -> (s c) k"), in_=yo[ub][:],
                              reads=["yo"])
                S.barrier()

        def phase_mixB(l):
            w_in = P["w_in"][l]
            c0 = gcol(1, l)
            HALO = CW - 1
            with ExitStack() as st:
                wB = sb("b_w", [128, DT, 4096], BF16, st)
                wglu = sb("b_wglu", [128, 4, 2048], BF16, st)
                wpw = sb("b_wpw", [128, 4, D], BF16, st)
                wo = sb("b_wo", [128, 4, D], BF16, st)
                wout = sb("b_wout", [128, DT, D], BF16, st)
                cw = sb("b_cw", [128, 4, CW], F32, st)
                cb = sb("b_cb", [128, 4], F32, st)
                lng = sb("b_lng", [128, 4], F32, st)
                lnb = sb("b_lnb", [128, 4], F32, st)
                eps_ln = sb("b_epsln", [128, 1], F32, st)
                hs = sb("b_hs", [128, DT, 512], F32, st)
                xn = sb("b_xn", [128, DT, 512], BF16, st)
                sqb = [sb("b_sq%d" % i, [128, 512], BF16, st) for i in range(2)]
                rstd = sb("b_rstd", [128, 512], F32, st)
                hc = sb("b_hc", [128, 4, HALO + 512], F32, st)
                cacc = sb("b_cacc", [128, 4, 512], F32, st)
                hcv = sb("b_hcv", [128, 4, 512], BF16, st)
                yst = [sb("b_yst%d" % i, [128, 8, 64], F32, st) for i in range(2)]
                gy = sb("b_gy", [128, 4, 512], BF16, st)
                oat = sb("b_oat", [128, 4, 512], BF16, st)
                mrg = sb("b_mrg", [128, DT, 512], BF16, st)
                sg = [sb("b_sg%d" % i, [128, 512], F32, st) for i in range(2)]
                tm = [sb("b_tm%d" % i, [128, 512], F32, st) for i in range(3)]
                macc = sb("b_macc", [128, 512], F32, st)
                mu = sb("b_mu", [128, 512], F32, st)
                lrs = sb("b_lrs", [128, 512], F32, st)
                pss = ps("b_pss", [128, 512], F32, st)
                NPB = 6
                pp = [ps("b_pp%d" % i, [128, 512], F32, st) for i in range(NPB)]
                pcnt = [0]

                def nxt():
                    b = pcnt[0] % NPB
                    pcnt[0] += 1
                    return b

                for dt in range(DT):
                    load_w_cast(wB[:, dt, 0:1024], w_in[dt * 128:(dt + 1) * 128, 512:1536], ("wB", dt))
                    load_w_cast(wB[:, dt, 1024:4096], w_in[dt * 128:(dt + 1) * 128, 3072:6144], ("wB", dt, 1))
                    load_w_cast(wout[:, dt, :], P["w_out"][l][dt * 128:(dt + 1) * 128, :], ("wout", dt))
                for ct in range(4):
                    load_w_cast(wglu[:, ct, :], P["ssm_w_glu"][l][ct * 128:(ct + 1) * 128, :], ("wglu", ct))
                    load_w_cast(wpw[:, ct, :], P["conv_w_out"][l][ct * 128:(ct + 1) * 128, :], ("wpw", ct))
                    load_w_cast(wo[:, ct, :], P["attn_w_o"][l][ct * 128:(ct + 1) * 128, :], ("wo", ct))
                wBr = [("wB", dt) for dt in range(DT)] + [("wB", dt, 1) for dt in range(DT)]
                for ct in range(4):
                    S.dma("sp", out=cw[:, ct, :], in_=P["conv_w"][l][:, ct * 128:(ct + 1) * 128].rearrange("j p -> p j"),
                          allow_slow_non_contiguous=True, writes=["cw"])
                for nm, tl in (("conv_b", cb), ("conv_ln_g", lng), ("conv_ln_b", lnb)):
                    S.dma("sp", out=tl[:], in_=P[nm][l].rearrange("(t p) -> p t", p=128),
                          allow_slow_non_contiguous=True, writes=[nm])
                S.op("pool", "memset", eps_ln[:], LN_EPS, writes=["eps_ln"])

                ycnt = 0
                for s in range(NS):
                    for (t0, n) in cfg.tiles:
                        nk = n // 8
                        k0 = t0 // 8
                        load_h(hs, s, t0, n)
                        S.dma("sp", out=oat[:, :, 0:n],
                              in_=oattT[s, :, t0:t0 + n].rearrange("(t p) n -> p t n", p=128), writes=["oat"])
                        rmsnorm(hs, xn, sqb, pss, rstd, c0, n)
                        if t0 == 0:
                            S.op("pool", "memset", hc[:, :, 0:HALO], 0.0, writes=["hc"])
                        for ct in range(4):
                            ba, bg = nxt(), nxt()
                            for dt in range(DT):
                                S.op("pe", "matmul", pp[ba][:, 0:n], lhsT=wB[:, dt, ct * 128:(ct + 1) * 128],
                                     rhs=xn[:, dt, 0:n], start=(dt == 0), stop=(dt == DT - 1),
                                     reads=wBr + ["xn"], writes=[("pp", ba)])
                            for dt in range(DT):
                                S.op("pe", "matmul", pp[bg][:, 0:n], lhsT=wB[:, dt, 512 + ct * 128:512 + (ct + 1) * 128],
                                     rhs=xn[:, dt, 0:n], start=(dt == 0), stop=(dt == DT - 1),
                                     reads=wBr + ["xn"], writes=[("pp", bg)])
                            S.op("act", "activation", out=sg[ct % 2][:, 0:n], in_=pp[bg][:, 0:n], func=AF.Sigmoid,
                                 reads=[("pp", bg)], writes=[("sg", ct % 2)])
                            S.op("dve", "tensor_tensor", out=hc[:, ct, HALO:HALO + n], in0=pp[ba][:, 0:n],
                                 in1=sg[ct % 2][:, 0:n], op=ALU.mult,
                                 reads=[("pp", ba), ("sg", ct % 2)], writes=["hc"])
                        for ct in range(4):
                            cres = ("cacc", ct)
                            if ct % 2 == 0:
                                S.op("dve", "tensor_scalar", out=cacc[:, ct, 0:n], in0=hc[:, ct, 0:n],
                                     scalar1=cw[:, ct, 0:1], scalar2=cb[:, ct:ct + 1], op0=ALU.mult, op1=ALU.add,
                                     reads=["hc", "cw", "conv_b"], writes=[cres])
                                for j in range(1, CW):
                                    S.op("dve", "scalar_tensor_tensor", out=cacc[:, ct, 0:n], in0=hc[:, ct, j:j + n],
                                         scalar=cw[:, ct, j:j + 1], in1=cacc[:, ct, 0:n], op0=ALU.mult, op1=ALU.add,
                                         reads=["hc", "cw", cres], writes=[cres])
                            else:
                                pt_ = tm[2]
                                S.op("pool", "tensor_scalar", out=cacc[:, ct, 0:n], in0=hc[:, ct, 0:n],
                                     scalar1=cw[:, ct, 0:1], scalar2=cb[:, ct:ct + 1], op0=ALU.mult, op1=ALU.add,
                                     reads=["hc", "cw", "conv_b"], writes=[cres])
                                for j in range(1, CW):
                                    S.op("pool", "tensor_scalar", out=pt_[:, 0:n], in0=hc[:, ct, j:j + n],
                                         scalar1=cw[:, ct, j:j + 1], scalar2=None, op0=ALU.mult,
                                         reads=["hc", "cw"], writes=["ptmp"])
                                    S.op("pool", "tensor_tensor", out=cacc[:, ct, 0:n], in0=cacc[:, ct, 0:n],
                                         in1=pt_[:, 0:n], op=ALU.add, reads=["ptmp", cres], writes=[cres])
                        if n >= HALO:
                            S.op("act", "activation", out=hc[:, :, 0:HALO], in_=hc[:, :, n:n + HALO], func=AF.Copy,
                                 reads=["hc"] + [("cacc", c) for c in range(4)], writes=["hc"])
                        bmu, bvar = nxt(), nxt()
                        for ct in range(4):
                            b2 = ct % 2
                            S.op("act", "activation", out=sqb[b2][:, 0:n], in_=cacc[:, ct, 0:n], func=AF.Copy,
                                 reads=[("cacc", ct)], writes=[("sqb", b2)])
                            S.op("pe", "matmul", pp[bmu][:, 0:n], lhsT=ones_bf[:], rhs=sqb[b2][:, 0:n],
                                 start=(ct == 0), stop=(ct == 3), reads=[("sqb", b2), "ones_bf"], writes=[("pp", bmu)])
                        for ct in range(4):
                            b2 = ct % 2
                            S.op("act", "activation", out=sqb[b2][:, 0:n], in_=cacc[:, ct, 0:n], func=AF.Square,
                                 reads=[("cacc", ct)], writes=[("sqb", b2)])
                            S.op("pe", "matmul", pp[bvar][:, 0:n], lhsT=ones_bf[:], rhs=sqb[b2][:, 0:n],
                                 start=(ct == 0), stop=(ct == 3), reads=[("sqb", b2), "ones_bf"], writes=[("pp", bvar)])
                        S.op("act", "activation", out=mu[:, 0:n], in_=pp[bmu][:, 0:n], func=AF.Copy, scale=1.0 / 512,
                             reads=[("pp", bmu)], writes=["mu"])
                        S.op("dve", "tensor_tensor", out=tm[0][:, 0:n], in0=mu[:, 0:n], in1=mu[:, 0:n], op=ALU.mult,
                             reads=["mu"], writes=[("tm", 0)])
                        S.op("dve", "scalar_tensor_tensor", out=lrs[:, 0:n], in0=pp[bvar][:, 0:n], scalar=1.0 / 512,
                             in1=tm[0][:, 0:n], op0=ALU.mult, op1=ALU.subtract,
                             reads=[("pp", bvar), ("tm", 0)], writes=["lrs"])
                        S.op("act", "activation", out=lrs[:, 0:n], in_=lrs[:, 0:n], func=AF.Sqrt, bias=eps_ln[:], scale=1.0,
                             reads=["lrs", "eps_ln"], writes=["lrs"])
                        S.op("dve", "reciprocal", out=lrs[:, 0:n], in_=lrs[:, 0:n], reads=["lrs"], writes=["lrs"])
                        for ct in range(4):
                            S.op("dve", "tensor_tensor", out=tm[0][:, 0:n], in0=cacc[:, ct, 0:n], in1=mu[:, 0:n],
                                 op=ALU.subtract, reads=[("cacc", ct), "mu"], writes=[("tm", 0)])
                            S.op("dve", "tensor_tensor", out=tm[0][:, 0:n], in0=tm[0][:, 0:n], in1=lrs[:, 0:n],
                                 op=ALU.mult, reads=[("tm", 0), "lrs"], writes=[("tm", 0)])
                            S.op("act", "activation", out=hcv[:, ct, 0:n], in_=tm[0][:, 0:n], func=AF.Silu,
                                 scale=lng[:, ct:ct + 1], bias=lnb[:, ct:ct + 1],
                                 reads=[("tm", 0), "conv_ln_g", "conv_ln_b"], writes=["hcv"])
                        for ct in range(4):
                            yb = ycnt % 2
                            ycnt += 1
                            yt = yst[yb]
                            for gl in range(8):
                                S.dma("sp", out=yt[gl * 16:(gl + 1) * 16, :, 0:nk],
                                      in_=ysS[s, ct * 8 + gl, :, :, k0:k0 + nk].rearrange("j c k -> c j k"),
                                      writes=[("yst", yb)])
                            yv = yt[:, :, 0:nk]
                            t1 = tm[0][:, 0:n].rearrange("p (j k) -> p j k", j=8)
                            t2 = tm[1][:, 0:n].rearrange("p (j k) -> p j k", j=8)
                            S.op("act", "activation", out=t1, in_=yv, func=AF.Square,
                                 reads=[("yst", yb)], writes=[("tm", 0)])
                            S.op("dve", "tensor_scalar", out=t1, in0=t1, scalar1=0.044715, scalar2=1.0,
                                 op0=ALU.mult, op1=ALU.add, reads=[("tm", 0)], writes=[("tm", 0)])
                            S.op("dve", "tensor_tensor", out=t1, in0=t1, in1=yv, op=ALU.mult,
                                 reads=[("tm", 0), ("yst", yb)], writes=[("tm", 0)])
                            S.op("act", "activation", out=t2, in_=t1, func=AF.Tanh, scale=0.7978845608,
                                 reads=[("tm", 0)], writes=[("tm", 1)])
                            S.op("dve", "tensor_scalar", out=t2, in0=t2, scalar1=0.5, scalar2=0.5,
                                 op0=ALU.mult, op1=ALU.add, reads=[("tm", 1)], writes=[("tm", 1)])
                            S.op("dve", "tensor_tensor", out=gy[:, ct, 0:n].rearrange("p (k j) -> p j k", j=8),
                                 in0=t2, in1=yv, op=ALU.mult,
                                 reads=[("tm", 1), ("yst", yb)], writes=["gy"])
                        for f in range(DT):
                            def gate(br):
                                bgt = nxt()
                                c1 = 1024 + br * 1024 + f * 128
                                for dt in range(DT):
                                    S.op("pe", "matmul", pp[bgt][:, 0:n], lhsT=wB[:, dt, c1:c1 + 128],
                                         rhs=xn[:, dt, 0:n], start=(dt == 0), stop=(dt == DT - 1),
                                         reads=wBr + ["xn"], writes=[("pp", bgt)])
                                S.op("act", "activation", out=sg[br % 2][:, 0:n], in_=pp[bgt][:, 0:n], func=AF.Sigmoid,
                                     reads=[("pp", bgt)], writes=[("sg", br % 2)])
                                return sg[br % 2], ("sg", br % 2)
                            ba, bg = nxt(), nxt()
                            for ct in range(4):
                                S.op("pe", "matmul", pp[ba][:, 0:n], lhsT=wglu[:, ct, f * 128:(f + 1) * 128],
                                     rhs=gy[:, ct, 0:n], start=(ct == 0), stop=(ct == 3),
                                     reads=[("wglu", c) for c in range(4)] + ["gy"], writes=[("pp", ba)])
                            for ct in range(4):
                                S.op("pe", "matmul", pp[bg][:, 0:n], lhsT=wglu[:, ct, 1024 + f * 128:1024 + (f + 1) * 128],
                                     rhs=gy[:, ct, 0:n], start=(ct == 0), stop=(ct == 3),
                                     reads=[("wglu", c) for c in range(4)] + ["gy"], writes=[("pp", bg)])
                            S.op("act", "activation", out=tm[1][:, 0:n], in_=pp[bg][:, 0:n], func=AF.Sigmoid,
                                 reads=[("pp", bg)], writes=[("tm", 1)])
                            S.op("dve", "tensor_tensor", out=tm[0][:, 0:n], in0=pp[ba][:, 0:n], in1=tm[1][:, 0:n],
                                 op=ALU.mult, reads=[("pp", ba), ("tm", 1)], writes=[("tm", 0)])
                            g0, g0r = gate(0)
                            S.op("dve", "tensor_tensor", out=macc[:, 0:n], in0=tm[0][:, 0:n], in1=g0[:, 0:n],
                                 op=ALU.mult, reads=[("tm", 0), g0r], writes=["macc"])
                            bc_ = nxt()
                            for ct in range(4):
                                S.op("pe", "matmul", pp[bc_][:, 0:n], lhsT=wpw[:, ct, f * 128:(f + 1) * 128],
                                     rhs=hcv[:, ct, 0:n], start=(ct == 0), stop=(ct == 3),
                                     reads=[("wpw", c) for c in range(4)] + ["hcv"], writes=[("pp", bc_)])
                            g1, g1r = gate(1)
                            S.op("dve", "tensor_tensor", out=tm[0][:, 0:n], in0=pp[bc_][:, 0:n], in1=g1[:, 0:n],
                                 op=ALU.mult, reads=[("pp", bc_), g1r], writes=[("tm", 0)])
                            S.op("dve", "tensor_tensor", out=macc[:, 0:n], in0=macc[:, 0:n], in1=tm[0][:, 0:n],
                                 op=ALU.add, reads=["macc", ("tm", 0)], writes=["macc"])
                            bo_ = nxt()
                            for ct in range(4):
                                S.op("pe", "matmul", pp[bo_][:, 0:n], lhsT=wo[:, ct, f * 128:(f + 1) * 128],
                                     rhs=oat[:, ct, 0:n], start=(ct == 0), stop=(ct == 3),
                                     reads=[("wo", c) for c in range(4)] + ["oat"], writes=[("pp", bo_)])
                            g2, g2r = gate(2)
                            S.op("dve", "tensor_tensor", out=tm[0][:, 0:n], in0=pp[bo_][:, 0:n], in1=g2[:, 0:n],
                                 op=ALU.mult, reads=[("pp", bo_), g2r], writes=[("tm", 0)])
                            S.op("dve", "tensor_tensor", out=mrg[:, f, 0:n], in0=macc[:, 0:n], in1=tm[0][:, 0:n],
                                 op=ALU.add, reads=["macc", ("tm", 0)], writes=[("mrg", f)])
                        for o in range(DT):
                            bo_ = nxt()
                            for f in range(DT):
                                S.op("pe", "matmul", pp[bo_][:, 0:n], lhsT=wout[:, f, o * 128:(o + 1) * 128],
                                     rhs=mrg[:, f, 0:n], start=(f == 0), stop=(f == DT - 1),
                                     reads=[("wout", f), ("mrg", f)], writes=[("pp", bo_)])
                            S.op("dve", "tensor_tensor", out=hs[:, o, 0:n], in0=pp[bo_][:, 0:n], in1=hs[:, o, 0:n],
                                 op=ALU.add, reads=[("pp", bo_), "hs"], writes=["hs"])
                        store_h(hs, s, t0, n)
                S.barrier()

        phases = cfg.phases

        def on(name):
            return phases is None or name in phases

        phase_in()
        for l in range(DEPTH):
            if on("ffn1"):
                phase_ffn(l, 0)
            if on("mixa"):
                phase_mixA(l)
            if on("att"):
                phase_att(l)
            if on("ssm"):
                phase_ssm(l)
            if on("mixb"):
                phase_mixB(l)
            if on("ffn2"):
                phase_ffn(l, 1)
        phase_out()

        with nc.Block() as block:
            S.emit(block)
    return nc


def kernel(**inputs):
    cfg = Cfg()
    nc = build(cfg)
    n = 8
    in_maps = []
    for c in range(n):
        m = {}
        for k, v in inputs.items():
            a = np.asarray(v)
            if k == "x":
                a = a[c * cfg.nseq:(c + 1) * cfg.nseq]
            m[k] = np.ascontiguousarray(a, dtype=np.float32)
        in_maps.append(m)
    res = run_bass_kernel_spmd(nc, in_maps, core_ids=list(range(n)))
    return np.concatenate([np.asarray(r["out"]) for r in res.results], axis=0).astype(np.float32)
```

```python
import math
import os
from contextlib import ExitStack

import numpy as np
import concourse.bass as bass
import concourse.mybir as mybir
from concourse.bass_utils import run_bass_kernel_spmd

F32 = mybir.dt.float32
BF16 = mybir.dt.bfloat16
I32 = mybir.dt.int32
AF = mybir.ActivationFunctionType
ALU = mybir.AluOpType

D = 1024
DT = 8
DFF = 2816
FT = 22
NMETA = 16
DIN = 6144
NG = 32
NST = 64
CW = 31
RMS_EPS = 1e-6
LN_EPS = 1e-5

ENGS = ["pe", "act", "dve", "pool", "sp"]


class Sched:
    def __init__(self, nc, es, n_dma_sems=24):
        self.nc = nc
        self.ops = {e: [] for e in ENGS}
        self.sem = {e: es.enter_context(nc.semaphore("sem_" + e)) for e in ENGS}
        self.cnt = {e: 0 for e in ENGS}
        self.dsem = [es.enter_context(nc.semaphore("semd%d" % i)) for i in range(n_dma_sems)]
        self.duse = [0] * n_dma_sems
        self.dnext = 0
        self.waited = {}
        self.lastw = {}
        self.readers = {}

    def _semof(self, key):
        if isinstance(key, str):
            return self.sem[key]
        return self.dsem[key[1]]

    def _deps(self, eng, reads, writes):
        toks = {}
        def add(t):
            k, v = t
            if toks.get(k, 0) < v:
                toks[k] = v
        for r in reads:
            if r in self.lastw:
                add(self.lastw[r])
        for w in writes:
            if w in self.lastw:
                add(self.lastw[w])
            for k, v in self.readers.get(w, {}).items():
                add((k, v))
        waits = []
        for k, v in toks.items():
            if k == "pe" and eng == "pe":
                continue
            if self.waited.get((eng, k), 0) >= v:
                continue
            self.waited[(eng, k)] = v
            waits.append((k, v))
        return waits

    def _record(self, tok, reads, writes):
        k, v = tok
        for r in reads:
            d = self.readers.setdefault(r, {})
            if d.get(k, 0) < v:
                d[k] = v
        for w in writes:
            self.lastw[w] = tok
            self.readers[w] = {}

    def op(self, eng, meth, *args, reads=(), writes=(), excl=(), **kw):
        fn = (meth, args, kw)
        if excl:
            reads = list(reads) + list(excl)
            writes = list(writes) + list(excl)
        waits = self._deps(eng, reads, writes)
        self.cnt[eng] += 1
        tok = (eng, self.cnt[eng])
        self.ops[eng].append((waits, fn, eng, 1))
        self._record(tok, reads, writes)

    def dma(self, eng, reads=(), writes=(), **kw):
        fn = ("dma_start", (), kw)
        waits = self._deps(eng, reads, writes)
        slot = self.dnext
        self.dnext = (self.dnext + 1) % len(self.dsem)
        key = ("d", slot)
        if self.duse[slot] > 0:
            v = 16 * self.duse[slot]
            if self.waited.get((eng, key), 0) < v:
                self.waited[(eng, key)] = v
                waits.append((key, v))
        self.duse[slot] += 1
        tok = (key, 16 * self.duse[slot])
        self.ops[eng].append((waits, fn, key, 16))
        self._record(tok, reads, writes)

    def barrier(self):
        for eng in ENGS:
            waits = []
            for k in ENGS:
                v = self.cnt[k]
                if v > 0 and k != eng and self.waited.get((eng, k), 0) < v:
                    self.waited[(eng, k)] = v
                    waits.append((k, v))
            if eng != "pe":
                v = self.cnt[eng]
                if v > 0 and self.waited.get((eng, eng), 0) < v:
                    self.waited[(eng, eng)] = v
                    waits.append((eng, v))
            for i, u in enumerate(self.duse):
                key = ("d", i)
                if u > 0 and self.waited.get((eng, key), 0) < 16 * u:
                    self.waited[(eng, key)] = 16 * u
                    waits.append((key, 16 * u))
            if waits:
                self.ops[eng].append((waits, None, None, 0))
        self.lastw = {}
        self.readers = {}

    def emit(self, block):
        def replay(name):
            def run(e):
                for waits, fn, inc_key, inc in self.ops[name]:
                    for k, v in waits:
                        e.wait_ge(self._semof(k), v)
                    if fn is not None:
                        meth, args, kw = fn
                        getattr(e, meth)(*args, **kw).then_inc(self._semof(inc_key), inc)
            return run
        block.tensor(replay("pe"))
        block.scalar(replay("act"))
        block.vector(replay("dve"))
        block.gpsimd(replay("pool"))
        block.sync(replay("sp"))


class Cfg:
    def __init__(self, seq=4096, nseq=2, depth=4, phases=None, dump=()):
        self.dump = dump
        self.seq = seq
        self.nseq = nseq
        self.depth = depth
        self.L = seq + NMETA
        tiles = []
        t = 0
        while t + 512 <= self.L:
            tiles.append((t, 512))
            t += 512
        if t < self.L:
            tiles.append((t, self.L - t))
        self.tiles = tiles
        self.nkb = (self.L + 127) // 128
        self.LP = self.nkb * 128
        self.NK = self.L // 8
        self.phases = phases


def build(cfg):
    nc = bass.Bass("TRN2", target_bir_lowering=False)
    L, NS, DEPTH = cfg.L, cfg.nseq, cfg.depth

    def din(name, shape):
        return nc.dram_tensor(name, list(shape), F32, kind="ExternalInput").ap()

    x = din("x", (NS, cfg.seq, D))
    meta = din("meta_tokens", (NMETA, D))
    P = {}
    for name, shape in [
        ("ffn1_norm", (DEPTH, D)), ("ffn1_w13", (DEPTH, D, 2 * DFF)), ("ffn1_w2", (DEPTH, DFF, D)),
        ("mix_norm", (DEPTH, D)), ("w_in", (DEPTH, D, DIN)),
        ("ssm_lam_re", (DEPTH, NG, NST)), ("ssm_lam_im", (DEPTH, NG, NST)), ("ssm_log_dt", (DEPTH, NG)),
        ("ssm_b_re", (DEPTH, NG, NST, 16)), ("ssm_b_im", (DEPTH, NG, NST, 16)),
        ("ssm_c_re", (DEPTH, NG, 16, NST)), ("ssm_c_im", (DEPTH, NG, 16, NST)),
        ("ssm_d", (DEPTH, 512)), ("ssm_w_glu", (DEPTH, 512, 2048)),
        ("conv_w", (DEPTH, CW, 512)), ("conv_b", (DEPTH, 512)),
        ("conv_ln_g", (DEPTH, 512)), ("conv_ln_b", (DEPTH, 512)),
        ("conv_w_out", (DEPTH, 512, D)), ("attn_w_o", (DEPTH, 512, D)), ("w_out", (DEPTH, D, D)),
        ("ffn2_norm", (DEPTH, D)), ("ffn2_w13", (DEPTH, D, 2 * DFF)), ("ffn2_w2", (DEPTH, DFF, D)),
        ("final_norm", (D,)),
    ]:
        P[name] = din(name, shape)
    out = nc.dram_tensor("out", [NS, cfg.seq, D], F32, kind="ExternalOutput").ap()

    def scratch(name, shape, dt):
        kind = "ExternalOutput" if name in cfg.dump else "Internal"
        return nc.dram_tensor(name, list(shape), dt, kind=kind).ap()

    hT = scratch("hT", (NS, D, L), F32)
    NK = cfg.NK
    Usc = scratch("Usc", (NS, NG, 8, 16, NK), BF16)
    qT = scratch("qT", (NS, 512, L), BF16)
    kT = scratch("kT", (NS, 512, L), BF16)
    vS = scratch("vS", (NS, L, 512), BF16)
    oattT = scratch("oattT", (NS, 512, L), BF16)
    ysS = scratch("ysS", (NS, NG, 8, 16, NK), F32)
    hcvS = scratch("hcvS", (NS, 512, L), BF16)

    with ExitStack() as es:
        S = Sched(nc, es)

        uid = [0]

        def sb(name, shape, dt, stack=es):
            uid[0] += 1
            return stack.enter_context(nc.sbuf_tensor("%s_%d" % (name, uid[0]), list(shape), dt))

        def ps(name, shape, dt=F32, stack=es):
            uid[0] += 1
            return stack.enter_context(nc.psum_tensor("%s_%d" % (name, uid[0]), list(shape), dt))

        ones_bf = sb("ones_bf", [128, 128], BF16)
        ident_f = sb("ident_f", [128, 128], F32)
        gcols = sb("gcols", [128, (3 * DEPTH + 1) * DT], F32)
        eps_rms = sb("eps_rms", [128, 1], F32)
        rn_tmp = [sb("rn_tmp%d" % i, [128, 512], F32) for i in range(2)]
        S.op("pool", "memset", ones_bf[:], 1.0, writes=["ones_bf"])
        S.op("pool", "memset", ident_f[:], 1.0, writes=["ident_f"])
        S.op("pool", "affine_select", out=ident_f[:], in_=ident_f[:], pattern=[[-1, 128]],
                                               compare_op=ALU.is_equal, fill=0.0, base=0,
                                               channel_multiplier=1,
             reads=["ident_f"], writes=["ident_f"])
        S.op("pool", "memset", eps_rms[:], RMS_EPS, writes=["eps_rms"])
        for wi, nm in enumerate(["ffn1_norm", "mix_norm", "ffn2_norm"]):
            for l in range(DEPTH):
                c0 = (wi * DEPTH + l) * DT
                S.dma("sp",
                    out=gcols[:, c0:c0 + DT], in_=P[nm][l].rearrange("(t p) -> p t", p=128),
                    allow_slow_non_contiguous=True, writes=["gcols"])
        cF = 3 * DEPTH * DT
        S.dma("sp", out=gcols[:, cF:cF + DT],
                                          in_=P["final_norm"].rearrange("(t p) -> p t", p=128),
                                          allow_slow_non_contiguous=True, writes=["gcols"])
        S.barrier()

        def gcol(which, l):
            c0 = (which * DEPTH + l) * DT if which < 3 else cF
            return c0

        def load_h(hs, s, t0, n):
            S.dma("sp",
                out=hs[:, :, 0:n], in_=hT[s, :, t0:t0 + n].rearrange("(t p) n -> p t n", p=128),
                writes=["hs"])

        def store_h(hs, s, t0, n):
            S.dma("sp",
                out=hT[s, :, t0:t0 + n].rearrange("(t p) n -> p t n", p=128), in_=hs[:, :, 0:n],
                reads=["hs"])

        def rmsnorm(hs, xn, sqb, pss, rstd, c0, n, xn_name="xn", hs_name="hs"):
            for dt in range(DT):
                b = dt % 2
                S.op("act", "activation", out=sqb[b][:, 0:n], in_=hs[:, dt, 0:n],
                                                                 func=AF.Square,
                     reads=[hs_name], writes=[("sqb", b)])
                S.op("pe", "matmul", pss[:, 0:n], lhsT=ones_bf[:], rhs=sqb[b][:, 0:n],
                                                           start=(dt == 0), stop=(dt == DT - 1),
                     reads=[("sqb", b), "ones_bf"], writes=["pss"])
            S.op("act", "activation", out=rstd[:, 0:n], in_=pss[:, 0:n], func=AF.Sqrt,
                                               scale=1.0 / D, bias=eps_rms[:],
                 reads=["pss"], writes=["rstd"])
            S.op("dve", "reciprocal", out=rstd[:, 0:n], in_=rstd[:, 0:n],
                 reads=["rstd"], writes=["rstd"])
            for dt in range(DT):
                b = dt % 2
                S.op("dve", "tensor_tensor", out=rn_tmp[b][:, 0:n], in0=hs[:, dt, 0:n], in1=rstd[:, 0:n],
                     op=ALU.mult, reads=[hs_name, "rstd"], writes=[("rn_tmp", b)])
                S.op("act", "activation", out=xn[:, dt, 0:n], in_=rn_tmp[b][:, 0:n], func=AF.Identity,
                     scale=gcols[:, c0 + dt:c0 + dt + 1],
                     reads=[("rn_tmp", b), "gcols"], writes=[xn_name])

        def load_w_cast(dst_ap, src_ap, res):
            S.dma("pool", out=dst_ap, in_=src_ap, max_dma_last_dim=8192,
                  writes=[res])

        def phase_in():
            with ExitStack() as st:
                hs = sb("in_hs", [128, DT, 512], F32, st)
                xt = [sb("in_xt%d" % i, [128, D], F32, st) for i in range(2)]
                pt = [ps("in_pt%d" % i, [128, 512], F32, st) for i in range(2)]
                blk = 0
                for s in range(NS):
                    for (t0, n) in cfg.tiles:
                        for j in range((n + 127) // 128):
                            tb = t0 + j * 128
                            nb = min(128, t0 + n - tb)
                            xb = xt[blk % 2]
                            xr = ("xt", blk % 2)
                            if tb == 0:
                                S.dma("sp", out=xb[0:NMETA, :], in_=meta[:, :],
                                      writes=[xr])
                                S.dma("sp",
                                    out=xb[NMETA:nb, :], in_=x[s, 0:nb - NMETA, :], writes=[(xr, 1)],
                                    reads=[])
                                rd = [xr, (xr, 1)]
                            else:
                                S.dma("sp",
                                    out=xb[0:nb, :], in_=x[s, tb - NMETA:tb - NMETA + nb, :], writes=[xr, (xr, 1)])
                                rd = [xr, (xr, 1)]
                            for half in range(2):
                                pp = pt[half]
                                for q in range(4):
                                    dt = half * 4 + q
                                    S.op("pe", "transpose",
                                        out=pp[:, q * 128:q * 128 + nb], in_=xb[0:nb, dt * 128:(dt + 1) * 128],
                                        identity=ident_f[0:nb, 0:nb],
                                        reads=rd + ["ident_f"], writes=[("pt", half)])
                                eng = "act" if half == 0 else "dve"
                                if eng == "act":
                                    S.op("act", "activation",
                                        out=hs[:, half * 4:half * 4 + 4, j * 128:j * 128 + nb],
                                        in_=pp[:].rearrange("p (q c) -> p q c", q=4)[:, :, 0:nb], func=AF.Copy,
                                        reads=[("pt", half)], writes=["hs"])
                                else:
                                    S.op("dve", "tensor_copy",
                                        out=hs[:, half * 4:half * 4 + 4, j * 128:j * 128 + nb],
                                        in_=pp[:].rearrange("p (q c) -> p q c", q=4)[:, :, 0:nb],
                                        reads=[("pt", half)], writes=["hs"])
                            blk += 1
                        store_h(hs, s, t0, n)
                S.barrier()

        def phase_out():
            with ExitStack() as st:
                hs = sb("o_hs", [128, DT, 512], F32, st)
                xn = sb("o_xn", [128, DT, 512], F32, st)
                sqb = [sb("o_sq%d" % i, [128, 512], BF16, st) for i in range(2)]
                rstd = sb("o_rstd", [128, 512], F32, st)
                ot = [sb("o_ot%d" % i, [128, D], F32, st) for i in range(2)]
                pss = ps("o_pss", [128, 512], F32, st)
                pt = [ps("o_pt%d" % i, [128, 512], F32, st) for i in range(2)]
                blk = 0
                for s in range(NS):
                    for (t0, n) in cfg.tiles:
                        load_h(hs, s, t0, n)
                        rmsnorm(hs, xn, sqb, pss, rstd, cF, n)
                        for j in range((n + 127) // 128):
                            tb = t0 + j * 128
                            nb = min(128, t0 + n - tb)
                            ob = ot[blk % 2]
                            orr = ("ot", blk % 2)
                            for half in range(2):
                                pp = pt[half]
                                for q in range(4):
                                    dt = half * 4 + q
                                    S.op("pe", "transpose",
                                        out=pp[0:nb, q * 128:(q + 1) * 128], in_=xn[:, dt, j * 128:j * 128 + nb],
                                        identity=ident_f[:, :],
                                        reads=["xn", "ident_f"], writes=[("pt", half)])
                                if half == 0:
                                    S.op("act", "activation",
                                        out=ob[0:nb, 0:512], in_=pp[0:nb, :], func=AF.Copy,
                                        reads=[("pt", half)], writes=[orr])
                                else:
                                    S.op("dve", "tensor_copy",
                                        out=ob[0:nb, 512:1024], in_=pp[0:nb, :],
                                        reads=[("pt", half)], writes=[(orr, 1)])
                            lo = NMETA if tb == 0 else 0
                            S.dma("sp",
                                out=out[s, tb + lo - NMETA:tb + nb - NMETA, :], in_=ob[lo:nb, :],
                                reads=[orr, (orr, 1)])
                            blk += 1
                S.barrier()

        def phase_ffn(l, which):
            pre = "ffn1" if which == 0 else "ffn2"
            w13 = P[pre + "_w13"][l]
            w2 = P[pre + "_w2"][l]
            c0 = gcol(0 if which == 0 else 2, l)
            with ExitStack() as st:
                w13s = sb("f_w13", [128, DT, 2 * DFF], BF16, st)
                w2s = sb("f_w2", [128, FT, D], BF16, st)
                hs = sb("f_hs", [128, DT, 512], F32, st)
                xn = sb("f_xn", [128, DT, 512], BF16, st)
                sqb = [sb("f_sq%d" % i, [128, 512], BF16, st) for i in range(2)]
                rstd = sb("f_rstd", [128, 512], F32, st)
                gh = sb("f_g", [128, FT, 512], BF16, st)
                sa = [sb("f_sa%d" % i, [128, 512], F32, st) for i in range(2)]
                pss = ps("f_pss", [128, 512], F32, st)
                pa = [ps("f_pa%d" % i, [128, 512], F32, st) for i in range(2)]
                pb = [ps("f_pb%d" % i, [128, 512], F32, st) for i in range(2)]
                po = [ps("f_po%d" % i, [128, 512], F32, st) for i in range(2)]
                for dt in range(DT):
                    load_w_cast(w13s[:, dt, :], w13[dt * 128:(dt + 1) * 128, :], ("w13", dt))
                for f in range(FT):
                    load_w_cast(w2s[:, f, :], w2[f * 128:(f + 1) * 128, :], ("w2", f))
                for s in range(NS):
                    for (t0, n) in cfg.tiles:
                        load_h(hs, s, t0, n)
                        rmsnorm(hs, xn, sqb, pss, rstd, c0, n)
                        for f in range(FT):
                            b = f % 2
                            for dt in range(DT):
                                S.op("pe", "matmul",
                                    pa[b][:, 0:n], lhsT=w13s[:, dt, f * 128:(f + 1) * 128], rhs=xn[:, dt, 0:n],
                                    start=(dt == 0), stop=(dt == DT - 1),
                                    reads=[("w13", dt), "xn"], writes=[("pa", b)])
                            for dt in range(DT):
                                S.op("pe", "matmul",
                                    pb[b][:, 0:n], lhsT=w13s[:, dt, DFF + f * 128:DFF + (f + 1) * 128],
                                    rhs=xn[:, dt, 0:n], start=(dt == 0), stop=(dt == DT - 1),
                                    reads=[("w13", dt), "xn"], writes=[("pb", b)])
                            S.op("act", "activation", out=sa[b][:, 0:n], in_=pa[b][:, 0:n],
                                                                    func=AF.Silu,
                                 reads=[("pa", b)], writes=[("sa", b)])
                            S.op("dve", "tensor_tensor",
                                out=gh[:, f, 0:n], in0=pb[b][:, 0:n], in1=sa[b][:, 0:n], op=ALU.mult,
                                reads=[("pb", b), ("sa", b)], writes=[("gh", f)])
                        for o in range(DT):
                            b = o % 2
                            for f in range(FT):
                                S.op("pe", "matmul",
                                    po[b][:, 0:n], lhsT=w2s[:, f, o * 128:(o + 1) * 128], rhs=gh[:, f, 0:n],
                                    start=(f == 0), stop=(f == FT - 1),
                                    reads=[("w2", f), ("gh", f)], writes=[("po", b)])
                            import os
                            if os.environ.get("DBG") == "po":
                                S.op("dve", "tensor_copy", out=hs[:, o, 0:n], in_=po[b][:, 0:n],
                                     reads=[("po", b), "hs"], writes=["hs"])
                            elif os.environ.get("DBG") == "xn":
                                S.op("dve", "tensor_copy", out=hs[:, o, 0:n], in_=xn[:, o, 0:n],
                                     reads=[("po", b), "hs", "xn"], writes=["hs"])
                            elif os.environ.get("DBG") == "gh":
                                S.op("dve", "tensor_copy", out=hs[:, o, 0:n], in_=gh[:, o, 0:n],
                                     reads=[("po", b), "hs", "xn", ("gh", o)], writes=["hs"])
                            else:
                              S.op("act", "activation", out=sa[b][:, 0:n], in_=po[b][:, 0:n], func=AF.Copy, scale=0.5,
                                   reads=[("po", b)], writes=[("sa", b)])
                              S.op("dve", "tensor_tensor", out=hs[:, o, 0:n], in0=sa[b][:, 0:n], in1=hs[:, o, 0:n],
                                   op=ALU.add, reads=[("sa", b), "hs"], writes=["hs"])
                        store_h(hs, s, t0, n)
                S.barrier()


        def phase_mixA(l):
            w_in = P["w_in"][l]
            c0 = gcol(1, l)
            with ExitStack() as st:
                wA = sb("a_w", [128, DT, 2048], BF16, st)
                hs = sb("a_hs", [128, DT, 512], F32, st)
                xn = sb("a_xn", [128, DT, 512], BF16, st)
                sqb = [sb("a_sq%d" % i, [128, 512], BF16, st) for i in range(2)]
                rstd = sb("a_rstd", [128, 512], F32, st)
                stg = [sb("a_stg%d" % i, [128, 512], BF16, st) for i in range(4)]
                pss = ps("a_pss", [128, 512], F32, st)
                pp = [ps("a_pp%d" % i, [128, 512], F32, st) for i in range(4)]
                for dt in range(DT):
                    load_w_cast(wA[:, dt, 0:512], w_in[dt * 128:(dt + 1) * 128, 0:512], ("wA", dt))
                    load_w_cast(wA[:, dt, 512:2048], w_in[dt * 128:(dt + 1) * 128, 1536:3072], ("wA", dt, 1))
                cnt = 0
                for s in range(NS):
                    for (t0, n) in cfg.tiles:
                        load_h(hs, s, t0, n)
                        rmsnorm(hs, xn, sqb, pss, rstd, c0, n)
                        nk = n // 8
                        k0 = t0 // 8
                        for f in range(12):
                            b = cnt % 4
                            cnt += 1
                            pb = pp[b]
                            sg = stg[b]
                            for dt in range(DT):
                                S.op("pe", "matmul", pb[:, 0:n], lhsT=wA[:, dt, f * 128:(f + 1) * 128],
                                     rhs=xn[:, dt, 0:n], start=(dt == 0), stop=(dt == DT - 1),
                                     reads=[("wA", dt), ("wA", dt, 1), "xn"], writes=[("pp", b)])
                            if f < 4:
                                S.op("act", "activation",
                                     out=sg[:, 0:n].rearrange("p (s k) -> p s k", s=8),
                                     in_=pb[:, 0:n].rearrange("p (k s) -> p s k", s=8), func=AF.Copy,
                                     reads=[("pp", b)], writes=[("stg", b)])
                                for gl in range(8):
                                    S.dma("sp", out=Usc[s, f * 8 + gl, :, :, k0:k0 + nk].rearrange("s c k -> c s k"),
                                          in_=sg[gl * 16:(gl + 1) * 16, 0:n].rearrange("p (s k) -> p s k", s=8),
                                          reads=[("stg", b)])
                            else:
                                if f % 2 == 0:
                                    S.op("act", "activation", out=sg[:, 0:n], in_=pb[:, 0:n], func=AF.Copy,
                                         reads=[("pp", b)], writes=[("stg", b)])
                                else:
                                    S.op("dve", "tensor_copy", out=sg[:, 0:n], in_=pb[:, 0:n],
                                         reads=[("pp", b)], writes=[("stg", b)])
                                dst = qT if f < 8 else kT
                                r0 = (f % 4) * 128
                                S.dma("sp", out=dst[s, r0:r0 + 128, t0:t0 + n], in_=sg[:, 0:n],
                                      reads=[("stg", b)])
                        for j in range((n + 127) // 128):
                            nb = min(128, n - j * 128)
                            b = cnt % 4
                            cnt += 1
                            pb = pp[b]
                            sg = stg[b]
                            for dt in range(DT):
                                S.op("pe", "matmul", pb[0:nb, 0:512], lhsT=xn[:, dt, j * 128:j * 128 + nb],
                                     rhs=wA[:, dt, 1536:2048], start=(dt == 0), stop=(dt == DT - 1),
                                     reads=[("wA", dt), ("wA", dt, 1), "xn"], writes=[("pp", b)])
                            S.op("dve", "tensor_copy", out=sg[0:nb, :], in_=pb[0:nb, :],
                                 reads=[("pp", b)], writes=[("stg", b)])
                            S.dma("sp", out=vS[s, t0 + j * 128:t0 + j * 128 + nb, :], in_=sg[0:nb, :],
                                  reads=[("stg", b)])
                S.barrier()

        def phase_att(l):
            nkb, LP = cfg.nkb, cfg.LP
            tail = L - (nkb - 1) * 128
            with ExitStack() as st:
                triT = sb("t_tri", [128, 128], BF16, st)
                sel0 = sb("t_sel0", [128, 128], BF16, st)
                onec = sb("t_onec", [128, 1], F32, st)
                masks = [sb("t_mask%d" % i, [128, 512], F32, st) for i in range(4)]
                kT2 = [sb("t_k%d" % i, [128, LP], BF16, st) for i in range(2)]
                qz = [[sb("t_q%d_%d" % (i, h), [128, L], BF16, st) for h in range(2)] for i in range(2)]
                vz = [[sb("t_v%d_%d" % (i, h), [128, nkb, 128], BF16, st) for h in range(2)] for i in range(2)]
                NB3 = 3
                e_t = [sb("t_e%d" % i, [128, 512], F32, st) for i in range(NB3)]
                sp_t = [sb("t_sp%d" % i, [128, 512], BF16, st) for i in range(NB3)]
                g_t = [sb("t_g%d" % i, [128, 512], F32, st) for i in range(NB3)]
                w_t = [sb("t_w%d" % i, [128, 512], BF16, st) for i in range(NB3)]
                chi = [sb("t_chi%d" % i, [128, 512], BF16, st) for i in range(2)]
                clo = [sb("t_clo%d" % i, [128, 512], BF16, st) for i in range(2)]
                ob = [sb("t_ob%d" % i, [128, 512], BF16, st) for i in range(2)]
                pz = [ps("t_pz%d" % i, [128, 512], F32, st) for i in range(2)]
                pcs = [ps("t_pcs%d" % i, [128, 512], F32, st) for i in range(2)]
                po = [ps("t_po%d" % i, [128, 512], F32, st) for i in range(2)]
                S.op("pool", "memset", triT[:], 1.0, writes=["triT"])
                S.op("pool", "affine_select", out=triT[:], in_=triT[:], pattern=[[-1, 128]],
                     compare_op=ALU.is_ge, fill=0.0, base=0, channel_multiplier=1,
                     reads=["triT"], writes=["triT"])
                S.op("pool", "memset", sel0[:], 1.0, writes=["sel0"])
                S.op("pool", "affine_select", out=sel0[:], in_=sel0[:], pattern=[[0, 128]],
                     compare_op=ALU.is_equal, fill=0.0, base=0, channel_multiplier=1,
                     reads=["sel0"], writes=["sel0"])
                S.op("pool", "memset", onec[:], 1.0, writes=["onec"])
                for i in range(4):
                    S.op("pool", "memset", masks[i][:], 1.0, writes=[("mask", i)])
                    S.op("pool", "affine_select", out=masks[i][:], in_=masks[i][:], pattern=[[1, 512]],
                         compare_op=ALU.is_gt, fill=0.0, base=-128 * i, channel_multiplier=-1,
                         reads=[("mask", i)], writes=[("mask", i)])
                for i in range(2):
                    S.op("pool", "memset", kT2[i][:], 0.0, writes=[("kT2", i)])
                    S.op("pool", "memset", chi[i][:], 0.0, writes=[("chi", i)])
                    S.op("pool", "memset", clo[i][:], 0.0, writes=[("clo", i)])
                    for h in range(2):
                        S.op("pool", "memset", qz[i][h][:], 0.0, writes=[("qz", i, h)])
                        S.op("pool", "memset", vz[i][h][:], 0.0, writes=[("vz", i, h), ("vz", i, h, 1)])
                blocks = []
                it = 0
                qcnt = 0
                for s in range(NS):
                    for hp in range(4):
                        bb = it % 2
                        it += 1
                        first_of_load = True
                        for (q0, nq) in cfg.tiles:
                            qb = qcnt % 2
                            qcnt += 1
                            nblk = max((q0 + nq - 1 + 127) // 128, 1)
                            for h in range(2):
                                for bi, kb in enumerate(reversed(range(nblk))):
                                    blocks.append(dict(
                                        s=s, hp=hp, bb=bb, q0=q0, nq=nq, qb=qb, h=h, bi=bi, kb=kb, nblk=nblk,
                                        load=first_of_load, first_pv=(h == 0 and bi == 0),
                                        last_pv=(h == 1 and bi == nblk - 1)))
                                    first_of_load = False
                for i, B_ in enumerate(blocks):
                    B_["i3"] = i % NB3
                    B_["i2"] = i % 2

                def load_inputs(B_):
                    s, hp, bb = B_["s"], B_["hp"], B_["bb"]
                    S.dma("sp", out=kT2[bb][:, 0:L], in_=kT[s, hp * 128:(hp + 1) * 128, :],
                          reads=[], writes=[("kT2", bb)])
                    for h in range(2):
                        r0 = hp * 128 + h * 64
                        S.dma("sp", out=qz[bb][h][h * 64:(h + 1) * 64, :], in_=qT[s, r0:r0 + 64, :],
                              writes=[("qz", bb, h)])
                        if nkb > 1:
                            S.dma("sp", out=vz[bb][h][:, 0:nkb - 1, h * 64:(h + 1) * 64],
                                  in_=vS[s, 0:(nkb - 1) * 128, r0:r0 + 64].rearrange("(b p) d -> p b d", p=128),
                                  writes=[("vz", bb, h)])
                        S.dma("sp", out=vz[bb][h][0:tail, nkb - 1, h * 64:(h + 1) * 64],
                              in_=vS[s, (nkb - 1) * 128:L, r0:r0 + 64],
                              writes=[("vz", bb, h, 1)])

                def stageA(B_):
                    bb, h, q0, nq, kb, i3, i2 = (B_[k] for k in ("bb", "h", "q0", "nq", "kb", "i3", "i2"))
                    if B_["load"]:
                        load_inputs(B_)
                    diag = (kb * 128 + 128 > q0)
                    S.op("pe", "matmul", pz[i2][:, 0:nq], lhsT=kT2[bb][:, kb * 128:(kb + 1) * 128],
                         rhs=qz[bb][h][:, q0:q0 + nq], start=True, stop=True,
                         reads=[("kT2", bb), ("qz", bb, h)], writes=[("pz", i2)])
                    S.op("act", "activation", out=e_t[i3][:, 0:nq], in_=pz[i2][:, 0:nq],
                         func=AF.Exp, scale=0.125,
                         reads=[("pz", i2)], writes=[("e", i3)])
                    if diag:
                        mi = (kb * 128 - q0) // 128
                        assert 0 <= mi < 4
                        S.op("pool", "tensor_tensor", out=e_t[i3][:, 0:nq], in0=e_t[i3][:, 0:nq],
                             in1=masks[mi][:, 0:nq], op=ALU.mult,
                             reads=[("e", i3), ("mask", mi)], writes=[("e", i3)])
                    S.op("act", "activation", out=sp_t[i3][:, 0:nq], in_=e_t[i3][:, 0:nq],
                         func=AF.Ln, bias=onec[:], scale=1.0,
                         reads=[("e", i3), "onec"], writes=[("sp", i3)])

                def stageB(B_):
                    nq, bi, nblk, i3, i2 = (B_[k] for k in ("nq", "bi", "nblk", "i3", "i2"))
                    S.op("pe", "matmul", pcs[i2][:, 0:nq], lhsT=triT[:], rhs=sp_t[i3][:, 0:nq],
                         start=True, stop=(bi == 0),
                         reads=["triT", ("sp", i3)], writes=[("pcs", i2)])
                    if bi > 0:
                        S.op("pe", "matmul", pcs[i2][:, 0:nq], lhsT=sel0[:], rhs=chi[i2][:, 0:nq],
                             start=False, stop=False,
                             reads=["sel0", ("chi", i2)], writes=[("pcs", i2)])
                        S.op("pe", "matmul", pcs[i2][:, 0:nq], lhsT=sel0[:], rhs=clo[i2][:, 0:nq],
                             start=False, stop=True,
                             reads=["sel0", ("clo", i2)], writes=[("pcs", i2)])
                    if bi < nblk - 1:
                        n2 = (i2 + 1) % 2
                        S.op("dve", "tensor_copy", out=chi[n2][0:1, 0:nq], in_=pcs[i2][0:1, 0:nq],
                             excl=[("pcs", i2)], writes=[("chi", n2)])
                        S.op("dve", "tensor_tensor", out=clo[n2][0:1, 0:nq], in0=pcs[i2][0:1, 0:nq],
                             in1=chi[n2][0:1, 0:nq], op=ALU.subtract,
                             excl=[("pcs", i2)], reads=[("chi", n2)], writes=[("clo", n2)])
                    S.op("act", "activation", out=g_t[i3][:, 0:nq], in_=pcs[i2][:, 0:nq],
                         func=AF.Exp, scale=-1.0,
                         excl=[("pcs", i2)], writes=[("g", i3)])
                    S.op("dve", "tensor_tensor", out=w_t[i3][:, 0:nq], in0=e_t[i3][:, 0:nq],
                         in1=g_t[i3][:, 0:nq], op=ALU.mult,
                         reads=[("e", i3), ("g", i3)], writes=[("w", i3)])

                def stageC(B_):
                    s, hp, bb, h, q0, nq, kb, i3, qb = (B_[k] for k in ("s", "hp", "bb", "h", "q0", "nq", "kb", "i3", "qb"))
                    S.op("pe", "matmul", po[qb][:, 0:nq], lhsT=vz[bb][h][:, kb, :],
                         rhs=w_t[i3][:, 0:nq], start=B_["first_pv"], stop=B_["last_pv"],
                         reads=[("vz", bb, h), ("vz", bb, h, 1), ("w", i3)], writes=[("po", qb)])
                    if B_["last_pv"]:
                        S.op("act", "activation", out=ob[qb][:, 0:nq], in_=po[qb][:, 0:nq], func=AF.Copy,
                             reads=[("po", qb)], writes=[("ob", qb)])
                        S.dma("sp", out=oattT[s, hp * 128:(hp + 1) * 128, q0:q0 + nq], in_=ob[qb][:, 0:nq],
                              reads=[("ob", qb)])

                NBLK = len(blocks)
                for i in range(NBLK + 2):
                    if i < NBLK:
                        stageA(blocks[i])
                    if 0 <= i - 1 < NBLK:
                        stageB(blocks[i - 1])
                    if 0 <= i - 2 < NBLK:
                        stageC(blocks[i - 2])
                S.barrier()

        def phase_ssm(l):
            NKc = cfg.NK
            if NKc <= 512:
                halves = [(0, NKc)]
            else:
                halves = [(0, NKc // 2), (NKc // 2, NKc - NKc // 2)]
            TWO_PI = 2.0 * math.pi
            with ExitStack() as st:
                stp = ExitStack()

                def t32(name, shape=(128, 32), dt=F32, stack=None):
                    return sb("s_" + name, list(shape), dt, st if stack is None else stack)

                def t32p(name, shape=(128, 32), dt=F32):
                    return t32(name, shape, dt, stp)
                W1, W1s, W2, W2s, W3 = (t32("W%d" % i, (128, 32, 128), BF16) for i in range(5))
                rho8, f8 = t32("rho8"), t32("f8")
                kidx = t32("kidx", (128, NKc))
                ki = t32("ki", (128, NKc), I32)
                kidx_i = ki
                pw = [ps("s_pw%d" % i, [128, 512], F32, st) for i in range(2)]
                pL = ps("s_pL", [128, 2, 512], F32, st)
                pLs = ps("s_pLs", [128, 2, 512], F32, st)
                py = ps("s_py", [128, 2, 512], F32, st)
                lr, li, dtb, ar, ft = t32p("lr"), t32p("li"), t32p("dtb"), t32p("ar"), t32p("ft")
                yp, ti, tf, tt, sn, cs, mag = (t32p("yp"), t32p("ti", dt=I32), t32p("tf"), t32p("tt"),
                                               t32p("sn"), t32p("cs"), t32p("mag"))
                pwr = {p: t32p("pwr%d" % (p + 7)) for p in range(-7, 9)}
                pwi = {p: t32p("pwi%d" % (p + 7)) for p in range(-7, 9)}
                cre, cim, t_a, t_b, dcol = (t32p("cre"), t32p("cim"), t32p("ta"), t32p("tb"), t32p("dcol"))
                Bre, Bim = t32p("Bre", (128, 32, 16)), t32p("Bim", (128, 32, 16))
                bbr, bbi = t32p("bbr", (128, 32, 16)), t32p("bbi", (128, 32, 16))
                Craw = [t32p("Craw%d" % i, (128, 4, 128)) for i in range(2)]
                Cn = [t32p("Cn%d" % i, (128, 32, 16)) for i in range(2)]
                q1, q2, q3, q4 = (t32p("q%d" % i, (128, 32, 16)) for i in range(4))
                X1, X1s, X2, X2s, X1p, X2p = (t32p("X%d" % i, (128, 32, 8, 16)) for i in range(6))
                maskBL = t32p("maskBL", (128, 128))
                w3t = t32p("w3t", (128, 128))
                w3r = [t32p("w3r%d" % i, (128, 128)) for i in range(2)]

                for hf in (0, 64):
                    S.dma("sp", out=lr[hf:hf + 64, :], in_=P["ssm_lam_re"][l].rearrange("g n -> n g"),
                          allow_slow_non_contiguous=True, writes=["lr"])
                    S.dma("sp", out=li[hf:hf + 64, :], in_=P["ssm_lam_im"][l].rearrange("g n -> n g"),
                          allow_slow_non_contiguous=True, writes=["li"])
                    S.dma("sp", out=dtb[hf:hf + 64, :], in_=P["ssm_log_dt"][l].partition_broadcast(64),
                          writes=["dtb"])
                    S.dma("sp", out=Bre[hf:hf + 64, :, :], in_=P["ssm_b_re"][l].rearrange("g n c -> n g c"),
                          writes=["Bre"])
                    S.dma("sp", out=Bim[hf:hf + 64, :, :], in_=P["ssm_b_im"][l].rearrange("g n c -> n g c"),
                          writes=["Bim"])
                for i, nm in enumerate(["ssm_c_re", "ssm_c_im"]):
                    for dup in range(2):
                        S.dma("sp", out=Craw[i][:, :, dup * 64:(dup + 1) * 64],
                              in_=P[nm][l].rearrange("g c n -> (g c) n").rearrange("(t p) n -> p t n", p=128),
                              writes=[("Craw", i)])
                for s8 in range(8):
                    S.dma("sp", out=dcol[s8 * 16:(s8 + 1) * 16, :], in_=P["ssm_d"][l].rearrange("(g c) -> c g", c=16),
                          allow_slow_non_contiguous=True, writes=["dcol"])
                S.op("pool", "memset", maskBL[:], 1.0, writes=["maskBL"])
                S.op("pool", "affine_select", out=maskBL[:].rearrange("p (j c) -> p j c", c=16),
                     in_=maskBL[:].rearrange("p (j c) -> p j c", c=16), pattern=[[16, 8], [0, 16]],
                     compare_op=ALU.is_ge, fill=0.0, base=15, channel_multiplier=-1,
                     reads=["maskBL"], writes=["maskBL"])
                S.op("pool", "iota", kidx_i[:], pattern=[[1, NKc]], base=0, channel_multiplier=0,
                     writes=["kti"])
                S.op("pool", "tensor_copy", out=kidx[:], in_=kidx_i[:], reads=["kti"], writes=["kidx"])

                if int(os.environ.get("SSMSTOP", "9")) <= 1:
                    S.barrier()
                    return
                def V(eng, meth, *a, r=(), w=(), **kw):
                    S.op(eng, meth, *a, reads=list(r), writes=list(w), **kw)

                def sin_turns(dst, dname, y, yname, ti_, tf_, tt_, pre):
                    V("dve", "tensor_copy", out=ti_, in_=y, r=[yname], w=[pre + "ti"])
                    V("dve", "tensor_copy", out=tf_, in_=ti_, r=[pre + "ti"], w=[pre + "tf"])
                    V("dve", "tensor_tensor", out=tf_, in0=y, in1=tf_, op=ALU.subtract, r=[yname, pre + "tf"], w=[pre + "tf"])
                    V("dve", "tensor_single_scalar", out=tt_, in_=tf_, scalar=0.5, op=ALU.is_gt, r=[pre + "tf"], w=[pre + "tt"])
                    V("dve", "tensor_tensor", out=tf_, in0=tf_, in1=tt_, op=ALU.subtract, r=[pre + "tf", pre + "tt"], w=[pre + "tf"])
                    V("dve", "tensor_single_scalar", out=tt_, in_=tf_, scalar=-0.5, op=ALU.is_lt, r=[pre + "tf"], w=[pre + "tt"])
                    V("dve", "tensor_tensor", out=tf_, in0=tf_, in1=tt_, op=ALU.add, r=[pre + "tf", pre + "tt"], w=[pre + "tf"])
                    if dst is not None:
                        V("act", "activation", out=dst, in_=tf_, func=AF.Sin, scale=6.283185, r=[pre + "tf"], w=[dname])

                V("act", "activation", out=dtb[:], in_=dtb[:], func=AF.Exp, r=["dtb"], w=["dtb"])
                V("dve", "tensor_tensor", out=ar[:], in0=lr[:], in1=dtb[:], op=ALU.mult, r=["lr", "dtb"], w=["ar"])
                V("dve", "tensor_tensor", out=ft[:], in0=li[:], in1=dtb[:], op=ALU.mult, r=["li", "dtb"], w=["ft"])
                V("dve", "tensor_scalar_mul", out=ft[:], in0=ft[:], scalar1=1.0 / TWO_PI, r=["ft"], w=["ft"])
                for p in range(-7, 9):
                    V("act", "activation", out=mag[:], in_=ar[:], func=AF.Exp, scale=float(p), r=["ar"], w=["mag"])
                    if p == 8:
                        V("dve", "tensor_copy", out=rho8[:], in_=mag[:], r=["mag"], w=["rho8"])
                    V("dve", "tensor_scalar_mul", out=yp[:], in0=ft[:], scalar1=float(p), r=["ft"], w=["yp"])
                    sin_turns(sn[:], "sn", yp[:], "yp", ti[:], tf[:], tt[:], "a")
                    if p == 8:
                        V("dve", "tensor_copy", out=f8[:], in_=tf[:], r=["atf"], w=["f8"])
                    V("dve", "tensor_scalar_add", out=yp[:], in0=yp[:], scalar1=0.25, r=["yp"], w=["yp"])
                    sin_turns(cs[:], "cs", yp[:], "yp", ti[:], tf[:], tt[:], "a")
                    V("dve", "tensor_tensor", out=pwr[p][:], in0=mag[:], in1=cs[:], op=ALU.mult, r=["mag", "cs"], w=[("pwr", p)])
                    V("dve", "tensor_tensor", out=pwi[p][:], in0=mag[:], in1=sn[:], op=ALU.mult, r=["mag", "sn"], w=[("pwi", p)])
                if int(os.environ.get("SSMSTOP", "9")) <= 2:
                    S.barrier()
                    return
                V("dve", "tensor_scalar_add", out=t_a[:], in0=pwr[1][:], scalar1=-1.0, r=[("pwr", 1)], w=["ta"])
                V("dve", "tensor_tensor", out=cre[:], in0=t_a[:], in1=lr[:], op=ALU.mult, r=["ta", "lr"], w=["cre"])
                V("dve", "tensor_tensor", out=t_b[:], in0=pwi[1][:], in1=li[:], op=ALU.mult, r=[("pwi", 1), "li"], w=["tb"])
                V("dve", "tensor_tensor", out=cre[:], in0=cre[:], in1=t_b[:], op=ALU.add, r=["cre", "tb"], w=["cre"])
                V("dve", "tensor_tensor", out=cim[:], in0=pwi[1][:], in1=lr[:], op=ALU.mult, r=[("pwi", 1), "lr"], w=["cim"])
                V("dve", "tensor_tensor", out=t_b[:], in0=t_a[:], in1=li[:], op=ALU.mult, r=["ta", "li"], w=["tb"])
                V("dve", "tensor_tensor", out=cim[:], in0=cim[:], in1=t_b[:], op=ALU.subtract, r=["cim", "tb"], w=["cim"])
                V("dve", "tensor_tensor", out=t_a[:], in0=lr[:], in1=lr[:], op=ALU.mult, r=["lr"], w=["ta"])
                V("dve", "tensor_tensor", out=t_b[:], in0=li[:], in1=li[:], op=ALU.mult, r=["li"], w=["tb"])
                V("dve", "tensor_tensor", out=t_a[:], in0=t_a[:], in1=t_b[:], op=ALU.add, r=["ta", "tb"], w=["ta"])
                V("dve", "reciprocal", out=t_a[:], in_=t_a[:], r=["ta"], w=["ta"])
                V("dve", "tensor_tensor", out=cre[:], in0=cre[:], in1=t_a[:], op=ALU.mult, r=["cre", "ta"], w=["cre"])
                V("dve", "tensor_tensor", out=cim[:], in0=cim[:], in1=t_a[:], op=ALU.mult, r=["cim", "ta"], w=["cim"])

                def bc(t):
                    return t[:, :].unsqueeze(2).broadcast_to([128, 32, 16])

                def cplx(fr, fi, frn, fin, xr, xi, xrn, xin):
                    V("dve", "tensor_tensor", out=q1[:], in0=xr[:], in1=bc(fr), op=ALU.mult, r=[xrn, frn, "q1"], w=["q1"])
                    V("dve", "tensor_tensor", out=q3[:], in0=xi[:], in1=bc(fi), op=ALU.mult, r=[xin, fin, "q3"], w=["q3"])
                    V("dve", "tensor_tensor", out=q1[:], in0=q1[:], in1=q3[:], op=ALU.subtract, r=["q1", "q3"], w=["q1"])
                    V("dve", "tensor_tensor", out=q2[:], in0=xi[:], in1=bc(fr), op=ALU.mult, r=[xin, frn, "q2"], w=["q2"])
                    V("dve", "tensor_tensor", out=q4[:], in0=xr[:], in1=bc(fi), op=ALU.mult, r=[xrn, fin, "q4"], w=["q4"])
                    V("dve", "tensor_tensor", out=q2[:], in0=q2[:], in1=q4[:], op=ALU.add, r=["q2", "q4"], w=["q2"])

                def put(dst, dname, blk, lo_src, lo_sign, up_src, up_sign):
                    for (rows, src, sign, eng) in ((slice(0, 64), lo_src, lo_sign, "act"),
                                                   (slice(64, 128), up_src, up_sign, "pool")):
                        srcn = "q1" if src is q1 else "q2"
                        if eng == "act":
                            V("act", "activation", out=dst[rows, :, blk, :], in_=src[rows, :, :], func=AF.Copy,
                              scale=float(sign), r=[srcn], w=[dname])
                        else:
                            V("pool", "tensor_scalar", out=dst[rows, :, blk, :], in0=src[rows, :, :],
                              scalar1=float(sign), scalar2=None, op0=ALU.mult, r=[srcn], w=[dname])

                if int(os.environ.get("SSMSTOP", "9")) <= 3:
                    S.barrier()
                    return
                cplx(cre, cim, "cre", "cim", Bre, Bim, "Bre", "Bim")
                V("dve", "tensor_copy", out=bbr[:], in_=q1[:], r=["q1"], w=["bbr"])
                V("dve", "tensor_copy", out=bbi[:], in_=q2[:], r=["q2"], w=["bbi"])
                for i in range(2):
                    for t4 in range(4):
                        pb = pw[(i * 4 + t4) % 2]
                        V("pe", "transpose", out=pb[:, 0:128], in_=Craw[i][:, t4, :], identity=ident_f[:, :],
                          r=[("Craw", i), "ident_f"], w=[("pw", (i * 4 + t4) % 2)])
                        V("act", "activation", out=Cn[i][:, t4 * 8:(t4 + 1) * 8, :],
                          in_=pb[:, 0:128].rearrange("p (g c) -> p g c", c=16), func=AF.Copy,
                          r=[("pw", (i * 4 + t4) % 2)], w=[("Cn", i)])
                for s8 in range(8):
                    p = 7 - s8
                    cplx(pwr[p], pwi[p], ("pwr", p), ("pwi", p), bbr, bbi, "bbr", "bbi")
                    put(X1, "X1", s8, q1, 1, q2, 1)
                    put(X1s, "X1s", s8, q2, 1, q1, -1)
                    p = -s8
                    cplx(pwr[p], pwi[p], ("pwr", p), ("pwi", p), bbr, bbi, "bbr", "bbi")
                    put(X1p, "X1p", s8, q1, 1, q2, 1)
                    p = s8 + 1
                    cplx(pwr[p], pwi[p], ("pwr", p), ("pwi", p), Cn[0], Cn[1], ("Cn", 0), ("Cn", 1))
                    put(X2, "X2", s8, q1, 1, q2, -1)
                    put(X2s, "X2s", s8, q2, -1, q1, -1)
                    p = s8
                    cplx(pwr[p], pwi[p], ("pwr", p), ("pwi", p), Cn[0], Cn[1], ("Cn", 0), ("Cn", 1))
                    put(X2p, "X2p", s8, q1, 1, q2, -1)
                if int(os.environ.get("SSMSTOP", "9")) <= 4:
                    S.barrier()
                    return
                V("act", "activation", out=W2[:].rearrange("p g x -> p (g x)"),
                  in_=X2[:].rearrange("p g s c -> p (g s c)"), func=AF.Copy, r=["X2"], w=["W2"])
                V("dve", "tensor_copy", out=W2s[:].rearrange("p g x -> p (g x)"),
                  in_=X2s[:].rearrange("p g s c -> p (g s c)"), r=["X2s"], w=["W2s"])
                for g in range(NG):
                    b = g % 2
                    V("pe", "transpose", out=pw[b][:, 0:128], in_=X1[:, g].rearrange("p s c -> p (s c)"),
                      identity=ident_f[:, :], r=["X1", "ident_f"], w=[("pw", b)])
                    V("pe", "transpose", out=pw[b][:, 128:256], in_=X1s[:, g].rearrange("p s c -> p (s c)"),
                      identity=ident_f[:, :], r=["X1s", "ident_f"], w=[("pw", b)])
                    V("pe", "matmul", pw[b][:, 256:384], lhsT=X1p[:, g].rearrange("p s c -> p (s c)"),
                      rhs=X2p[:, g].rearrange("p s c -> p (s c)"), start=True, stop=True,
                      r=["X1p", "X2p"], w=[("pw", b)])
                    V("act", "activation", out=W1[:, g, :], in_=pw[b][:, 0:128], func=AF.Copy,
                      r=[("pw", b)], w=["W1"])
                    V("act", "activation", out=W1s[:, g, :], in_=pw[b][:, 128:256], func=AF.Copy,
                      r=[("pw", b)], w=["W1s"])
                    V("act", "activation", out=w3r[b][:], in_=pw[b][:, 256:384], func=AF.Copy,
                      r=[("pw", b)], w=[("w3r", b)])
                    V("dve", "tensor_tensor", out=w3t[:], in0=w3r[b][:], in1=maskBL[:], op=ALU.mult,
                      r=[("w3r", b), "maskBL"], w=["w3t"])
                    V("dve", "scalar_tensor_tensor", out=W3[:, g, :], in0=ident_f[:], scalar=dcol[:, g:g + 1],
                      in1=w3t[:], op0=ALU.mult, op1=ALU.add, r=["ident_f", "dcol", "w3t"], w=["W3"])

                if int(os.environ.get("SSMSTOP", "9")) <= 5:
                    S.barrier()
                    return
                S.barrier()
                stp.close()
                cT = [t32("cT%d" % i, (128, NKc)) for i in range(2)]
                sT = [t32("sT%d" % i, (128, NKc)) for i in range(2)]
                rtab = [t32("rtab%d" % i, (128, NKc)) for i in range(2)]
                yk = t32("yk", (128, NKc))
                kf = t32("kf", (128, NKc))
                kt = t32("kt", (128, NKc))
                Ut = [t32("U%d" % i, (128, NKc), BF16) for i in range(2)]
                m1, m2, Tt = t32("m1", (128, NKc)), t32("m2", (128, NKc)), t32("Tt", (128, NKc))
                Pm = [t32("Pm%d" % i, (128, NKc + 2), BF16) for i in range(2)]
                Qm = [t32("Qm%d" % i, (128, NKc + 2), BF16) for i in range(2)]
                yo = [t32("yo%d" % i, (128, NKc)) for i in range(2)]
                for i in range(2):
                    S.op("pool", "memset", Pm[i][:], 0.0, writes=[("Pm", i)])
                    S.op("pool", "memset", Qm[i][:], 0.0, writes=[("Qm", i)])
                it = 0
                for g in range(NG):
                    gb = g % 2
                    V("dve", "tensor_scalar_mul", out=yk[:], in0=kidx[:], scalar1=f8[:, g:g + 1],
                      r=["kidx", "f8"], w=["yk"])
                    sin_turns(sT[gb][:], ("sT", gb), yk[:], "yk", ki[:], kf[:], kt[:], "k")
                    V("dve", "tensor_scalar_add", out=yk[:], in0=yk[:], scalar1=0.25, r=["yk"], w=["yk"])
                    sin_turns(cT[gb][:], ("cT", gb), yk[:], "yk", ki[:], kf[:], kt[:], "k")
                    V("act", "activation", out=rtab[gb][:], in_=kidx[:], func=AF.Identity, scale=0.0,
                      bias=rho8[:, g:g + 1], r=["kidx", "rho8"], w=[("rtab", gb)])
                    RL = int(os.environ.get("RUNLVL", "9"))
                    for s in range(NS):
                        if RL < 2:
                            continue
                        ub = it % 2
                        it += 1
                        S.dma("sp", out=Ut[ub][:], in_=Usc[s, g].rearrange("s c k -> (s c) k"),
                              writes=[("U", ub)])
                        for hi, (lo, wd) in enumerate(halves):
                            V("pe", "matmul", pL[:, hi, 0:wd], lhsT=W1[:, g, :], rhs=Ut[ub][:, lo:lo + wd],
                              start=True, stop=True, r=["W1", ("U", ub)], w=["pL"])
                            V("pe", "matmul", pLs[:, hi, 0:wd], lhsT=W1s[:, g, :], rhs=Ut[ub][:, lo:lo + wd],
                              start=True, stop=True, r=["W1s", ("U", ub)], w=["pLs"])
                        if RL < 3:
                            continue
                        for hi, (lo, wd) in enumerate(halves):
                            V("dve", "tensor_tensor", out=m1[:, lo:lo + wd], in0=pL[:, hi, 0:wd],
                              in1=cT[gb][:, lo:lo + wd], op=ALU.mult, r=["pL", ("cT", gb)], w=["m1"])
                            V("dve", "tensor_tensor", out=m2[:, lo:lo + wd], in0=pLs[:, hi, 0:wd],
                              in1=sT[gb][:, lo:lo + wd], op=ALU.mult, r=["pLs", ("sT", gb)], w=["m2"])
                        V("dve", "tensor_tensor", out=m1[:], in0=m1[:], in1=m2[:], op=ALU.add,
                          r=["m1", "m2"], w=["m1"])
                        V("dve", "tensor_tensor_scan", out=Tt[:], data0=rtab[gb][:], data1=m1[:], initial=0.0,
                          op0=ALU.mult, op1=ALU.add, r=[("rtab", gb), "m1"], w=["Tt"])
                        V("dve", "tensor_tensor", out=Pm[ub][:, 1:NKc + 1], in0=Tt[:], in1=cT[gb][:], op=ALU.mult,
                          r=["Tt", ("cT", gb)], w=[("Pm", ub)])
                        V("pool", "tensor_tensor", out=Qm[ub][:, 1:NKc + 1], in0=Tt[:], in1=sT[gb][:], op=ALU.mult,
                          r=["Tt", ("sT", gb)], w=[("Qm", ub)])
                        if RL < 4:
                            continue
                        for hi, (lo, wd) in enumerate(halves):
                            V("pe", "matmul", py[:, hi, 0:wd], lhsT=W3[:, g, :], rhs=Ut[ub][:, lo:lo + wd],
                              start=True, stop=False, r=["W3", ("U", ub)], w=["py"])
                            V("pe", "matmul", py[:, hi, 0:wd], lhsT=W2[:, g, :],
                              rhs=Pm[ub][:, lo:lo + wd], start=False, stop=False,
                              r=["W2", ("Pm", ub)], w=["py"])
                            V("pe", "matmul", py[:, hi, 0:wd], lhsT=W2s[:, g, :],
                              rhs=Qm[ub][:, lo:lo + wd], start=False, stop=True,
                              r=["W2s", ("Qm", ub)], w=["py"])
                        if RL < 5:
                            continue
                        for hi, (lo, wd) in enumerate(halves):
                            V("act", "activation", out=yo[ub][:, lo:lo + wd], in_=py[:, hi, 0:wd], func=AF.Copy,
                              r=["py"], w=[("yo", ub)])
                        S.dma("sp", out=ysS[s, g].rearrange("s c k -> (s c) k"), in_=yo[ub][:],
                              reads=[("yo", ub)])
                S.barrier()

        def phase_mixB1(l):
            w_in = P["w_in"][l]
            c0 = gcol(1, l)
            HALO = CW - 1
            with ExitStack() as st:
                wC = sb("c_w", [128, DT, 1024], BF16, st)
                dg = sb("c_dg", [128, 4 * CW, 128], BF16, st)
                cw = sb("c_cw", [128, 4, CW], F32, st)
                cb = sb("c_cb", [128, 4], F32, st)
                lng = sb("c_lng", [128, 4], F32, st)
                lnb = sb("c_lnb", [128, 4], F32, st)
                eps_ln = sb("c_epsln", [128, 1], F32, st)
                hsb = [sb("c_hs%d" % i, [128, DT, 512], F32, st) for i in range(2)]
                xn = sb("c_xn", [128, DT, 512], BF16, st)
                sqb = [sb("c_sq%d" % i, [128, 512], BF16, st) for i in range(2)]
                rstd = sb("c_rstd", [128, 512], F32, st)
                hc = sb("c_hc", [128, 4, HALO + 512], BF16, st)
                cacc = sb("c_cacc", [128, 4, 512], F32, st)
                hcvb = [sb("c_hcv%d" % i, [128, 4, 512], BF16, st) for i in range(2)]
                sg = [sb("c_sg%d" % i, [128, 512], F32, st) for i in range(2)]
                tmq = sb("c_tm", [128, 512], F32, st)
                mu = sb("c_mu", [128, 512], F32, st)
                lrs = sb("c_lrs", [128, 512], F32, st)
                pss = ps("c_pss", [128, 512], F32, st)
                NPB = 6
                pp = [ps("c_pp%d" % i, [128, 512], F32, st) for i in range(NPB)]
                pcnt = [0]

                def nxt():
                    b = pcnt[0] % NPB
                    pcnt[0] += 1
                    return b

                for dt in range(DT):
                    load_w_cast(wC[:, dt, :], w_in[dt * 128:(dt + 1) * 128, 512:1536], ("wC", dt))
                wCr = [("wC", dt) for dt in range(DT)]
                for ct in range(4):
                    S.dma("sp", out=cw[:, ct, :], in_=P["conv_w"][l][:, ct * 128:(ct + 1) * 128].rearrange("j p -> p j"),
                          allow_slow_non_contiguous=True, writes=["cw"])
                for nm, tl in (("conv_b", cb), ("conv_ln_g", lng), ("conv_ln_b", lnb)):
                    S.dma("sp", out=tl[:], in_=P[nm][l].rearrange("(t p) -> p t", p=128),
                          allow_slow_non_contiguous=True, writes=[nm])
                S.op("pool", "memset", eps_ln[:], LN_EPS, writes=["eps_ln"])
                for ct in range(4):
                    for j in range(CW):
                        eng = "dve" if (j % 2 == 0) else "pool"
                        S.op(eng, "tensor_scalar", out=dg[:, ct * CW + j, :], in0=ident_f[:], scalar1=cw[:, ct, j:j + 1],
                             scalar2=None, op0=ALU.mult, reads=["ident_f", "cw"], writes=[("dg", ct)])
                tcnt = 0
                for s in range(NS):
                    for (t0, n) in cfg.tiles:
                        hs = hsb[tcnt % 2]
                        hcv = hcvb[tcnt % 2]
                        hsr, hcvr = ("hsb", tcnt % 2), ("hcvb", tcnt % 2)
                        tcnt += 1
                        S.dma("sp", out=hs[:, :, 0:n], in_=hT[s, :, t0:t0 + n].rearrange("(t p) n -> p t n", p=128),
                              writes=[hsr])
                        rmsnorm(hs, xn, sqb, pss, rstd, c0, n, hs_name=hsr)
                        if t0 == 0:
                            S.op("pool", "memset", hc[:, :, 0:HALO], 0.0, writes=["hc"])
                        for ct in range(4):
                            ba, bg = nxt(), nxt()
                            for dt in range(DT):
                                S.op("pe", "matmul", pp[ba][:, 0:n], lhsT=wC[:, dt, ct * 128:(ct + 1) * 128],
                                     rhs=xn[:, dt, 0:n], start=(dt == 0), stop=(dt == DT - 1),
                                     reads=wCr + ["xn"], writes=[("pp", ba)])
                            for dt in range(DT):
                                S.op("pe", "matmul", pp[bg][:, 0:n], lhsT=wC[:, dt, 512 + ct * 128:512 + (ct + 1) * 128],
                                     rhs=xn[:, dt, 0:n], start=(dt == 0), stop=(dt == DT - 1),
                                     reads=wCr + ["xn"], writes=[("pp", bg)])
                            S.op("act", "activation", out=sg[ct % 2][:, 0:n], in_=pp[bg][:, 0:n], func=AF.Sigmoid,
                                 reads=[("pp", bg)], writes=[("sg", ct % 2)])
                            S.op("dve", "tensor_tensor", out=hc[:, ct, HALO:HALO + n], in0=pp[ba][:, 0:n],
                                 in1=sg[ct % 2][:, 0:n], op=ALU.mult,
                                 reads=[("pp", ba), ("sg", ct % 2)], writes=["hc"])
                        for ct in range(4):
                            bc_ = nxt()
                            for j in range(CW):
                                S.op("pe", "matmul", pp[bc_][:, 0:n], lhsT=dg[:, ct * CW + j, :], rhs=hc[:, ct, j:j + n],
                                     start=(j == 0), stop=(j == CW - 1),
                                     reads=[("dg", ct), "hc"], writes=[("pp", bc_)])
                            S.op("act", "activation", out=cacc[:, ct, 0:n], in_=pp[bc_][:, 0:n], func=AF.Identity,
                                 bias=cb[:, ct:ct + 1], scale=1.0,
                                 reads=[("pp", bc_), "conv_b"], writes=[("cacc", ct)])
                        if n >= HALO:
                            S.op("pool", "tensor_copy", out=hc[:, :, 0:HALO], in_=hc[:, :, n:n + HALO],
                                 reads=["hc"], writes=["hc"])
                        bmu, bvar = nxt(), nxt()
                        for ct in range(4):
                            b2 = ct % 2
                            S.op("act", "activation", out=sqb[b2][:, 0:n], in_=cacc[:, ct, 0:n], func=AF.Copy,
                                 reads=[("cacc", ct)], writes=[("sqb", b2)])
                            S.op("pe", "matmul", pp[bmu][:, 0:n], lhsT=ones_bf[:], rhs=sqb[b2][:, 0:n],
                                 start=(ct == 0), stop=(ct == 3), reads=[("sqb", b2), "ones_bf"], writes=[("pp", bmu)])
                        for ct in range(4):
                            b2 = ct % 2
                            S.op("act", "activation", out=sqb[b2][:, 0:n], in_=cacc[:, ct, 0:n], func=AF.Square,
                                 reads=[("cacc", ct)], writes=[("sqb", b2)])
                            S.op("pe", "matmul", pp[bvar][:, 0:n], lhsT=ones_bf[:], rhs=sqb[b2][:, 0:n],
                                 start=(ct == 0), stop=(ct == 3), reads=[("sqb", b2), "ones_bf"], writes=[("pp", bvar)])
                        S.op("act", "activation", out=mu[:, 0:n], in_=pp[bmu][:, 0:n], func=AF.Copy, scale=1.0 / 512,
                             reads=[("pp", bmu)], writes=["mu"])
                        S.op("dve", "tensor_tensor", out=tmq[:, 0:n], in0=mu[:, 0:n], in1=mu[:, 0:n], op=ALU.mult,
                             reads=["mu"], writes=["tmq"])
                        S.op("dve", "scalar_tensor_tensor", out=lrs[:, 0:n], in0=pp[bvar][:, 0:n], scalar=1.0 / 512,
                             in1=tmq[:, 0:n], op0=ALU.mult, op1=ALU.subtract,
                             reads=[("pp", bvar), "tmq"], writes=["lrs"])
                        S.op("act", "activation", out=lrs[:, 0:n], in_=lrs[:, 0:n], func=AF.Sqrt, bias=eps_ln[:], scale=1.0,
                             reads=["lrs", "eps_ln"], writes=["lrs"])
                        S.op("dve", "reciprocal", out=lrs[:, 0:n], in_=lrs[:, 0:n], reads=["lrs"], writes=["lrs"])
                        for ct in range(4):
                            S.op("dve", "tensor_tensor", out=tmq[:, 0:n], in0=cacc[:, ct, 0:n], in1=mu[:, 0:n],
                                 op=ALU.subtract, reads=[("cacc", ct), "mu"], writes=["tmq"])
                            S.op("dve", "tensor_tensor", out=tmq[:, 0:n], in0=tmq[:, 0:n], in1=lrs[:, 0:n],
                                 op=ALU.mult, reads=["tmq", "lrs"], writes=["tmq"])
                            S.op("act", "activation", out=hcv[:, ct, 0:n], in_=tmq[:, 0:n], func=AF.Silu,
                                 scale=lng[:, ct:ct + 1], bias=lnb[:, ct:ct + 1],
                                 reads=["tmq", "conv_ln_g", "conv_ln_b"], writes=[hcvr])
                        S.dma("sp", out=hcvS[s, :, t0:t0 + n].rearrange("(t p) n -> p t n", p=128), in_=hcv[:, :, 0:n],
                              reads=[hcvr])
                S.barrier()

        def phase_mixB2(l):
            w_in = P["w_in"][l]
            c0 = gcol(1, l)
            HALO = CW - 1
            with ExitStack() as st:
                wB = sb("b_w", [128, DT, 3072], BF16, st)
                wglu = sb("b_wglu", [128, 4, 2048], BF16, st)
                wpw = sb("b_wpw", [128, 4, D], BF16, st)
                wo = sb("b_wo", [128, 4, D], BF16, st)
                wout = sb("b_wout", [128, DT, D], BF16, st)
                hs = sb("b_hs", [128, DT, 512], F32, st)
                xn = sb("b_xn", [128, DT, 512], BF16, st)
                sqb = [sb("b_sq%d" % i, [128, 512], BF16, st) for i in range(2)]
                rstd = sb("b_rstd", [128, 512], F32, st)
                hcv = sb("b_hcv", [128, 4, 512], BF16, st)
                yst = [sb("b_yst%d" % i, [128, 8, 64], F32, st) for i in range(2)]
                gy = sb("b_gy", [128, 4, 512], BF16, st)
                oat = sb("b_oat", [128, 4, 512], BF16, st)
                mrg = sb("b_mrg", [128, DT, 512], BF16, st)
                sg = [sb("b_sg%d" % i, [128, 512], F32, st) for i in range(2)]
                tm = [sb("b_tm%d" % i, [128, 512], F32, st) for i in range(3)]
                macc = sb("b_macc", [128, 512], F32, st)
                pss = ps("b_pss", [128, 512], F32, st)
                NPB = 6
                pp = [ps("b_pp%d" % i, [128, 512], F32, st) for i in range(NPB)]
                pcnt = [0]

                def nxt():
                    b = pcnt[0] % NPB
                    pcnt[0] += 1
                    return b

                for dt in range(DT):
                    load_w_cast(wB[:, dt, 0:1536], w_in[dt * 128:(dt + 1) * 128, 3072:4608], ("wB", dt))
                    load_w_cast(wB[:, dt, 1536:3072], w_in[dt * 128:(dt + 1) * 128, 4608:6144], ("wB", dt, 1))
                    load_w_cast(wout[:, dt, :], P["w_out"][l][dt * 128:(dt + 1) * 128, :], ("wout", dt))
                for ct in range(4):
                    load_w_cast(wglu[:, ct, :], P["ssm_w_glu"][l][ct * 128:(ct + 1) * 128, :], ("wglu", ct))
                    load_w_cast(wpw[:, ct, :], P["conv_w_out"][l][ct * 128:(ct + 1) * 128, :], ("wpw", ct))
                    load_w_cast(wo[:, ct, :], P["attn_w_o"][l][ct * 128:(ct + 1) * 128, :], ("wo", ct))
                wBr = [("wB", dt) for dt in range(DT)] + [("wB", dt, 1) for dt in range(DT)]
                ycnt = 0
                for s in range(NS):
                    for (t0, n) in cfg.tiles:
                        nk = n // 8
                        k0 = t0 // 8
                        load_h(hs, s, t0, n)
                        S.dma("sp", out=oat[:, :, 0:n],
                              in_=oattT[s, :, t0:t0 + n].rearrange("(t p) n -> p t n", p=128), writes=["oat"])
                        S.dma("sp", out=hcv[:, :, 0:n],
                              in_=hcvS[s, :, t0:t0 + n].rearrange("(t p) n -> p t n", p=128), writes=["hcv"])
                        rmsnorm(hs, xn, sqb, pss, rstd, c0, n)
                        for ct in range(4):
                            yb = ycnt % 2
                            ycnt += 1
                            yt = yst[yb]
                            for gl in range(8):
                                S.dma("sp", out=yt[gl * 16:(gl + 1) * 16, :, 0:nk],
                                      in_=ysS[s, ct * 8 + gl, :, :, k0:k0 + nk].rearrange("j c k -> c j k"),
                                      writes=[("yst", yb)])
                            yv = yt[:, :, 0:nk]
                            t1 = tm[0][:, 0:n].rearrange("p (j k) -> p j k", j=8)
                            t2 = tm[1][:, 0:n].rearrange("p (j k) -> p j k", j=8)
                            S.op("act", "activation", out=t1, in_=yv, func=AF.Square,
                                 reads=[("yst", yb)], writes=[("tm", 0)])
                            S.op("dve", "tensor_scalar", out=t1, in0=t1, scalar1=0.044715, scalar2=1.0,
                                 op0=ALU.mult, op1=ALU.add, reads=[("tm", 0)], writes=[("tm", 0)])
                            S.op("dve", "tensor_tensor", out=t1, in0=t1, in1=yv, op=ALU.mult,
                                 reads=[("tm", 0), ("yst", yb)], writes=[("tm", 0)])
                            S.op("act", "activation", out=t2, in_=t1, func=AF.Tanh, scale=0.7978845608,
                                 reads=[("tm", 0)], writes=[("tm", 1)])
                            S.op("dve", "tensor_scalar", out=t2, in0=t2, scalar1=0.5, scalar2=0.5,
                                 op0=ALU.mult, op1=ALU.add, reads=[("tm", 1)], writes=[("tm", 1)])
                            S.op("dve", "tensor_tensor", out=gy[:, ct, 0:n].rearrange("p (k j) -> p j k", j=8),
                                 in0=t2, in1=yv, op=ALU.mult,
                                 reads=[("tm", 1), ("yst", yb)], writes=["gy"])
                        for f in range(DT):
                            def gate(br):
                                bgt = nxt()
                                c1 = br * 1024 + f * 128
                                for dt in range(DT):
                                    S.op("pe", "matmul", pp[bgt][:, 0:n], lhsT=wB[:, dt, c1:c1 + 128],
                                         rhs=xn[:, dt, 0:n], start=(dt == 0), stop=(dt == DT - 1),
                                         reads=wBr + ["xn"], writes=[("pp", bgt)])
                                S.op("act", "activation", out=sg[br % 2][:, 0:n], in_=pp[bgt][:, 0:n], func=AF.Sigmoid,
                                     reads=[("pp", bgt)], writes=[("sg", br % 2)])
                                return sg[br % 2], ("sg", br % 2)
                            ba, bg = nxt(), nxt()
                            for ct in range(4):
                                S.op("pe", "matmul", pp[ba][:, 0:n], lhsT=wglu[:, ct, f * 128:(f + 1) * 128],
                                     rhs=gy[:, ct, 0:n], start=(ct == 0), stop=(ct == 3),
                                     reads=[("wglu", c) for c in range(4)] + ["gy"], writes=[("pp", ba)])
                            for ct in range(4):
                                S.op("pe", "matmul", pp[bg][:, 0:n], lhsT=wglu[:, ct, 1024 + f * 128:1024 + (f + 1) * 128],
                                     rhs=gy[:, ct, 0:n], start=(ct == 0), stop=(ct == 3),
                                     reads=[("wglu", c) for c in range(4)] + ["gy"], writes=[("pp", bg)])
                            S.op("act", "activation", out=tm[1][:, 0:n], in_=pp[bg][:, 0:n], func=AF.Sigmoid,
                                 reads=[("pp", bg)], writes=[("tm", 1)])
                            S.op("dve", "tensor_tensor", out=tm[0][:, 0:n], in0=pp[ba][:, 0:n], in1=tm[1][:, 0:n],
                                 op=ALU.mult, reads=[("pp", ba), ("tm", 1)], writes=[("tm", 0)])
                            g0, g0r = gate(0)
                            S.op("dve", "tensor_tensor", out=macc[:, 0:n], in0=tm[0][:, 0:n], in1=g0[:, 0:n],
                                 op=ALU.mult, reads=[("tm", 0), g0r], writes=["macc"])
                            bc_ = nxt()
                            for ct in range(4):
                                S.op("pe", "matmul", pp[bc_][:, 0:n], lhsT=wpw[:, ct, f * 128:(f + 1) * 128],
                                     rhs=hcv[:, ct, 0:n], start=(ct == 0), stop=(ct == 3),
                                     reads=[("wpw", c) for c in range(4)] + ["hcv"], writes=[("pp", bc_)])
                            g1, g1r = gate(1)
                            S.op("dve", "tensor_tensor", out=tm[0][:, 0:n], in0=pp[bc_][:, 0:n], in1=g1[:, 0:n],
                                 op=ALU.mult, reads=[("pp", bc_), g1r], writes=[("tm", 0)])
                            S.op("dve", "tensor_tensor", out=macc[:, 0:n], in0=macc[:, 0:n], in1=tm[0][:, 0:n],
                                 op=ALU.add, reads=["macc", ("tm", 0)], writes=["macc"])
                            bo_ = nxt()
                            for ct in range(4):
                                S.op("pe", "matmul", pp[bo_][:, 0:n], lhsT=wo[:, ct, f * 128:(f + 1) * 128],
                                     rhs=oat[:, ct, 0:n], start=(ct == 0), stop=(ct == 3),
                                     reads=[("wo", c) for c in range(4)] + ["oat"], writes=[("pp", bo_)])
                            g2, g2r = gate(2)
                            S.op("dve", "tensor_tensor", out=tm[0][:, 0:n], in0=pp[bo_][:, 0:n], in1=g2[:, 0:n],
                                 op=ALU.mult, reads=[("pp", bo_), g2r], writes=[("tm", 0)])
                            S.op("dve", "tensor_tensor", out=mrg[:, f, 0:n], in0=macc[:, 0:n], in1=tm[0][:, 0:n],
                                 op=ALU.add, reads=["macc", ("tm", 0)], writes=[("mrg", f)])
                        for o in range(DT):
                            bo_ = nxt()
                            for f in range(DT):
                                S.op("pe", "matmul", pp[bo_][:, 0:n], lhsT=wout[:, f, o * 128:(o + 1) * 128],
                                     rhs=mrg[:, f, 0:n], start=(f == 0), stop=(f == DT - 1),
                                     reads=[("wout", f), ("mrg", f)], writes=[("pp", bo_)])
                            S.op("dve", "tensor_tensor", out=hs[:, o, 0:n], in0=pp[bo_][:, 0:n], in1=hs[:, o, 0:n],
                                 op=ALU.add, reads=[("pp", bo_), "hs"], writes=["hs"])
                        store_h(hs, s, t0, n)
                S.barrier()

        phases = cfg.phases

        def on(name):
            return phases is None or name in phases

        phase_in()
        for l in range(DEPTH):
            if on("ffn1"):
                phase_ffn(l, 0)
            if on("mixa"):
                phase_mixA(l)
            if on("att"):
                phase_att(l)
            if on("ssm"):
                phase_ssm(l)
            if on("mixb"):
                phase_mixB1(l)
                phase_mixB2(l)
            if on("ffn2"):
                phase_ffn(l, 1)
        phase_out()

        with nc.Block() as block:
            S.emit(block)
    return nc


def kernel(**inputs):
    cfg = Cfg()
    nc = build(cfg)
    n = 8
    in_maps = []
    for c in range(n):
        m = {}
        for k, v in inputs.items():
            a = np.asarray(v)
            if k == "x":
                a = a[c * cfg.nseq:(c + 1) * cfg.nseq]
            m[k] = np.ascontiguousarray(a, dtype=np.float32)
        in_maps.append(m)
    res = run_bass_kernel_spmd(nc, in_maps, core_ids=list(range(n)))
    return np.concatenate([np.asarray(r["out"]) for r in res.results], axis=0).astype(np.float32)
```

```python
import math
import os
from contextlib import ExitStack

import numpy as np
import concourse.bass as bass
import concourse.mybir as mybir
from concourse.bass_utils import run_bass_kernel_spmd

F32 = mybir.dt.float32
BF16 = mybir.dt.bfloat16
I32 = mybir.dt.int32
AF = mybir.ActivationFunctionType
ALU = mybir.AluOpType

D = 1024
DT = 8
DFF = 2816
FT = 22
NMETA = 16
DIN = 6144
NG = 32
NST = 64
CW = 31
RMS_EPS = 1e-6
LN_EPS = 1e-5

ENGS = ["pe", "act", "dve", "pool", "sp"]


class Sched:
    def __init__(self, nc, es, n_dma_sems=24):
        self.nc = nc
        self.ops = {e: [] for e in ENGS}
        self.sem = {e: es.enter_context(nc.semaphore("sem_" + e)) for e in ENGS}
        self.cnt = {e: 0 for e in ENGS}
        self.dsem = [es.enter_context(nc.semaphore("semd%d" % i)) for i in range(n_dma_sems)]
        self.duse = [0] * n_dma_sems
        self.dnext = 0
        self.waited = {}
        self.lastw = {}
        self.readers = {}

    def _semof(self, key):
        if isinstance(key, str):
            return self.sem[key]
        return self.dsem[key[1]]

    def _deps(self, eng, reads, writes):
        toks = {}
        def add(t):
            k, v = t
            if toks.get(k, 0) < v:
                toks[k] = v
        for r in reads:
            if r in self.lastw:
                add(self.lastw[r])
        for w in writes:
            if w in self.lastw:
                add(self.lastw[w])
            for k, v in self.readers.get(w, {}).items():
                add((k, v))
        waits = []
        for k, v in toks.items():
            if k == "pe" and eng == "pe":
                continue
            if self.waited.get((eng, k), 0) >= v:
                continue
            self.waited[(eng, k)] = v
            waits.append((k, v))
        return waits

    def _record(self, tok, reads, writes):
        k, v = tok
        for r in reads:
            d = self.readers.setdefault(r, {})
            if d.get(k, 0) < v:
                d[k] = v
        for w in writes:
            self.lastw[w] = tok
            self.readers[w] = {}

    def op(self, eng, meth, *args, reads=(), writes=(), excl=(), **kw):
        fn = (meth, args, kw)
        if excl:
            reads = list(reads) + list(excl)
            writes = list(writes) + list(excl)
        waits = self._deps(eng, reads, writes)
        self.cnt[eng] += 1
        tok = (eng, self.cnt[eng])
        self.ops[eng].append((waits, fn, eng, 1))
        self._record(tok, reads, writes)

    def dma(self, eng, reads=(), writes=(), **kw):
        fn = ("dma_start", (), kw)
        waits = self._deps(eng, reads, writes)
        slot = self.dnext
        self.dnext = (self.dnext + 1) % len(self.dsem)
        key = ("d", slot)
        if self.duse[slot] > 0:
            v = 16 * self.duse[slot]
            if self.waited.get((eng, key), 0) < v:
                self.waited[(eng, key)] = v
                waits.append((key, v))
        self.duse[slot] += 1
        tok = (key, 16 * self.duse[slot])
        self.ops[eng].append((waits, fn, key, 16))
        self._record(tok, reads, writes)

    def barrier(self):
        for eng in ENGS:
            waits = []
            for k in ENGS:
                v = self.cnt[k]
                if v > 0 and k != eng and self.waited.get((eng, k), 0) < v:
                    self.waited[(eng, k)] = v
                    waits.append((k, v))
            if eng != "pe":
                v = self.cnt[eng]
                if v > 0 and self.waited.get((eng, eng), 0) < v:
                    self.waited[(eng, eng)] = v
                    waits.append((eng, v))
            for i, u in enumerate(self.duse):
                key = ("d", i)
                if u > 0 and self.waited.get((eng, key), 0) < 16 * u:
                    self.waited[(eng, key)] = 16 * u
                    waits.append((key, 16 * u))
            if waits:
                self.ops[eng].append((waits, None, None, 0))
        self.lastw = {}
        self.readers = {}

    def emit(self, block):
        def replay(name):
            def run(e):
                for waits, fn, inc_key, inc in self.ops[name]:
                    for k, v in waits:
                        e.wait_ge(self._semof(k), v)
                    if fn is not None:
                        meth, args, kw = fn
                        getattr(e, meth)(*args, **kw).then_inc(self._semof(inc_key), inc)
            return run
        block.tensor(replay("pe"))
        block.scalar(replay("act"))
        block.vector(replay("dve"))
        block.gpsimd(replay("pool"))
        block.sync(replay("sp"))


class Cfg:
    def __init__(self, seq=4096, nseq=2, depth=4, phases=None, dump=()):
        self.dump = dump
        self.seq = seq
        self.nseq = nseq
        self.depth = depth
        self.L = seq + NMETA
        tiles = []
        t = 0
        while t + 512 <= self.L:
            tiles.append((t, 512))
            t += 512
        if t < self.L:
            tiles.append((t, self.L - t))
        self.tiles = tiles
        self.nkb = (self.L + 127) // 128
        self.LP = self.nkb * 128
        self.NK = self.L // 8
        self.phases = phases


def build(cfg):
    nc = bass.Bass("TRN2", target_bir_lowering=False)
    L, NS, DEPTH = cfg.L, cfg.nseq, cfg.depth

    def din(name, shape):
        return nc.dram_tensor(name, list(shape), F32, kind="ExternalInput").ap()

    x = din("x", (NS, cfg.seq, D))
    meta = din("meta_tokens", (NMETA, D))
    P = {}
    for name, shape in [
        ("ffn1_norm", (DEPTH, D)), ("ffn1_w13", (DEPTH, D, 2 * DFF)), ("ffn1_w2", (DEPTH, DFF, D)),
        ("mix_norm", (DEPTH, D)), ("w_in", (DEPTH, D, DIN)),
        ("ssm_lam_re", (DEPTH, NG, NST)), ("ssm_lam_im", (DEPTH, NG, NST)), ("ssm_log_dt", (DEPTH, NG)),
        ("ssm_b_re", (DEPTH, NG, NST, 16)), ("ssm_b_im", (DEPTH, NG, NST, 16)),
        ("ssm_c_re", (DEPTH, NG, 16, NST)), ("ssm_c_im", (DEPTH, NG, 16, NST)),
        ("ssm_d", (DEPTH, 512)), ("ssm_w_glu", (DEPTH, 512, 2048)),
        ("conv_w", (DEPTH, CW, 512)), ("conv_b", (DEPTH, 512)),
        ("conv_ln_g", (DEPTH, 512)), ("conv_ln_b", (DEPTH, 512)),
        ("conv_w_out", (DEPTH, 512, D)), ("attn_w_o", (DEPTH, 512, D)), ("w_out", (DEPTH, D, D)),
        ("ffn2_norm", (DEPTH, D)), ("ffn2_w13", (DEPTH, D, 2 * DFF)), ("ffn2_w2", (DEPTH, DFF, D)),
        ("final_norm", (D,)),
    ]:
        P[name] = din(name, shape)
    out = nc.dram_tensor("out", [NS, cfg.seq, D], F32, kind="ExternalOutput").ap()

    def scratch(name, shape, dt):
        kind = "ExternalOutput" if name in cfg.dump else "Internal"
        return nc.dram_tensor(name, list(shape), dt, kind=kind).ap()

    hT = scratch("hT", (NS, D, L), F32)
    NK = cfg.NK
    Usc = scratch("Usc", (NS, NG, 8, 16, NK), BF16)
    qT = scratch("qT", (NS, 512, L), BF16)
    kT = scratch("kT", (NS, 512, L), BF16)
    vS = scratch("vS", (NS, L, 512), BF16)
    oattT = scratch("oattT", (NS, 512, L), BF16)
    ysS = scratch("ysS", (NS, NG, 8, 16, NK), F32)
    hcvS = scratch("hcvS", (NS, 512, L), BF16)

    with ExitStack() as es:
        S = Sched(nc, es)

        uid = [0]

        def sb(name, shape, dt, stack=es):
            uid[0] += 1
            return stack.enter_context(nc.sbuf_tensor("%s_%d" % (name, uid[0]), list(shape), dt))

        def ps(name, shape, dt=F32, stack=es):
            uid[0] += 1
            return stack.enter_context(nc.psum_tensor("%s_%d" % (name, uid[0]), list(shape), dt))

        ones_bf = sb("ones_bf", [128, 128], BF16)
        ident_f = sb("ident_f", [128, 128], F32)
        gcols = sb("gcols", [128, (3 * DEPTH + 1) * DT], F32)
        eps_rms = sb("eps_rms", [128, 1], F32)
        rn_tmp = [sb("rn_tmp%d" % i, [128, 512], F32) for i in range(2)]
        S.op("pool", "memset", ones_bf[:], 1.0, writes=["ones_bf"])
        S.op("pool", "memset", ident_f[:], 1.0, writes=["ident_f"])
        S.op("pool", "affine_select", out=ident_f[:], in_=ident_f[:], pattern=[[-1, 128]],
                                               compare_op=ALU.is_equal, fill=0.0, base=0,
                                               channel_multiplier=1,
             reads=["ident_f"], writes=["ident_f"])
        S.op("pool", "memset", eps_rms[:], RMS_EPS, writes=["eps_rms"])
        for wi, nm in enumerate(["ffn1_norm", "mix_norm", "ffn2_norm"]):
            for l in range(DEPTH):
                c0 = (wi * DEPTH + l) * DT
                S.dma("sp",
                    out=gcols[:, c0:c0 + DT], in_=P[nm][l].rearrange("(t p) -> p t", p=128),
                    allow_slow_non_contiguous=True, writes=["gcols"])
        cF = 3 * DEPTH * DT
        S.dma("sp", out=gcols[:, cF:cF + DT],
                                          in_=P["final_norm"].rearrange("(t p) -> p t", p=128),
                                          allow_slow_non_contiguous=True, writes=["gcols"])
        S.barrier()

        def gcol(which, l):
            c0 = (which * DEPTH + l) * DT if which < 3 else cF
            return c0

        def load_h(hs, s, t0, n):
            S.dma("sp",
                out=hs[:, :, 0:n], in_=hT[s, :, t0:t0 + n].rearrange("(t p) n -> p t n", p=128),
                writes=["hs"])

        def store_h(hs, s, t0, n):
            S.dma("sp",
                out=hT[s, :, t0:t0 + n].rearrange("(t p) n -> p t n", p=128), in_=hs[:, :, 0:n],
                reads=["hs"])

        def rmsnorm(hs, xn, sqb, pss, rstd, c0, n, xn_name="xn", hs_name="hs"):
            for dt in range(DT):
                b = dt % 2
                S.op("act", "activation", out=sqb[b][:, 0:n], in_=hs[:, dt, 0:n],
                                                                 func=AF.Square,
                     reads=[hs_name], writes=[("sqb", b)])
                S.op("pe", "matmul", pss[:, 0:n], lhsT=ones_bf[:], rhs=sqb[b][:, 0:n],
                                                           start=(dt == 0), stop=(dt == DT - 1),
                     reads=[("sqb", b), "ones_bf"], writes=["pss"])
            S.op("act", "activation", out=rstd[:, 0:n], in_=pss[:, 0:n], func=AF.Sqrt,
                                               scale=1.0 / D, bias=eps_rms[:],
                 reads=["pss"], writes=["rstd"])
            S.op("dve", "reciprocal", out=rstd[:, 0:n], in_=rstd[:, 0:n],
                 reads=["rstd"], writes=["rstd"])
            for dt in range(DT):
                b = dt % 2
                S.op("dve", "tensor_tensor", out=rn_tmp[b][:, 0:n], in0=hs[:, dt, 0:n], in1=rstd[:, 0:n],
                     op=ALU.mult, reads=[hs_name, "rstd"], writes=[("rn_tmp", b)])
                S.op("act", "activation", out=xn[:, dt, 0:n], in_=rn_tmp[b][:, 0:n], func=AF.Identity,
                     scale=gcols[:, c0 + dt:c0 + dt + 1],
                     reads=[("rn_tmp", b), "gcols"], writes=[xn_name])

        def load_w_cast(dst_ap, src_ap, res):
            S.dma("pool", out=dst_ap, in_=src_ap, max_dma_last_dim=8192,
                  writes=[res])

        def phase_in():
            with ExitStack() as st:
                hs = sb("in_hs", [128, DT, 512], F32, st)
                xt = [sb("in_xt%d" % i, [128, D], F32, st) for i in range(2)]
                pt = [ps("in_pt%d" % i, [128, 512], F32, st) for i in range(2)]
                blk = 0
                for s in range(NS):
                    for (t0, n) in cfg.tiles:
                        for j in range((n + 127) // 128):
                            tb = t0 + j * 128
                            nb = min(128, t0 + n - tb)
                            xb = xt[blk % 2]
                            xr = ("xt", blk % 2)
                            if tb == 0:
                                S.dma("sp", out=xb[0:NMETA, :], in_=meta[:, :],
                                      writes=[xr])
                                S.dma("sp",
                                    out=xb[NMETA:nb, :], in_=x[s, 0:nb - NMETA, :], writes=[(xr, 1)],
                                    reads=[])
                                rd = [xr, (xr, 1)]
                            else:
                                S.dma("sp",
                                    out=xb[0:nb, :], in_=x[s, tb - NMETA:tb - NMETA + nb, :], writes=[xr, (xr, 1)])
                                rd = [xr, (xr, 1)]
                            for half in range(2):
                                pp = pt[half]
                                for q in range(4):
                                    dt = half * 4 + q
                                    S.op("pe", "transpose",
                                        out=pp[:, q * 128:q * 128 + nb], in_=xb[0:nb, dt * 128:(dt + 1) * 128],
                                        identity=ident_f[0:nb, 0:nb],
                                        reads=rd + ["ident_f"], writes=[("pt", half)])
                                eng = "act" if half == 0 else "dve"
                                if eng == "act":
                                    S.op("act", "activation",
                                        out=hs[:, half * 4:half * 4 + 4, j * 128:j * 128 + nb],
                                        in_=pp[:].rearrange("p (q c) -> p q c", q=4)[:, :, 0:nb], func=AF.Copy,
                                        reads=[("pt", half)], writes=["hs"])
                                else:
                                    S.op("dve", "tensor_copy",
                                        out=hs[:, half * 4:half * 4 + 4, j * 128:j * 128 + nb],
                                        in_=pp[:].rearrange("p (q c) -> p q c", q=4)[:, :, 0:nb],
                                        reads=[("pt", half)], writes=["hs"])
                            blk += 1
                        store_h(hs, s, t0, n)
                S.barrier()

        def phase_out():
            with ExitStack() as st:
                hs = sb("o_hs", [128, DT, 512], F32, st)
                xn = sb("o_xn", [128, DT, 512], F32, st)
                sqb = [sb("o_sq%d" % i, [128, 512], BF16, st) for i in range(2)]
                rstd = sb("o_rstd", [128, 512], F32, st)
                ot = [sb("o_ot%d" % i, [128, D], F32, st) for i in range(2)]
                pss = ps("o_pss", [128, 512], F32, st)
                pt = [ps("o_pt%d" % i, [128, 512], F32, st) for i in range(2)]
                blk = 0
                for s in range(NS):
                    for (t0, n) in cfg.tiles:
                        load_h(hs, s, t0, n)
                        rmsnorm(hs, xn, sqb, pss, rstd, cF, n)
                        for j in range((n + 127) // 128):
                            tb = t0 + j * 128
                            nb = min(128, t0 + n - tb)
                            ob = ot[blk % 2]
                            orr = ("ot", blk % 2)
                            for half in range(2):
                                pp = pt[half]
                                for q in range(4):
                                    dt = half * 4 + q
                                    S.op("pe", "transpose",
                                        out=pp[0:nb, q * 128:(q + 1) * 128], in_=xn[:, dt, j * 128:j * 128 + nb],
                                        identity=ident_f[:, :],
                                        reads=["xn", "ident_f"], writes=[("pt", half)])
                                if half == 0:
                                    S.op("act", "activation",
                                        out=ob[0:nb, 0:512], in_=pp[0:nb, :], func=AF.Copy,
                                        reads=[("pt", half)], writes=[orr])
                                else:
                                    S.op("dve", "tensor_copy",
                                        out=ob[0:nb, 512:1024], in_=pp[0:nb, :],
                                        reads=[("pt", half)], writes=[(orr, 1)])
                            lo = NMETA if tb == 0 else 0
                            S.dma("sp",
                                out=out[s, tb + lo - NMETA:tb + nb - NMETA, :], in_=ob[lo:nb, :],
                                reads=[orr, (orr, 1)])
                            blk += 1
                S.barrier()

        def phase_ffn(l, which):
            pre = "ffn1" if which == 0 else "ffn2"
            w13 = P[pre + "_w13"][l]
            w2 = P[pre + "_w2"][l]
            c0 = gcol(0 if which == 0 else 2, l)
            with ExitStack() as st:
                w13s = sb("f_w13", [128, DT, 2 * DFF], BF16, st)
                w2s = sb("f_w2", [128, FT, D], BF16, st)
                hsb = [sb("f_hs%d" % i, [128, DT, 512], F32, st) for i in range(2)]
                xn = sb("f_xn", [128, DT, 512], BF16, st)
                sqb = [sb("f_sq%d" % i, [128, 512], BF16, st) for i in range(2)]
                rstd = sb("f_rstd", [128, 512], F32, st)
                gh = sb("f_g", [128, FT, 512], BF16, st)
                sa = [sb("f_sa%d" % i, [128, 512], F32, st) for i in range(2)]
                pss = ps("f_pss", [128, 512], F32, st)
                pa = [ps("f_pa%d" % i, [128, 512], F32, st) for i in range(2)]
                pb = [ps("f_pb%d" % i, [128, 512], F32, st) for i in range(2)]
                po = [ps("f_po%d" % i, [128, 512], F32, st) for i in range(2)]
                for dt in range(DT):
                    load_w_cast(w13s[:, dt, :], w13[dt * 128:(dt + 1) * 128, :], ("w13", dt))
                for f in range(FT):
                    load_w_cast(w2s[:, f, :], w2[f * 128:(f + 1) * 128, :], ("w2", f))
                tiles_all = [(s, t0, n) for s in range(NS) for (t0, n) in cfg.tiles]

                def ld(idx):
                    s_, t0_, n_ = tiles_all[idx]
                    S.dma("sp", out=hsb[idx % 2][:, :, 0:n_],
                          in_=hT[s_, :, t0_:t0_ + n_].rearrange("(t p) n -> p t n", p=128), writes=[("hsb", idx % 2)])

                def nrm(idx):
                    s_, t0_, n_ = tiles_all[idx]
                    rmsnorm(hsb[idx % 2], xn, sqb, pss, rstd, c0, n_, hs_name=("hsb", idx % 2))

                ld(0)
                nrm(0)
                for idx, (s, t0, n) in enumerate(tiles_all):
                    if True:
                        hs = hsb[idx % 2]
                        hsr = ("hsb", idx % 2)
                        if idx + 1 < len(tiles_all):
                            ld(idx + 1)
                        for f in range(FT):
                            b = f % 2
                            for dt in range(DT):
                                S.op("pe", "matmul",
                                    pa[b][:, 0:n], lhsT=w13s[:, dt, f * 128:(f + 1) * 128], rhs=xn[:, dt, 0:n],
                                    start=(dt == 0), stop=(dt == DT - 1),
                                    reads=[("w13", dt), "xn"], writes=[("pa", b)])
                            for dt in range(DT):
                                S.op("pe", "matmul",
                                    pb[b][:, 0:n], lhsT=w13s[:, dt, DFF + f * 128:DFF + (f + 1) * 128],
                                    rhs=xn[:, dt, 0:n], start=(dt == 0), stop=(dt == DT - 1),
                                    reads=[("w13", dt), "xn"], writes=[("pb", b)])
                            S.op("act", "activation", out=sa[b][:, 0:n], in_=pa[b][:, 0:n],
                                                                    func=AF.Silu,
                                 reads=[("pa", b)], writes=[("sa", b)])
                            S.op("dve", "tensor_tensor",
                                out=gh[:, f, 0:n], in0=pb[b][:, 0:n], in1=sa[b][:, 0:n], op=ALU.mult,
                                reads=[("pb", b), ("sa", b)], writes=[("gh", f)])
                        if idx + 1 < len(tiles_all):
                            nrm(idx + 1)
                        for o in range(DT):
                            b = o % 2
                            for f in range(FT):
                                S.op("pe", "matmul",
                                    po[b][:, 0:n], lhsT=w2s[:, f, o * 128:(o + 1) * 128], rhs=gh[:, f, 0:n],
                                    start=(f == 0), stop=(f == FT - 1),
                                    reads=[("w2", f), ("gh", f)], writes=[("po", b)])
                            import os
                            if os.environ.get("DBG") == "po":
                                S.op("dve", "tensor_copy", out=hs[:, o, 0:n], in_=po[b][:, 0:n],
                                     reads=[("po", b), "hs"], writes=["hs"])
                            elif os.environ.get("DBG") == "xn":
                                S.op("dve", "tensor_copy", out=hs[:, o, 0:n], in_=xn[:, o, 0:n],
                                     reads=[("po", b), "hs", "xn"], writes=["hs"])
                            elif os.environ.get("DBG") == "gh":
                                S.op("dve", "tensor_copy", out=hs[:, o, 0:n], in_=gh[:, o, 0:n],
                                     reads=[("po", b), "hs", "xn", ("gh", o)], writes=["hs"])
                            else:
                              S.op("act", "activation", out=sa[b][:, 0:n], in_=po[b][:, 0:n], func=AF.Copy, scale=0.5,
                                   reads=[("po", b)], writes=[("sa", b)])
                              S.op("dve", "tensor_tensor", out=hs[:, o, 0:n], in0=sa[b][:, 0:n], in1=hs[:, o, 0:n],
                                   op=ALU.add, reads=[("sa", b), hsr], writes=[hsr])
                        S.dma("sp", out=hT[s, :, t0:t0 + n].rearrange("(t p) n -> p t n", p=128), in_=hs[:, :, 0:n],
                              reads=[hsr])
                S.barrier()


        def phase_mixA(l):
            w_in = P["w_in"][l]
            c0 = gcol(1, l)
            with ExitStack() as st:
                wA = sb("a_w", [128, DT, 2048], BF16, st)
                hs = sb("a_hs", [128, DT, 512], F32, st)
                xn = sb("a_xn", [128, DT, 512], BF16, st)
                sqb = [sb("a_sq%d" % i, [128, 512], BF16, st) for i in range(2)]
                rstd = sb("a_rstd", [128, 512], F32, st)
                stg = [sb("a_stg%d" % i, [128, 512], BF16, st) for i in range(4)]
                pss = ps("a_pss", [128, 512], F32, st)
                pp = [ps("a_pp%d" % i, [128, 512], F32, st) for i in range(4)]
                for dt in range(DT):
                    load_w_cast(wA[:, dt, 0:512], w_in[dt * 128:(dt + 1) * 128, 0:512], ("wA", dt))
                    load_w_cast(wA[:, dt, 512:2048], w_in[dt * 128:(dt + 1) * 128, 1536:3072], ("wA", dt, 1))
                cnt = 0
                for s in range(NS):
                    for (t0, n) in cfg.tiles:
                        load_h(hs, s, t0, n)
                        rmsnorm(hs, xn, sqb, pss, rstd, c0, n)
                        nk = n // 8
                        k0 = t0 // 8
                        for f in range(12):
                            b = cnt % 4
                            cnt += 1
                            pb = pp[b]
                            sg = stg[b]
                            for dt in range(DT):
                                S.op("pe", "matmul", pb[:, 0:n], lhsT=wA[:, dt, f * 128:(f + 1) * 128],
                                     rhs=xn[:, dt, 0:n], start=(dt == 0), stop=(dt == DT - 1),
                                     reads=[("wA", dt), ("wA", dt, 1), "xn"], writes=[("pp", b)])
                            if f < 4:
                                S.op("act", "activation",
                                     out=sg[:, 0:n].rearrange("p (s k) -> p s k", s=8),
                                     in_=pb[:, 0:n].rearrange("p (k s) -> p s k", s=8), func=AF.Copy,
                                     reads=[("pp", b)], writes=[("stg", b)])
                                for gl in range(8):
                                    S.dma("sp", out=Usc[s, f * 8 + gl, :, :, k0:k0 + nk].rearrange("s c k -> c s k"),
                                          in_=sg[gl * 16:(gl + 1) * 16, 0:n].rearrange("p (s k) -> p s k", s=8),
                                          reads=[("stg", b)])
                            else:
                                if f % 2 == 0:
                                    S.op("act", "activation", out=sg[:, 0:n], in_=pb[:, 0:n], func=AF.Copy,
                                         reads=[("pp", b)], writes=[("stg", b)])
                                else:
                                    S.op("dve", "tensor_copy", out=sg[:, 0:n], in_=pb[:, 0:n],
                                         reads=[("pp", b)], writes=[("stg", b)])
                                dst = qT if f < 8 else kT
                                r0 = (f % 4) * 128
                                S.dma("sp", out=dst[s, r0:r0 + 128, t0:t0 + n], in_=sg[:, 0:n],
                                      reads=[("stg", b)])
                        for j in range((n + 127) // 128):
                            nb = min(128, n - j * 128)
                            b = cnt % 4
                            cnt += 1
                            pb = pp[b]
                            sg = stg[b]
                            for dt in range(DT):
                                S.op("pe", "matmul", pb[0:nb, 0:512], lhsT=xn[:, dt, j * 128:j * 128 + nb],
                                     rhs=wA[:, dt, 1536:2048], start=(dt == 0), stop=(dt == DT - 1),
                                     reads=[("wA", dt), ("wA", dt, 1), "xn"], writes=[("pp", b)])
                            S.op("dve", "tensor_copy", out=sg[0:nb, :], in_=pb[0:nb, :],
                                 reads=[("pp", b)], writes=[("stg", b)])
                            S.dma("sp", out=vS[s, t0 + j * 128:t0 + j * 128 + nb, :], in_=sg[0:nb, :],
                                  reads=[("stg", b)])
                S.barrier()

        def phase_att(l):
            nkb, LP = cfg.nkb, cfg.LP
            tail = L - (nkb - 1) * 128
            with ExitStack() as st:
                triT = sb("t_tri", [128, 128], BF16, st)
                sel0 = sb("t_sel0", [128, 128], BF16, st)
                onec = sb("t_onec", [128, 1], F32, st)
                masks = [sb("t_mask%d" % i, [128, 512], F32, st) for i in range(4)]
                kT2 = [sb("t_k%d" % i, [128, LP], BF16, st) for i in range(2)]
                qz = [[sb("t_q%d_%d" % (i, h), [128, L], BF16, st) for h in range(2)] for i in range(2)]
                vz = [[sb("t_v%d_%d" % (i, h), [128, nkb, 128], BF16, st) for h in range(2)] for i in range(2)]
                NB3 = 4
                e_t = [sb("t_e%d" % i, [128, 512], F32, st) for i in range(NB3)]
                sp_t = [sb("t_sp%d" % i, [128, 512], BF16, st) for i in range(NB3)]
                g_t = [sb("t_g%d" % i, [128, 512], F32, st) for i in range(NB3)]
                w_t = [sb("t_w%d" % i, [128, 512], BF16, st) for i in range(NB3)]
                Rs = [sb("t_Rs%d" % i, [128, 512], BF16, st) for i in range(2)]
                ob = [sb("t_ob%d" % i, [128, 512], BF16, st) for i in range(2)]
                pz = [ps("t_pz%d" % i, [128, 512], F32, st) for i in range(2)]
                pcs = [ps("t_pcs%d" % i, [128, 512], F32, st) for i in range(2)]
                po = [ps("t_po%d" % i, [128, 512], F32, st) for i in range(2)]
                S.op("pool", "memset", triT[:], 1.0, writes=["triT"])
                S.op("pool", "affine_select", out=triT[:], in_=triT[:], pattern=[[-1, 128]],
                     compare_op=ALU.is_ge, fill=0.0, base=0, channel_multiplier=1,
                     reads=["triT"], writes=["triT"])
                S.op("pool", "memset", sel0[:], 1.0, writes=["sel0"])
                S.op("pool", "affine_select", out=sel0[:], in_=sel0[:], pattern=[[0, 128]],
                     compare_op=ALU.is_equal, fill=0.0, base=0, channel_multiplier=1,
                     reads=["sel0"], writes=["sel0"])
                S.op("pool", "memset", onec[:], 1.0, writes=["onec"])
                for i in range(4):
                    S.op("pool", "memset", masks[i][:], 1.0, writes=[("mask", i)])
                    S.op("pool", "affine_select", out=masks[i][:], in_=masks[i][:], pattern=[[1, 512]],
                         compare_op=ALU.is_gt, fill=0.0, base=-128 * i, channel_multiplier=-1,
                         reads=[("mask", i)], writes=[("mask", i)])
                for i in range(2):
                    S.op("pool", "memset", kT2[i][:], 0.0, writes=[("kT2", i)])
                    for h in range(2):
                        S.op("pool", "memset", qz[i][h][:], 0.0, writes=[("qz", i, h)])
                        S.op("pool", "memset", vz[i][h][:], 0.0, writes=[("vz", i, h), ("vz", i, h, 1)])
                blocks = []
                it = 0
                qcnt = 0
                for s in range(NS):
                    for hp in range(4):
                        bb = it % 2
                        it += 1
                        first_of_load = True
                        for (q0, nq) in cfg.tiles:
                            qb = qcnt % 2
                            qcnt += 1
                            nblk = max((q0 + nq - 1 + 127) // 128, 1)
                            for bi, kb in enumerate(reversed(range(nblk))):
                                for h in range(2):
                                    blocks.append(dict(
                                        s=s, hp=hp, bb=bb, q0=q0, nq=nq, qb=qb, h=h, bi=bi, kb=kb, nblk=nblk,
                                        load=first_of_load, first_pv=(h == 0 and bi == 0),
                                        last_pv=(h == 1 and bi == nblk - 1)))
                                    first_of_load = False
                for i, B_ in enumerate(blocks):
                    B_["i3"] = i % NB3
                    B_["i2"] = i % 2

                def load_inputs(B_):
                    s, hp, bb = B_["s"], B_["hp"], B_["bb"]
                    S.dma("sp", out=kT2[bb][:, 0:L], in_=kT[s, hp * 128:(hp + 1) * 128, :],
                          reads=[], writes=[("kT2", bb)])
                    for h in range(2):
                        r0 = hp * 128 + h * 64
                        S.dma("sp", out=qz[bb][h][h * 64:(h + 1) * 64, :], in_=qT[s, r0:r0 + 64, :],
                              writes=[("qz", bb, h)])
                        if nkb > 1:
                            S.dma("sp", out=vz[bb][h][:, 0:nkb - 1, h * 64:(h + 1) * 64],
                                  in_=vS[s, 0:(nkb - 1) * 128, r0:r0 + 64].rearrange("(b p) d -> p b d", p=128),
                                  writes=[("vz", bb, h)])
                        S.dma("sp", out=vz[bb][h][0:tail, nkb - 1, h * 64:(h + 1) * 64],
                              in_=vS[s, (nkb - 1) * 128:L, r0:r0 + 64],
                              writes=[("vz", bb, h, 1)])

                def stageA(B_):
                    bb, h, q0, nq, kb, i3, i2 = (B_[k] for k in ("bb", "h", "q0", "nq", "kb", "i3", "i2"))
                    if B_["load"]:
                        load_inputs(B_)
                    diag = (kb * 128 + 128 > q0)
                    S.op("pe", "matmul", pz[i2][:, 0:nq], lhsT=kT2[bb][:, kb * 128:(kb + 1) * 128],
                         rhs=qz[bb][h][:, q0:q0 + nq], start=True, stop=True,
                         reads=[("kT2", bb), ("qz", bb, h)], writes=[("pz", i2)])
                    S.op("act", "activation", out=e_t[i3][:, 0:nq], in_=pz[i2][:, 0:nq],
                         func=AF.Exp, scale=0.125,
                         reads=[("pz", i2)], writes=[("e", i3)])
                    if diag:
                        mi = (kb * 128 - q0) // 128
                        assert 0 <= mi < 4
                        S.op("dve", "tensor_tensor", out=e_t[i3][:, 0:nq], in0=e_t[i3][:, 0:nq],
                             in1=masks[mi][:, 0:nq], op=ALU.mult,
                             reads=[("e", i3), ("mask", mi)], writes=[("e", i3)])
                    S.op("act", "activation", out=sp_t[i3][:, 0:nq], in_=e_t[i3][:, 0:nq],
                         func=AF.Ln, bias=onec[:], scale=1.0,
                         reads=[("e", i3), "onec"], writes=[("sp", i3)])

                def stageB(B_):
                    nq, bi, nblk, i3, i2, hh = (B_[k] for k in ("nq", "bi", "nblk", "i3", "i2", "h"))
                    S.op("pe", "matmul", pcs[i2][:, 0:nq], lhsT=triT[:], rhs=sp_t[i3][:, 0:nq],
                         start=True, stop=(bi == 0),
                         reads=["triT", ("sp", i3)], writes=[("pcs", i2)])
                    if bi > 0:
                        S.op("pe", "matmul", pcs[i2][:, 0:nq], lhsT=ones_bf[:], rhs=Rs[hh][:, 0:nq],
                             start=False, stop=True,
                             reads=["ones_bf", ("Rs", hh)], writes=[("pcs", i2)])
                    if bi < nblk - 1:
                        if bi == 0:
                            S.op("pool", "tensor_copy", out=Rs[hh][:, 0:nq], in_=sp_t[i3][:, 0:nq],
                                 reads=[("sp", i3)], writes=[("Rs", hh)])
                        else:
                            S.op("pool", "tensor_tensor", out=Rs[hh][:, 0:nq], in0=Rs[hh][:, 0:nq],
                                 in1=sp_t[i3][:, 0:nq], op=ALU.add,
                                 reads=[("sp", i3), ("Rs", hh)], writes=[("Rs", hh)])
                    S.op("act", "activation", out=g_t[i3][:, 0:nq], in_=pcs[i2][:, 0:nq],
                         func=AF.Exp, scale=-1.0,
                         reads=[("pcs", i2)], writes=[("g", i3)])
                    S.op("dve", "tensor_tensor", out=w_t[i3][:, 0:nq], in0=e_t[i3][:, 0:nq],
                         in1=g_t[i3][:, 0:nq], op=ALU.mult,
                         reads=[("e", i3), ("g", i3)], writes=[("w", i3)])

                def stageC(B_):
                    s, hp, bb, h, q0, nq, kb, i3, qb = (B_[k] for k in ("s", "hp", "bb", "h", "q0", "nq", "kb", "i3", "qb"))
                    S.op("pe", "matmul", po[qb][:, 0:nq], lhsT=vz[bb][h][:, kb, :],
                         rhs=w_t[i3][:, 0:nq], start=B_["first_pv"], stop=B_["last_pv"],
                         reads=[("vz", bb, h), ("vz", bb, h, 1), ("w", i3)], writes=[("po", qb)])
                    if B_["last_pv"]:
                        S.op("act", "activation", out=ob[qb][:, 0:nq], in_=po[qb][:, 0:nq], func=AF.Copy,
                             reads=[("po", qb)], writes=[("ob", qb)])
                        S.dma("sp", out=oattT[s, hp * 128:(hp + 1) * 128, q0:q0 + nq], in_=ob[qb][:, 0:nq],
                              reads=[("ob", qb)])

                NBLK = len(blocks)
                for i in range(NBLK + 2):
                    if i < NBLK:
                        stageA(blocks[i])
                    if 0 <= i - 1 < NBLK:
                        stageB(blocks[i - 1])
                    if 0 <= i - 2 < NBLK:
                        stageC(blocks[i - 2])
                S.barrier()

        def phase_ssm(l):
            NKc = cfg.NK
            if NKc <= 512:
                halves = [(0, NKc)]
            else:
                halves = [(0, NKc // 2), (NKc // 2, NKc - NKc // 2)]
            TWO_PI = 2.0 * math.pi
            with ExitStack() as st:
                stp = ExitStack()

                def t32(name, shape=(128, 32), dt=F32, stack=None):
                    return sb("s_" + name, list(shape), dt, st if stack is None else stack)

                def t32p(name, shape=(128, 32), dt=F32):
                    return t32(name, shape, dt, stp)
                W1, W1s, W2, W2s, W3 = (t32("W%d" % i, (128, 32, 128), BF16) for i in range(5))
                rho8, f8 = t32("rho8"), t32("f8")
                kidx = t32("kidx", (128, NKc))
                ki = t32("ki", (128, NKc), I32)
                kidx_i = ki
                pw = [ps("s_pw%d" % i, [128, 512], F32, st) for i in range(2)]
                pL = ps("s_pL", [128, 2, 512], F32, st)
                pLs = ps("s_pLs", [128, 2, 512], F32, st)
                py = ps("s_py", [128, 2, 512], F32, st)
                lr, li, dtb, ar, ft = t32p("lr"), t32p("li"), t32p("dtb"), t32p("ar"), t32p("ft")
                yp, ti, tf, tt, sn, cs, mag = (t32p("yp"), t32p("ti", dt=I32), t32p("tf"), t32p("tt"),
                                               t32p("sn"), t32p("cs"), t32p("mag"))
                pwr = {p: t32p("pwr%d" % (p + 7)) for p in range(-7, 9)}
                pwi = {p: t32p("pwi%d" % (p + 7)) for p in range(-7, 9)}
                cre, cim, t_a, t_b, dcol = (t32p("cre"), t32p("cim"), t32p("ta"), t32p("tb"), t32p("dcol"))
                Bre, Bim = t32p("Bre", (128, 32, 16)), t32p("Bim", (128, 32, 16))
                bbr, bbi = t32p("bbr", (128, 32, 16)), t32p("bbi", (128, 32, 16))
                Craw = [t32p("Craw%d" % i, (128, 4, 128)) for i in range(2)]
                Cn = [t32p("Cn%d" % i, (128, 32, 16)) for i in range(2)]
                q1, q2, q3, q4 = (t32p("q%d" % i, (128, 32, 16)) for i in range(4))
                X1, X1s, X2, X2s, X1p, X2p = (t32p("X%d" % i, (128, 32, 8, 16)) for i in range(6))
                maskBL = t32p("maskBL", (128, 128))
                w3t = t32p("w3t", (128, 128))
                w3r = [t32p("w3r%d" % i, (128, 128)) for i in range(2)]

                for hf in (0, 64):
                    S.dma("sp", out=lr[hf:hf + 64, :], in_=P["ssm_lam_re"][l].rearrange("g n -> n g"),
                          allow_slow_non_contiguous=True, writes=["lr"])
                    S.dma("sp", out=li[hf:hf + 64, :], in_=P["ssm_lam_im"][l].rearrange("g n -> n g"),
                          allow_slow_non_contiguous=True, writes=["li"])
                    S.dma("sp", out=dtb[hf:hf + 64, :], in_=P["ssm_log_dt"][l].partition_broadcast(64),
                          writes=["dtb"])
                    S.dma("sp", out=Bre[hf:hf + 64, :, :], in_=P["ssm_b_re"][l].rearrange("g n c -> n g c"),
                          writes=["Bre"])
                    S.dma("sp", out=Bim[hf:hf + 64, :, :], in_=P["ssm_b_im"][l].rearrange("g n c -> n g c"),
                          writes=["Bim"])
                for i, nm in enumerate(["ssm_c_re", "ssm_c_im"]):
                    for dup in range(2):
                        S.dma("sp", out=Craw[i][:, :, dup * 64:(dup + 1) * 64],
                              in_=P[nm][l].rearrange("g c n -> (g c) n").rearrange("(t p) n -> p t n", p=128),
                              writes=[("Craw", i)])
                for s8 in range(8):
                    S.dma("sp", out=dcol[s8 * 16:(s8 + 1) * 16, :], in_=P["ssm_d"][l].rearrange("(g c) -> c g", c=16),
                          allow_slow_non_contiguous=True, writes=["dcol"])
                S.op("pool", "memset", maskBL[:], 1.0, writes=["maskBL"])
                S.op("pool", "affine_select", out=maskBL[:].rearrange("p (j c) -> p j c", c=16),
                     in_=maskBL[:].rearrange("p (j c) -> p j c", c=16), pattern=[[16, 8], [0, 16]],
                     compare_op=ALU.is_ge, fill=0.0, base=15, channel_multiplier=-1,
                     reads=["maskBL"], writes=["maskBL"])
                S.op("pool", "iota", kidx_i[:], pattern=[[1, NKc]], base=0, channel_multiplier=0,
                     writes=["kti"])
                S.op("pool", "tensor_copy", out=kidx[:], in_=kidx_i[:], reads=["kti"], writes=["kidx"])

                if int(os.environ.get("SSMSTOP", "9")) <= 1:
                    S.barrier()
                    return
                def V(eng, meth, *a, r=(), w=(), **kw):
                    S.op(eng, meth, *a, reads=list(r), writes=list(w), **kw)

                def sin_turns(dst, dname, y, yname, ti_, tf_, tt_, pre):
                    V("dve", "tensor_copy", out=ti_, in_=y, r=[yname], w=[pre + "ti"])
                    V("dve", "tensor_copy", out=tf_, in_=ti_, r=[pre + "ti"], w=[pre + "tf"])
                    V("dve", "tensor_tensor", out=tf_, in0=y, in1=tf_, op=ALU.subtract, r=[yname, pre + "tf"], w=[pre + "tf"])
                    V("dve", "tensor_single_scalar", out=tt_, in_=tf_, scalar=0.5, op=ALU.is_gt, r=[pre + "tf"], w=[pre + "tt"])
                    V("dve", "tensor_tensor", out=tf_, in0=tf_, in1=tt_, op=ALU.subtract, r=[pre + "tf", pre + "tt"], w=[pre + "tf"])
                    V("dve", "tensor_single_scalar", out=tt_, in_=tf_, scalar=-0.5, op=ALU.is_lt, r=[pre + "tf"], w=[pre + "tt"])
                    V("dve", "tensor_tensor", out=tf_, in0=tf_, in1=tt_, op=ALU.add, r=[pre + "tf", pre + "tt"], w=[pre + "tf"])
                    if dst is not None:
                        V("act", "activation", out=dst, in_=tf_, func=AF.Sin, scale=6.283185, r=[pre + "tf"], w=[dname])

                V("act", "activation", out=dtb[:], in_=dtb[:], func=AF.Exp, r=["dtb"], w=["dtb"])
                V("dve", "tensor_tensor", out=ar[:], in0=lr[:], in1=dtb[:], op=ALU.mult, r=["lr", "dtb"], w=["ar"])
                V("dve", "tensor_tensor", out=ft[:], in0=li[:], in1=dtb[:], op=ALU.mult, r=["li", "dtb"], w=["ft"])
                V("dve", "tensor_scalar_mul", out=ft[:], in0=ft[:], scalar1=1.0 / TWO_PI, r=["ft"], w=["ft"])
                for p in range(-7, 9):
                    V("act", "activation", out=mag[:], in_=ar[:], func=AF.Exp, scale=float(p), r=["ar"], w=["mag"])
                    if p == 8:
                        V("dve", "tensor_copy", out=rho8[:], in_=mag[:], r=["mag"], w=["rho8"])
                    V("dve", "tensor_scalar_mul", out=yp[:], in0=ft[:], scalar1=float(p), r=["ft"], w=["yp"])
                    sin_turns(sn[:], "sn", yp[:], "yp", ti[:], tf[:], tt[:], "a")
                    if p == 8:
                        V("dve", "tensor_copy", out=f8[:], in_=tf[:], r=["atf"], w=["f8"])
                    V("dve", "tensor_scalar_add", out=yp[:], in0=yp[:], scalar1=0.25, r=["yp"], w=["yp"])
                    sin_turns(cs[:], "cs", yp[:], "yp", ti[:], tf[:], tt[:], "a")
                    V("dve", "tensor_tensor", out=pwr[p][:], in0=mag[:], in1=cs[:], op=ALU.mult, r=["mag", "cs"], w=[("pwr", p)])
                    V("dve", "tensor_tensor", out=pwi[p][:], in0=mag[:], in1=sn[:], op=ALU.mult, r=["mag", "sn"], w=[("pwi", p)])
                if int(os.environ.get("SSMSTOP", "9")) <= 2:
                    S.barrier()
                    return
                V("dve", "tensor_scalar_add", out=t_a[:], in0=pwr[1][:], scalar1=-1.0, r=[("pwr", 1)], w=["ta"])
                V("dve", "tensor_tensor", out=cre[:], in0=t_a[:], in1=lr[:], op=ALU.mult, r=["ta", "lr"], w=["cre"])
                V("dve", "tensor_tensor", out=t_b[:], in0=pwi[1][:], in1=li[:], op=ALU.mult, r=[("pwi", 1), "li"], w=["tb"])
                V("dve", "tensor_tensor", out=cre[:], in0=cre[:], in1=t_b[:], op=ALU.add, r=["cre", "tb"], w=["cre"])
                V("dve", "tensor_tensor", out=cim[:], in0=pwi[1][:], in1=lr[:], op=ALU.mult, r=[("pwi", 1), "lr"], w=["cim"])
                V("dve", "tensor_tensor", out=t_b[:], in0=t_a[:], in1=li[:], op=ALU.mult, r=["ta", "li"], w=["tb"])
                V("dve", "tensor_tensor", out=cim[:], in0=cim[:], in1=t_b[:], op=ALU.subtract, r=["cim", "tb"], w=["cim"])
                V("dve", "tensor_tensor", out=t_a[:], in0=lr[:], in1=lr[:], op=ALU.mult, r=["lr"], w=["ta"])
                V("dve", "tensor_tensor", out=t_b[:], in0=li[:], in1=li[:], op=ALU.mult, r=["li"], w=["tb"])
                V("dve", "tensor_tensor", out=t_a[:], in0=t_a[:], in1=t_b[:], op=ALU.add, r=["ta", "tb"], w=["ta"])
                V("dve", "reciprocal", out=t_a[:], in_=t_a[:], r=["ta"], w=["ta"])
                V("dve", "tensor_tensor", out=cre[:], in0=cre[:], in1=t_a[:], op=ALU.mult, r=["cre", "ta"], w=["cre"])
                V("dve", "tensor_tensor", out=cim[:], in0=cim[:], in1=t_a[:], op=ALU.mult, r=["cim", "ta"], w=["cim"])

                def bc(t):
                    return t[:, :].unsqueeze(2).broadcast_to([128, 32, 16])

                def cplx(fr, fi, frn, fin, xr, xi, xrn, xin):
                    V("dve", "tensor_tensor", out=q1[:], in0=xr[:], in1=bc(fr), op=ALU.mult, r=[xrn, frn, "q1"], w=["q1"])
                    V("dve", "tensor_tensor", out=q3[:], in0=xi[:], in1=bc(fi), op=ALU.mult, r=[xin, fin, "q3"], w=["q3"])
                    V("dve", "tensor_tensor", out=q1[:], in0=q1[:], in1=q3[:], op=ALU.subtract, r=["q1", "q3"], w=["q1"])
                    V("dve", "tensor_tensor", out=q2[:], in0=xi[:], in1=bc(fr), op=ALU.mult, r=[xin, frn, "q2"], w=["q2"])
                    V("dve", "tensor_tensor", out=q4[:], in0=xr[:], in1=bc(fi), op=ALU.mult, r=[xrn, fin, "q4"], w=["q4"])
                    V("dve", "tensor_tensor", out=q2[:], in0=q2[:], in1=q4[:], op=ALU.add, r=["q2", "q4"], w=["q2"])

                def put(dst, dname, blk, lo_src, lo_sign, up_src, up_sign):
                    for (rows, src, sign, eng) in ((slice(0, 64), lo_src, lo_sign, "act"),
                                                   (slice(64, 128), up_src, up_sign, "pool")):
                        srcn = "q1" if src is q1 else "q2"
                        if eng == "act":
                            V("act", "activation", out=dst[rows, :, blk, :], in_=src[rows, :, :], func=AF.Copy,
                              scale=float(sign), r=[srcn], w=[dname])
                        else:
                            V("pool", "tensor_scalar", out=dst[rows, :, blk, :], in0=src[rows, :, :],
                              scalar1=float(sign), scalar2=None, op0=ALU.mult, r=[srcn], w=[dname])

                if int(os.environ.get("SSMSTOP", "9")) <= 3:
                    S.barrier()
                    return
                cplx(cre, cim, "cre", "cim", Bre, Bim, "Bre", "Bim")
                V("dve", "tensor_copy", out=bbr[:], in_=q1[:], r=["q1"], w=["bbr"])
                V("dve", "tensor_copy", out=bbi[:], in_=q2[:], r=["q2"], w=["bbi"])
                for i in range(2):
                    for t4 in range(4):
                        pb = pw[(i * 4 + t4) % 2]
                        V("pe", "transpose", out=pb[:, 0:128], in_=Craw[i][:, t4, :], identity=ident_f[:, :],
                          r=[("Craw", i), "ident_f"], w=[("pw", (i * 4 + t4) % 2)])
                        V("act", "activation", out=Cn[i][:, t4 * 8:(t4 + 1) * 8, :],
                          in_=pb[:, 0:128].rearrange("p (g c) -> p g c", c=16), func=AF.Copy,
                          r=[("pw", (i * 4 + t4) % 2)], w=[("Cn", i)])
                for s8 in range(8):
                    p = 7 - s8
                    cplx(pwr[p], pwi[p], ("pwr", p), ("pwi", p), bbr, bbi, "bbr", "bbi")
                    put(X1, "X1", s8, q1, 1, q2, 1)
                    put(X1s, "X1s", s8, q2, 1, q1, -1)
                    p = -s8
                    cplx(pwr[p], pwi[p], ("pwr", p), ("pwi", p), bbr, bbi, "bbr", "bbi")
                    put(X1p, "X1p", s8, q1, 1, q2, 1)
                    p = s8 + 1
                    cplx(pwr[p], pwi[p], ("pwr", p), ("pwi", p), Cn[0], Cn[1], ("Cn", 0), ("Cn", 1))
                    put(X2, "X2", s8, q1, 1, q2, -1)
                    put(X2s, "X2s", s8, q2, -1, q1, -1)
                    p = s8
                    cplx(pwr[p], pwi[p], ("pwr", p), ("pwi", p), Cn[0], Cn[1], ("Cn", 0), ("Cn", 1))
                    put(X2p, "X2p", s8, q1, 1, q2, -1)
                if int(os.environ.get("SSMSTOP", "9")) <= 4:
                    S.barrier()
                    return
                V("act", "activation", out=W2[:].rearrange("p g x -> p (g x)"),
                  in_=X2[:].rearrange("p g s c -> p (g s c)"), func=AF.Copy, r=["X2"], w=["W2"])
                V("dve", "tensor_copy", out=W2s[:].rearrange("p g x -> p (g x)"),
                  in_=X2s[:].rearrange("p g s c -> p (g s c)"), r=["X2s"], w=["W2s"])
                for g in range(NG):
                    b = g % 2
                    V("pe", "transpose", out=pw[b][:, 0:128], in_=X1[:, g].rearrange("p s c -> p (s c)"),
                      identity=ident_f[:, :], r=["X1", "ident_f"], w=[("pw", b)])
                    V("pe", "transpose", out=pw[b][:, 128:256], in_=X1s[:, g].rearrange("p s c -> p (s c)"),
                      identity=ident_f[:, :], r=["X1s", "ident_f"], w=[("pw", b)])
                    V("pe", "matmul", pw[b][:, 256:384], lhsT=X1p[:, g].rearrange("p s c -> p (s c)"),
                      rhs=X2p[:, g].rearrange("p s c -> p (s c)"), start=True, stop=True,
                      r=["X1p", "X2p"], w=[("pw", b)])
                    V("act", "activation", out=W1[:, g, :], in_=pw[b][:, 0:128], func=AF.Copy,
                      r=[("pw", b)], w=["W1"])
                    V("act", "activation", out=W1s[:, g, :], in_=pw[b][:, 128:256], func=AF.Copy,
                      r=[("pw", b)], w=["W1s"])
                    V("act", "activation", out=w3r[b][:], in_=pw[b][:, 256:384], func=AF.Copy,
                      r=[("pw", b)], w=[("w3r", b)])
                    V("dve", "tensor_tensor", out=w3t[:], in0=w3r[b][:], in1=maskBL[:], op=ALU.mult,
                      r=[("w3r", b), "maskBL"], w=["w3t"])
                    V("dve", "scalar_tensor_tensor", out=W3[:, g, :], in0=ident_f[:], scalar=dcol[:, g:g + 1],
                      in1=w3t[:], op0=ALU.mult, op1=ALU.add, r=["ident_f", "dcol", "w3t"], w=["W3"])

                if int(os.environ.get("SSMSTOP", "9")) <= 5:
                    S.barrier()
                    return
                S.barrier()
                stp.close()
                cT = [t32("cT%d" % i, (128, NKc)) for i in range(2)]
                sT = [t32("sT%d" % i, (128, NKc)) for i in range(2)]
                rtab = [t32("rtab%d" % i, (128, NKc)) for i in range(2)]
                yk = t32("yk", (128, NKc))
                kf = t32("kf", (128, NKc))
                kt = t32("kt", (128, NKc))
                Ut = [t32("U%d" % i, (128, NKc), BF16) for i in range(2)]
                m1, m2, Tt = t32("m1", (128, NKc)), t32("m2", (128, NKc)), t32("Tt", (128, NKc))
                Pm = [t32("Pm%d" % i, (128, NKc + 2), BF16) for i in range(2)]
                Qm = [t32("Qm%d" % i, (128, NKc + 2), BF16) for i in range(2)]
                yo = [t32("yo%d" % i, (128, NKc)) for i in range(2)]
                for i in range(2):
                    S.op("pool", "memset", Pm[i][:], 0.0, writes=[("Pm", i)])
                    S.op("pool", "memset", Qm[i][:], 0.0, writes=[("Qm", i)])
                it = 0
                for g in range(NG):
                    gb = g % 2
                    V("dve", "tensor_scalar_mul", out=yk[:], in0=kidx[:], scalar1=f8[:, g:g + 1],
                      r=["kidx", "f8"], w=["yk"])
                    sin_turns(sT[gb][:], ("sT", gb), yk[:], "yk", ki[:], kf[:], kt[:], "k")
                    V("dve", "tensor_scalar_add", out=yk[:], in0=yk[:], scalar1=0.25, r=["yk"], w=["yk"])
                    sin_turns(cT[gb][:], ("cT", gb), yk[:], "yk", ki[:], kf[:], kt[:], "k")
                    V("act", "activation", out=rtab[gb][:], in_=kidx[:], func=AF.Identity, scale=0.0,
                      bias=rho8[:, g:g + 1], r=["kidx", "rho8"], w=[("rtab", gb)])
                    RL = int(os.environ.get("RUNLVL", "9"))
                    for s in range(NS):
                        if RL < 2:
                            continue
                        ub = it % 2
                        it += 1
                        S.dma("sp", out=Ut[ub][:], in_=Usc[s, g].rearrange("s c k -> (s c) k"),
                              writes=[("U", ub)])
                        for hi, (lo, wd) in enumerate(halves):
                            V("pe", "matmul", pL[:, hi, 0:wd], lhsT=W1[:, g, :], rhs=Ut[ub][:, lo:lo + wd],
                              start=True, stop=True, r=["W1", ("U", ub)], w=["pL"])
                            V("pe", "matmul", pLs[:, hi, 0:wd], lhsT=W1s[:, g, :], rhs=Ut[ub][:, lo:lo + wd],
                              start=True, stop=True, r=["W1s", ("U", ub)], w=["pLs"])
                        if RL < 3:
                            continue
                        for hi, (lo, wd) in enumerate(halves):
                            V("dve", "tensor_tensor", out=m1[:, lo:lo + wd], in0=pL[:, hi, 0:wd],
                              in1=cT[gb][:, lo:lo + wd], op=ALU.mult, r=["pL", ("cT", gb)], w=["m1"])
                            V("dve", "tensor_tensor", out=m2[:, lo:lo + wd], in0=pLs[:, hi, 0:wd],
                              in1=sT[gb][:, lo:lo + wd], op=ALU.mult, r=["pLs", ("sT", gb)], w=["m2"])
                        V("dve", "tensor_tensor", out=m1[:], in0=m1[:], in1=m2[:], op=ALU.add,
                          r=["m1", "m2"], w=["m1"])
                        V("dve", "tensor_tensor_scan", out=Tt[:], data0=rtab[gb][:], data1=m1[:], initial=0.0,
                          op0=ALU.mult, op1=ALU.add, r=[("rtab", gb), "m1"], w=["Tt"])
                        V("dve", "tensor_tensor", out=Pm[ub][:, 1:NKc + 1], in0=Tt[:], in1=cT[gb][:], op=ALU.mult,
                          r=["Tt", ("cT", gb)], w=[("Pm", ub)])
                        V("pool", "tensor_tensor", out=Qm[ub][:, 1:NKc + 1], in0=Tt[:], in1=sT[gb][:], op=ALU.mult,
                          r=["Tt", ("sT", gb)], w=[("Qm", ub)])
                        if RL < 4:
                            continue
                        for hi, (lo, wd) in enumerate(halves):
                            V("pe", "matmul", py[:, hi, 0:wd], lhsT=W3[:, g, :], rhs=Ut[ub][:, lo:lo + wd],
                              start=True, stop=False, r=["W3", ("U", ub)], w=["py"])
                            V("pe", "matmul", py[:, hi, 0:wd], lhsT=W2[:, g, :],
                              rhs=Pm[ub][:, lo:lo + wd], start=False, stop=False,
                              r=["W2", ("Pm", ub)], w=["py"])
                            V("pe", "matmul", py[:, hi, 0:wd], lhsT=W2s[:, g, :],
                              rhs=Qm[ub][:, lo:lo + wd], start=False, stop=True,
                              r=["W2s", ("Qm", ub)], w=["py"])
                        if RL < 5:
                            continue
                        for hi, (lo, wd) in enumerate(halves):
                            V("act", "activation", out=yo[ub][:, lo:lo + wd], in_=py[:, hi, 0:wd], func=AF.Copy,
                              r=["py"], w=[("yo", ub)])
                        S.dma("sp", out=ysS[s, g].rearrange("s c k -> (s c) k"), in_=yo[ub][:],
                              reads=[("yo", ub)])
                S.barrier()

        def phase_mixB1(l):
            w_in = P["w_in"][l]
            c0 = gcol(1, l)
            HALO = CW - 1
            with ExitStack() as st:
                wC = sb("c_w", [128, DT, 1024], BF16, st)
                dg = sb("c_dg", [128, 4 * CW, 128], BF16, st)
                cw = sb("c_cw", [128, 4, CW], F32, st)
                cb = sb("c_cb", [128, 4], F32, st)
                lng = sb("c_lng", [128, 4], F32, st)
                lnb = sb("c_lnb", [128, 4], F32, st)
                eps_ln = sb("c_epsln", [128, 1], F32, st)
                hsb = [sb("c_hs%d" % i, [128, DT, 512], F32, st) for i in range(2)]
                xn = sb("c_xn", [128, DT, 512], BF16, st)
                sqb = [sb("c_sq%d" % i, [128, 512], BF16, st) for i in range(2)]
                rstd = sb("c_rstd", [128, 512], F32, st)
                hc = sb("c_hc", [128, 4, HALO + 512], BF16, st)
                cacc = sb("c_cacc", [128, 4, 512], F32, st)
                hcvb = [sb("c_hcv%d" % i, [128, 4, 512], BF16, st) for i in range(2)]
                sg = [sb("c_sg%d" % i, [128, 512], F32, st) for i in range(2)]
                tmq = sb("c_tm", [128, 512], F32, st)
                mu = sb("c_mu", [128, 512], F32, st)
                lrs = sb("c_lrs", [128, 512], F32, st)
                pss = ps("c_pss", [128, 512], F32, st)
                NPB = 6
                pp = [ps("c_pp%d" % i, [128, 512], F32, st) for i in range(NPB)]
                pcnt = [0]

                def nxt():
                    b = pcnt[0] % NPB
                    pcnt[0] += 1
                    return b

                for dt in range(DT):
                    load_w_cast(wC[:, dt, :], w_in[dt * 128:(dt + 1) * 128, 512:1536], ("wC", dt))
                wCr = [("wC", dt) for dt in range(DT)]
                for ct in range(4):
                    S.dma("sp", out=cw[:, ct, :], in_=P["conv_w"][l][:, ct * 128:(ct + 1) * 128].rearrange("j p -> p j"),
                          allow_slow_non_contiguous=True, writes=["cw"])
                for nm, tl in (("conv_b", cb), ("conv_ln_g", lng), ("conv_ln_b", lnb)):
                    S.dma("sp", out=tl[:], in_=P[nm][l].rearrange("(t p) -> p t", p=128),
                          allow_slow_non_contiguous=True, writes=[nm])
                S.op("pool", "memset", eps_ln[:], LN_EPS, writes=["eps_ln"])
                for ct in range(4):
                    for j in range(CW):
                        eng = "dve" if (j % 2 == 0) else "pool"
                        S.op(eng, "tensor_scalar", out=dg[:, ct * CW + j, :], in0=ident_f[:], scalar1=cw[:, ct, j:j + 1],
                             scalar2=None, op0=ALU.mult, reads=["ident_f", "cw"], writes=[("dg", ct)])
                tcnt = 0
                for s in range(NS):
                    for (t0, n) in cfg.tiles:
                        hs = hsb[tcnt % 2]
                        hcv = hcvb[tcnt % 2]
                        hsr, hcvr = ("hsb", tcnt % 2), ("hcvb", tcnt % 2)
                        tcnt += 1
                        S.dma("sp", out=hs[:, :, 0:n], in_=hT[s, :, t0:t0 + n].rearrange("(t p) n -> p t n", p=128),
                              writes=[hsr])
                        rmsnorm(hs, xn, sqb, pss, rstd, c0, n, hs_name=hsr)
                        if t0 == 0:
                            S.op("pool", "memset", hc[:, :, 0:HALO], 0.0, writes=["hc"])
                        for ct in range(4):
                            ba, bg = nxt(), nxt()
                            for dt in range(DT):
                                S.op("pe", "matmul", pp[ba][:, 0:n], lhsT=wC[:, dt, ct * 128:(ct + 1) * 128],
                                     rhs=xn[:, dt, 0:n], start=(dt == 0), stop=(dt == DT - 1),
                                     reads=wCr + ["xn"], writes=[("pp", ba)])
                            for dt in range(DT):
                                S.op("pe", "matmul", pp[bg][:, 0:n], lhsT=wC[:, dt, 512 + ct * 128:512 + (ct + 1) * 128],
                                     rhs=xn[:, dt, 0:n], start=(dt == 0), stop=(dt == DT - 1),
                                     reads=wCr + ["xn"], writes=[("pp", bg)])
                            S.op("act", "activation", out=sg[ct % 2][:, 0:n], in_=pp[bg][:, 0:n], func=AF.Sigmoid,
                                 reads=[("pp", bg)], writes=[("sg", ct % 2)])
                            S.op("dve", "tensor_tensor", out=hc[:, ct, HALO:HALO + n], in0=pp[ba][:, 0:n],
                                 in1=sg[ct % 2][:, 0:n], op=ALU.mult,
                                 reads=[("pp", ba), ("sg", ct % 2)], writes=["hc"])
                        for ct in range(4):
                            bc_ = nxt()
                            for j in range(CW):
                                S.op("pe", "matmul", pp[bc_][:, 0:n], lhsT=dg[:, ct * CW + j, :], rhs=hc[:, ct, j:j + n],
                                     start=(j == 0), stop=(j == CW - 1),
                                     reads=[("dg", ct), "hc"], writes=[("pp", bc_)])
                            S.op("act", "activation", out=cacc[:, ct, 0:n], in_=pp[bc_][:, 0:n], func=AF.Identity,
                                 bias=cb[:, ct:ct + 1], scale=1.0,
                                 reads=[("pp", bc_), "conv_b"], writes=[("cacc", ct)])
                        if n >= HALO:
                            S.op("pool", "tensor_copy", out=hc[:, :, 0:HALO], in_=hc[:, :, n:n + HALO],
                                 reads=["hc"], writes=["hc"])
                        bmu, bvar = nxt(), nxt()
                        for ct in range(4):
                            b2 = ct % 2
                            S.op("act", "activation", out=sqb[b2][:, 0:n], in_=cacc[:, ct, 0:n], func=AF.Copy,
                                 reads=[("cacc", ct)], writes=[("sqb", b2)])
                            S.op("pe", "matmul", pp[bmu][:, 0:n], lhsT=ones_bf[:], rhs=sqb[b2][:, 0:n],
                                 start=(ct == 0), stop=(ct == 3), reads=[("sqb", b2), "ones_bf"], writes=[("pp", bmu)])
                        for ct in range(4):
                            b2 = ct % 2
                            S.op("act", "activation", out=sqb[b2][:, 0:n], in_=cacc[:, ct, 0:n], func=AF.Square,
                                 reads=[("cacc", ct)], writes=[("sqb", b2)])
                            S.op("pe", "matmul", pp[bvar][:, 0:n], lhsT=ones_bf[:], rhs=sqb[b2][:, 0:n],
                                 start=(ct == 0), stop=(ct == 3), reads=[("sqb", b2), "ones_bf"], writes=[("pp", bvar)])
                        S.op("act", "activation", out=mu[:, 0:n], in_=pp[bmu][:, 0:n], func=AF.Copy, scale=1.0 / 512,
                             reads=[("pp", bmu)], writes=["mu"])
                        S.op("dve", "tensor_tensor", out=tmq[:, 0:n], in0=mu[:, 0:n], in1=mu[:, 0:n], op=ALU.mult,
                             reads=["mu"], writes=["tmq"])
                        S.op("dve", "scalar_tensor_tensor", out=lrs[:, 0:n], in0=pp[bvar][:, 0:n], scalar=1.0 / 512,
                             in1=tmq[:, 0:n], op0=ALU.mult, op1=ALU.subtract,
                             reads=[("pp", bvar), "tmq"], writes=["lrs"])
                        S.op("act", "activation", out=lrs[:, 0:n], in_=lrs[:, 0:n], func=AF.Sqrt, bias=eps_ln[:], scale=1.0,
                             reads=["lrs", "eps_ln"], writes=["lrs"])
                        S.op("dve", "reciprocal", out=lrs[:, 0:n], in_=lrs[:, 0:n], reads=["lrs"], writes=["lrs"])
                        for ct in range(4):
                            S.op("dve", "tensor_tensor", out=tmq[:, 0:n], in0=cacc[:, ct, 0:n], in1=mu[:, 0:n],
                                 op=ALU.subtract, reads=[("cacc", ct), "mu"], writes=["tmq"])
                            S.op("dve", "tensor_tensor", out=tmq[:, 0:n], in0=tmq[:, 0:n], in1=lrs[:, 0:n],
                                 op=ALU.mult, reads=["tmq", "lrs"], writes=["tmq"])
                            S.op("act", "activation", out=hcv[:, ct, 0:n], in_=tmq[:, 0:n], func=AF.Silu,
                                 scale=lng[:, ct:ct + 1], bias=lnb[:, ct:ct + 1],
                                 reads=["tmq", "conv_ln_g", "conv_ln_b"], writes=[hcvr])
                        S.dma("sp", out=hcvS[s, :, t0:t0 + n].rearrange("(t p) n -> p t n", p=128), in_=hcv[:, :, 0:n],
                              reads=[hcvr])
                S.barrier()

        def phase_mixB2(l):
            w_in = P["w_in"][l]
            c0 = gcol(1, l)
            HALO = CW - 1
            with ExitStack() as st:
                wB = sb("b_w", [128, DT, 3072], BF16, st)
                wglu = sb("b_wglu", [128, 4, 2048], BF16, st)
                wpw = sb("b_wpw", [128, 4, D], BF16, st)
                wo = sb("b_wo", [128, 4, D], BF16, st)
                wout = sb("b_wout", [128, DT, D], BF16, st)
                hs = sb("b_hs", [128, DT, 512], F32, st)
                xn = sb("b_xn", [128, DT, 512], BF16, st)
                sqb = [sb("b_sq%d" % i, [128, 512], BF16, st) for i in range(2)]
                rstd = sb("b_rstd", [128, 512], F32, st)
                hcv = sb("b_hcv", [128, 4, 512], BF16, st)
                yst = [sb("b_yst%d" % i, [128, 8, 64], F32, st) for i in range(2)]
                gy = sb("b_gy", [128, 4, 512], BF16, st)
                oat = sb("b_oat", [128, 4, 512], BF16, st)
                mrg = sb("b_mrg", [128, DT, 512], BF16, st)
                sg = [sb("b_sg%d" % i, [128, 512], F32, st) for i in range(2)]
                tm = [sb("b_tm%d" % i, [128, 512], F32, st) for i in range(3)]
                macc = sb("b_macc", [128, 512], F32, st)
                pss = ps("b_pss", [128, 512], F32, st)
                NPB = 6
                pp = [ps("b_pp%d" % i, [128, 512], F32, st) for i in range(NPB)]
                pcnt = [0]

                def nxt():
                    b = pcnt[0] % NPB
                    pcnt[0] += 1
                    return b

                for dt in range(DT):
                    load_w_cast(wB[:, dt, 0:1536], w_in[dt * 128:(dt + 1) * 128, 3072:4608], ("wB", dt))
                    load_w_cast(wB[:, dt, 1536:3072], w_in[dt * 128:(dt + 1) * 128, 4608:6144], ("wB", dt, 1))
                    load_w_cast(wout[:, dt, :], P["w_out"][l][dt * 128:(dt + 1) * 128, :], ("wout", dt))
                for ct in range(4):
                    load_w_cast(wglu[:, ct, :], P["ssm_w_glu"][l][ct * 128:(ct + 1) * 128, :], ("wglu", ct))
                    load_w_cast(wpw[:, ct, :], P["conv_w_out"][l][ct * 128:(ct + 1) * 128, :], ("wpw", ct))
                    load_w_cast(wo[:, ct, :], P["attn_w_o"][l][ct * 128:(ct + 1) * 128, :], ("wo", ct))
                wBr = [("wB", dt) for dt in range(DT)] + [("wB", dt, 1) for dt in range(DT)]
                ycnt = 0
                for s in range(NS):
                    for (t0, n) in cfg.tiles:
                        nk = n // 8
                        k0 = t0 // 8
                        load_h(hs, s, t0, n)
                        S.dma("sp", out=oat[:, :, 0:n],
                              in_=oattT[s, :, t0:t0 + n].rearrange("(t p) n -> p t n", p=128), writes=["oat"])
                        S.dma("sp", out=hcv[:, :, 0:n],
                              in_=hcvS[s, :, t0:t0 + n].rearrange("(t p) n -> p t n", p=128), writes=["hcv"])
                        rmsnorm(hs, xn, sqb, pss, rstd, c0, n)
                        for ct in range(4):
                            yb = ycnt % 2
                            ycnt += 1
                            yt = yst[yb]
                            for gl in range(8):
                                S.dma("sp", out=yt[gl * 16:(gl + 1) * 16, :, 0:nk],
                                      in_=ysS[s, ct * 8 + gl, :, :, k0:k0 + nk].rearrange("j c k -> c j k"),
                                      writes=[("yst", yb)])
                            yv = yt[:, :, 0:nk]
                            t1 = tm[0][:, 0:n].rearrange("p (j k) -> p j k", j=8)
                            t2 = tm[1][:, 0:n].rearrange("p (j k) -> p j k", j=8)
                            S.op("act", "activation", out=t1, in_=yv, func=AF.Square,
                                 reads=[("yst", yb)], writes=[("tm", 0)])
                            S.op("dve", "tensor_scalar", out=t1, in0=t1, scalar1=0.044715, scalar2=1.0,
                                 op0=ALU.mult, op1=ALU.add, reads=[("tm", 0)], writes=[("tm", 0)])
                            S.op("dve", "tensor_tensor", out=t1, in0=t1, in1=yv, op=ALU.mult,
                                 reads=[("tm", 0), ("yst", yb)], writes=[("tm", 0)])
                            S.op("act", "activation", out=t2, in_=t1, func=AF.Tanh, scale=0.7978845608,
                                 reads=[("tm", 0)], writes=[("tm", 1)])
                            S.op("dve", "tensor_scalar", out=t2, in0=t2, scalar1=0.5, scalar2=0.5,
                                 op0=ALU.mult, op1=ALU.add, reads=[("tm", 1)], writes=[("tm", 1)])
                            S.op("dve", "tensor_tensor", out=gy[:, ct, 0:n].rearrange("p (k j) -> p j k", j=8),
                                 in0=t2, in1=yv, op=ALU.mult,
                                 reads=[("tm", 1), ("yst", yb)], writes=["gy"])
                        for f in range(DT):
                            def gate(br):
                                bgt = nxt()
                                c1 = br * 1024 + f * 128
                                for dt in range(DT):
                                    S.op("pe", "matmul", pp[bgt][:, 0:n], lhsT=wB[:, dt, c1:c1 + 128],
                                         rhs=xn[:, dt, 0:n], start=(dt == 0), stop=(dt == DT - 1),
                                         reads=wBr + ["xn"], writes=[("pp", bgt)])
                                S.op("act", "activation", out=sg[br % 2][:, 0:n], in_=pp[bgt][:, 0:n], func=AF.Sigmoid,
                                     reads=[("pp", bgt)], writes=[("sg", br % 2)])
                                return sg[br % 2], ("sg", br % 2)
                            ba, bg = nxt(), nxt()
                            for ct in range(4):
                                S.op("pe", "matmul", pp[ba][:, 0:n], lhsT=wglu[:, ct, f * 128:(f + 1) * 128],
                                     rhs=gy[:, ct, 0:n], start=(ct == 0), stop=(ct == 3),
                                     reads=[("wglu", c) for c in range(4)] + ["gy"], writes=[("pp", ba)])
                            for ct in range(4):
                                S.op("pe", "matmul", pp[bg][:, 0:n], lhsT=wglu[:, ct, 1024 + f * 128:1024 + (f + 1) * 128],
                                     rhs=gy[:, ct, 0:n], start=(ct == 0), stop=(ct == 3),
                                     reads=[("wglu", c) for c in range(4)] + ["gy"], writes=[("pp", bg)])
                            S.op("act", "activation", out=tm[1][:, 0:n], in_=pp[bg][:, 0:n], func=AF.Sigmoid,
                                 reads=[("pp", bg)], writes=[("tm", 1)])
                            S.op("dve", "tensor_tensor", out=tm[0][:, 0:n], in0=pp[ba][:, 0:n], in1=tm[1][:, 0:n],
                                 op=ALU.mult, reads=[("pp", ba), ("tm", 1)], writes=[("tm", 0)])
                            g0, g0r = gate(0)
                            S.op("dve", "tensor_tensor", out=macc[:, 0:n], in0=tm[0][:, 0:n], in1=g0[:, 0:n],
                                 op=ALU.mult, reads=[("tm", 0), g0r], writes=["macc"])
                            bc_ = nxt()
                            for ct in range(4):
                                S.op("pe", "matmul", pp[bc_][:, 0:n], lhsT=wpw[:, ct, f * 128:(f + 1) * 128],
                                     rhs=hcv[:, ct, 0:n], start=(ct == 0), stop=(ct == 3),
                                     reads=[("wpw", c) for c in range(4)] + ["hcv"], writes=[("pp", bc_)])
                            g1, g1r = gate(1)
                            S.op("dve", "tensor_tensor", out=tm[0][:, 0:n], in0=pp[bc_][:, 0:n], in1=g1[:, 0:n],
                                 op=ALU.mult, reads=[("pp", bc_), g1r], writes=[("tm", 0)])
                            S.op("dve", "tensor_tensor", out=macc[:, 0:n], in0=macc[:, 0:n], in1=tm[0][:, 0:n],
                                 op=ALU.add, reads=["macc", ("tm", 0)], writes=["macc"])
                            bo_ = nxt()
                            for ct in range(4):
                                S.op("pe", "matmul", pp[bo_][:, 0:n], lhsT=wo[:, ct, f * 128:(f + 1) * 128],
                                     rhs=oat[:, ct, 0:n], start=(ct == 0), stop=(ct == 3),
                                     reads=[("wo", c) for c in range(4)] + ["oat"], writes=[("pp", bo_)])
                            g2, g2r = gate(2)
                            S.op("dve", "tensor_tensor", out=tm[0][:, 0:n], in0=pp[bo_][:, 0:n], in1=g2[:, 0:n],
                                 op=ALU.mult, reads=[("pp", bo_), g2r], writes=[("tm", 0)])
                            S.op("dve", "tensor_tensor", out=mrg[:, f, 0:n], in0=macc[:, 0:n], in1=tm[0][:, 0:n],
                                 op=ALU.add, reads=["macc", ("tm", 0)], writes=[("mrg", f)])
                        for o in range(DT):
                            bo_ = nxt()
                            for f in range(DT):
                                S.op("pe", "matmul", pp[bo_][:, 0:n], lhsT=wout[:, f, o * 128:(o + 1) * 128],
                                     rhs=mrg[:, f, 0:n], start=(f == 0), stop=(f == DT - 1),
                                     reads=[("wout", f), ("mrg", f)], writes=[("pp", bo_)])
                            S.op("dve", "tensor_tensor", out=hs[:, o, 0:n], in0=pp[bo_][:, 0:n], in1=hs[:, o, 0:n],
                                 op=ALU.add, reads=[("pp", bo_), "hs"], writes=["hs"])
                        store_h(hs, s, t0, n)
                S.barrier()

        phases = cfg.phases

        def on(name):
            return phases is None or name in phases

        phase_in()
        for l in range(DEPTH):
            if on("ffn1"):
                phase_ffn(l, 0)
            if on("mixa"):
                phase_mixA(l)
            if on("att"):
                phase_att(l)
            if on("ssm"):
                phase_ssm(l)
            if on("mixb"):
                phase_mixB1(l)
                phase_mixB2(l)
            if on("ffn2"):
                phase_ffn(l, 1)
        phase_out()

        with nc.Block() as block:
            S.emit(block)
    return nc


def kernel(**inputs):
    cfg = Cfg()
    nc = build(cfg)
    n = 8
    in_maps = []
    for c in range(n):
        m = {}
        for k, v in inputs.items():
            a = np.asarray(v)
            if k == "x":
                a = a[c * cfg.nseq:(c + 1) * cfg.nseq]
            m[k] = np.ascontiguousarray(a, dtype=np.float32)
        in_maps.append(m)
    res = run_bass_kernel_spmd(nc, in_maps, core_ids=list(range(n)))
    return np.concatenate([np.asarray(r["out"]) for r in res.results], axis=0).astype(np.float32)
```

```python
import math
import os
from contextlib import ExitStack

import numpy as np
import concourse.bass as bass
import concourse.mybir as mybir
from concourse.bass_utils import run_bass_kernel_spmd

F32 = mybir.dt.float32
BF16 = mybir.dt.bfloat16
I32 = mybir.dt.int32
AF = mybir.ActivationFunctionType
ALU = mybir.AluOpType

D = 1024
DT = 8
DFF = 2816
FT = 22
NMETA = 16
DIN = 6144
NG = 32
NST = 64
CW = 31
RMS_EPS = 1e-6
LN_EPS = 1e-5

ENGS = ["pe", "act", "dve", "pool", "sp"]


class Sched:
    def __init__(self, nc, es, n_dma_sems=24):
        self.nc = nc
        self.ops = {e: [] for e in ENGS}
        self.sem = {e: es.enter_context(nc.semaphore("sem_" + e)) for e in ENGS}
        self.cnt = {e: 0 for e in ENGS}
        self.dsem = [es.enter_context(nc.semaphore("semd%d" % i)) for i in range(n_dma_sems)]
        self.duse = [0] * n_dma_sems
        self.dnext = 0
        self.waited = {}
        self.lastw = {}
        self.readers = {}

    def _semof(self, key):
        if isinstance(key, str):
            return self.sem[key]
        return self.dsem[key[1]]

    def _deps(self, eng, reads, writes):
        toks = {}
        def add(t):
            k, v = t
            if toks.get(k, 0) < v:
                toks[k] = v
        for r in reads:
            if r in self.lastw:
                add(self.lastw[r])
        for w in writes:
            if w in self.lastw:
                add(self.lastw[w])
            for k, v in self.readers.get(w, {}).items():
                add((k, v))
        waits = []
        for k, v in toks.items():
            if k == "pe" and eng == "pe":
                continue
            if self.waited.get((eng, k), 0) >= v:
                continue
            self.waited[(eng, k)] = v
            waits.append((k, v))
        return waits

    def _record(self, tok, reads, writes):
        k, v = tok
        for r in reads:
            d = self.readers.setdefault(r, {})
            if d.get(k, 0) < v:
                d[k] = v
        for w in writes:
            self.lastw[w] = tok
            self.readers[w] = {}

    def op(self, eng, meth, *args, reads=(), writes=(), excl=(), **kw):
        fn = (meth, args, kw)
        if excl:
            reads = list(reads) + list(excl)
            writes = list(writes) + list(excl)
        waits = self._deps(eng, reads, writes)
        self.cnt[eng] += 1
        tok = (eng, self.cnt[eng])
        self.ops[eng].append((waits, fn, eng, 1))
        self._record(tok, reads, writes)

    def dma(self, eng, reads=(), writes=(), **kw):
        fn = ("dma_start", (), kw)
        waits = self._deps(eng, reads, writes)
        slot = self.dnext
        self.dnext = (self.dnext + 1) % len(self.dsem)
        key = ("d", slot)
        if self.duse[slot] > 0:
            v = 16 * self.duse[slot]
            if self.waited.get((eng, key), 0) < v:
                self.waited[(eng, key)] = v
                waits.append((key, v))
        self.duse[slot] += 1
        tok = (key, 16 * self.duse[slot])
        self.ops[eng].append((waits, fn, key, 16))
        self._record(tok, reads, writes)

    def barrier(self):
        for eng in ENGS:
            waits = []
            for k in ENGS:
                v = self.cnt[k]
                if v > 0 and k != eng and self.waited.get((eng, k), 0) < v:
                    self.waited[(eng, k)] = v
                    waits.append((k, v))
            if eng != "pe":
                v = self.cnt[eng]
                if v > 0 and self.waited.get((eng, eng), 0) < v:
                    self.waited[(eng, eng)] = v
                    waits.append((eng, v))
            for i, u in enumerate(self.duse):
                key = ("d", i)
                if u > 0 and self.waited.get((eng, key), 0) < 16 * u:
                    self.waited[(eng, key)] = 16 * u
                    waits.append((key, 16 * u))
            if waits:
                self.ops[eng].append((waits, None, None, 0))
        self.lastw = {}
        self.readers = {}

    def emit(self, block):
        def replay(name):
            def run(e):
                for waits, fn, inc_key, inc in self.ops[name]:
                    for k, v in waits:
                        e.wait_ge(self._semof(k), v)
                    if fn is not None:
                        meth, args, kw = fn
                        getattr(e, meth)(*args, **kw).then_inc(self._semof(inc_key), inc)
            return run
        block.tensor(replay("pe"))
        block.scalar(replay("act"))
        block.vector(replay("dve"))
        block.gpsimd(replay("pool"))
        block.sync(replay("sp"))


class Cfg:
    def __init__(self, seq=4096, nseq=2, depth=4, phases=None, dump=()):
        self.dump = dump
        self.seq = seq
        self.nseq = nseq
        self.depth = depth
        self.L = seq + NMETA
        tiles = []
        t = 0
        while t + 512 <= self.L:
            tiles.append((t, 512))
            t += 512
        if t < self.L:
            tiles.append((t, self.L - t))
        self.tiles = tiles
        self.nkb = (self.L + 127) // 128
        self.LP = self.nkb * 128
        self.NK = self.L // 8
        self.phases = phases


def build(cfg):
    nc = bass.Bass("TRN2", target_bir_lowering=False)
    L, NS, DEPTH = cfg.L, cfg.nseq, cfg.depth

    def din(name, shape):
        return nc.dram_tensor(name, list(shape), F32, kind="ExternalInput").ap()

    x = din("x", (NS, cfg.seq, D))
    meta = din("meta_tokens", (NMETA, D))
    P = {}
    for name, shape in [
        ("ffn1_norm", (DEPTH, D)), ("ffn1_w13", (DEPTH, D, 2 * DFF)), ("ffn1_w2", (DEPTH, DFF, D)),
        ("mix_norm", (DEPTH, D)), ("w_in", (DEPTH, D, DIN)),
        ("ssm_lam_re", (DEPTH, NG, NST)), ("ssm_lam_im", (DEPTH, NG, NST)), ("ssm_log_dt", (DEPTH, NG)),
        ("ssm_b_re", (DEPTH, NG, NST, 16)), ("ssm_b_im", (DEPTH, NG, NST, 16)),
        ("ssm_c_re", (DEPTH, NG, 16, NST)), ("ssm_c_im", (DEPTH, NG, 16, NST)),
        ("ssm_d", (DEPTH, 512)), ("ssm_w_glu", (DEPTH, 512, 2048)),
        ("conv_w", (DEPTH, CW, 512)), ("conv_b", (DEPTH, 512)),
        ("conv_ln_g", (DEPTH, 512)), ("conv_ln_b", (DEPTH, 512)),
        ("conv_w_out", (DEPTH, 512, D)), ("attn_w_o", (DEPTH, 512, D)), ("w_out", (DEPTH, D, D)),
        ("ffn2_norm", (DEPTH, D)), ("ffn2_w13", (DEPTH, D, 2 * DFF)), ("ffn2_w2", (DEPTH, DFF, D)),
        ("final_norm", (D,)),
    ]:
        P[name] = din(name, shape)
    out = nc.dram_tensor("out", [NS, cfg.seq, D], F32, kind="ExternalOutput").ap()

    def scratch(name, shape, dt):
        kind = "ExternalOutput" if name in cfg.dump else "Internal"
        return nc.dram_tensor(name, list(shape), dt, kind=kind).ap()

    hT = scratch("hT", (NS, D, L), F32)
    NK = cfg.NK
    Usc = scratch("Usc", (NS, NG, 8, 16, NK), BF16)
    qT = scratch("qT", (NS, 512, L), BF16)
    kT = scratch("kT", (NS, 512, L), BF16)
    vS = scratch("vS", (NS, L, 512), BF16)
    oattT = scratch("oattT", (NS, 512, L), BF16)
    ysS = scratch("ysS", (NS, NG, 8, 16, NK), F32)
    hcvS = scratch("hcvS", (NS, 512, L), BF16)

    with ExitStack() as es:
        S = Sched(nc, es)

        uid = [0]

        def sb(name, shape, dt, stack=es):
            uid[0] += 1
            return stack.enter_context(nc.sbuf_tensor("%s_%d" % (name, uid[0]), list(shape), dt))

        def ps(name, shape, dt=F32, stack=es):
            uid[0] += 1
            return stack.enter_context(nc.psum_tensor("%s_%d" % (name, uid[0]), list(shape), dt))

        ones_bf = sb("ones_bf", [128, 128], BF16)
        ident_f = sb("ident_f", [128, 128], F32)
        gcols = sb("gcols", [128, (3 * DEPTH + 1) * DT], F32)
        eps_rms = sb("eps_rms", [128, 1], F32)
        rn_tmp = [sb("rn_tmp%d" % i, [128, 512], F32) for i in range(2)]
        S.op("pool", "memset", ones_bf[:], 1.0, writes=["ones_bf"])
        S.op("pool", "memset", ident_f[:], 1.0, writes=["ident_f"])
        S.op("pool", "affine_select", out=ident_f[:], in_=ident_f[:], pattern=[[-1, 128]],
                                               compare_op=ALU.is_equal, fill=0.0, base=0,
                                               channel_multiplier=1,
             reads=["ident_f"], writes=["ident_f"])
        S.op("pool", "memset", eps_rms[:], RMS_EPS, writes=["eps_rms"])
        for wi, nm in enumerate(["ffn1_norm", "mix_norm", "ffn2_norm"]):
            for l in range(DEPTH):
                c0 = (wi * DEPTH + l) * DT
                S.dma("sp",
                    out=gcols[:, c0:c0 + DT], in_=P[nm][l].rearrange("(t p) -> p t", p=128),
                    allow_slow_non_contiguous=True, writes=["gcols"])
        cF = 3 * DEPTH * DT
        S.dma("sp", out=gcols[:, cF:cF + DT],
                                          in_=P["final_norm"].rearrange("(t p) -> p t", p=128),
                                          allow_slow_non_contiguous=True, writes=["gcols"])
        S.barrier()

        def gcol(which, l):
            c0 = (which * DEPTH + l) * DT if which < 3 else cF
            return c0

        def load_h(hs, s, t0, n):
            S.dma("sp",
                out=hs[:, :, 0:n], in_=hT[s, :, t0:t0 + n].rearrange("(t p) n -> p t n", p=128),
                writes=["hs"])

        def store_h(hs, s, t0, n):
            S.dma("sp",
                out=hT[s, :, t0:t0 + n].rearrange("(t p) n -> p t n", p=128), in_=hs[:, :, 0:n],
                reads=["hs"])

        def rmsnorm(hs, xn, sqb, pss, rstd, c0, n, xn_name="xn", hs_name="hs"):
            for dt in range(DT):
                b = dt % 2
                S.op("act", "activation", out=sqb[b][:, 0:n], in_=hs[:, dt, 0:n],
                                                                 func=AF.Square,
                     reads=[hs_name], writes=[("sqb", b)])
                S.op("pe", "matmul", pss[:, 0:n], lhsT=ones_bf[:], rhs=sqb[b][:, 0:n],
                                                           start=(dt == 0), stop=(dt == DT - 1),
                     reads=[("sqb", b), "ones_bf"], writes=["pss"])
            S.op("act", "activation", out=rstd[:, 0:n], in_=pss[:, 0:n], func=AF.Sqrt,
                                               scale=1.0 / D, bias=eps_rms[:],
                 reads=["pss"], writes=["rstd"])
            S.op("dve", "reciprocal", out=rstd[:, 0:n], in_=rstd[:, 0:n],
                 reads=["rstd"], writes=["rstd"])
            for dt in range(DT):
                b = dt % 2
                S.op("dve", "tensor_tensor", out=rn_tmp[b][:, 0:n], in0=hs[:, dt, 0:n], in1=rstd[:, 0:n],
                     op=ALU.mult, reads=[hs_name, "rstd"], writes=[("rn_tmp", b)])
                S.op("act", "activation", out=xn[:, dt, 0:n], in_=rn_tmp[b][:, 0:n], func=AF.Identity,
                     scale=gcols[:, c0 + dt:c0 + dt + 1],
                     reads=[("rn_tmp", b), "gcols"], writes=[xn_name])

        def load_w_cast(dst_ap, src_ap, res):
            S.dma("pool", out=dst_ap, in_=src_ap, max_dma_last_dim=8192,
                  writes=[res])

        def phase_in():
            with ExitStack() as st:
                hs = sb("in_hs", [128, DT, 512], F32, st)
                xt = [sb("in_xt%d" % i, [128, D], F32, st) for i in range(2)]
                pt = [ps("in_pt%d" % i, [128, 512], F32, st) for i in range(2)]
                blk = 0
                for s in range(NS):
                    for (t0, n) in cfg.tiles:
                        for j in range((n + 127) // 128):
                            tb = t0 + j * 128
                            nb = min(128, t0 + n - tb)
                            xb = xt[blk % 2]
                            xr = ("xt", blk % 2)
                            if tb == 0:
                                S.dma("sp", out=xb[0:NMETA, :], in_=meta[:, :],
                                      writes=[xr])
                                S.dma("sp",
                                    out=xb[NMETA:nb, :], in_=x[s, 0:nb - NMETA, :], writes=[(xr, 1)],
                                    reads=[])
                                rd = [xr, (xr, 1)]
                            else:
                                S.dma("sp",
                                    out=xb[0:nb, :], in_=x[s, tb - NMETA:tb - NMETA + nb, :], writes=[xr, (xr, 1)])
                                rd = [xr, (xr, 1)]
                            for half in range(2):
                                pp = pt[half]
                                for q in range(4):
                                    dt = half * 4 + q
                                    S.op("pe", "transpose",
                                        out=pp[:, q * 128:q * 128 + nb], in_=xb[0:nb, dt * 128:(dt + 1) * 128],
                                        identity=ident_f[0:nb, 0:nb],
                                        reads=rd + ["ident_f"], writes=[("pt", half)])
                                eng = "act" if half == 0 else "dve"
                                if eng == "act":
                                    S.op("act", "activation",
                                        out=hs[:, half * 4:half * 4 + 4, j * 128:j * 128 + nb],
                                        in_=pp[:].rearrange("p (q c) -> p q c", q=4)[:, :, 0:nb], func=AF.Copy,
                                        reads=[("pt", half)], writes=["hs"])
                                else:
                                    S.op("dve", "tensor_copy",
                                        out=hs[:, half * 4:half * 4 + 4, j * 128:j * 128 + nb],
                                        in_=pp[:].rearrange("p (q c) -> p q c", q=4)[:, :, 0:nb],
                                        reads=[("pt", half)], writes=["hs"])
                            blk += 1
                        store_h(hs, s, t0, n)
                S.barrier()

        def phase_out():
            with ExitStack() as st:
                hs = sb("o_hs", [128, DT, 512], F32, st)
                xn = sb("o_xn", [128, DT, 512], F32, st)
                sqb = [sb("o_sq%d" % i, [128, 512], BF16, st) for i in range(2)]
                rstd = sb("o_rstd", [128, 512], F32, st)
                ot = [sb("o_ot%d" % i, [128, D], F32, st) for i in range(2)]
                pss = ps("o_pss", [128, 512], F32, st)
                pt = [ps("o_pt%d" % i, [128, 512], F32, st) for i in range(2)]
                blk = 0
                for s in range(NS):
                    for (t0, n) in cfg.tiles:
                        load_h(hs, s, t0, n)
                        rmsnorm(hs, xn, sqb, pss, rstd, cF, n)
                        for j in range((n + 127) // 128):
                            tb = t0 + j * 128
                            nb = min(128, t0 + n - tb)
                            ob = ot[blk % 2]
                            orr = ("ot", blk % 2)
                            for half in range(2):
                                pp = pt[half]
                                for q in range(4):
                                    dt = half * 4 + q
                                    S.op("pe", "transpose",
                                        out=pp[0:nb, q * 128:(q + 1) * 128], in_=xn[:, dt, j * 128:j * 128 + nb],
                                        identity=ident_f[:, :],
                                        reads=["xn", "ident_f"], writes=[("pt", half)])
                                if half == 0:
                                    S.op("act", "activation",
                                        out=ob[0:nb, 0:512], in_=pp[0:nb, :], func=AF.Copy,
                                        reads=[("pt", half)], writes=[orr])
                                else:
                                    S.op("dve", "tensor_copy",
                                        out=ob[0:nb, 512:1024], in_=pp[0:nb, :],
                                        reads=[("pt", half)], writes=[(orr, 1)])
                            lo = NMETA if tb == 0 else 0
                            S.dma("sp",
                                out=out[s, tb + lo - NMETA:tb + nb - NMETA, :], in_=ob[lo:nb, :],
                                reads=[orr, (orr, 1)])
                            blk += 1
                S.barrier()

        def phase_ffn(l, which):
            pre = "ffn1" if which == 0 else "ffn2"
            w13 = P[pre + "_w13"][l]
            w2 = P[pre + "_w2"][l]
            c0 = gcol(0 if which == 0 else 2, l)
            with ExitStack() as st:
                w13s = sb("f_w13", [128, DT, 2 * DFF], BF16, st)
                w2s = sb("f_w2", [128, FT, D], BF16, st)
                hsb = [sb("f_hs%d" % i, [128, DT, 512], F32, st) for i in range(2)]
                xn = sb("f_xn", [128, DT, 512], BF16, st)
                sqb = [sb("f_sq%d" % i, [128, 512], BF16, st) for i in range(2)]
                rstd = sb("f_rstd", [128, 512], F32, st)
                gh = sb("f_g", [128, FT, 512], BF16, st)
                sa = [sb("f_sa%d" % i, [128, 512], F32, st) for i in range(2)]
                pss = ps("f_pss", [128, 512], F32, st)
                pa = [ps("f_pa%d" % i, [128, 512], F32, st) for i in range(2)]
                pb = [ps("f_pb%d" % i, [128, 512], F32, st) for i in range(2)]
                po = [ps("f_po%d" % i, [128, 512], F32, st) for i in range(2)]
                for dt in range(DT):
                    load_w_cast(w13s[:, dt, :], w13[dt * 128:(dt + 1) * 128, :], ("w13", dt))
                for f in range(FT):
                    load_w_cast(w2s[:, f, :], w2[f * 128:(f + 1) * 128, :], ("w2", f))
                tiles_all = [(s, t0, n) for s in range(NS) for (t0, n) in cfg.tiles]

                def ld(idx):
                    s_, t0_, n_ = tiles_all[idx]
                    S.dma("sp", out=hsb[idx % 2][:, :, 0:n_],
                          in_=hT[s_, :, t0_:t0_ + n_].rearrange("(t p) n -> p t n", p=128), writes=[("hsb", idx % 2)])

                def nrm(idx):
                    s_, t0_, n_ = tiles_all[idx]
                    rmsnorm(hsb[idx % 2], xn, sqb, pss, rstd, c0, n_, hs_name=("hsb", idx % 2))

                ld(0)
                nrm(0)
                for idx, (s, t0, n) in enumerate(tiles_all):
                    if True:
                        hs = hsb[idx % 2]
                        hsr = ("hsb", idx % 2)
                        if idx + 1 < len(tiles_all):
                            ld(idx + 1)
                        for f in range(FT):
                            b = f % 2
                            for dt in range(DT):
                                S.op("pe", "matmul",
                                    pa[b][:, 0:n], lhsT=w13s[:, dt, f * 128:(f + 1) * 128], rhs=xn[:, dt, 0:n],
                                    start=(dt == 0), stop=(dt == DT - 1),
                                    reads=[("w13", dt), "xn"], writes=[("pa", b)])
                            for dt in range(DT):
                                S.op("pe", "matmul",
                                    pb[b][:, 0:n], lhsT=w13s[:, dt, DFF + f * 128:DFF + (f + 1) * 128],
                                    rhs=xn[:, dt, 0:n], start=(dt == 0), stop=(dt == DT - 1),
                                    reads=[("w13", dt), "xn"], writes=[("pb", b)])
                            S.op("act", "activation", out=sa[b][:, 0:n], in_=pa[b][:, 0:n],
                                                                    func=AF.Silu,
                                 reads=[("pa", b)], writes=[("sa", b)])
                            S.op("dve", "tensor_tensor",
                                out=gh[:, f, 0:n], in0=pb[b][:, 0:n], in1=sa[b][:, 0:n], op=ALU.mult,
                                reads=[("pb", b), ("sa", b)], writes=[("gh", f)])
                        if idx + 1 < len(tiles_all):
                            nrm(idx + 1)
                        for o in range(DT):
                            b = o % 2
                            for f in range(FT):
                                S.op("pe", "matmul",
                                    po[b][:, 0:n], lhsT=w2s[:, f, o * 128:(o + 1) * 128], rhs=gh[:, f, 0:n],
                                    start=(f == 0), stop=(f == FT - 1),
                                    reads=[("w2", f), ("gh", f)], writes=[("po", b)])
                            import os
                            if os.environ.get("DBG") == "po":
                                S.op("dve", "tensor_copy", out=hs[:, o, 0:n], in_=po[b][:, 0:n],
                                     reads=[("po", b), "hs"], writes=["hs"])
                            elif os.environ.get("DBG") == "xn":
                                S.op("dve", "tensor_copy", out=hs[:, o, 0:n], in_=xn[:, o, 0:n],
                                     reads=[("po", b), "hs", "xn"], writes=["hs"])
                            elif os.environ.get("DBG") == "gh":
                                S.op("dve", "tensor_copy", out=hs[:, o, 0:n], in_=gh[:, o, 0:n],
                                     reads=[("po", b), "hs", "xn", ("gh", o)], writes=["hs"])
                            else:
                              S.op("act", "activation", out=sa[b][:, 0:n], in_=po[b][:, 0:n], func=AF.Copy, scale=0.5,
                                   reads=[("po", b)], writes=[("sa", b)])
                              S.op("dve", "tensor_tensor", out=hs[:, o, 0:n], in0=sa[b][:, 0:n], in1=hs[:, o, 0:n],
                                   op=ALU.add, reads=[("sa", b), hsr], writes=[hsr])
                        S.dma("sp", out=hT[s, :, t0:t0 + n].rearrange("(t p) n -> p t n", p=128), in_=hs[:, :, 0:n],
                              reads=[hsr])
                S.barrier()


        def phase_mixA(l):
            w_in = P["w_in"][l]
            c0 = gcol(1, l)
            with ExitStack() as st:
                wA = sb("a_w", [128, DT, 2048], BF16, st)
                hs = sb("a_hs", [128, DT, 512], F32, st)
                xn = sb("a_xn", [128, DT, 512], BF16, st)
                sqb = [sb("a_sq%d" % i, [128, 512], BF16, st) for i in range(2)]
                rstd = sb("a_rstd", [128, 512], F32, st)
                stg = [sb("a_stg%d" % i, [128, 512], BF16, st) for i in range(4)]
                pss = ps("a_pss", [128, 512], F32, st)
                pp = [ps("a_pp%d" % i, [128, 512], F32, st) for i in range(4)]
                for dt in range(DT):
                    load_w_cast(wA[:, dt, 0:512], w_in[dt * 128:(dt + 1) * 128, 0:512], ("wA", dt))
                    load_w_cast(wA[:, dt, 512:2048], w_in[dt * 128:(dt + 1) * 128, 1536:3072], ("wA", dt, 1))
                cnt = 0
                for s in range(NS):
                    for (t0, n) in cfg.tiles:
                        load_h(hs, s, t0, n)
                        rmsnorm(hs, xn, sqb, pss, rstd, c0, n)
                        nk = n // 8
                        k0 = t0 // 8
                        for f in range(12):
                            b = cnt % 4
                            cnt += 1
                            pb = pp[b]
                            sg = stg[b]
                            for dt in range(DT):
                                S.op("pe", "matmul", pb[:, 0:n], lhsT=wA[:, dt, f * 128:(f + 1) * 128],
                                     rhs=xn[:, dt, 0:n], start=(dt == 0), stop=(dt == DT - 1),
                                     reads=[("wA", dt), ("wA", dt, 1), "xn"], writes=[("pp", b)])
                            if f < 4:
                                S.op("act", "activation",
                                     out=sg[:, 0:n].rearrange("p (s k) -> p s k", s=8),
                                     in_=pb[:, 0:n].rearrange("p (k s) -> p s k", s=8), func=AF.Copy,
                                     reads=[("pp", b)], writes=[("stg", b)])
                                for gl in range(8):
                                    S.dma("sp", out=Usc[s, f * 8 + gl, :, :, k0:k0 + nk].rearrange("s c k -> c s k"),
                                          in_=sg[gl * 16:(gl + 1) * 16, 0:n].rearrange("p (s k) -> p s k", s=8),
                                          reads=[("stg", b)])
                            else:
                                if f % 2 == 0:
                                    S.op("act", "activation", out=sg[:, 0:n], in_=pb[:, 0:n], func=AF.Copy,
                                         reads=[("pp", b)], writes=[("stg", b)])
                                else:
                                    S.op("dve", "tensor_copy", out=sg[:, 0:n], in_=pb[:, 0:n],
                                         reads=[("pp", b)], writes=[("stg", b)])
                                dst = qT if f < 8 else kT
                                r0 = (f % 4) * 128
                                S.dma("sp", out=dst[s, r0:r0 + 128, t0:t0 + n], in_=sg[:, 0:n],
                                      reads=[("stg", b)])
                        for j in range((n + 127) // 128):
                            nb = min(128, n - j * 128)
                            b = cnt % 4
                            cnt += 1
                            pb = pp[b]
                            sg = stg[b]
                            for dt in range(DT):
                                S.op("pe", "matmul", pb[0:nb, 0:512], lhsT=xn[:, dt, j * 128:j * 128 + nb],
                                     rhs=wA[:, dt, 1536:2048], start=(dt == 0), stop=(dt == DT - 1),
                                     reads=[("wA", dt), ("wA", dt, 1), "xn"], writes=[("pp", b)])
                            S.op("dve", "tensor_copy", out=sg[0:nb, :], in_=pb[0:nb, :],
                                 reads=[("pp", b)], writes=[("stg", b)])
                            S.dma("sp", out=vS[s, t0 + j * 128:t0 + j * 128 + nb, :], in_=sg[0:nb, :],
                                  reads=[("stg", b)])
                S.barrier()

        def phase_att(l):
            nkb, LP = cfg.nkb, cfg.LP
            tail = L - (nkb - 1) * 128
            with ExitStack() as st:
                triT = sb("t_tri", [128, 128], BF16, st)
                sel0 = sb("t_sel0", [128, 128], BF16, st)
                onec = sb("t_onec", [128, 1], F32, st)
                masks = [sb("t_mask%d" % i, [128, 512], F32, st) for i in range(4)]
                kT2 = [sb("t_k%d" % i, [128, LP], BF16, st) for i in range(2)]
                qz = [[sb("t_q%d_%d" % (i, h), [128, L], BF16, st) for h in range(2)] for i in range(2)]
                vz = [[sb("t_v%d_%d" % (i, h), [128, nkb, 128], BF16, st) for h in range(2)] for i in range(2)]
                NB3 = 4
                e_t = [sb("t_e%d" % i, [128, 512], F32, st) for i in range(NB3)]
                sp_t = [sb("t_sp%d" % i, [128, 512], BF16, st) for i in range(NB3)]
                g_t = [sb("t_g%d" % i, [128, 512], F32, st) for i in range(NB3)]
                w_t = [sb("t_w%d" % i, [128, 512], BF16, st) for i in range(NB3)]
                Rs = [sb("t_Rs%d" % i, [128, 512], BF16, st) for i in range(2)]
                ob = [sb("t_ob%d" % i, [128, 512], BF16, st) for i in range(2)]
                pz = [ps("t_pz%d" % i, [128, 512], F32, st) for i in range(2)]
                pcs = [ps("t_pcs%d" % i, [128, 512], F32, st) for i in range(2)]
                po = [ps("t_po%d" % i, [128, 512], F32, st) for i in range(2)]
                S.op("pool", "memset", triT[:], 1.0, writes=["triT"])
                S.op("pool", "affine_select", out=triT[:], in_=triT[:], pattern=[[-1, 128]],
                     compare_op=ALU.is_ge, fill=0.0, base=0, channel_multiplier=1,
                     reads=["triT"], writes=["triT"])
                S.op("pool", "memset", sel0[:], 1.0, writes=["sel0"])
                S.op("pool", "affine_select", out=sel0[:], in_=sel0[:], pattern=[[0, 128]],
                     compare_op=ALU.is_equal, fill=0.0, base=0, channel_multiplier=1,
                     reads=["sel0"], writes=["sel0"])
                S.op("pool", "memset", onec[:], 1.0, writes=["onec"])
                for i in range(4):
                    S.op("pool", "memset", masks[i][:], 1.0, writes=[("mask", i)])
                    S.op("pool", "affine_select", out=masks[i][:], in_=masks[i][:], pattern=[[1, 512]],
                         compare_op=ALU.is_gt, fill=0.0, base=-128 * i, channel_multiplier=-1,
                         reads=[("mask", i)], writes=[("mask", i)])
                for i in range(2):
                    S.op("pool", "memset", kT2[i][:], 0.0, writes=[("kT2", i)])
                    for h in range(2):
                        S.op("pool", "memset", qz[i][h][:], 0.0, writes=[("qz", i, h)])
                        S.op("pool", "memset", vz[i][h][:], 0.0, writes=[("vz", i, h), ("vz", i, h, 1)])
                blocks = []
                it = 0
                qcnt = 0
                for s in range(NS):
                    for hp in range(4):
                        bb = it % 2
                        it += 1
                        first_of_load = True
                        for (q0, nq) in cfg.tiles:
                            qb = qcnt % 2
                            qcnt += 1
                            nblk = max((q0 + nq - 1 + 127) // 128, 1)
                            for bi, kb in enumerate(reversed(range(nblk))):
                                for h in range(2):
                                    blocks.append(dict(
                                        s=s, hp=hp, bb=bb, q0=q0, nq=nq, qb=qb, h=h, bi=bi, kb=kb, nblk=nblk,
                                        load=first_of_load, first_pv=(h == 0 and bi == 0),
                                        last_pv=(h == 1 and bi == nblk - 1)))
                                    first_of_load = False
                for i, B_ in enumerate(blocks):
                    B_["i3"] = i % NB3
                    B_["i2"] = i % 2

                def load_inputs(B_):
                    s, hp, bb = B_["s"], B_["hp"], B_["bb"]
                    S.dma("sp", out=kT2[bb][:, 0:L], in_=kT[s, hp * 128:(hp + 1) * 128, :],
                          reads=[], writes=[("kT2", bb)])
                    for h in range(2):
                        r0 = hp * 128 + h * 64
                        S.dma("sp", out=qz[bb][h][h * 64:(h + 1) * 64, :], in_=qT[s, r0:r0 + 64, :],
                              writes=[("qz", bb, h)])
                        if nkb > 1:
                            S.dma("sp", out=vz[bb][h][:, 0:nkb - 1, h * 64:(h + 1) * 64],
                                  in_=vS[s, 0:(nkb - 1) * 128, r0:r0 + 64].rearrange("(b p) d -> p b d", p=128),
                                  writes=[("vz", bb, h)])
                        S.dma("sp", out=vz[bb][h][0:tail, nkb - 1, h * 64:(h + 1) * 64],
                              in_=vS[s, (nkb - 1) * 128:L, r0:r0 + 64],
                              writes=[("vz", bb, h, 1)])

                def stageA(B_):
                    bb, h, q0, nq, kb, i3, i2 = (B_[k] for k in ("bb", "h", "q0", "nq", "kb", "i3", "i2"))
                    if B_["load"]:
                        load_inputs(B_)
                    diag = (kb * 128 + 128 > q0)
                    S.op("pe", "matmul", pz[i2][:, 0:nq], lhsT=kT2[bb][:, kb * 128:(kb + 1) * 128],
                         rhs=qz[bb][h][:, q0:q0 + nq], start=True, stop=True,
                         reads=[("kT2", bb), ("qz", bb, h)], writes=[("pz", i2)])
                    S.op("act", "activation", out=e_t[i3][:, 0:nq], in_=pz[i2][:, 0:nq],
                         func=AF.Exp, scale=0.125,
                         reads=[("pz", i2)], writes=[("e", i3)])
                    if diag:
                        mi = (kb * 128 - q0) // 128
                        assert 0 <= mi < 4
                        S.op("dve", "tensor_tensor", out=e_t[i3][:, 0:nq], in0=e_t[i3][:, 0:nq],
                             in1=masks[mi][:, 0:nq], op=ALU.mult,
                             reads=[("e", i3), ("mask", mi)], writes=[("e", i3)])
                    S.op("act", "activation", out=sp_t[i3][:, 0:nq], in_=e_t[i3][:, 0:nq],
                         func=AF.Ln, bias=onec[:], scale=1.0,
                         reads=[("e", i3), "onec"], writes=[("sp", i3)])

                def stageB(B_):
                    nq, bi, nblk, i3, i2, hh = (B_[k] for k in ("nq", "bi", "nblk", "i3", "i2", "h"))
                    S.op("pe", "matmul", pcs[i2][:, 0:nq], lhsT=triT[:], rhs=sp_t[i3][:, 0:nq],
                         start=True, stop=(bi == 0),
                         reads=["triT", ("sp", i3)], writes=[("pcs", i2)])
                    if bi > 0:
                        S.op("pe", "matmul", pcs[i2][:, 0:nq], lhsT=ones_bf[:], rhs=Rs[hh][:, 0:nq],
                             start=False, stop=True,
                             reads=["ones_bf", ("Rs", hh)], writes=[("pcs", i2)])
                    if bi < nblk - 1:
                        if bi == 0:
                            S.op("pool", "tensor_copy", out=Rs[hh][:, 0:nq], in_=sp_t[i3][:, 0:nq],
                                 reads=[("sp", i3)], writes=[("Rs", hh)])
                        else:
                            S.op("pool", "tensor_tensor", out=Rs[hh][:, 0:nq], in0=Rs[hh][:, 0:nq],
                                 in1=sp_t[i3][:, 0:nq], op=ALU.add,
                                 reads=[("sp", i3), ("Rs", hh)], writes=[("Rs", hh)])
                    S.op("act", "activation", out=g_t[i3][:, 0:nq], in_=pcs[i2][:, 0:nq],
                         func=AF.Exp, scale=-1.0,
                         reads=[("pcs", i2)], writes=[("g", i3)])
                    S.op("dve", "tensor_tensor", out=w_t[i3][:, 0:nq], in0=e_t[i3][:, 0:nq],
                         in1=g_t[i3][:, 0:nq], op=ALU.mult,
                         reads=[("e", i3), ("g", i3)], writes=[("w", i3)])

                def stageC(B_):
                    s, hp, bb, h, q0, nq, kb, i3, qb = (B_[k] for k in ("s", "hp", "bb", "h", "q0", "nq", "kb", "i3", "qb"))
                    S.op("pe", "matmul", po[qb][:, 0:nq], lhsT=vz[bb][h][:, kb, :],
                         rhs=w_t[i3][:, 0:nq], start=B_["first_pv"], stop=B_["last_pv"],
                         reads=[("vz", bb, h), ("vz", bb, h, 1), ("w", i3)], writes=[("po", qb)])
                    if B_["last_pv"]:
                        S.op("act", "activation", out=ob[qb][:, 0:nq], in_=po[qb][:, 0:nq], func=AF.Copy,
                             reads=[("po", qb)], writes=[("ob", qb)])
                        S.dma("sp", out=oattT[s, hp * 128:(hp + 1) * 128, q0:q0 + nq], in_=ob[qb][:, 0:nq],
                              reads=[("ob", qb)])

                NBLK = len(blocks)
                for i in range(NBLK + 2):
                    if i < NBLK:
                        stageA(blocks[i])
                    if 0 <= i - 1 < NBLK:
                        stageB(blocks[i - 1])
                    if 0 <= i - 2 < NBLK:
                        stageC(blocks[i - 2])
                S.barrier()

        def phase_ssm(l):
            NKc = cfg.NK
            if NKc <= 512:
                halves = [(0, NKc)]
            else:
                halves = [(0, NKc // 2), (NKc // 2, NKc - NKc // 2)]
            TWO_PI = 2.0 * math.pi
            with ExitStack() as st:
                stp = ExitStack()

                def t32(name, shape=(128, 32), dt=F32, stack=None):
                    return sb("s_" + name, list(shape), dt, st if stack is None else stack)

                def t32p(name, shape=(128, 32), dt=F32):
                    return t32(name, shape, dt, stp)
                W1, W1s, W2, W2s, W3 = (t32("W%d" % i, (128, 32, 128), BF16) for i in range(5))
                rho8, f8 = t32("rho8"), t32("f8")
                kidx = t32("kidx", (128, NKc))
                ki = t32("ki", (128, NKc), I32)
                kidx_i = ki
                pw = [ps("s_pw%d" % i, [128, 512], F32, st) for i in range(2)]
                pL = ps("s_pL", [128, 2, 512], F32, st)
                pLs = ps("s_pLs", [128, 2, 512], F32, st)
                py = ps("s_py", [128, 2, 512], F32, st)
                lr, li, dtb, ar, ft = t32p("lr"), t32p("li"), t32p("dtb"), t32p("ar"), t32p("ft")
                yp, ti, tf, tt, sn, cs, mag = (t32p("yp"), t32p("ti", dt=I32), t32p("tf"), t32p("tt"),
                                               t32p("sn"), t32p("cs"), t32p("mag"))
                pwr = {p: t32p("pwr%d" % (p + 7)) for p in range(-7, 9)}
                pwi = {p: t32p("pwi%d" % (p + 7)) for p in range(-7, 9)}
                cre, cim, t_a, t_b, dcol = (t32p("cre"), t32p("cim"), t32p("ta"), t32p("tb"), t32p("dcol"))
                Bre, Bim = t32p("Bre", (128, 32, 16)), t32p("Bim", (128, 32, 16))
                bbr, bbi = t32p("bbr", (128, 32, 16)), t32p("bbi", (128, 32, 16))
                Craw = [t32p("Craw%d" % i, (128, 4, 128)) for i in range(2)]
                Cn = [t32p("Cn%d" % i, (128, 32, 16)) for i in range(2)]
                q1, q2, q3, q4 = (t32p("q%d" % i, (128, 32, 16)) for i in range(4))
                X1, X1s, X2, X2s, X1p, X2p = (t32p("X%d" % i, (128, 32, 8, 16)) for i in range(6))
                maskBL = t32p("maskBL", (128, 128))
                w3t = t32p("w3t", (128, 128))
                w3r = [t32p("w3r%d" % i, (128, 128)) for i in range(2)]

                for hf in (0, 64):
                    S.dma("sp", out=lr[hf:hf + 64, :], in_=P["ssm_lam_re"][l].rearrange("g n -> n g"),
                          allow_slow_non_contiguous=True, writes=["lr"])
                    S.dma("sp", out=li[hf:hf + 64, :], in_=P["ssm_lam_im"][l].rearrange("g n -> n g"),
                          allow_slow_non_contiguous=True, writes=["li"])
                    S.dma("sp", out=dtb[hf:hf + 64, :], in_=P["ssm_log_dt"][l].partition_broadcast(64),
                          writes=["dtb"])
                    S.dma("sp", out=Bre[hf:hf + 64, :, :], in_=P["ssm_b_re"][l].rearrange("g n c -> n g c"),
                          writes=["Bre"])
                    S.dma("sp", out=Bim[hf:hf + 64, :, :], in_=P["ssm_b_im"][l].rearrange("g n c -> n g c"),
                          writes=["Bim"])
                for i, nm in enumerate(["ssm_c_re", "ssm_c_im"]):
                    for dup in range(2):
                        S.dma("sp", out=Craw[i][:, :, dup * 64:(dup + 1) * 64],
                              in_=P[nm][l].rearrange("g c n -> (g c) n").rearrange("(t p) n -> p t n", p=128),
                              writes=[("Craw", i)])
                for s8 in range(8):
                    S.dma("sp", out=dcol[s8 * 16:(s8 + 1) * 16, :], in_=P["ssm_d"][l].rearrange("(g c) -> c g", c=16),
                          allow_slow_non_contiguous=True, writes=["dcol"])
                S.op("pool", "memset", maskBL[:], 1.0, writes=["maskBL"])
                S.op("pool", "affine_select", out=maskBL[:].rearrange("p (j c) -> p j c", c=16),
                     in_=maskBL[:].rearrange("p (j c) -> p j c", c=16), pattern=[[16, 8], [0, 16]],
                     compare_op=ALU.is_ge, fill=0.0, base=15, channel_multiplier=-1,
                     reads=["maskBL"], writes=["maskBL"])
                S.op("pool", "iota", kidx_i[:], pattern=[[1, NKc]], base=0, channel_multiplier=0,
                     writes=["kti"])
                S.op("pool", "tensor_copy", out=kidx[:], in_=kidx_i[:], reads=["kti"], writes=["kidx"])

                if int(os.environ.get("SSMSTOP", "9")) <= 1:
                    S.barrier()
                    return
                def V(eng, meth, *a, r=(), w=(), **kw):
                    S.op(eng, meth, *a, reads=list(r), writes=list(w), **kw)

                def sin_turns(dst, dname, y, yname, ti_, tf_, tt_, pre):
                    V("dve", "tensor_copy", out=ti_, in_=y, r=[yname], w=[pre + "ti"])
                    V("dve", "tensor_copy", out=tf_, in_=ti_, r=[pre + "ti"], w=[pre + "tf"])
                    V("dve", "tensor_tensor", out=tf_, in0=y, in1=tf_, op=ALU.subtract, r=[yname, pre + "tf"], w=[pre + "tf"])
                    V("dve", "tensor_single_scalar", out=tt_, in_=tf_, scalar=0.5, op=ALU.is_gt, r=[pre + "tf"], w=[pre + "tt"])
                    V("dve", "tensor_tensor", out=tf_, in0=tf_, in1=tt_, op=ALU.subtract, r=[pre + "tf", pre + "tt"], w=[pre + "tf"])
                    V("dve", "tensor_single_scalar", out=tt_, in_=tf_, scalar=-0.5, op=ALU.is_lt, r=[pre + "tf"], w=[pre + "tt"])
                    V("dve", "tensor_tensor", out=tf_, in0=tf_, in1=tt_, op=ALU.add, r=[pre + "tf", pre + "tt"], w=[pre + "tf"])
                    if dst is not None:
                        V("act", "activation", out=dst, in_=tf_, func=AF.Sin, scale=6.283185, r=[pre + "tf"], w=[dname])

                V("act", "activation", out=dtb[:], in_=dtb[:], func=AF.Exp, r=["dtb"], w=["dtb"])
                V("dve", "tensor_tensor", out=ar[:], in0=lr[:], in1=dtb[:], op=ALU.mult, r=["lr", "dtb"], w=["ar"])
                V("dve", "tensor_tensor", out=ft[:], in0=li[:], in1=dtb[:], op=ALU.mult, r=["li", "dtb"], w=["ft"])
                V("dve", "tensor_scalar_mul", out=ft[:], in0=ft[:], scalar1=1.0 / TWO_PI, r=["ft"], w=["ft"])
                for p in range(-7, 9):
                    V("act", "activation", out=mag[:], in_=ar[:], func=AF.Exp, scale=float(p), r=["ar"], w=["mag"])
                    if p == 8:
                        V("dve", "tensor_copy", out=rho8[:], in_=mag[:], r=["mag"], w=["rho8"])
                    V("dve", "tensor_scalar_mul", out=yp[:], in0=ft[:], scalar1=float(p), r=["ft"], w=["yp"])
                    sin_turns(sn[:], "sn", yp[:], "yp", ti[:], tf[:], tt[:], "a")
                    if p == 8:
                        V("dve", "tensor_copy", out=f8[:], in_=tf[:], r=["atf"], w=["f8"])
                    V("dve", "tensor_scalar_add", out=yp[:], in0=yp[:], scalar1=0.25, r=["yp"], w=["yp"])
                    sin_turns(cs[:], "cs", yp[:], "yp", ti[:], tf[:], tt[:], "a")
                    V("dve", "tensor_tensor", out=pwr[p][:], in0=mag[:], in1=cs[:], op=ALU.mult, r=["mag", "cs"], w=[("pwr", p)])
                    V("dve", "tensor_tensor", out=pwi[p][:], in0=mag[:], in1=sn[:], op=ALU.mult, r=["mag", "sn"], w=[("pwi", p)])
                if int(os.environ.get("SSMSTOP", "9")) <= 2:
                    S.barrier()
                    return
                V("dve", "tensor_scalar_add", out=t_a[:], in0=pwr[1][:], scalar1=-1.0, r=[("pwr", 1)], w=["ta"])
                V("dve", "tensor_tensor", out=cre[:], in0=t_a[:], in1=lr[:], op=ALU.mult, r=["ta", "lr"], w=["cre"])
                V("dve", "tensor_tensor", out=t_b[:], in0=pwi[1][:], in1=li[:], op=ALU.mult, r=[("pwi", 1), "li"], w=["tb"])
                V("dve", "tensor_tensor", out=cre[:], in0=cre[:], in1=t_b[:], op=ALU.add, r=["cre", "tb"], w=["cre"])
                V("dve", "tensor_tensor", out=cim[:], in0=pwi[1][:], in1=lr[:], op=ALU.mult, r=[("pwi", 1), "lr"], w=["cim"])
                V("dve", "tensor_tensor", out=t_b[:], in0=t_a[:], in1=li[:], op=ALU.mult, r=["ta", "li"], w=["tb"])
                V("dve", "tensor_tensor", out=cim[:], in0=cim[:], in1=t_b[:], op=ALU.subtract, r=["cim", "tb"], w=["cim"])
                V("dve", "tensor_tensor", out=t_a[:], in0=lr[:], in1=lr[:], op=ALU.mult, r=["lr"], w=["ta"])
                V("dve", "tensor_tensor", out=t_b[:], in0=li[:], in1=li[:], op=ALU.mult, r=["li"], w=["tb"])
                V("dve", "tensor_tensor", out=t_a[:], in0=t_a[:], in1=t_b[:], op=ALU.add, r=["ta", "tb"], w=["ta"])
                V("dve", "reciprocal", out=t_a[:], in_=t_a[:], r=["ta"], w=["ta"])
                V("dve", "tensor_tensor", out=cre[:], in0=cre[:], in1=t_a[:], op=ALU.mult, r=["cre", "ta"], w=["cre"])
                V("dve", "tensor_tensor", out=cim[:], in0=cim[:], in1=t_a[:], op=ALU.mult, r=["cim", "ta"], w=["cim"])

                def bc(t):
                    return t[:, :].unsqueeze(2).broadcast_to([128, 32, 16])

                def cplx(fr, fi, frn, fin, xr, xi, xrn, xin):
                    V("dve", "tensor_tensor", out=q1[:], in0=xr[:], in1=bc(fr), op=ALU.mult, r=[xrn, frn, "q1"], w=["q1"])
                    V("dve", "tensor_tensor", out=q3[:], in0=xi[:], in1=bc(fi), op=ALU.mult, r=[xin, fin, "q3"], w=["q3"])
                    V("dve", "tensor_tensor", out=q1[:], in0=q1[:], in1=q3[:], op=ALU.subtract, r=["q1", "q3"], w=["q1"])
                    V("dve", "tensor_tensor", out=q2[:], in0=xi[:], in1=bc(fr), op=ALU.mult, r=[xin, frn, "q2"], w=["q2"])
                    V("dve", "tensor_tensor", out=q4[:], in0=xr[:], in1=bc(fi), op=ALU.mult, r=[xrn, fin, "q4"], w=["q4"])
                    V("dve", "tensor_tensor", out=q2[:], in0=q2[:], in1=q4[:], op=ALU.add, r=["q2", "q4"], w=["q2"])

                def put(dst, dname, blk, lo_src, lo_sign, up_src, up_sign):
                    for (rows, src, sign, eng) in ((slice(0, 64), lo_src, lo_sign, "act"),
                                                   (slice(64, 128), up_src, up_sign, "pool")):
                        srcn = "q1" if src is q1 else "q2"
                        if eng == "act":
                            V("act", "activation", out=dst[rows, :, blk, :], in_=src[rows, :, :], func=AF.Copy,
                              scale=float(sign), r=[srcn], w=[dname])
                        else:
                            V("pool", "tensor_scalar", out=dst[rows, :, blk, :], in0=src[rows, :, :],
                              scalar1=float(sign), scalar2=None, op0=ALU.mult, r=[srcn], w=[dname])

                if int(os.environ.get("SSMSTOP", "9")) <= 3:
                    S.barrier()
                    return
                cplx(cre, cim, "cre", "cim", Bre, Bim, "Bre", "Bim")
                V("dve", "tensor_copy", out=bbr[:], in_=q1[:], r=["q1"], w=["bbr"])
                V("dve", "tensor_copy", out=bbi[:], in_=q2[:], r=["q2"], w=["bbi"])
                for i in range(2):
                    for t4 in range(4):
                        pb = pw[(i * 4 + t4) % 2]
                        V("pe", "transpose", out=pb[:, 0:128], in_=Craw[i][:, t4, :], identity=ident_f[:, :],
                          r=[("Craw", i), "ident_f"], w=[("pw", (i * 4 + t4) % 2)])
                        V("act", "activation", out=Cn[i][:, t4 * 8:(t4 + 1) * 8, :],
                          in_=pb[:, 0:128].rearrange("p (g c) -> p g c", c=16), func=AF.Copy,
                          r=[("pw", (i * 4 + t4) % 2)], w=[("Cn", i)])
                for s8 in range(8):
                    p = 7 - s8
                    cplx(pwr[p], pwi[p], ("pwr", p), ("pwi", p), bbr, bbi, "bbr", "bbi")
                    put(X1, "X1", s8, q1, 1, q2, 1)
                    put(X1s, "X1s", s8, q2, 1, q1, -1)
                    p = -s8
                    cplx(pwr[p], pwi[p], ("pwr", p), ("pwi", p), bbr, bbi, "bbr", "bbi")
                    put(X1p, "X1p", s8, q1, 1, q2, 1)
                    p = s8 + 1
                    cplx(pwr[p], pwi[p], ("pwr", p), ("pwi", p), Cn[0], Cn[1], ("Cn", 0), ("Cn", 1))
                    put(X2, "X2", s8, q1, 1, q2, -1)
                    put(X2s, "X2s", s8, q2, -1, q1, -1)
                    p = s8
                    cplx(pwr[p], pwi[p], ("pwr", p), ("pwi", p), Cn[0], Cn[1], ("Cn", 0), ("Cn", 1))
                    put(X2p, "X2p", s8, q1, 1, q2, -1)
                if int(os.environ.get("SSMSTOP", "9")) <= 4:
                    S.barrier()
                    return
                V("act", "activation", out=W2[:].rearrange("p g x -> p (g x)"),
                  in_=X2[:].rearrange("p g s c -> p (g s c)"), func=AF.Copy, r=["X2"], w=["W2"])
                V("dve", "tensor_copy", out=W2s[:].rearrange("p g x -> p (g x)"),
                  in_=X2s[:].rearrange("p g s c -> p (g s c)"), r=["X2s"], w=["W2s"])
                for g in range(NG):
                    b = g % 2
                    V("pe", "transpose", out=pw[b][:, 0:128], in_=X1[:, g].rearrange("p s c -> p (s c)"),
                      identity=ident_f[:, :], r=["X1", "ident_f"], w=[("pw", b)])
                    V("pe", "transpose", out=pw[b][:, 128:256], in_=X1s[:, g].rearrange("p s c -> p (s c)"),
                      identity=ident_f[:, :], r=["X1s", "ident_f"], w=[("pw", b)])
                    V("pe", "matmul", pw[b][:, 256:384], lhsT=X1p[:, g].rearrange("p s c -> p (s c)"),
                      rhs=X2p[:, g].rearrange("p s c -> p (s c)"), start=True, stop=True,
                      r=["X1p", "X2p"], w=[("pw", b)])
                    V("act", "activation", out=W1[:, g, :], in_=pw[b][:, 0:128], func=AF.Copy,
                      r=[("pw", b)], w=["W1"])
                    V("act", "activation", out=W1s[:, g, :], in_=pw[b][:, 128:256], func=AF.Copy,
                      r=[("pw", b)], w=["W1s"])
                    V("act", "activation", out=w3r[b][:], in_=pw[b][:, 256:384], func=AF.Copy,
                      r=[("pw", b)], w=[("w3r", b)])
                    V("dve", "tensor_tensor", out=w3t[:], in0=w3r[b][:], in1=maskBL[:], op=ALU.mult,
                      r=[("w3r", b), "maskBL"], w=["w3t"])
                    V("dve", "scalar_tensor_tensor", out=W3[:, g, :], in0=ident_f[:], scalar=dcol[:, g:g + 1],
                      in1=w3t[:], op0=ALU.mult, op1=ALU.add, r=["ident_f", "dcol", "w3t"], w=["W3"])

                if int(os.environ.get("SSMSTOP", "9")) <= 5:
                    S.barrier()
                    return
                S.barrier()
                stp.close()
                cT = [t32("cT%d" % i, (128, NKc)) for i in range(2)]
                sT = [t32("sT%d" % i, (128, NKc)) for i in range(2)]
                rtab = [t32("rtab%d" % i, (128, NKc)) for i in range(2)]
                yk = t32("yk", (128, NKc))
                kf = t32("kf", (128, NKc))
                kt = t32("kt", (128, NKc))
                Ut = [t32("U%d" % i, (128, NKc), BF16) for i in range(2)]
                m1, m2, Tt = t32("m1", (128, NKc)), t32("m2", (128, NKc)), t32("Tt", (128, NKc))
                Pm = [t32("Pm%d" % i, (128, NKc + 2), BF16) for i in range(2)]
                Qm = [t32("Qm%d" % i, (128, NKc + 2), BF16) for i in range(2)]
                yo = [t32("yo%d" % i, (128, NKc)) for i in range(2)]
                for i in range(2):
                    S.op("pool", "memset", Pm[i][:], 0.0, writes=[("Pm", i)])
                    S.op("pool", "memset", Qm[i][:], 0.0, writes=[("Qm", i)])
                it = 0
                for g in range(NG):
                    gb = g % 2
                    V("dve", "tensor_scalar_mul", out=yk[:], in0=kidx[:], scalar1=f8[:, g:g + 1],
                      r=["kidx", "f8"], w=["yk"])
                    sin_turns(sT[gb][:], ("sT", gb), yk[:], "yk", ki[:], kf[:], kt[:], "k")
                    V("dve", "tensor_scalar_add", out=yk[:], in0=yk[:], scalar1=0.25, r=["yk"], w=["yk"])
                    sin_turns(cT[gb][:], ("cT", gb), yk[:], "yk", ki[:], kf[:], kt[:], "k")
                    V("act", "activation", out=rtab[gb][:], in_=kidx[:], func=AF.Identity, scale=0.0,
                      bias=rho8[:, g:g + 1], r=["kidx", "rho8"], w=[("rtab", gb)])
                    RL = int(os.environ.get("RUNLVL", "9"))
                    for s in range(NS):
                        if RL < 2:
                            continue
                        ub = it % 2
                        it += 1
                        S.dma("sp", out=Ut[ub][:], in_=Usc[s, g].rearrange("s c k -> (s c) k"),
                              writes=[("U", ub)])
                        for hi, (lo, wd) in enumerate(halves):
                            V("pe", "matmul", pL[:, hi, 0:wd], lhsT=W1[:, g, :], rhs=Ut[ub][:, lo:lo + wd],
                              start=True, stop=True, r=["W1", ("U", ub)], w=["pL"])
                            V("pe", "matmul", pLs[:, hi, 0:wd], lhsT=W1s[:, g, :], rhs=Ut[ub][:, lo:lo + wd],
                              start=True, stop=True, r=["W1s", ("U", ub)], w=["pLs"])
                        if RL < 3:
                            continue
                        for hi, (lo, wd) in enumerate(halves):
                            V("dve", "tensor_tensor", out=m1[:, lo:lo + wd], in0=pL[:, hi, 0:wd],
                              in1=cT[gb][:, lo:lo + wd], op=ALU.mult, r=["pL", ("cT", gb)], w=["m1"])
                            V("dve", "tensor_tensor", out=m2[:, lo:lo + wd], in0=pLs[:, hi, 0:wd],
                              in1=sT[gb][:, lo:lo + wd], op=ALU.mult, r=["pLs", ("sT", gb)], w=["m2"])
                        V("dve", "tensor_tensor", out=m1[:], in0=m1[:], in1=m2[:], op=ALU.add,
                          r=["m1", "m2"], w=["m1"])
                        V("dve", "tensor_tensor_scan", out=Tt[:], data0=rtab[gb][:], data1=m1[:], initial=0.0,
                          op0=ALU.mult, op1=ALU.add, r=[("rtab", gb), "m1"], w=["Tt"])
                        V("dve", "tensor_tensor", out=Pm[ub][:, 1:NKc + 1], in0=Tt[:], in1=cT[gb][:], op=ALU.mult,
                          r=["Tt", ("cT", gb)], w=[("Pm", ub)])
                        V("pool", "tensor_tensor", out=Qm[ub][:, 1:NKc + 1], in0=Tt[:], in1=sT[gb][:], op=ALU.mult,
                          r=["Tt", ("sT", gb)], w=[("Qm", ub)])
                        if RL < 4:
                            continue
                        for hi, (lo, wd) in enumerate(halves):
                            V("pe", "matmul", py[:, hi, 0:wd], lhsT=W3[:, g, :], rhs=Ut[ub][:, lo:lo + wd],
                              start=True, stop=False, r=["W3", ("U", ub)], w=["py"])
                            V("pe", "matmul", py[:, hi, 0:wd], lhsT=W2[:, g, :],
                              rhs=Pm[ub][:, lo:lo + wd], start=False, stop=False,
                              r=["W2", ("Pm", ub)], w=["py"])
                            V("pe", "matmul", py[:, hi, 0:wd], lhsT=W2s[:, g, :],
                              rhs=Qm[ub][:, lo:lo + wd], start=False, stop=True,
                              r=["W2s", ("Qm", ub)], w=["py"])
                        if RL < 5:
                            continue
                        for hi, (lo, wd) in enumerate(halves):
                            V("act", "activation", out=yo[ub][:, lo:lo + wd], in_=py[:, hi, 0:wd], func=AF.Copy,
                              r=["py"], w=[("yo", ub)])
                        S.dma("sp", out=ysS[s, g].rearrange("s c k -> (s c) k"), in_=yo[ub][:],
                              reads=[("yo", ub)])
                S.barrier()

        def phase_mixB1(l):
            w_in = P["w_in"][l]
            c0 = gcol(1, l)
            HALO = CW - 1
            with ExitStack() as st:
                wC = sb("c_w", [128, DT, 1024], BF16, st)
                dg = sb("c_dg", [128, 4 * CW, 128], BF16, st)
                cw = sb("c_cw", [128, 4, CW], F32, st)
                cb = sb("c_cb", [128, 4], F32, st)
                lng = sb("c_lng", [128, 4], F32, st)
                lnb = sb("c_lnb", [128, 4], F32, st)
                eps_ln = sb("c_epsln", [128, 1], F32, st)
                hsb = [sb("c_hs%d" % i, [128, DT, 512], F32, st) for i in range(2)]
                xn = sb("c_xn", [128, DT, 512], BF16, st)
                sqb = [sb("c_sq%d" % i, [128, 512], BF16, st) for i in range(2)]
                rstd = sb("c_rstd", [128, 512], F32, st)
                hc = sb("c_hc", [128, 4, HALO + 512], BF16, st)
                cacc = sb("c_cacc", [128, 4, 512], F32, st)
                hcvb = [sb("c_hcv%d" % i, [128, 4, 512], BF16, st) for i in range(2)]
                sg = [sb("c_sg%d" % i, [128, 512], F32, st) for i in range(2)]
                tmq = sb("c_tm", [128, 512], F32, st)
                mu = sb("c_mu", [128, 512], F32, st)
                lrs = sb("c_lrs", [128, 512], F32, st)
                pss = ps("c_pss", [128, 512], F32, st)
                NPB = 6
                pp = [ps("c_pp%d" % i, [128, 512], F32, st) for i in range(NPB)]
                pcnt = [0]

                def nxt():
                    b = pcnt[0] % NPB
                    pcnt[0] += 1
                    return b

                for dt in range(DT):
                    load_w_cast(wC[:, dt, :], w_in[dt * 128:(dt + 1) * 128, 512:1536], ("wC", dt))
                wCr = [("wC", dt) for dt in range(DT)]
                for ct in range(4):
                    S.dma("sp", out=cw[:, ct, :], in_=P["conv_w"][l][:, ct * 128:(ct + 1) * 128].rearrange("j p -> p j"),
                          allow_slow_non_contiguous=True, writes=["cw"])
                for nm, tl in (("conv_b", cb), ("conv_ln_g", lng), ("conv_ln_b", lnb)):
                    S.dma("sp", out=tl[:], in_=P[nm][l].rearrange("(t p) -> p t", p=128),
                          allow_slow_non_contiguous=True, writes=[nm])
                S.op("pool", "memset", eps_ln[:], LN_EPS, writes=["eps_ln"])
                for ct in range(4):
                    for j in range(CW):
                        eng = "dve" if (j % 2 == 0) else "pool"
                        S.op(eng, "tensor_scalar", out=dg[:, ct * CW + j, :], in0=ident_f[:], scalar1=cw[:, ct, j:j + 1],
                             scalar2=None, op0=ALU.mult, reads=["ident_f", "cw"], writes=[("dg", ct)])
                tcnt = 0
                for s in range(NS):
                    for (t0, n) in cfg.tiles:
                        hs = hsb[tcnt % 2]
                        hcv = hcvb[tcnt % 2]
                        hsr, hcvr = ("hsb", tcnt % 2), ("hcvb", tcnt % 2)
                        tcnt += 1
                        S.dma("sp", out=hs[:, :, 0:n], in_=hT[s, :, t0:t0 + n].rearrange("(t p) n -> p t n", p=128),
                              writes=[hsr])
                        rmsnorm(hs, xn, sqb, pss, rstd, c0, n, hs_name=hsr)
                        if t0 == 0:
                            S.op("pool", "memset", hc[:, :, 0:HALO], 0.0, writes=["hc"])
                        for ct in range(4):
                            ba, bg = nxt(), nxt()
                            for dt in range(DT):
                                S.op("pe", "matmul", pp[ba][:, 0:n], lhsT=wC[:, dt, ct * 128:(ct + 1) * 128],
                                     rhs=xn[:, dt, 0:n], start=(dt == 0), stop=(dt == DT - 1),
                                     reads=wCr + ["xn"], writes=[("pp", ba)])
                            for dt in range(DT):
                                S.op("pe", "matmul", pp[bg][:, 0:n], lhsT=wC[:, dt, 512 + ct * 128:512 + (ct + 1) * 128],
                                     rhs=xn[:, dt, 0:n], start=(dt == 0), stop=(dt == DT - 1),
                                     reads=wCr + ["xn"], writes=[("pp", bg)])
                            S.op("act", "activation", out=sg[ct % 2][:, 0:n], in_=pp[bg][:, 0:n], func=AF.Sigmoid,
                                 reads=[("pp", bg)], writes=[("sg", ct % 2)])
                            S.op("dve", "tensor_tensor", out=hc[:, ct, HALO:HALO + n], in0=pp[ba][:, 0:n],
                                 in1=sg[ct % 2][:, 0:n], op=ALU.mult,
                                 reads=[("pp", ba), ("sg", ct % 2)], writes=["hc"])
                        for ct in range(4):
                            bc_ = nxt()
                            for j in range(CW):
                                S.op("pe", "matmul", pp[bc_][:, 0:n], lhsT=dg[:, ct * CW + j, :], rhs=hc[:, ct, j:j + n],
                                     start=(j == 0), stop=(j == CW - 1),
                                     reads=[("dg", ct), "hc"], writes=[("pp", bc_)])
                            S.op("act", "activation", out=cacc[:, ct, 0:n], in_=pp[bc_][:, 0:n], func=AF.Identity,
                                 bias=cb[:, ct:ct + 1], scale=1.0,
                                 reads=[("pp", bc_), "conv_b"], writes=[("cacc", ct)])
                        if n >= HALO:
                            S.op("pool", "tensor_copy", out=hc[:, :, 0:HALO], in_=hc[:, :, n:n + HALO],
                                 reads=["hc"], writes=["hc"])
                        bmu, bvar = nxt(), nxt()
                        for ct in range(4):
                            b2 = ct % 2
                            S.op("act", "activation", out=sqb[b2][:, 0:n], in_=cacc[:, ct, 0:n], func=AF.Copy,
                                 reads=[("cacc", ct)], writes=[("sqb", b2)])
                            S.op("pe", "matmul", pp[bmu][:, 0:n], lhsT=ones_bf[:], rhs=sqb[b2][:, 0:n],
                                 start=(ct == 0), stop=(ct == 3), reads=[("sqb", b2), "ones_bf"], writes=[("pp", bmu)])
                        for ct in range(4):
                            b2 = ct % 2
                            S.op("act", "activation", out=sqb[b2][:, 0:n], in_=cacc[:, ct, 0:n], func=AF.Square,
                                 reads=[("cacc", ct)], writes=[("sqb", b2)])
                            S.op("pe", "matmul", pp[bvar][:, 0:n], lhsT=ones_bf[:], rhs=sqb[b2][:, 0:n],
                                 start=(ct == 0), stop=(ct == 3), reads=[("sqb", b2), "ones_bf"], writes=[("pp", bvar)])
                        S.op("act", "activation", out=mu[:, 0:n], in_=pp[bmu][:, 0:n], func=AF.Copy, scale=1.0 / 512,
                             reads=[("pp", bmu)], writes=["mu"])
                        S.op("dve", "tensor_tensor", out=tmq[:, 0:n], in0=mu[:, 0:n], in1=mu[:, 0:n], op=ALU.mult,
                             reads=["mu"], writes=["tmq"])
                        S.op("dve", "scalar_tensor_tensor", out=lrs[:, 0:n], in0=pp[bvar][:, 0:n], scalar=1.0 / 512,
                             in1=tmq[:, 0:n], op0=ALU.mult, op1=ALU.subtract,
                             reads=[("pp", bvar), "tmq"], writes=["lrs"])
                        S.op("act", "activation", out=lrs[:, 0:n], in_=lrs[:, 0:n], func=AF.Sqrt, bias=eps_ln[:], scale=1.0,
                             reads=["lrs", "eps_ln"], writes=["lrs"])
                        S.op("dve", "reciprocal", out=lrs[:, 0:n], in_=lrs[:, 0:n], reads=["lrs"], writes=["lrs"])
                        for ct in range(4):
                            S.op("dve", "tensor_tensor", out=tmq[:, 0:n], in0=cacc[:, ct, 0:n], in1=mu[:, 0:n],
                                 op=ALU.subtract, reads=[("cacc", ct), "mu"], writes=["tmq"])
                            S.op("dve", "tensor_tensor", out=tmq[:, 0:n], in0=tmq[:, 0:n], in1=lrs[:, 0:n],
                                 op=ALU.mult, reads=["tmq", "lrs"], writes=["tmq"])
                            S.op("act", "activation", out=hcv[:, ct, 0:n], in_=tmq[:, 0:n], func=AF.Silu,
                                 scale=lng[:, ct:ct + 1], bias=lnb[:, ct:ct + 1],
                                 reads=["tmq", "conv_ln_g", "conv_ln_b"], writes=[hcvr])
                        S.dma("sp", out=hcvS[s, :, t0:t0 + n].rearrange("(t p) n -> p t n", p=128), in_=hcv[:, :, 0:n],
                              reads=[hcvr])
                S.barrier()

        def phase_mixB2(l):
            w_in = P["w_in"][l]
            c0 = gcol(1, l)
            HALO = CW - 1
            with ExitStack() as st:
                wB = sb("b_w", [128, DT, 3072], BF16, st)
                wglu = sb("b_wglu", [128, 4, 2048], BF16, st)
                wpw = sb("b_wpw", [128, 4, D], BF16, st)
                wo = sb("b_wo", [128, 4, D], BF16, st)
                wout = sb("b_wout", [128, DT, D], BF16, st)
                hsb = [sb("b_hs%d" % i, [128, DT, 512], F32, st) for i in range(2)]
                xnb = [sb("b_xn%d" % i, [128, DT, 512], BF16, st) for i in range(2)]
                sqb = [sb("b_sq%d" % i, [128, 512], BF16, st) for i in range(2)]
                rstd = sb("b_rstd", [128, 512], F32, st)
                hcvb = [sb("b_hcv%d" % i, [128, 4, 512], BF16, st) for i in range(2)]
                yst = [sb("b_yst%d" % i, [128, 8, 64], F32, st) for i in range(4)]
                gyb = [sb("b_gy%d" % i, [128, 4, 512], BF16, st) for i in range(2)]
                oatb = [sb("b_oat%d" % i, [128, 4, 512], BF16, st) for i in range(2)]
                tg = [sb("b_tg%d" % i, [128, 512], F32, st) for i in range(2)]
                mrg = sb("b_mrg", [128, DT, 512], BF16, st)
                sg = [sb("b_sg%d" % i, [128, 512], F32, st) for i in range(2)]
                tm = [sb("b_tm%d" % i, [128, 512], F32, st) for i in range(2)]
                macc = sb("b_macc", [128, 512], F32, st)
                pss = ps("b_pss", [128, 512], F32, st)
                NPB = 6
                pp = [ps("b_pp%d" % i, [128, 512], F32, st) for i in range(NPB)]
                pcnt = [0]

                def nxt():
                    b = pcnt[0] % NPB
                    pcnt[0] += 1
                    return b

                for dt in range(DT):
                    load_w_cast(wB[:, dt, 0:1536], w_in[dt * 128:(dt + 1) * 128, 3072:4608], ("wB", dt))
                    load_w_cast(wB[:, dt, 1536:3072], w_in[dt * 128:(dt + 1) * 128, 4608:6144], ("wB", dt, 1))
                    load_w_cast(wout[:, dt, :], P["w_out"][l][dt * 128:(dt + 1) * 128, :], ("wout", dt))
                for ct in range(4):
                    load_w_cast(wglu[:, ct, :], P["ssm_w_glu"][l][ct * 128:(ct + 1) * 128, :], ("wglu", ct))
                    load_w_cast(wpw[:, ct, :], P["conv_w_out"][l][ct * 128:(ct + 1) * 128, :], ("wpw", ct))
                    load_w_cast(wo[:, ct, :], P["attn_w_o"][l][ct * 128:(ct + 1) * 128, :], ("wo", ct))
                wBr = [("wB", dt) for dt in range(DT)] + [("wB", dt, 1) for dt in range(DT)]
                tiles_all = [(s, t0, n) for s in range(NS) for (t0, n) in cfg.tiles]

                def pro_loads(idx):
                    s, t0, n = tiles_all[idx]
                    ib = idx % 2
                    hs, xn, oat, hcv, gy = hsb[ib], xnb[ib], oatb[ib], hcvb[ib], gyb[ib]
                    hsr, xnr, oatr, hcvr, gyr = ("hs", ib), ("xn", ib), ("oat", ib), ("hcv", ib), ("gy", ib)
                    nk = n // 8
                    k0 = t0 // 8
                    S.dma("sp", out=hs[:, :, 0:n], in_=hT[s, :, t0:t0 + n].rearrange("(t p) n -> p t n", p=128),
                          writes=[hsr])
                    S.dma("sp", out=oat[:, :, 0:n],
                          in_=oattT[s, :, t0:t0 + n].rearrange("(t p) n -> p t n", p=128), writes=[oatr])
                    S.dma("sp", out=hcv[:, :, 0:n],
                          in_=hcvS[s, :, t0:t0 + n].rearrange("(t p) n -> p t n", p=128), writes=[hcvr])
                    for ct in range(4):
                        yt = yst[ct]
                        for gl in range(8):
                            S.dma("sp", out=yt[gl * 16:(gl + 1) * 16, :, 0:nk],
                                  in_=ysS[s, ct * 8 + gl, :, :, k0:k0 + nk].rearrange("j c k -> c j k"),
                                  writes=[("yst", ct)])

                def pro_compute(idx):
                    s, t0, n = tiles_all[idx]
                    ib = idx % 2
                    hs, xn, oat, hcv, gy = hsb[ib], xnb[ib], oatb[ib], hcvb[ib], gyb[ib]
                    hsr, xnr, oatr, hcvr, gyr = ("hs", ib), ("xn", ib), ("oat", ib), ("hcv", ib), ("gy", ib)
                    nk = n // 8
                    rmsnorm(hs, xn, sqb, pss, rstd, c0, n, xn_name=xnr, hs_name=hsr)
                    for ct in range(4):
                        yb = ct
                        yt = yst[yb]
                        yv = yt[:, :, 0:nk]
                        t1 = tg[0][:, 0:n].rearrange("p (j k) -> p j k", j=8)
                        t2 = tg[1][:, 0:n].rearrange("p (j k) -> p j k", j=8)
                        S.op("act", "activation", out=t1, in_=yv, func=AF.Square,
                             reads=[("yst", yb)], writes=[("tg", 0)])
                        S.op("dve", "tensor_scalar", out=t1, in0=t1, scalar1=0.044715, scalar2=1.0,
                             op0=ALU.mult, op1=ALU.add, reads=[("tg", 0)], writes=[("tg", 0)])
                        S.op("dve", "tensor_tensor", out=t1, in0=t1, in1=yv, op=ALU.mult,
                             reads=[("tg", 0), ("yst", yb)], writes=[("tg", 0)])
                        S.op("act", "activation", out=t2, in_=t1, func=AF.Tanh, scale=0.7978845608,
                             reads=[("tg", 0)], writes=[("tg", 1)])
                        S.op("dve", "tensor_scalar", out=t2, in0=t2, scalar1=0.5, scalar2=0.5,
                             op0=ALU.mult, op1=ALU.add, reads=[("tg", 1)], writes=[("tg", 1)])
                        S.op("dve", "tensor_tensor", out=gy[:, ct, 0:n].rearrange("p (k j) -> p j k", j=8),
                             in0=t2, in1=yv, op=ALU.mult,
                             reads=[("tg", 1), ("yst", yb)], writes=[gyr])

                def main_tile(idx):
                    s, t0, n = tiles_all[idx]
                    ib = idx % 2
                    hs, xn, oat, hcv, gy = hsb[ib], xnb[ib], oatb[ib], hcvb[ib], gyb[ib]
                    hsr, xnr, oatr, hcvr, gyr = ("hs", ib), ("xn", ib), ("oat", ib), ("hcv", ib), ("gy", ib)
                    if idx + 1 < len(tiles_all):
                        pro_loads(idx + 1)
                    for f in range(DT):
                        if f == 2 and idx + 1 < len(tiles_all):
                            pro_compute(idx + 1)
                        def gate(br):
                            bgt = nxt()
                            c1 = br * 1024 + f * 128
                            for dt in range(DT):
                                S.op("pe", "matmul", pp[bgt][:, 0:n], lhsT=wB[:, dt, c1:c1 + 128],
                                     rhs=xn[:, dt, 0:n], start=(dt == 0), stop=(dt == DT - 1),
                                     reads=wBr + [xnr], writes=[("pp", bgt)])
                            S.op("act", "activation", out=sg[br % 2][:, 0:n], in_=pp[bgt][:, 0:n], func=AF.Sigmoid,
                                 reads=[("pp", bgt)], writes=[("sg", br % 2)])
                            return sg[br % 2], ("sg", br % 2)
                        ba, bg = nxt(), nxt()
                        for ct in range(4):
                            S.op("pe", "matmul", pp[ba][:, 0:n], lhsT=wglu[:, ct, f * 128:(f + 1) * 128],
                                 rhs=gy[:, ct, 0:n], start=(ct == 0), stop=(ct == 3),
                                 reads=[("wglu", c) for c in range(4)] + [gyr], writes=[("pp", ba)])
                        for ct in range(4):
                            S.op("pe", "matmul", pp[bg][:, 0:n], lhsT=wglu[:, ct, 1024 + f * 128:1024 + (f + 1) * 128],
                                 rhs=gy[:, ct, 0:n], start=(ct == 0), stop=(ct == 3),
                                 reads=[("wglu", c) for c in range(4)] + [gyr], writes=[("pp", bg)])
                        S.op("act", "activation", out=tm[1][:, 0:n], in_=pp[bg][:, 0:n], func=AF.Sigmoid,
                             reads=[("pp", bg)], writes=[("tm", 1)])
                        S.op("dve", "tensor_tensor", out=tm[0][:, 0:n], in0=pp[ba][:, 0:n], in1=tm[1][:, 0:n],
                             op=ALU.mult, reads=[("pp", ba), ("tm", 1)], writes=[("tm", 0)])
                        g0, g0r = gate(0)
                        S.op("dve", "tensor_tensor", out=macc[:, 0:n], in0=tm[0][:, 0:n], in1=g0[:, 0:n],
                             op=ALU.mult, reads=[("tm", 0), g0r], writes=["macc"])
                        bc_ = nxt()
                        for ct in range(4):
                            S.op("pe", "matmul", pp[bc_][:, 0:n], lhsT=wpw[:, ct, f * 128:(f + 1) * 128],
                                 rhs=hcv[:, ct, 0:n], start=(ct == 0), stop=(ct == 3),
                                 reads=[("wpw", c) for c in range(4)] + [hcvr], writes=[("pp", bc_)])
                        g1, g1r = gate(1)
                        S.op("dve", "tensor_tensor", out=tm[0][:, 0:n], in0=pp[bc_][:, 0:n], in1=g1[:, 0:n],
                             op=ALU.mult, reads=[("pp", bc_), g1r], writes=[("tm", 0)])
                        S.op("dve", "tensor_tensor", out=macc[:, 0:n], in0=macc[:, 0:n], in1=tm[0][:, 0:n],
                             op=ALU.add, reads=["macc", ("tm", 0)], writes=["macc"])
                        bo_ = nxt()
                        for ct in range(4):
                            S.op("pe", "matmul", pp[bo_][:, 0:n], lhsT=wo[:, ct, f * 128:(f + 1) * 128],
                                 rhs=oat[:, ct, 0:n], start=(ct == 0), stop=(ct == 3),
                                 reads=[("wo", c) for c in range(4)] + [oatr], writes=[("pp", bo_)])
                        g2, g2r = gate(2)
                        S.op("dve", "tensor_tensor", out=tm[0][:, 0:n], in0=pp[bo_][:, 0:n], in1=g2[:, 0:n],
                             op=ALU.mult, reads=[("pp", bo_), g2r], writes=[("tm", 0)])
                        S.op("dve", "tensor_tensor", out=mrg[:, f, 0:n], in0=macc[:, 0:n], in1=tm[0][:, 0:n],
                             op=ALU.add, reads=["macc", ("tm", 0)], writes=[("mrg", f)])
                    for o in range(DT):
                        bo_ = nxt()
                        for f in range(DT):
                            S.op("pe", "matmul", pp[bo_][:, 0:n], lhsT=wout[:, f, o * 128:(o + 1) * 128],
                                 rhs=mrg[:, f, 0:n], start=(f == 0), stop=(f == DT - 1),
                                 reads=[("wout", f), ("mrg", f)], writes=[("pp", bo_)])
                        S.op("dve", "tensor_tensor", out=hs[:, o, 0:n], in0=pp[bo_][:, 0:n], in1=hs[:, o, 0:n],
                             op=ALU.add, reads=[("pp", bo_), hsr], writes=[hsr])
                    S.dma("sp", out=hT[s, :, t0:t0 + n].rearrange("(t p) n -> p t n", p=128), in_=hs[:, :, 0:n],
                          reads=[hsr])

                pro_loads(0)
                pro_compute(0)
                for idx in range(len(tiles_all)):
                    main_tile(idx)
                S.barrier()

        phases = cfg.phases

        def on(name):
            return phases is None or name in phases

        phase_in()
        for l in range(DEPTH):
            if on("ffn1"):
                phase_ffn(l, 0)
            if on("mixa"):
                phase_mixA(l)
            if on("att"):
                phase_att(l)
            if on("ssm"):
                phase_ssm(l)
            if on("mixb"):
                phase_mixB1(l)
                phase_mixB2(l)
            if on("ffn2"):
                phase_ffn(l, 1)
        phase_out()

        with nc.Block() as block:
            S.emit(block)
    return nc


def kernel(**inputs):
    cfg = Cfg()
    nc = build(cfg)
    n = 8
    in_maps = []
    for c in range(n):
        m = {}
        for k, v in inputs.items():
            a = np.asarray(v)
            if k == "x":
                a = a[c * cfg.nseq:(c + 1) * cfg.nseq]
            m[k] = np.ascontiguousarray(a, dtype=np.float32)
        in_maps.append(m)
    res = run_bass_kernel_spmd(nc, in_maps, core_ids=list(range(n)))
    return np.concatenate([np.asarray(r["out"]) for r in res.results], axis=0).astype(np.float32)
```

```python
import math
import os
from contextlib import ExitStack

import numpy as np
import concourse.bass as bass
import concourse.mybir as mybir
from concourse.bass_utils import run_bass_kernel_spmd

F32 = mybir.dt.float32
BF16 = mybir.dt.bfloat16
I32 = mybir.dt.int32
AF = mybir.ActivationFunctionType
ALU = mybir.AluOpType

D = 1024
DT = 8
DFF = 2816
FT = 22
NMETA = 16
DIN = 6144
NG = 32
NST = 64
CW = 31
RMS_EPS = 1e-6
LN_EPS = 1e-5

ENGS = ["pe", "act", "dve", "pool", "sp"]


class Sched:
    def __init__(self, nc, es, n_dma_sems=24):
        self.nc = nc
        self.ops = {e: [] for e in ENGS}
        self.sem = {e: es.enter_context(nc.semaphore("sem_" + e)) for e in ENGS}
        self.cnt = {e: 0 for e in ENGS}
        self.dsem = [es.enter_context(nc.semaphore("semd%d" % i)) for i in range(n_dma_sems)]
        self.duse = [0] * n_dma_sems
        self.dnext = 0
        self.waited = {}
        self.lastw = {}
        self.readers = {}

    def _semof(self, key):
        if isinstance(key, str):
            return self.sem[key]
        return self.dsem[key[1]]

    def _deps(self, eng, reads, writes):
        toks = {}
        def add(t):
            k, v = t
            if toks.get(k, 0) < v:
                toks[k] = v
        for r in reads:
            if r in self.lastw:
                add(self.lastw[r])
        for w in writes:
            if w in self.lastw:
                add(self.lastw[w])
            for k, v in self.readers.get(w, {}).items():
                add((k, v))
        waits = []
        for k, v in toks.items():
            if k == "pe" and eng == "pe":
                continue
            if self.waited.get((eng, k), 0) >= v:
                continue
            self.waited[(eng, k)] = v
            waits.append((k, v))
        return waits

    def _record(self, tok, reads, writes):
        k, v = tok
        for r in reads:
            d = self.readers.setdefault(r, {})
            if d.get(k, 0) < v:
                d[k] = v
        for w in writes:
            self.lastw[w] = tok
            self.readers[w] = {}

    def op(self, eng, meth, *args, reads=(), writes=(), excl=(), **kw):
        fn = (meth, args, kw)
        if excl:
            reads = list(reads) + list(excl)
            writes = list(writes) + list(excl)
        waits = self._deps(eng, reads, writes)
        self.cnt[eng] += 1
        tok = (eng, self.cnt[eng])
        self.ops[eng].append((waits, fn, eng, 1))
        self._record(tok, reads, writes)

    def dma(self, eng, reads=(), writes=(), **kw):
        fn = ("dma_start", (), kw)
        waits = self._deps(eng, reads, writes)
        slot = self.dnext
        self.dnext = (self.dnext + 1) % len(self.dsem)
        key = ("d", slot)
        if self.duse[slot] > 0:
            v = 16 * self.duse[slot]
            if self.waited.get((eng, key), 0) < v:
                self.waited[(eng, key)] = v
                waits.append((key, v))
        self.duse[slot] += 1
        tok = (key, 16 * self.duse[slot])
        self.ops[eng].append((waits, fn, key, 16))
        self._record(tok, reads, writes)

    def barrier(self):
        for eng in ENGS:
            waits = []
            for k in ENGS:
                v = self.cnt[k]
                if v > 0 and k != eng and self.waited.get((eng, k), 0) < v:
                    self.waited[(eng, k)] = v
                    waits.append((k, v))
            if eng != "pe":
                v = self.cnt[eng]
                if v > 0 and self.waited.get((eng, eng), 0) < v:
                    self.waited[(eng, eng)] = v
                    waits.append((eng, v))
            for i, u in enumerate(self.duse):
                key = ("d", i)
                if u > 0 and self.waited.get((eng, key), 0) < 16 * u:
                    self.waited[(eng, key)] = 16 * u
                    waits.append((key, 16 * u))
            if waits:
                self.ops[eng].append((waits, None, None, 0))
        self.lastw = {}
        self.readers = {}

    def emit(self, block):
        def replay(name):
            def run(e):
                for waits, fn, inc_key, inc in self.ops[name]:
                    for k, v in waits:
                        e.wait_ge(self._semof(k), v)
                    if fn is not None:
                        meth, args, kw = fn
                        getattr(e, meth)(*args, **kw).then_inc(self._semof(inc_key), inc)
            return run
        block.tensor(replay("pe"))
        block.scalar(replay("act"))
        block.vector(replay("dve"))
        block.gpsimd(replay("pool"))
        block.sync(replay("sp"))


class Cfg:
    def __init__(self, seq=4096, nseq=2, depth=4, phases=None, dump=()):
        self.dump = dump
        self.seq = seq
        self.nseq = nseq
        self.depth = depth
        self.L = seq + NMETA
        tiles = []
        t = 0
        while t + 512 <= self.L:
            tiles.append((t, 512))
            t += 512
        if t < self.L:
            tiles.append((t, self.L - t))
        self.tiles = tiles
        self.nkb = (self.L + 127) // 128
        self.LP = self.nkb * 128
        self.NK = self.L // 8
        self.phases = phases


def build(cfg):
    nc = bass.Bass("TRN2", target_bir_lowering=False)
    L, NS, DEPTH = cfg.L, cfg.nseq, cfg.depth

    def din(name, shape):
        return nc.dram_tensor(name, list(shape), F32, kind="ExternalInput").ap()

    x = din("x", (NS, cfg.seq, D))
    meta = din("meta_tokens", (NMETA, D))
    P = {}
    for name, shape in [
        ("ffn1_norm", (DEPTH, D)), ("ffn1_w13", (DEPTH, D, 2 * DFF)), ("ffn1_w2", (DEPTH, DFF, D)),
        ("mix_norm", (DEPTH, D)), ("w_in", (DEPTH, D, DIN)),
        ("ssm_lam_re", (DEPTH, NG, NST)), ("ssm_lam_im", (DEPTH, NG, NST)), ("ssm_log_dt", (DEPTH, NG)),
        ("ssm_b_re", (DEPTH, NG, NST, 16)), ("ssm_b_im", (DEPTH, NG, NST, 16)),
        ("ssm_c_re", (DEPTH, NG, 16, NST)), ("ssm_c_im", (DEPTH, NG, 16, NST)),
        ("ssm_d", (DEPTH, 512)), ("ssm_w_glu", (DEPTH, 512, 2048)),
        ("conv_w", (DEPTH, CW, 512)), ("conv_b", (DEPTH, 512)),
        ("conv_ln_g", (DEPTH, 512)), ("conv_ln_b", (DEPTH, 512)),
        ("conv_w_out", (DEPTH, 512, D)), ("attn_w_o", (DEPTH, 512, D)), ("w_out", (DEPTH, D, D)),
        ("ffn2_norm", (DEPTH, D)), ("ffn2_w13", (DEPTH, D, 2 * DFF)), ("ffn2_w2", (DEPTH, DFF, D)),
        ("final_norm", (D,)),
    ]:
        P[name] = din(name, shape)
    out = nc.dram_tensor("out", [NS, cfg.seq, D], F32, kind="ExternalOutput").ap()

    def scratch(name, shape, dt):
        kind = "ExternalOutput" if name in cfg.dump else "Internal"
        return nc.dram_tensor(name, list(shape), dt, kind=kind).ap()

    hT = scratch("hT", (NS, D, L), F32)
    NK = cfg.NK
    Usc = scratch("Usc", (NS, NG, 8, 16, NK), BF16)
    qT = scratch("qT", (NS, 512, L), BF16)
    kT = scratch("kT", (NS, 512, L), BF16)
    vS = scratch("vS", (NS, L, 512), BF16)
    oattT = scratch("oattT", (NS, 512, L), BF16)
    ysS = scratch("ysS", (NS, NG, 8, 16, NK), F32)
    hcvS = scratch("hcvS", (NS, 512, L), BF16)

    with ExitStack() as es:
        S = Sched(nc, es)

        uid = [0]

        def sb(name, shape, dt, stack=es):
            uid[0] += 1
            return stack.enter_context(nc.sbuf_tensor("%s_%d" % (name, uid[0]), list(shape), dt))

        def ps(name, shape, dt=F32, stack=es):
            uid[0] += 1
            return stack.enter_context(nc.psum_tensor("%s_%d" % (name, uid[0]), list(shape), dt))

        ones_bf = sb("ones_bf", [128, 128], BF16)
        ident_f = sb("ident_f", [128, 128], F32)
        gcols = sb("gcols", [128, (3 * DEPTH + 1) * DT], F32)
        eps_rms = sb("eps_rms", [128, 1], F32)
        rn_tmp = [sb("rn_tmp%d" % i, [128, 512], F32) for i in range(2)]
        S.op("pool", "memset", ones_bf[:], 1.0, writes=["ones_bf"])
        S.op("pool", "memset", ident_f[:], 1.0, writes=["ident_f"])
        S.op("pool", "affine_select", out=ident_f[:], in_=ident_f[:], pattern=[[-1, 128]],
                                               compare_op=ALU.is_equal, fill=0.0, base=0,
                                               channel_multiplier=1,
             reads=["ident_f"], writes=["ident_f"])
        S.op("pool", "memset", eps_rms[:], RMS_EPS, writes=["eps_rms"])
        for wi, nm in enumerate(["ffn1_norm", "mix_norm", "ffn2_norm"]):
            for l in range(DEPTH):
                c0 = (wi * DEPTH + l) * DT
                S.dma("sp",
                    out=gcols[:, c0:c0 + DT], in_=P[nm][l].rearrange("(t p) -> p t", p=128),
                    allow_slow_non_contiguous=True, writes=["gcols"])
        cF = 3 * DEPTH * DT
        S.dma("sp", out=gcols[:, cF:cF + DT],
                                          in_=P["final_norm"].rearrange("(t p) -> p t", p=128),
                                          allow_slow_non_contiguous=True, writes=["gcols"])
        S.barrier()

        def gcol(which, l):
            c0 = (which * DEPTH + l) * DT if which < 3 else cF
            return c0

        def load_h(hs, s, t0, n):
            S.dma("sp",
                out=hs[:, :, 0:n], in_=hT[s, :, t0:t0 + n].rearrange("(t p) n -> p t n", p=128),
                writes=["hs"])

        def store_h(hs, s, t0, n):
            S.dma("sp",
                out=hT[s, :, t0:t0 + n].rearrange("(t p) n -> p t n", p=128), in_=hs[:, :, 0:n],
                reads=["hs"])

        def rmsnorm(hs, xn, sqb, pss, rstd, c0, n, xn_name="xn", hs_name="hs"):
            for dt in range(DT):
                b = dt % 2
                S.op("act", "activation", out=sqb[b][:, 0:n], in_=hs[:, dt, 0:n],
                                                                 func=AF.Square,
                     reads=[hs_name], writes=[("sqb", b)])
                S.op("pe", "matmul", pss[:, 0:n], lhsT=ones_bf[:], rhs=sqb[b][:, 0:n],
                                                           start=(dt == 0), stop=(dt == DT - 1),
                     reads=[("sqb", b), "ones_bf"], writes=["pss"])
            S.op("act", "activation", out=rstd[:, 0:n], in_=pss[:, 0:n], func=AF.Sqrt,
                                               scale=1.0 / D, bias=eps_rms[:],
                 reads=["pss"], writes=["rstd"])
            S.op("dve", "reciprocal", out=rstd[:, 0:n], in_=rstd[:, 0:n],
                 reads=["rstd"], writes=["rstd"])
            for dt in range(DT):
                b = dt % 2
                S.op("dve", "tensor_tensor", out=rn_tmp[b][:, 0:n], in0=hs[:, dt, 0:n], in1=rstd[:, 0:n],
                     op=ALU.mult, reads=[hs_name, "rstd"], writes=[("rn_tmp", b)])
                S.op("act", "activation", out=xn[:, dt, 0:n], in_=rn_tmp[b][:, 0:n], func=AF.Identity,
                     scale=gcols[:, c0 + dt:c0 + dt + 1],
                     reads=[("rn_tmp", b), "gcols"], writes=[xn_name])

        def load_w_cast(dst_ap, src_ap, res):
            S.dma("pool", out=dst_ap, in_=src_ap, max_dma_last_dim=8192,
                  writes=[res])

        def phase_in():
            with ExitStack() as st:
                hs = sb("in_hs", [128, DT, 512], F32, st)
                xt = [sb("in_xt%d" % i, [128, D], F32, st) for i in range(2)]
                pt = [ps("in_pt%d" % i, [128, 512], F32, st) for i in range(2)]
                blk = 0
                for s in range(NS):
                    for (t0, n) in cfg.tiles:
                        for j in range((n + 127) // 128):
                            tb = t0 + j * 128
                            nb = min(128, t0 + n - tb)
                            xb = xt[blk % 2]
                            xr = ("xt", blk % 2)
                            if tb == 0:
                                S.dma("sp", out=xb[0:NMETA, :], in_=meta[:, :],
                                      writes=[xr])
                                S.dma("sp",
                                    out=xb[NMETA:nb, :], in_=x[s, 0:nb - NMETA, :], writes=[(xr, 1)],
                                    reads=[])
                                rd = [xr, (xr, 1)]
                            else:
                                S.dma("sp",
                                    out=xb[0:nb, :], in_=x[s, tb - NMETA:tb - NMETA + nb, :], writes=[xr, (xr, 1)])
                                rd = [xr, (xr, 1)]
                            for half in range(2):
                                pp = pt[half]
                                for q in range(4):
                                    dt = half * 4 + q
                                    S.op("pe", "transpose",
                                        out=pp[:, q * 128:q * 128 + nb], in_=xb[0:nb, dt * 128:(dt + 1) * 128],
                                        identity=ident_f[0:nb, 0:nb],
                                        reads=rd + ["ident_f"], writes=[("pt", half)])
                                eng = "act" if half == 0 else "dve"
                                if eng == "act":
                                    S.op("act", "activation",
                                        out=hs[:, half * 4:half * 4 + 4, j * 128:j * 128 + nb],
                                        in_=pp[:].rearrange("p (q c) -> p q c", q=4)[:, :, 0:nb], func=AF.Copy,
                                        reads=[("pt", half)], writes=["hs"])
                                else:
                                    S.op("dve", "tensor_copy",
                                        out=hs[:, half * 4:half * 4 + 4, j * 128:j * 128 + nb],
                                        in_=pp[:].rearrange("p (q c) -> p q c", q=4)[:, :, 0:nb],
                                        reads=[("pt", half)], writes=["hs"])
                            blk += 1
                        store_h(hs, s, t0, n)
                S.barrier()

        def phase_out():
            with ExitStack() as st:
                hs = sb("o_hs", [128, DT, 512], F32, st)
                xn = sb("o_xn", [128, DT, 512], F32, st)
                sqb = [sb("o_sq%d" % i, [128, 512], BF16, st) for i in range(2)]
                rstd = sb("o_rstd", [128, 512], F32, st)
                ot = [sb("o_ot%d" % i, [128, D], F32, st) for i in range(2)]
                pss = ps("o_pss", [128, 512], F32, st)
                pt = [ps("o_pt%d" % i, [128, 512], F32, st) for i in range(2)]
                blk = 0
                for s in range(NS):
                    for (t0, n) in cfg.tiles:
                        load_h(hs, s, t0, n)
                        rmsnorm(hs, xn, sqb, pss, rstd, cF, n)
                        for j in range((n + 127) // 128):
                            tb = t0 + j * 128
                            nb = min(128, t0 + n - tb)
                            ob = ot[blk % 2]
                            orr = ("ot", blk % 2)
                            for half in range(2):
                                pp = pt[half]
                                for q in range(4):
                                    dt = half * 4 + q
                                    S.op("pe", "transpose",
                                        out=pp[0:nb, q * 128:(q + 1) * 128], in_=xn[:, dt, j * 128:j * 128 + nb],
                                        identity=ident_f[:, :],
                                        reads=["xn", "ident_f"], writes=[("pt", half)])
                                if half == 0:
                                    S.op("act", "activation",
                                        out=ob[0:nb, 0:512], in_=pp[0:nb, :], func=AF.Copy,
                                        reads=[("pt", half)], writes=[orr])
                                else:
                                    S.op("dve", "tensor_copy",
                                        out=ob[0:nb, 512:1024], in_=pp[0:nb, :],
                                        reads=[("pt", half)], writes=[(orr, 1)])
                            lo = NMETA if tb == 0 else 0
                            S.dma("sp",
                                out=out[s, tb + lo - NMETA:tb + nb - NMETA, :], in_=ob[lo:nb, :],
                                reads=[orr, (orr, 1)])
                            blk += 1
                S.barrier()

        def phase_ffn(l, which):
            pre = "ffn1" if which == 0 else "ffn2"
            w13 = P[pre + "_w13"][l]
            w2 = P[pre + "_w2"][l]
            c0 = gcol(0 if which == 0 else 2, l)
            with ExitStack() as st:
                w13s = sb("f_w13", [128, DT, 2 * DFF], BF16, st)
                w2s = sb("f_w2", [128, FT, D], BF16, st)
                hsb = [sb("f_hs%d" % i, [128, DT, 512], F32, st) for i in range(2)]
                xn = sb("f_xn", [128, DT, 512], BF16, st)
                sqb = [sb("f_sq%d" % i, [128, 512], BF16, st) for i in range(2)]
                rstd = sb("f_rstd", [128, 512], F32, st)
                gh = sb("f_g", [128, FT, 512], BF16, st)
                sa = [sb("f_sa%d" % i, [128, 512], F32, st) for i in range(2)]
                pss = ps("f_pss", [128, 512], F32, st)
                pa = [ps("f_pa%d" % i, [128, 512], F32, st) for i in range(2)]
                pb = [ps("f_pb%d" % i, [128, 512], F32, st) for i in range(2)]
                po = [ps("f_po%d" % i, [128, 512], F32, st) for i in range(2)]
                for dt in range(DT):
                    load_w_cast(w13s[:, dt, :], w13[dt * 128:(dt + 1) * 128, :], ("w13", dt))
                for f in range(FT):
                    load_w_cast(w2s[:, f, :], w2[f * 128:(f + 1) * 128, :], ("w2", f))
                tiles_all = [(s, t0, n) for s in range(NS) for (t0, n) in cfg.tiles]

                def ld(idx):
                    s_, t0_, n_ = tiles_all[idx]
                    S.dma("sp", out=hsb[idx % 2][:, :, 0:n_],
                          in_=hT[s_, :, t0_:t0_ + n_].rearrange("(t p) n -> p t n", p=128), writes=[("hsb", idx % 2)])

                def nrm(idx):
                    s_, t0_, n_ = tiles_all[idx]
                    rmsnorm(hsb[idx % 2], xn, sqb, pss, rstd, c0, n_, hs_name=("hsb", idx % 2))

                ld(0)
                nrm(0)
                for idx, (s, t0, n) in enumerate(tiles_all):
                    if True:
                        hs = hsb[idx % 2]
                        hsr = ("hsb", idx % 2)
                        if idx + 1 < len(tiles_all):
                            ld(idx + 1)
                        for f in range(FT):
                            b = f % 2
                            for dt in range(DT):
                                S.op("pe", "matmul",
                                    pa[b][:, 0:n], lhsT=w13s[:, dt, f * 128:(f + 1) * 128], rhs=xn[:, dt, 0:n],
                                    start=(dt == 0), stop=(dt == DT - 1),
                                    reads=[("w13", dt), "xn"], writes=[("pa", b)])
                            for dt in range(DT):
                                S.op("pe", "matmul",
                                    pb[b][:, 0:n], lhsT=w13s[:, dt, DFF + f * 128:DFF + (f + 1) * 128],
                                    rhs=xn[:, dt, 0:n], start=(dt == 0), stop=(dt == DT - 1),
                                    reads=[("w13", dt), "xn"], writes=[("pb", b)])
                            S.op("act", "activation", out=sa[b][:, 0:n], in_=pa[b][:, 0:n],
                                                                    func=AF.Silu,
                                 reads=[("pa", b)], writes=[("sa", b)])
                            S.op("dve", "tensor_tensor",
                                out=gh[:, f, 0:n], in0=pb[b][:, 0:n], in1=sa[b][:, 0:n], op=ALU.mult,
                                reads=[("pb", b), ("sa", b)], writes=[("gh", f)])
                        if idx + 1 < len(tiles_all):
                            nrm(idx + 1)
                        for o in range(DT):
                            b = o % 2
                            for f in range(FT):
                                S.op("pe", "matmul",
                                    po[b][:, 0:n], lhsT=w2s[:, f, o * 128:(o + 1) * 128], rhs=gh[:, f, 0:n],
                                    start=(f == 0), stop=(f == FT - 1),
                                    reads=[("w2", f), ("gh", f)], writes=[("po", b)])
                            import os
                            if os.environ.get("DBG") == "po":
                                S.op("dve", "tensor_copy", out=hs[:, o, 0:n], in_=po[b][:, 0:n],
                                     reads=[("po", b), "hs"], writes=["hs"])
                            elif os.environ.get("DBG") == "xn":
                                S.op("dve", "tensor_copy", out=hs[:, o, 0:n], in_=xn[:, o, 0:n],
                                     reads=[("po", b), "hs", "xn"], writes=["hs"])
                            elif os.environ.get("DBG") == "gh":
                                S.op("dve", "tensor_copy", out=hs[:, o, 0:n], in_=gh[:, o, 0:n],
                                     reads=[("po", b), "hs", "xn", ("gh", o)], writes=["hs"])
                            else:
                              S.op("act", "activation", out=sa[b][:, 0:n], in_=po[b][:, 0:n], func=AF.Copy, scale=0.5,
                                   reads=[("po", b)], writes=[("sa", b)])
                              S.op("dve", "tensor_tensor", out=hs[:, o, 0:n], in0=sa[b][:, 0:n], in1=hs[:, o, 0:n],
                                   op=ALU.add, reads=[("sa", b), hsr], writes=[hsr])
                        S.dma("sp", out=hT[s, :, t0:t0 + n].rearrange("(t p) n -> p t n", p=128), in_=hs[:, :, 0:n],
                              reads=[hsr])
                S.barrier()


        def phase_mixA(l):
            w_in = P["w_in"][l]
            c0 = gcol(1, l)
            with ExitStack() as st:
                wA = sb("a_w", [128, DT, 2048], BF16, st)
                hs = sb("a_hs", [128, DT, 512], F32, st)
                xn = sb("a_xn", [128, DT, 512], BF16, st)
                sqb = [sb("a_sq%d" % i, [128, 512], BF16, st) for i in range(2)]
                rstd = sb("a_rstd", [128, 512], F32, st)
                stg = [sb("a_stg%d" % i, [128, 512], BF16, st) for i in range(4)]
                pss = ps("a_pss", [128, 512], F32, st)
                pp = [ps("a_pp%d" % i, [128, 512], F32, st) for i in range(4)]
                for dt in range(DT):
                    load_w_cast(wA[:, dt, 0:512], w_in[dt * 128:(dt + 1) * 128, 0:512], ("wA", dt))
                    load_w_cast(wA[:, dt, 512:2048], w_in[dt * 128:(dt + 1) * 128, 1536:3072], ("wA", dt, 1))
                cnt = 0
                for s in range(NS):
                    for (t0, n) in cfg.tiles:
                        load_h(hs, s, t0, n)
                        rmsnorm(hs, xn, sqb, pss, rstd, c0, n)
                        nk = n // 8
                        k0 = t0 // 8
                        for f in range(12):
                            b = cnt % 4
                            cnt += 1
                            pb = pp[b]
                            sg = stg[b]
                            for dt in range(DT):
                                S.op("pe", "matmul", pb[:, 0:n], lhsT=wA[:, dt, f * 128:(f + 1) * 128],
                                     rhs=xn[:, dt, 0:n], start=(dt == 0), stop=(dt == DT - 1),
                                     reads=[("wA", dt), ("wA", dt, 1), "xn"], writes=[("pp", b)])
                            if f < 4:
                                S.op("act", "activation",
                                     out=sg[:, 0:n].rearrange("p (s k) -> p s k", s=8),
                                     in_=pb[:, 0:n].rearrange("p (k s) -> p s k", s=8), func=AF.Copy,
                                     reads=[("pp", b)], writes=[("stg", b)])
                                for gl in range(8):
                                    S.dma("sp", out=Usc[s, f * 8 + gl, :, :, k0:k0 + nk].rearrange("s c k -> c s k"),
                                          in_=sg[gl * 16:(gl + 1) * 16, 0:n].rearrange("p (s k) -> p s k", s=8),
                                          reads=[("stg", b)])
                            else:
                                if f % 2 == 0:
                                    S.op("act", "activation", out=sg[:, 0:n], in_=pb[:, 0:n], func=AF.Copy,
                                         reads=[("pp", b)], writes=[("stg", b)])
                                else:
                                    S.op("dve", "tensor_copy", out=sg[:, 0:n], in_=pb[:, 0:n],
                                         reads=[("pp", b)], writes=[("stg", b)])
                                dst = qT if f < 8 else kT
                                r0 = (f % 4) * 128
                                S.dma("sp", out=dst[s, r0:r0 + 128, t0:t0 + n], in_=sg[:, 0:n],
                                      reads=[("stg", b)])
                        for j in range((n + 127) // 128):
                            nb = min(128, n - j * 128)
                            b = cnt % 4
                            cnt += 1
                            pb = pp[b]
                            sg = stg[b]
                            for dt in range(DT):
                                S.op("pe", "matmul", pb[0:nb, 0:512], lhsT=xn[:, dt, j * 128:j * 128 + nb],
                                     rhs=wA[:, dt, 1536:2048], start=(dt == 0), stop=(dt == DT - 1),
                                     reads=[("wA", dt), ("wA", dt, 1), "xn"], writes=[("pp", b)])
                            S.op("dve", "tensor_copy", out=sg[0:nb, :], in_=pb[0:nb, :],
                                 reads=[("pp", b)], writes=[("stg", b)])
                            S.dma("sp", out=vS[s, t0 + j * 128:t0 + j * 128 + nb, :], in_=sg[0:nb, :],
                                  reads=[("stg", b)])
                S.barrier()

        def phase_att(l):
            nkb, LP = cfg.nkb, cfg.LP
            tail = L - (nkb - 1) * 128
            with ExitStack() as st:
                triT = sb("t_tri", [128, 128], BF16, st)
                sel0 = sb("t_sel0", [128, 128], BF16, st)
                onec = sb("t_onec", [128, 1], F32, st)
                masks = [sb("t_mask%d" % i, [128, 512], F32, st) for i in range(4)]
                kT2 = [sb("t_k%d" % i, [128, LP], BF16, st) for i in range(2)]
                qz = [[sb("t_q%d_%d" % (i, h), [128, L], BF16, st) for h in range(2)] for i in range(2)]
                vz = [[sb("t_v%d_%d" % (i, h), [128, nkb, 128], BF16, st) for h in range(2)] for i in range(2)]
                NB3 = 3
                e_t = [sb("t_e%d" % i, [128, 2, 512], F32, st) for i in range(NB3)]
                sp_t = [sb("t_sp%d" % i, [128, 2, 512], BF16, st) for i in range(NB3)]
                g_t = [sb("t_g%d" % i, [128, 2, 512], F32, st) for i in range(NB3)]
                w_t = [sb("t_w%d" % i, [128, 2, 512], BF16, st) for i in range(NB3)]
                Rs = sb("t_Rs", [128, 2, 512], BF16, st)
                ob = [sb("t_ob%d" % i, [128, 512], BF16, st) for i in range(2)]
                pz = [ps("t_pz%d" % i, [128, 2, 512], F32, st) for i in range(2)]
                pcs = ps("t_pcs", [128, 2, 512], F32, st)
                po = [ps("t_po%d" % i, [128, 512], F32, st) for i in range(2)]
                S.op("pool", "memset", triT[:], 1.0, writes=["triT"])
                S.op("pool", "affine_select", out=triT[:], in_=triT[:], pattern=[[-1, 128]],
                     compare_op=ALU.is_ge, fill=0.0, base=0, channel_multiplier=1,
                     reads=["triT"], writes=["triT"])
                S.op("pool", "memset", sel0[:], 1.0, writes=["sel0"])
                S.op("pool", "affine_select", out=sel0[:], in_=sel0[:], pattern=[[0, 128]],
                     compare_op=ALU.is_equal, fill=0.0, base=0, channel_multiplier=1,
                     reads=["sel0"], writes=["sel0"])
                S.op("pool", "memset", onec[:], 1.0, writes=["onec"])
                for i in range(4):
                    S.op("pool", "memset", masks[i][:], 1.0, writes=[("mask", i)])
                    S.op("pool", "affine_select", out=masks[i][:], in_=masks[i][:], pattern=[[1, 512]],
                         compare_op=ALU.is_gt, fill=0.0, base=-128 * i, channel_multiplier=-1,
                         reads=[("mask", i)], writes=[("mask", i)])
                for i in range(2):
                    S.op("pool", "memset", kT2[i][:], 0.0, writes=[("kT2", i)])
                    for h in range(2):
                        S.op("pool", "memset", qz[i][h][:], 0.0, writes=[("qz", i, h)])
                        S.op("pool", "memset", vz[i][h][:], 0.0, writes=[("vz", i, h), ("vz", i, h, 1)])
                blocks = []
                it = 0
                qcnt = 0
                for s in range(NS):
                    for hp in range(4):
                        bb = it % 2
                        it += 1
                        first_of_load = True
                        for (q0, nq) in cfg.tiles:
                            qb = qcnt % 2
                            qcnt += 1
                            nblk = max((q0 + nq - 1 + 127) // 128, 1)
                            for bi, kb in enumerate(reversed(range(nblk))):
                                blocks.append(dict(s=s, hp=hp, bb=bb, q0=q0, nq=nq, qb=qb, bi=bi, kb=kb, nblk=nblk,
                                                   load=first_of_load))
                                first_of_load = False
                for i, B_ in enumerate(blocks):
                    B_["i3"] = i % NB3
                    B_["i2"] = i % 2

                def load_inputs(B_):
                    s, hp, bb = B_["s"], B_["hp"], B_["bb"]
                    S.dma("sp", out=kT2[bb][:, 0:L], in_=kT[s, hp * 128:(hp + 1) * 128, :],
                          reads=[], writes=[("kT2", bb)])
                    for h in range(2):
                        r0 = hp * 128 + h * 64
                        S.dma("sp", out=qz[bb][h][h * 64:(h + 1) * 64, :], in_=qT[s, r0:r0 + 64, :],
                              writes=[("qz", bb, h)])
                        if nkb > 1:
                            S.dma("sp", out=vz[bb][h][:, 0:nkb - 1, h * 64:(h + 1) * 64],
                                  in_=vS[s, 0:(nkb - 1) * 128, r0:r0 + 64].rearrange("(b p) d -> p b d", p=128),
                                  writes=[("vz", bb, h)])
                        S.dma("sp", out=vz[bb][h][0:tail, nkb - 1, h * 64:(h + 1) * 64],
                              in_=vS[s, (nkb - 1) * 128:L, r0:r0 + 64],
                              writes=[("vz", bb, h, 1)])

                def stageA(B_):
                    bb, q0, nq, kb, i3, i2 = (B_[k] for k in ("bb", "q0", "nq", "kb", "i3", "i2"))
                    if B_["load"]:
                        load_inputs(B_)
                    diag = (kb * 128 + 128 > q0)
                    for h in range(2):
                        S.op("pe", "matmul", pz[i2][:, h, 0:nq], lhsT=kT2[bb][:, kb * 128:(kb + 1) * 128],
                             rhs=qz[bb][h][:, q0:q0 + nq], start=True, stop=True,
                             reads=[("kT2", bb), ("qz", bb, h)], writes=[("pz", i2)])
                    S.op("act", "activation", out=e_t[i3][:, :, 0:nq], in_=pz[i2][:, :, 0:nq],
                         func=AF.Exp, scale=0.125,
                         reads=[("pz", i2)], writes=[("e", i3)])
                    if diag:
                        mi = (kb * 128 - q0) // 128
                        assert 0 <= mi < 4
                        for h in range(2):
                            S.op("dve", "tensor_tensor", out=e_t[i3][:, h, 0:nq], in0=e_t[i3][:, h, 0:nq],
                                 in1=masks[mi][:, 0:nq], op=ALU.mult,
                                 reads=[("e", i3), ("mask", mi)], writes=[("e", i3)])
                    S.op("act", "activation", out=sp_t[i3][:, :, 0:nq], in_=e_t[i3][:, :, 0:nq],
                         func=AF.Ln, bias=onec[:], scale=1.0,
                         reads=[("e", i3), "onec"], writes=[("sp", i3)])

                def stageB(B_):
                    nq, bi, nblk, i3 = (B_[k] for k in ("nq", "bi", "nblk", "i3"))
                    for h in range(2):
                        S.op("pe", "matmul", pcs[:, h, 0:nq], lhsT=triT[:], rhs=sp_t[i3][:, h, 0:nq],
                             start=True, stop=(bi == 0),
                             reads=["triT", ("sp", i3)], writes=["pcs"])
                        if bi > 0:
                            S.op("pe", "matmul", pcs[:, h, 0:nq], lhsT=ones_bf[:], rhs=Rs[:, h, 0:nq],
                                 start=False, stop=True,
                                 reads=["ones_bf", "Rs"], writes=["pcs"])
                    if bi < nblk - 1:
                        if bi == 0:
                            S.op("pool", "tensor_copy", out=Rs[:, :, 0:nq], in_=sp_t[i3][:, :, 0:nq],
                                 reads=[("sp", i3)], writes=["Rs"])
                        else:
                            S.op("pool", "tensor_tensor", out=Rs[:, :, 0:nq], in0=Rs[:, :, 0:nq],
                                 in1=sp_t[i3][:, :, 0:nq], op=ALU.add,
                                 reads=[("sp", i3), "Rs"], writes=["Rs"])
                    S.op("act", "activation", out=g_t[i3][:, :, 0:nq], in_=pcs[:, :, 0:nq],
                         func=AF.Exp, scale=-1.0,
                         reads=["pcs"], writes=[("g", i3)])
                    S.op("dve", "tensor_tensor", out=w_t[i3][:, :, 0:nq], in0=e_t[i3][:, :, 0:nq],
                         in1=g_t[i3][:, :, 0:nq], op=ALU.mult,
                         reads=[("e", i3), ("g", i3)], writes=[("w", i3)])

                def stageC(B_):
                    s, hp, bb, q0, nq, kb, i3, qb, bi, nblk = (B_[k] for k in ("s", "hp", "bb", "q0", "nq", "kb", "i3", "qb", "bi", "nblk"))
                    for h in range(2):
                        S.op("pe", "matmul", po[qb][:, 0:nq], lhsT=vz[bb][h][:, kb, :],
                             rhs=w_t[i3][:, h, 0:nq], start=(bi == 0 and h == 0), stop=(bi == nblk - 1 and h == 1),
                             reads=[("vz", bb, h), ("vz", bb, h, 1), ("w", i3)], writes=[("po", qb)])
                    if bi == nblk - 1:
                        S.op("act", "activation", out=ob[qb][:, 0:nq], in_=po[qb][:, 0:nq], func=AF.Copy,
                             reads=[("po", qb)], writes=[("ob", qb)])
                        S.dma("sp", out=oattT[s, hp * 128:(hp + 1) * 128, q0:q0 + nq], in_=ob[qb][:, 0:nq],
                              reads=[("ob", qb)])

                NBLK = len(blocks)
                for i in range(NBLK + 2):
                    if i < NBLK:
                        stageA(blocks[i])
                    if 0 <= i - 1 < NBLK:
                        stageB(blocks[i - 1])
                    if 0 <= i - 2 < NBLK:
                        stageC(blocks[i - 2])
                S.barrier()

        def phase_ssm(l):
            NKc = cfg.NK
            if NKc <= 512:
                halves = [(0, NKc)]
            else:
                halves = [(0, NKc // 2), (NKc // 2, NKc - NKc // 2)]
            TWO_PI = 2.0 * math.pi
            with ExitStack() as st:
                stp = ExitStack()

                def t32(name, shape=(128, 32), dt=F32, stack=None):
                    return sb("s_" + name, list(shape), dt, st if stack is None else stack)

                def t32p(name, shape=(128, 32), dt=F32):
                    return t32(name, shape, dt, stp)
                W1, W1s, W2, W2s, W3 = (t32("W%d" % i, (128, 32, 128), BF16) for i in range(5))
                rho8, f8 = t32("rho8"), t32("f8")
                kidx = t32("kidx", (128, NKc))
                ki = t32("ki", (128, NKc), I32)
                kidx_i = ki
                pw = [ps("s_pw%d" % i, [128, 512], F32, st) for i in range(2)]
                pL = ps("s_pL", [128, 2, 512], F32, st)
                pLs = ps("s_pLs", [128, 2, 512], F32, st)
                py = ps("s_py", [128, 2, 512], F32, st)
                lr, li, dtb, ar, ft = t32p("lr"), t32p("li"), t32p("dtb"), t32p("ar"), t32p("ft")
                yp, ti, tf, tt, sn, cs, mag = (t32p("yp"), t32p("ti", dt=I32), t32p("tf"), t32p("tt"),
                                               t32p("sn"), t32p("cs"), t32p("mag"))
                pwr = {p: t32p("pwr%d" % (p + 7)) for p in range(-7, 9)}
                pwi = {p: t32p("pwi%d" % (p + 7)) for p in range(-7, 9)}
                cre, cim, t_a, t_b, dcol = (t32p("cre"), t32p("cim"), t32p("ta"), t32p("tb"), t32p("dcol"))
                Bre, Bim = t32p("Bre", (128, 32, 16)), t32p("Bim", (128, 32, 16))
                bbr, bbi = t32p("bbr", (128, 32, 16)), t32p("bbi", (128, 32, 16))
                Craw = [t32p("Craw%d" % i, (128, 4, 128)) for i in range(2)]
                Cn = [t32p("Cn%d" % i, (128, 32, 16)) for i in range(2)]
                q1, q2, q3, q4 = (t32p("q%d" % i, (128, 32, 16)) for i in range(4))
                X1, X1s, X2, X2s, X1p, X2p = (t32p("X%d" % i, (128, 32, 8, 16)) for i in range(6))
                maskBL = t32p("maskBL", (128, 128))
                w3t = t32p("w3t", (128, 128))
                w3r = [t32p("w3r%d" % i, (128, 128)) for i in range(2)]

                for hf in (0, 64):
                    S.dma("sp", out=lr[hf:hf + 64, :], in_=P["ssm_lam_re"][l].rearrange("g n -> n g"),
                          allow_slow_non_contiguous=True, writes=["lr"])
                    S.dma("sp", out=li[hf:hf + 64, :], in_=P["ssm_lam_im"][l].rearrange("g n -> n g"),
                          allow_slow_non_contiguous=True, writes=["li"])
                    S.dma("sp", out=dtb[hf:hf + 64, :], in_=P["ssm_log_dt"][l].partition_broadcast(64),
                          writes=["dtb"])
                    S.dma("sp", out=Bre[hf:hf + 64, :, :], in_=P["ssm_b_re"][l].rearrange("g n c -> n g c"),
                          writes=["Bre"])
                    S.dma("sp", out=Bim[hf:hf + 64, :, :], in_=P["ssm_b_im"][l].rearrange("g n c -> n g c"),
                          writes=["Bim"])
                for i, nm in enumerate(["ssm_c_re", "ssm_c_im"]):
                    for dup in range(2):
                        S.dma("sp", out=Craw[i][:, :, dup * 64:(dup + 1) * 64],
                              in_=P[nm][l].rearrange("g c n -> (g c) n").rearrange("(t p) n -> p t n", p=128),
                              writes=[("Craw", i)])
                for s8 in range(8):
                    S.dma("sp", out=dcol[s8 * 16:(s8 + 1) * 16, :], in_=P["ssm_d"][l].rearrange("(g c) -> c g", c=16),
                          allow_slow_non_contiguous=True, writes=["dcol"])
                S.op("pool", "memset", maskBL[:], 1.0, writes=["maskBL"])
                S.op("pool", "affine_select", out=maskBL[:].rearrange("p (j c) -> p j c", c=16),
                     in_=maskBL[:].rearrange("p (j c) -> p j c", c=16), pattern=[[16, 8], [0, 16]],
                     compare_op=ALU.is_ge, fill=0.0, base=15, channel_multiplier=-1,
                     reads=["maskBL"], writes=["maskBL"])
                S.op("pool", "iota", kidx_i[:], pattern=[[1, NKc]], base=0, channel_multiplier=0,
                     writes=["kti"])
                S.op("pool", "tensor_copy", out=kidx[:], in_=kidx_i[:], reads=["kti"], writes=["kidx"])

                if int(os.environ.get("SSMSTOP", "9")) <= 1:
                    S.barrier()
                    return
                def V(eng, meth, *a, r=(), w=(), **kw):
                    S.op(eng, meth, *a, reads=list(r), writes=list(w), **kw)

                def sin_turns(dst, dname, y, yname, ti_, tf_, tt_, pre):
                    V("dve", "tensor_copy", out=ti_, in_=y, r=[yname], w=[pre + "ti"])
                    V("dve", "tensor_copy", out=tf_, in_=ti_, r=[pre + "ti"], w=[pre + "tf"])
                    V("dve", "tensor_tensor", out=tf_, in0=y, in1=tf_, op=ALU.subtract, r=[yname, pre + "tf"], w=[pre + "tf"])
                    V("dve", "tensor_single_scalar", out=tt_, in_=tf_, scalar=0.5, op=ALU.is_gt, r=[pre + "tf"], w=[pre + "tt"])
                    V("dve", "tensor_tensor", out=tf_, in0=tf_, in1=tt_, op=ALU.subtract, r=[pre + "tf", pre + "tt"], w=[pre + "tf"])
                    V("dve", "tensor_single_scalar", out=tt_, in_=tf_, scalar=-0.5, op=ALU.is_lt, r=[pre + "tf"], w=[pre + "tt"])
                    V("dve", "tensor_tensor", out=tf_, in0=tf_, in1=tt_, op=ALU.add, r=[pre + "tf", pre + "tt"], w=[pre + "tf"])
                    if dst is not None:
                        V("act", "activation", out=dst, in_=tf_, func=AF.Sin, scale=6.283185, r=[pre + "tf"], w=[dname])

                V("act", "activation", out=dtb[:], in_=dtb[:], func=AF.Exp, r=["dtb"], w=["dtb"])
                V("dve", "tensor_tensor", out=ar[:], in0=lr[:], in1=dtb[:], op=ALU.mult, r=["lr", "dtb"], w=["ar"])
                V("dve", "tensor_tensor", out=ft[:], in0=li[:], in1=dtb[:], op=ALU.mult, r=["li", "dtb"], w=["ft"])
                V("dve", "tensor_scalar_mul", out=ft[:], in0=ft[:], scalar1=1.0 / TWO_PI, r=["ft"], w=["ft"])
                for p in range(-7, 9):
                    V("act", "activation", out=mag[:], in_=ar[:], func=AF.Exp, scale=float(p), r=["ar"], w=["mag"])
                    if p == 8:
                        V("dve", "tensor_copy", out=rho8[:], in_=mag[:], r=["mag"], w=["rho8"])
                    V("dve", "tensor_scalar_mul", out=yp[:], in0=ft[:], scalar1=float(p), r=["ft"], w=["yp"])
                    sin_turns(sn[:], "sn", yp[:], "yp", ti[:], tf[:], tt[:], "a")
                    if p == 8:
                        V("dve", "tensor_copy", out=f8[:], in_=tf[:], r=["atf"], w=["f8"])
                    V("dve", "tensor_scalar_add", out=yp[:], in0=yp[:], scalar1=0.25, r=["yp"], w=["yp"])
                    sin_turns(cs[:], "cs", yp[:], "yp", ti[:], tf[:], tt[:], "a")
                    V("dve", "tensor_tensor", out=pwr[p][:], in0=mag[:], in1=cs[:], op=ALU.mult, r=["mag", "cs"], w=[("pwr", p)])
                    V("dve", "tensor_tensor", out=pwi[p][:], in0=mag[:], in1=sn[:], op=ALU.mult, r=["mag", "sn"], w=[("pwi", p)])
                if int(os.environ.get("SSMSTOP", "9")) <= 2:
                    S.barrier()
                    return
                V("dve", "tensor_scalar_add", out=t_a[:], in0=pwr[1][:], scalar1=-1.0, r=[("pwr", 1)], w=["ta"])
                V("dve", "tensor_tensor", out=cre[:], in0=t_a[:], in1=lr[:], op=ALU.mult, r=["ta", "lr"], w=["cre"])
                V("dve", "tensor_tensor", out=t_b[:], in0=pwi[1][:], in1=li[:], op=ALU.mult, r=[("pwi", 1), "li"], w=["tb"])
                V("dve", "tensor_tensor", out=cre[:], in0=cre[:], in1=t_b[:], op=ALU.add, r=["cre", "tb"], w=["cre"])
                V("dve", "tensor_tensor", out=cim[:], in0=pwi[1][:], in1=lr[:], op=ALU.mult, r=[("pwi", 1), "lr"], w=["cim"])
                V("dve", "tensor_tensor", out=t_b[:], in0=t_a[:], in1=li[:], op=ALU.mult, r=["ta", "li"], w=["tb"])
                V("dve", "tensor_tensor", out=cim[:], in0=cim[:], in1=t_b[:], op=ALU.subtract, r=["cim", "tb"], w=["cim"])
                V("dve", "tensor_tensor", out=t_a[:], in0=lr[:], in1=lr[:], op=ALU.mult, r=["lr"], w=["ta"])
                V("dve", "tensor_tensor", out=t_b[:], in0=li[:], in1=li[:], op=ALU.mult, r=["li"], w=["tb"])
                V("dve", "tensor_tensor", out=t_a[:], in0=t_a[:], in1=t_b[:], op=ALU.add, r=["ta", "tb"], w=["ta"])
                V("dve", "reciprocal", out=t_a[:], in_=t_a[:], r=["ta"], w=["ta"])
                V("dve", "tensor_tensor", out=cre[:], in0=cre[:], in1=t_a[:], op=ALU.mult, r=["cre", "ta"], w=["cre"])
                V("dve", "tensor_tensor", out=cim[:], in0=cim[:], in1=t_a[:], op=ALU.mult, r=["cim", "ta"], w=["cim"])

                def bc(t):
                    return t[:, :].unsqueeze(2).broadcast_to([128, 32, 16])

                def cplx(fr, fi, frn, fin, xr, xi, xrn, xin):
                    V("dve", "tensor_tensor", out=q1[:], in0=xr[:], in1=bc(fr), op=ALU.mult, r=[xrn, frn, "q1"], w=["q1"])
                    V("dve", "tensor_tensor", out=q3[:], in0=xi[:], in1=bc(fi), op=ALU.mult, r=[xin, fin, "q3"], w=["q3"])
                    V("dve", "tensor_tensor", out=q1[:], in0=q1[:], in1=q3[:], op=ALU.subtract, r=["q1", "q3"], w=["q1"])
                    V("dve", "tensor_tensor", out=q2[:], in0=xi[:], in1=bc(fr), op=ALU.mult, r=[xin, frn, "q2"], w=["q2"])
                    V("dve", "tensor_tensor", out=q4[:], in0=xr[:], in1=bc(fi), op=ALU.mult, r=[xrn, fin, "q4"], w=["q4"])
                    V("dve", "tensor_tensor", out=q2[:], in0=q2[:], in1=q4[:], op=ALU.add, r=["q2", "q4"], w=["q2"])

                def put(dst, dname, blk, lo_src, lo_sign, up_src, up_sign):
                    for (rows, src, sign, eng) in ((slice(0, 64), lo_src, lo_sign, "act"),
                                                   (slice(64, 128), up_src, up_sign, "pool")):
                        srcn = "q1" if src is q1 else "q2"
                        if eng == "act":
                            V("act", "activation", out=dst[rows, :, blk, :], in_=src[rows, :, :], func=AF.Copy,
                              scale=float(sign), r=[srcn], w=[dname])
                        else:
                            V("pool", "tensor_scalar", out=dst[rows, :, blk, :], in0=src[rows, :, :],
                              scalar1=float(sign), scalar2=None, op0=ALU.mult, r=[srcn], w=[dname])

                if int(os.environ.get("SSMSTOP", "9")) <= 3:
                    S.barrier()
                    return
                cplx(cre, cim, "cre", "cim", Bre, Bim, "Bre", "Bim")
                V("dve", "tensor_copy", out=bbr[:], in_=q1[:], r=["q1"], w=["bbr"])
                V("dve", "tensor_copy", out=bbi[:], in_=q2[:], r=["q2"], w=["bbi"])
                for i in range(2):
                    for t4 in range(4):
                        pb = pw[(i * 4 + t4) % 2]
                        V("pe", "transpose", out=pb[:, 0:128], in_=Craw[i][:, t4, :], identity=ident_f[:, :],
                          r=[("Craw", i), "ident_f"], w=[("pw", (i * 4 + t4) % 2)])
                        V("act", "activation", out=Cn[i][:, t4 * 8:(t4 + 1) * 8, :],
                          in_=pb[:, 0:128].rearrange("p (g c) -> p g c", c=16), func=AF.Copy,
                          r=[("pw", (i * 4 + t4) % 2)], w=[("Cn", i)])
                for s8 in range(8):
                    p = 7 - s8
                    cplx(pwr[p], pwi[p], ("pwr", p), ("pwi", p), bbr, bbi, "bbr", "bbi")
                    put(X1, "X1", s8, q1, 1, q2, 1)
                    put(X1s, "X1s", s8, q2, 1, q1, -1)
                    p = -s8
                    cplx(pwr[p], pwi[p], ("pwr", p), ("pwi", p), bbr, bbi, "bbr", "bbi")
                    put(X1p, "X1p", s8, q1, 1, q2, 1)
                    p = s8 + 1
                    cplx(pwr[p], pwi[p], ("pwr", p), ("pwi", p), Cn[0], Cn[1], ("Cn", 0), ("Cn", 1))
                    put(X2, "X2", s8, q1, 1, q2, -1)
                    put(X2s, "X2s", s8, q2, -1, q1, -1)
                    p = s8
                    cplx(pwr[p], pwi[p], ("pwr", p), ("pwi", p), Cn[0], Cn[1], ("Cn", 0), ("Cn", 1))
                    put(X2p, "X2p", s8, q1, 1, q2, -1)
                if int(os.environ.get("SSMSTOP", "9")) <= 4:
                    S.barrier()
                    return
                V("act", "activation", out=W2[:].rearrange("p g x -> p (g x)"),
                  in_=X2[:].rearrange("p g s c -> p (g s c)"), func=AF.Copy, r=["X2"], w=["W2"])
                V("dve", "tensor_copy", out=W2s[:].rearrange("p g x -> p (g x)"),
                  in_=X2s[:].rearrange("p g s c -> p (g s c)"), r=["X2s"], w=["W2s"])
                for g in range(NG):
                    b = g % 2
                    V("pe", "transpose", out=pw[b][:, 0:128], in_=X1[:, g].rearrange("p s c -> p (s c)"),
                      identity=ident_f[:, :], r=["X1", "ident_f"], w=[("pw", b)])
                    V("pe", "transpose", out=pw[b][:, 128:256], in_=X1s[:, g].rearrange("p s c -> p (s c)"),
                      identity=ident_f[:, :], r=["X1s", "ident_f"], w=[("pw", b)])
                    V("pe", "matmul", pw[b][:, 256:384], lhsT=X1p[:, g].rearrange("p s c -> p (s c)"),
                      rhs=X2p[:, g].rearrange("p s c -> p (s c)"), start=True, stop=True,
                      r=["X1p", "X2p"], w=[("pw", b)])
                    V("act", "activation", out=W1[:, g, :], in_=pw[b][:, 0:128], func=AF.Copy,
                      r=[("pw", b)], w=["W1"])
                    V("act", "activation", out=W1s[:, g, :], in_=pw[b][:, 128:256], func=AF.Copy,
                      r=[("pw", b)], w=["W1s"])
                    V("act", "activation", out=w3r[b][:], in_=pw[b][:, 256:384], func=AF.Copy,
                      r=[("pw", b)], w=[("w3r", b)])
                    V("dve", "tensor_tensor", out=w3t[:], in0=w3r[b][:], in1=maskBL[:], op=ALU.mult,
                      r=[("w3r", b), "maskBL"], w=["w3t"])
                    V("dve", "scalar_tensor_tensor", out=W3[:, g, :], in0=ident_f[:], scalar=dcol[:, g:g + 1],
                      in1=w3t[:], op0=ALU.mult, op1=ALU.add, r=["ident_f", "dcol", "w3t"], w=["W3"])

                if int(os.environ.get("SSMSTOP", "9")) <= 5:
                    S.barrier()
                    return
                S.barrier()
                stp.close()
                cT = [t32("cT%d" % i, (128, NKc)) for i in range(2)]
                sT = [t32("sT%d" % i, (128, NKc)) for i in range(2)]
                rtab = [t32("rtab%d" % i, (128, NKc)) for i in range(2)]
                yk = t32("yk", (128, NKc))
                kf = t32("kf", (128, NKc))
                kt = t32("kt", (128, NKc))
                Ut = [t32("U%d" % i, (128, NKc), BF16) for i in range(2)]
                m1, m2, Tt = t32("m1", (128, NKc)), t32("m2", (128, NKc)), t32("Tt", (128, NKc))
                Pm = [t32("Pm%d" % i, (128, NKc + 2), BF16) for i in range(2)]
                Qm = [t32("Qm%d" % i, (128, NKc + 2), BF16) for i in range(2)]
                yo = [t32("yo%d" % i, (128, NKc)) for i in range(2)]
                for i in range(2):
                    S.op("pool", "memset", Pm[i][:], 0.0, writes=[("Pm", i)])
                    S.op("pool", "memset", Qm[i][:], 0.0, writes=[("Qm", i)])
                it = 0
                for g in range(NG):
                    gb = g % 2
                    V("dve", "tensor_scalar_mul", out=yk[:], in0=kidx[:], scalar1=f8[:, g:g + 1],
                      r=["kidx", "f8"], w=["yk"])
                    sin_turns(sT[gb][:], ("sT", gb), yk[:], "yk", ki[:], kf[:], kt[:], "k")
                    V("dve", "tensor_scalar_add", out=yk[:], in0=yk[:], scalar1=0.25, r=["yk"], w=["yk"])
                    sin_turns(cT[gb][:], ("cT", gb), yk[:], "yk", ki[:], kf[:], kt[:], "k")
                    V("act", "activation", out=rtab[gb][:], in_=kidx[:], func=AF.Identity, scale=0.0,
                      bias=rho8[:, g:g + 1], r=["kidx", "rho8"], w=[("rtab", gb)])
                    RL = int(os.environ.get("RUNLVL", "9"))
                    for s in range(NS):
                        if RL < 2:
                            continue
                        ub = it % 2
                        it += 1
                        S.dma("sp", out=Ut[ub][:], in_=Usc[s, g].rearrange("s c k -> (s c) k"),
                              writes=[("U", ub)])
                        for hi, (lo, wd) in enumerate(halves):
                            V("pe", "matmul", pL[:, hi, 0:wd], lhsT=W1[:, g, :], rhs=Ut[ub][:, lo:lo + wd],
                              start=True, stop=True, r=["W1", ("U", ub)], w=["pL"])
                            V("pe", "matmul", pLs[:, hi, 0:wd], lhsT=W1s[:, g, :], rhs=Ut[ub][:, lo:lo + wd],
                              start=True, stop=True, r=["W1s", ("U", ub)], w=["pLs"])
                        if RL < 3:
                            continue
                        for hi, (lo, wd) in enumerate(halves):
                            V("dve", "tensor_tensor", out=m1[:, lo:lo + wd], in0=pL[:, hi, 0:wd],
                              in1=cT[gb][:, lo:lo + wd], op=ALU.mult, r=["pL", ("cT", gb)], w=["m1"])
                            V("dve", "tensor_tensor", out=m2[:, lo:lo + wd], in0=pLs[:, hi, 0:wd],
                              in1=sT[gb][:, lo:lo + wd], op=ALU.mult, r=["pLs", ("sT", gb)], w=["m2"])
                        V("dve", "tensor_tensor", out=m1[:], in0=m1[:], in1=m2[:], op=ALU.add,
                          r=["m1", "m2"], w=["m1"])
                        V("dve", "tensor_tensor_scan", out=Tt[:], data0=rtab[gb][:], data1=m1[:], initial=0.0,
                          op0=ALU.mult, op1=ALU.add, r=[("rtab", gb), "m1"], w=["Tt"])
                        V("dve", "tensor_tensor", out=Pm[ub][:, 1:NKc + 1], in0=Tt[:], in1=cT[gb][:], op=ALU.mult,
                          r=["Tt", ("cT", gb)], w=[("Pm", ub)])
                        V("pool", "tensor_tensor", out=Qm[ub][:, 1:NKc + 1], in0=Tt[:], in1=sT[gb][:], op=ALU.mult,
                          r=["Tt", ("sT", gb)], w=[("Qm", ub)])
                        if RL < 4:
                            continue
                        for hi, (lo, wd) in enumerate(halves):
                            V("pe", "matmul", py[:, hi, 0:wd], lhsT=W3[:, g, :], rhs=Ut[ub][:, lo:lo + wd],
                              start=True, stop=False, r=["W3", ("U", ub)], w=["py"])
                            V("pe", "matmul", py[:, hi, 0:wd], lhsT=W2[:, g, :],
                              rhs=Pm[ub][:, lo:lo + wd], start=False, stop=False,
                              r=["W2", ("Pm", ub)], w=["py"])
                            V("pe", "matmul", py[:, hi, 0:wd], lhsT=W2s[:, g, :],
                              rhs=Qm[ub][:, lo:lo + wd], start=False, stop=True,
                              r=["W2s", ("Qm", ub)], w=["py"])
                        if RL < 5:
                            continue
                        for hi, (lo, wd) in enumerate(halves):
                            V("act", "activation", out=yo[ub][:, lo:lo + wd], in_=py[:, hi, 0:wd], func=AF.Copy,
                              r=["py"], w=[("yo", ub)])
                        S.dma("sp", out=ysS[s, g].rearrange("s c k -> (s c) k"), in_=yo[ub][:],
                              reads=[("yo", ub)])
                S.barrier()

        def phase_mixB1(l):
            w_in = P["w_in"][l]
            c0 = gcol(1, l)
            HALO = CW - 1
            with ExitStack() as st:
                wC = sb("c_w", [128, DT, 1024], BF16, st)
                dg = sb("c_dg", [128, 4 * CW, 128], BF16, st)
                cw = sb("c_cw", [128, 4, CW], F32, st)
                cb = sb("c_cb", [128, 4], F32, st)
                lng = sb("c_lng", [128, 4], F32, st)
                lnb = sb("c_lnb", [128, 4], F32, st)
                eps_ln = sb("c_epsln", [128, 1], F32, st)
                hsb = [sb("c_hs%d" % i, [128, DT, 512], F32, st) for i in range(2)]
                xn = sb("c_xn", [128, DT, 512], BF16, st)
                sqb = [sb("c_sq%d" % i, [128, 512], BF16, st) for i in range(2)]
                rstd = sb("c_rstd", [128, 512], F32, st)
                hc = sb("c_hc", [128, 4, HALO + 512], BF16, st)
                cacc = sb("c_cacc", [128, 4, 512], F32, st)
                hcvb = [sb("c_hcv%d" % i, [128, 4, 512], BF16, st) for i in range(2)]
                sg = [sb("c_sg%d" % i, [128, 512], F32, st) for i in range(2)]
                tmq = sb("c_tm", [128, 512], F32, st)
                mu = sb("c_mu", [128, 512], F32, st)
                lrs = sb("c_lrs", [128, 512], F32, st)
                pss = ps("c_pss", [128, 512], F32, st)
                NPB = 6
                pp = [ps("c_pp%d" % i, [128, 512], F32, st) for i in range(NPB)]
                pcnt = [0]

                def nxt():
                    b = pcnt[0] % NPB
                    pcnt[0] += 1
                    return b

                for dt in range(DT):
                    load_w_cast(wC[:, dt, :], w_in[dt * 128:(dt + 1) * 128, 512:1536], ("wC", dt))
                wCr = [("wC", dt) for dt in range(DT)]
                for ct in range(4):
                    S.dma("sp", out=cw[:, ct, :], in_=P["conv_w"][l][:, ct * 128:(ct + 1) * 128].rearrange("j p -> p j"),
                          allow_slow_non_contiguous=True, writes=["cw"])
                for nm, tl in (("conv_b", cb), ("conv_ln_g", lng), ("conv_ln_b", lnb)):
                    S.dma("sp", out=tl[:], in_=P[nm][l].rearrange("(t p) -> p t", p=128),
                          allow_slow_non_contiguous=True, writes=[nm])
                S.op("pool", "memset", eps_ln[:], LN_EPS, writes=["eps_ln"])
                for ct in range(4):
                    for j in range(CW):
                        eng = "dve" if (j % 2 == 0) else "pool"
                        S.op(eng, "tensor_scalar", out=dg[:, ct * CW + j, :], in0=ident_f[:], scalar1=cw[:, ct, j:j + 1],
                             scalar2=None, op0=ALU.mult, reads=["ident_f", "cw"], writes=[("dg", ct)])
                tcnt = 0
                for s in range(NS):
                    for (t0, n) in cfg.tiles:
                        hs = hsb[tcnt % 2]
                        hcv = hcvb[tcnt % 2]
                        hsr, hcvr = ("hsb", tcnt % 2), ("hcvb", tcnt % 2)
                        tcnt += 1
                        S.dma("sp", out=hs[:, :, 0:n], in_=hT[s, :, t0:t0 + n].rearrange("(t p) n -> p t n", p=128),
                              writes=[hsr])
                        rmsnorm(hs, xn, sqb, pss, rstd, c0, n, hs_name=hsr)
                        if t0 == 0:
                            S.op("pool", "memset", hc[:, :, 0:HALO], 0.0, writes=["hc"])
                        for ct in range(4):
                            ba, bg = nxt(), nxt()
                            for dt in range(DT):
                                S.op("pe", "matmul", pp[ba][:, 0:n], lhsT=wC[:, dt, ct * 128:(ct + 1) * 128],
                                     rhs=xn[:, dt, 0:n], start=(dt == 0), stop=(dt == DT - 1),
                                     reads=wCr + ["xn"], writes=[("pp", ba)])
                            for dt in range(DT):
                                S.op("pe", "matmul", pp[bg][:, 0:n], lhsT=wC[:, dt, 512 + ct * 128:512 + (ct + 1) * 128],
                                     rhs=xn[:, dt, 0:n], start=(dt == 0), stop=(dt == DT - 1),
                                     reads=wCr + ["xn"], writes=[("pp", bg)])
                            S.op("act", "activation", out=sg[ct % 2][:, 0:n], in_=pp[bg][:, 0:n], func=AF.Sigmoid,
                                 reads=[("pp", bg)], writes=[("sg", ct % 2)])
                            S.op("dve", "tensor_tensor", out=hc[:, ct, HALO:HALO + n], in0=pp[ba][:, 0:n],
                                 in1=sg[ct % 2][:, 0:n], op=ALU.mult,
                                 reads=[("pp", ba), ("sg", ct % 2)], writes=["hc"])
                        for ct in range(4):
                            bc_ = nxt()
                            for j in range(CW):
                                S.op("pe", "matmul", pp[bc_][:, 0:n], lhsT=dg[:, ct * CW + j, :], rhs=hc[:, ct, j:j + n],
                                     start=(j == 0), stop=(j == CW - 1),
                                     reads=[("dg", ct), "hc"], writes=[("pp", bc_)])
                            S.op("act", "activation", out=cacc[:, ct, 0:n], in_=pp[bc_][:, 0:n], func=AF.Identity,
                                 bias=cb[:, ct:ct + 1], scale=1.0,
                                 reads=[("pp", bc_), "conv_b"], writes=[("cacc", ct)])
                        if n >= HALO:
                            S.op("pool", "tensor_copy", out=hc[:, :, 0:HALO], in_=hc[:, :, n:n + HALO],
                                 reads=["hc"], writes=["hc"])
                        bmu, bvar = nxt(), nxt()
                        for ct in range(4):
                            b2 = ct % 2
                            S.op("act", "activation", out=sqb[b2][:, 0:n], in_=cacc[:, ct, 0:n], func=AF.Copy,
                                 reads=[("cacc", ct)], writes=[("sqb", b2)])
                            S.op("pe", "matmul", pp[bmu][:, 0:n], lhsT=ones_bf[:], rhs=sqb[b2][:, 0:n],
                                 start=(ct == 0), stop=(ct == 3), reads=[("sqb", b2), "ones_bf"], writes=[("pp", bmu)])
                        for ct in range(4):
                            b2 = ct % 2
                            S.op("act", "activation", out=sqb[b2][:, 0:n], in_=cacc[:, ct, 0:n], func=AF.Square,
                                 reads=[("cacc", ct)], writes=[("sqb", b2)])
                            S.op("pe", "matmul", pp[bvar][:, 0:n], lhsT=ones_bf[:], rhs=sqb[b2][:, 0:n],
                                 start=(ct == 0), stop=(ct == 3), reads=[("sqb", b2), "ones_bf"], writes=[("pp", bvar)])
                        S.op("act", "activation", out=mu[:, 0:n], in_=pp[bmu][:, 0:n], func=AF.Copy, scale=1.0 / 512,
                             reads=[("pp", bmu)], writes=["mu"])
                        S.op("dve", "tensor_tensor", out=tmq[:, 0:n], in0=mu[:, 0:n], in1=mu[:, 0:n], op=ALU.mult,
                             reads=["mu"], writes=["tmq"])
                        S.op("dve", "scalar_tensor_tensor", out=lrs[:, 0:n], in0=pp[bvar][:, 0:n], scalar=1.0 / 512,
                             in1=tmq[:, 0:n], op0=ALU.mult, op1=ALU.subtract,
                             reads=[("pp", bvar), "tmq"], writes=["lrs"])
                        S.op("act", "activation", out=lrs[:, 0:n], in_=lrs[:, 0:n], func=AF.Sqrt, bias=eps_ln[:], scale=1.0,
                             reads=["lrs", "eps_ln"], writes=["lrs"])
                        S.op("dve", "reciprocal", out=lrs[:, 0:n], in_=lrs[:, 0:n], reads=["lrs"], writes=["lrs"])
                        for ct in range(4):
                            S.op("dve", "tensor_tensor", out=tmq[:, 0:n], in0=cacc[:, ct, 0:n], in1=mu[:, 0:n],
                                 op=ALU.subtract, reads=[("cacc", ct), "mu"], writes=["tmq"])
                            S.op("dve", "tensor_tensor", out=tmq[:, 0:n], in0=tmq[:, 0:n], in1=lrs[:, 0:n],
                                 op=ALU.mult, reads=["tmq", "lrs"], writes=["tmq"])
                            S.op("act", "activation", out=hcv[:, ct, 0:n], in_=tmq[:, 0:n], func=AF.Silu,
                                 scale=lng[:, ct:ct + 1], bias=lnb[:, ct:ct + 1],
                                 reads=["tmq", "conv_ln_g", "conv_ln_b"], writes=[hcvr])
                        S.dma("sp", out=hcvS[s, :, t0:t0 + n].rearrange("(t p) n -> p t n", p=128), in_=hcv[:, :, 0:n],
                              reads=[hcvr])
                S.barrier()

        def phase_mixB2(l):
            w_in = P["w_in"][l]
            c0 = gcol(1, l)
            HALO = CW - 1
            with ExitStack() as st:
                wB = sb("b_w", [128, DT, 3072], BF16, st)
                wglu = sb("b_wglu", [128, 4, 2048], BF16, st)
                wpw = sb("b_wpw", [128, 4, D], BF16, st)
                wo = sb("b_wo", [128, 4, D], BF16, st)
                wout = sb("b_wout", [128, DT, D], BF16, st)
                hsb = [sb("b_hs%d" % i, [128, DT, 512], F32, st) for i in range(2)]
                xnb = [sb("b_xn%d" % i, [128, DT, 512], BF16, st) for i in range(2)]
                sqb = [sb("b_sq%d" % i, [128, 512], BF16, st) for i in range(2)]
                rstd = sb("b_rstd", [128, 512], F32, st)
                hcvb = [sb("b_hcv%d" % i, [128, 4, 512], BF16, st) for i in range(2)]
                yst = [sb("b_yst%d" % i, [128, 8, 64], F32, st) for i in range(4)]
                gyb = [sb("b_gy%d" % i, [128, 4, 512], BF16, st) for i in range(2)]
                oatb = [sb("b_oat%d" % i, [128, 4, 512], BF16, st) for i in range(2)]
                tg = [sb("b_tg%d" % i, [128, 512], F32, st) for i in range(2)]
                mrg = sb("b_mrg", [128, DT, 512], BF16, st)
                sg = [sb("b_sg%d" % i, [128, 512], F32, st) for i in range(2)]
                tm = [sb("b_tm%d" % i, [128, 512], F32, st) for i in range(2)]
                macc = sb("b_macc", [128, 512], F32, st)
                pss = ps("b_pss", [128, 512], F32, st)
                NPB = 6
                pp = [ps("b_pp%d" % i, [128, 512], F32, st) for i in range(NPB)]
                pcnt = [0]

                def nxt():
                    b = pcnt[0] % NPB
                    pcnt[0] += 1
                    return b

                for dt in range(DT):
                    load_w_cast(wB[:, dt, 0:1536], w_in[dt * 128:(dt + 1) * 128, 3072:4608], ("wB", dt))
                    load_w_cast(wB[:, dt, 1536:3072], w_in[dt * 128:(dt + 1) * 128, 4608:6144], ("wB", dt, 1))
                    load_w_cast(wout[:, dt, :], P["w_out"][l][dt * 128:(dt + 1) * 128, :], ("wout", dt))
                for ct in range(4):
                    load_w_cast(wglu[:, ct, :], P["ssm_w_glu"][l][ct * 128:(ct + 1) * 128, :], ("wglu", ct))
                    load_w_cast(wpw[:, ct, :], P["conv_w_out"][l][ct * 128:(ct + 1) * 128, :], ("wpw", ct))
                    load_w_cast(wo[:, ct, :], P["attn_w_o"][l][ct * 128:(ct + 1) * 128, :], ("wo", ct))
                wBr = [("wB", dt) for dt in range(DT)] + [("wB", dt, 1) for dt in range(DT)]
                tiles_all = [(s, t0, n) for s in range(NS) for (t0, n) in cfg.tiles]

                def pro_loads(idx):
                    s, t0, n = tiles_all[idx]
                    ib = idx % 2
                    hs, xn, oat, hcv, gy = hsb[ib], xnb[ib], oatb[ib], hcvb[ib], gyb[ib]
                    hsr, xnr, oatr, hcvr, gyr = ("hs", ib), ("xn", ib), ("oat", ib), ("hcv", ib), ("gy", ib)
                    nk = n // 8
                    k0 = t0 // 8
                    S.dma("sp", out=hs[:, :, 0:n], in_=hT[s, :, t0:t0 + n].rearrange("(t p) n -> p t n", p=128),
                          writes=[hsr])
                    S.dma("sp", out=oat[:, :, 0:n],
                          in_=oattT[s, :, t0:t0 + n].rearrange("(t p) n -> p t n", p=128), writes=[oatr])
                    S.dma("sp", out=hcv[:, :, 0:n],
                          in_=hcvS[s, :, t0:t0 + n].rearrange("(t p) n -> p t n", p=128), writes=[hcvr])
                    for ct in range(4):
                        yt = yst[ct]
                        for gl in range(8):
                            S.dma("sp", out=yt[gl * 16:(gl + 1) * 16, :, 0:nk],
                                  in_=ysS[s, ct * 8 + gl, :, :, k0:k0 + nk].rearrange("j c k -> c j k"),
                                  writes=[("yst", ct)])

                def pro_compute(idx):
                    s, t0, n = tiles_all[idx]
                    ib = idx % 2
                    hs, xn, oat, hcv, gy = hsb[ib], xnb[ib], oatb[ib], hcvb[ib], gyb[ib]
                    hsr, xnr, oatr, hcvr, gyr = ("hs", ib), ("xn", ib), ("oat", ib), ("hcv", ib), ("gy", ib)
                    nk = n // 8
                    rmsnorm(hs, xn, sqb, pss, rstd, c0, n, xn_name=xnr, hs_name=hsr)
                    for ct in range(4):
                        yb = ct
                        yt = yst[yb]
                        yv = yt[:, :, 0:nk]
                        t1 = tg[0][:, 0:n].rearrange("p (j k) -> p j k", j=8)
                        t2 = tg[1][:, 0:n].rearrange("p (j k) -> p j k", j=8)
                        S.op("act", "activation", out=t1, in_=yv, func=AF.Square,
                             reads=[("yst", yb)], writes=[("tg", 0)])
                        S.op("dve", "tensor_scalar", out=t1, in0=t1, scalar1=0.044715, scalar2=1.0,
                             op0=ALU.mult, op1=ALU.add, reads=[("tg", 0)], writes=[("tg", 0)])
                        S.op("dve", "tensor_tensor", out=t1, in0=t1, in1=yv, op=ALU.mult,
                             reads=[("tg", 0), ("yst", yb)], writes=[("tg", 0)])
                        S.op("act", "activation", out=t2, in_=t1, func=AF.Tanh, scale=0.7978845608,
                             reads=[("tg", 0)], writes=[("tg", 1)])
                        S.op("dve", "tensor_scalar", out=t2, in0=t2, scalar1=0.5, scalar2=0.5,
                             op0=ALU.mult, op1=ALU.add, reads=[("tg", 1)], writes=[("tg", 1)])
                        S.op("dve", "tensor_tensor", out=gy[:, ct, 0:n].rearrange("p (k j) -> p j k", j=8),
                             in0=t2, in1=yv, op=ALU.mult,
                             reads=[("tg", 1), ("yst", yb)], writes=[gyr])

                def main_tile(idx):
                    s, t0, n = tiles_all[idx]
                    ib = idx % 2
                    hs, xn, oat, hcv, gy = hsb[ib], xnb[ib], oatb[ib], hcvb[ib], gyb[ib]
                    hsr, xnr, oatr, hcvr, gyr = ("hs", ib), ("xn", ib), ("oat", ib), ("hcv", ib), ("gy", ib)
                    if idx + 1 < len(tiles_all):
                        pro_loads(idx + 1)
                    for f in range(DT):
                        if f == 2 and idx + 1 < len(tiles_all):
                            pro_compute(idx + 1)
                        def gate(br):
                            bgt = nxt()
                            c1 = br * 1024 + f * 128
                            for dt in range(DT):
                                S.op("pe", "matmul", pp[bgt][:, 0:n], lhsT=wB[:, dt, c1:c1 + 128],
                                     rhs=xn[:, dt, 0:n], start=(dt == 0), stop=(dt == DT - 1),
                                     reads=wBr + [xnr], writes=[("pp", bgt)])
                            S.op("act", "activation", out=sg[br % 2][:, 0:n], in_=pp[bgt][:, 0:n], func=AF.Sigmoid,
                                 reads=[("pp", bgt)], writes=[("sg", br % 2)])
                            return sg[br % 2], ("sg", br % 2)
                        ba, bg = nxt(), nxt()
                        for ct in range(4):
                            S.op("pe", "matmul", pp[ba][:, 0:n], lhsT=wglu[:, ct, f * 128:(f + 1) * 128],
                                 rhs=gy[:, ct, 0:n], start=(ct == 0), stop=(ct == 3),
                                 reads=[("wglu", c) for c in range(4)] + [gyr], writes=[("pp", ba)])
                        for ct in range(4):
                            S.op("pe", "matmul", pp[bg][:, 0:n], lhsT=wglu[:, ct, 1024 + f * 128:1024 + (f + 1) * 128],
                                 rhs=gy[:, ct, 0:n], start=(ct == 0), stop=(ct == 3),
                                 reads=[("wglu", c) for c in range(4)] + [gyr], writes=[("pp", bg)])
                        S.op("act", "activation", out=tm[1][:, 0:n], in_=pp[bg][:, 0:n], func=AF.Sigmoid,
                             reads=[("pp", bg)], writes=[("tm", 1)])
                        S.op("dve", "tensor_tensor", out=tm[0][:, 0:n], in0=pp[ba][:, 0:n], in1=tm[1][:, 0:n],
                             op=ALU.mult, reads=[("pp", ba), ("tm", 1)], writes=[("tm", 0)])
                        g0, g0r = gate(0)
                        S.op("dve", "tensor_tensor", out=macc[:, 0:n], in0=tm[0][:, 0:n], in1=g0[:, 0:n],
                             op=ALU.mult, reads=[("tm", 0), g0r], writes=["macc"])
                        bc_ = nxt()
                        for ct in range(4):
                            S.op("pe", "matmul", pp[bc_][:, 0:n], lhsT=wpw[:, ct, f * 128:(f + 1) * 128],
                                 rhs=hcv[:, ct, 0:n], start=(ct == 0), stop=(ct == 3),
                                 reads=[("wpw", c) for c in range(4)] + [hcvr], writes=[("pp", bc_)])
                        g1, g1r = gate(1)
                        S.op("dve", "tensor_tensor", out=tm[0][:, 0:n], in0=pp[bc_][:, 0:n], in1=g1[:, 0:n],
                             op=ALU.mult, reads=[("pp", bc_), g1r], writes=[("tm", 0)])
                        S.op("dve", "tensor_tensor", out=macc[:, 0:n], in0=macc[:, 0:n], in1=tm[0][:, 0:n],
                             op=ALU.add, reads=["macc", ("tm", 0)], writes=["macc"])
                        bo_ = nxt()
                        for ct in range(4):
                            S.op("pe", "matmul", pp[bo_][:, 0:n], lhsT=wo[:, ct, f * 128:(f + 1) * 128],
                                 rhs=oat[:, ct, 0:n], start=(ct == 0), stop=(ct == 3),
                                 reads=[("wo", c) for c in range(4)] + [oatr], writes=[("pp", bo_)])
                        g2, g2r = gate(2)
                        S.op("dve", "tensor_tensor", out=tm[0][:, 0:n], in0=pp[bo_][:, 0:n], in1=g2[:, 0:n],
                             op=ALU.mult, reads=[("pp", bo_), g2r], writes=[("tm", 0)])
                        S.op("dve", "tensor_tensor", out=mrg[:, f, 0:n], in0=macc[:, 0:n], in1=tm[0][:, 0:n],
                             op=ALU.add, reads=["macc", ("tm", 0)], writes=[("mrg", f)])
                    for o in range(DT):
                        bo_ = nxt()
                        for f in range(DT):
                            S.op("pe", "matmul", pp[bo_][:, 0:n], lhsT=wout[:, f, o * 128:(o + 1) * 128],
                                 rhs=mrg[:, f, 0:n], start=(f == 0), stop=(f == DT - 1),
                                 reads=[("wout", f), ("mrg", f)], writes=[("pp", bo_)])
                        S.op("dve", "tensor_tensor", out=hs[:, o, 0:n], in0=pp[bo_][:, 0:n], in1=hs[:, o, 0:n],
                             op=ALU.add, reads=[("pp", bo_), hsr], writes=[hsr])
                    S.dma("sp", out=hT[s, :, t0:t0 + n].rearrange("(t p) n -> p t n", p=128), in_=hs[:, :, 0:n],
                          reads=[hsr])

                pro_loads(0)
                pro_compute(0)
                for idx in range(len(tiles_all)):
                    main_tile(idx)
                S.barrier()

        phases = cfg.phases

        def on(name):
            return phases is None or name in phases

        phase_in()
        for l in range(DEPTH):
            if on("ffn1"):
                phase_ffn(l, 0)
            if on("mixa"):
                phase_mixA(l)
            if on("att"):
                phase_att(l)
            if on("ssm"):
                phase_ssm(l)
            if on("mixb"):
                phase_mixB1(l)
                phase_mixB2(l)
            if on("ffn2"):
                phase_ffn(l, 1)
        phase_out()

        with nc.Block() as block:
            S.emit(block)
    return nc


def kernel(**inputs):
    cfg = Cfg()
    nc = build(cfg)
    n = 8
    in_maps = []
    for c in range(n):
        m = {}
        for k, v in inputs.items():
            a = np.asarray(v)
            if k == "x":
                a = a[c * cfg.nseq:(c + 1) * cfg.nseq]
            m[k] = np.ascontiguousarray(a, dtype=np.float32)
        in_maps.append(m)
    res = run_bass_kernel_spmd(nc, in_maps, core_ids=list(range(n)))
    return np.concatenate([np.asarray(r["out"]) for r in res.results], axis=0).astype(np.float32)
```

```python
import math
import os
from contextlib import ExitStack

import numpy as np
import concourse.bass as bass
import concourse.mybir as mybir
from concourse.bass_utils import run_bass_kernel_spmd

F32 = mybir.dt.float32
BF16 = mybir.dt.bfloat16
I32 = mybir.dt.int32
AF = mybir.ActivationFunctionType
ALU = mybir.AluOpType

D = 1024
DT = 8
DFF = 2816
FT = 22
NMETA = 16
DIN = 6144
NG = 32
NST = 64
CW = 31
RMS_EPS = 1e-6
LN_EPS = 1e-5

ENGS = ["pe", "act", "dve", "pool", "sp"]


class Sched:
    def __init__(self, nc, es, n_dma_sems=24):
        self.nc = nc
        self.ops = {e: [] for e in ENGS}
        self.sem = {e: es.enter_context(nc.semaphore("sem_" + e)) for e in ENGS}
        self.cnt = {e: 0 for e in ENGS}
        self.dsem = [es.enter_context(nc.semaphore("semd%d" % i)) for i in range(n_dma_sems)]
        self.duse = [0] * n_dma_sems
        self.dnext = 0
        self.waited = {}
        self.lastw = {}
        self.readers = {}

    def _semof(self, key):
        if isinstance(key, str):
            return self.sem[key]
        return self.dsem[key[1]]

    def _deps(self, eng, reads, writes):
        toks = {}
        def add(t):
            k, v = t
            if toks.get(k, 0) < v:
                toks[k] = v
        for r in reads:
            if r in self.lastw:
                add(self.lastw[r])
        for w in writes:
            if w in self.lastw:
                add(self.lastw[w])
            for k, v in self.readers.get(w, {}).items():
                add((k, v))
        waits = []
        for k, v in toks.items():
            if k == "pe" and eng == "pe":
                continue
            if self.waited.get((eng, k), 0) >= v:
                continue
            self.waited[(eng, k)] = v
            waits.append((k, v))
        return waits

    def _record(self, tok, reads, writes):
        k, v = tok
        for r in reads:
            d = self.readers.setdefault(r, {})
            if d.get(k, 0) < v:
                d[k] = v
        for w in writes:
            self.lastw[w] = tok
            self.readers[w] = {}

    def op(self, eng, meth, *args, reads=(), writes=(), excl=(), **kw):
        fn = (meth, args, kw)
        if excl:
            reads = list(reads) + list(excl)
            writes = list(writes) + list(excl)
        waits = self._deps(eng, reads, writes)
        self.cnt[eng] += 1
        tok = (eng, self.cnt[eng])
        self.ops[eng].append((waits, fn, eng, 1))
        self._record(tok, reads, writes)

    def dma(self, eng, reads=(), writes=(), **kw):
        fn = ("dma_start", (), kw)
        waits = self._deps(eng, reads, writes)
        slot = self.dnext
        self.dnext = (self.dnext + 1) % len(self.dsem)
        key = ("d", slot)
        if self.duse[slot] > 0:
            v = 16 * self.duse[slot]
            if self.waited.get((eng, key), 0) < v:
                self.waited[(eng, key)] = v
                waits.append((key, v))
        self.duse[slot] += 1
        tok = (key, 16 * self.duse[slot])
        self.ops[eng].append((waits, fn, key, 16))
        self._record(tok, reads, writes)

    def barrier(self):
        for eng in ENGS:
            waits = []
            for k in ENGS:
                v = self.cnt[k]
                if v > 0 and k != eng and self.waited.get((eng, k), 0) < v:
                    self.waited[(eng, k)] = v
                    waits.append((k, v))
            if eng != "pe":
                v = self.cnt[eng]
                if v > 0 and self.waited.get((eng, eng), 0) < v:
                    self.waited[(eng, eng)] = v
                    waits.append((eng, v))
            for i, u in enumerate(self.duse):
                key = ("d", i)
                if u > 0 and self.waited.get((eng, key), 0) < 16 * u:
                    self.waited[(eng, key)] = 16 * u
                    waits.append((key, 16 * u))
            if waits:
                self.ops[eng].append((waits, None, None, 0))
        self.lastw = {}
        self.readers = {}

    def emit(self, block):
        def replay(name):
            def run(e):
                for waits, fn, inc_key, inc in self.ops[name]:
                    for k, v in waits:
                        e.wait_ge(self._semof(k), v)
                    if fn is not None:
                        meth, args, kw = fn
                        getattr(e, meth)(*args, **kw).then_inc(self._semof(inc_key), inc)
            return run
        block.tensor(replay("pe"))
        block.scalar(replay("act"))
        block.vector(replay("dve"))
        block.gpsimd(replay("pool"))
        block.sync(replay("sp"))


class Cfg:
    def __init__(self, seq=4096, nseq=2, depth=4, phases=None, dump=()):
        self.dump = dump
        self.seq = seq
        self.nseq = nseq
        self.depth = depth
        self.L = seq + NMETA
        tiles = []
        t = 0
        while t + 512 <= self.L:
            tiles.append((t, 512))
            t += 512
        if t < self.L:
            tiles.append((t, self.L - t))
        self.tiles = tiles
        self.nkb = (self.L + 127) // 128
        self.LP = self.nkb * 128
        self.NK = self.L // 8
        self.phases = phases


def build(cfg):
    nc = bass.Bass("TRN2", target_bir_lowering=False)
    L, NS, DEPTH = cfg.L, cfg.nseq, cfg.depth

    def din(name, shape):
        return nc.dram_tensor(name, list(shape), F32, kind="ExternalInput").ap()

    x = din("x", (NS, cfg.seq, D))
    meta = din("meta_tokens", (NMETA, D))
    P = {}
    for name, shape in [
        ("ffn1_norm", (DEPTH, D)), ("ffn1_w13", (DEPTH, D, 2 * DFF)), ("ffn1_w2", (DEPTH, DFF, D)),
        ("mix_norm", (DEPTH, D)), ("w_in", (DEPTH, D, DIN)),
        ("ssm_lam_re", (DEPTH, NG, NST)), ("ssm_lam_im", (DEPTH, NG, NST)), ("ssm_log_dt", (DEPTH, NG)),
        ("ssm_b_re", (DEPTH, NG, NST, 16)), ("ssm_b_im", (DEPTH, NG, NST, 16)),
        ("ssm_c_re", (DEPTH, NG, 16, NST)), ("ssm_c_im", (DEPTH, NG, 16, NST)),
        ("ssm_d", (DEPTH, 512)), ("ssm_w_glu", (DEPTH, 512, 2048)),
        ("conv_w", (DEPTH, CW, 512)), ("conv_b", (DEPTH, 512)),
        ("conv_ln_g", (DEPTH, 512)), ("conv_ln_b", (DEPTH, 512)),
        ("conv_w_out", (DEPTH, 512, D)), ("attn_w_o", (DEPTH, 512, D)), ("w_out", (DEPTH, D, D)),
        ("ffn2_norm", (DEPTH, D)), ("ffn2_w13", (DEPTH, D, 2 * DFF)), ("ffn2_w2", (DEPTH, DFF, D)),
        ("final_norm", (D,)),
    ]:
        P[name] = din(name, shape)
    out = nc.dram_tensor("out", [NS, cfg.seq, D], F32, kind="ExternalOutput").ap()

    def scratch(name, shape, dt):
        kind = "ExternalOutput" if name in cfg.dump else "Internal"
        return nc.dram_tensor(name, list(shape), dt, kind=kind).ap()

    hT = scratch("hT", (NS, D, L), F32)
    NK = cfg.NK
    Usc = scratch("Usc", (NS, NG, 8, 16, NK), BF16)
    qT = scratch("qT", (NS, 512, L), BF16)
    kT = scratch("kT", (NS, 512, L), BF16)
    vS = scratch("vS", (NS, L, 512), BF16)
    oattT = scratch("oattT", (NS, 512, L), BF16)
    ysS = scratch("ysS", (NS, NG, 8, 16, NK), F32)
    hcvS = scratch("hcvS", (NS, 512, L), BF16)

    with ExitStack() as es:
        S = Sched(nc, es)

        uid = [0]

        def sb(name, shape, dt, stack=es):
            uid[0] += 1
            return stack.enter_context(nc.sbuf_tensor("%s_%d" % (name, uid[0]), list(shape), dt))

        def ps(name, shape, dt=F32, stack=es):
            uid[0] += 1
            return stack.enter_context(nc.psum_tensor("%s_%d" % (name, uid[0]), list(shape), dt))

        ones_bf = sb("ones_bf", [128, 128], BF16)
        ident_f = sb("ident_f", [128, 128], F32)
        gcols = sb("gcols", [128, (3 * DEPTH + 1) * DT], F32)
        eps_rms = sb("eps_rms", [128, 1], F32)
        rn_tmp = [sb("rn_tmp%d" % i, [128, 512], F32) for i in range(2)]
        S.op("pool", "memset", ones_bf[:], 1.0, writes=["ones_bf"])
        S.op("pool", "memset", ident_f[:], 1.0, writes=["ident_f"])
        S.op("pool", "affine_select", out=ident_f[:], in_=ident_f[:], pattern=[[-1, 128]],
                                               compare_op=ALU.is_equal, fill=0.0, base=0,
                                               channel_multiplier=1,
             reads=["ident_f"], writes=["ident_f"])
        S.op("pool", "memset", eps_rms[:], RMS_EPS, writes=["eps_rms"])
        for wi, nm in enumerate(["ffn1_norm", "mix_norm", "ffn2_norm"]):
            for l in range(DEPTH):
                c0 = (wi * DEPTH + l) * DT
                S.dma("sp",
                    out=gcols[:, c0:c0 + DT], in_=P[nm][l].rearrange("(t p) -> p t", p=128),
                    allow_slow_non_contiguous=True, writes=["gcols"])
        cF = 3 * DEPTH * DT
        S.dma("sp", out=gcols[:, cF:cF + DT],
                                          in_=P["final_norm"].rearrange("(t p) -> p t", p=128),
                                          allow_slow_non_contiguous=True, writes=["gcols"])
        S.barrier()

        def gcol(which, l):
            c0 = (which * DEPTH + l) * DT if which < 3 else cF
            return c0

        def load_h(hs, s, t0, n):
            S.dma("sp",
                out=hs[:, :, 0:n], in_=hT[s, :, t0:t0 + n].rearrange("(t p) n -> p t n", p=128),
                writes=["hs"])

        def store_h(hs, s, t0, n):
            S.dma("sp",
                out=hT[s, :, t0:t0 + n].rearrange("(t p) n -> p t n", p=128), in_=hs[:, :, 0:n],
                reads=["hs"])

        def rmsnorm(hs, xn, sqb, pss, rstd, c0, n, xn_name="xn", hs_name="hs"):
            for dt in range(DT):
                b = dt % 2
                S.op("act", "activation", out=sqb[b][:, 0:n], in_=hs[:, dt, 0:n],
                                                                 func=AF.Square,
                     reads=[hs_name], writes=[("sqb", b)])
                S.op("pe", "matmul", pss[:, 0:n], lhsT=ones_bf[:], rhs=sqb[b][:, 0:n],
                                                           start=(dt == 0), stop=(dt == DT - 1),
                     reads=[("sqb", b), "ones_bf"], writes=["pss"])
            S.op("act", "activation", out=rstd[:, 0:n], in_=pss[:, 0:n], func=AF.Sqrt,
                                               scale=1.0 / D, bias=eps_rms[:],
                 reads=["pss"], writes=["rstd"])
            S.op("dve", "reciprocal", out=rstd[:, 0:n], in_=rstd[:, 0:n],
                 reads=["rstd"], writes=["rstd"])
            for dt in range(DT):
                b = dt % 2
                S.op("dve", "tensor_tensor", out=rn_tmp[b][:, 0:n], in0=hs[:, dt, 0:n], in1=rstd[:, 0:n],
                     op=ALU.mult, reads=[hs_name, "rstd"], writes=[("rn_tmp", b)])
                S.op("act", "activation", out=xn[:, dt, 0:n], in_=rn_tmp[b][:, 0:n], func=AF.Identity,
                     scale=gcols[:, c0 + dt:c0 + dt + 1],
                     reads=[("rn_tmp", b), "gcols"], writes=[xn_name])

        def load_w_cast(dst_ap, src_ap, res):
            S.dma("pool", out=dst_ap, in_=src_ap, max_dma_last_dim=8192,
                  writes=[res])

        def phase_in():
            with ExitStack() as st:
                hs = sb("in_hs", [128, DT, 512], F32, st)
                xt = [sb("in_xt%d" % i, [128, D], F32, st) for i in range(2)]
                pt = [ps("in_pt%d" % i, [128, 512], F32, st) for i in range(2)]
                blk = 0
                for s in range(NS):
                    for (t0, n) in cfg.tiles:
                        for j in range((n + 127) // 128):
                            tb = t0 + j * 128
                            nb = min(128, t0 + n - tb)
                            xb = xt[blk % 2]
                            xr = ("xt", blk % 2)
                            if tb == 0:
                                S.dma("sp", out=xb[0:NMETA, :], in_=meta[:, :],
                                      writes=[xr])
                                S.dma("sp",
                                    out=xb[NMETA:nb, :], in_=x[s, 0:nb - NMETA, :], writes=[(xr, 1)],
                                    reads=[])
                                rd = [xr, (xr, 1)]
                            else:
                                S.dma("sp",
                                    out=xb[0:nb, :], in_=x[s, tb - NMETA:tb - NMETA + nb, :], writes=[xr, (xr, 1)])
                                rd = [xr, (xr, 1)]
                            for half in range(2):
                                pp = pt[half]
                                for q in range(4):
                                    dt = half * 4 + q
                                    S.op("pe", "transpose",
                                        out=pp[:, q * 128:q * 128 + nb], in_=xb[0:nb, dt * 128:(dt + 1) * 128],
                                        identity=ident_f[0:nb, 0:nb],
                                        reads=rd + ["ident_f"], writes=[("pt", half)])
                                eng = "act" if half == 0 else "dve"
                                if eng == "act":
                                    S.op("act", "activation",
                                        out=hs[:, half * 4:half * 4 + 4, j * 128:j * 128 + nb],
                                        in_=pp[:].rearrange("p (q c) -> p q c", q=4)[:, :, 0:nb], func=AF.Copy,
                                        reads=[("pt", half)], writes=["hs"])
                                else:
                                    S.op("dve", "tensor_copy",
                                        out=hs[:, half * 4:half * 4 + 4, j * 128:j * 128 + nb],
                                        in_=pp[:].rearrange("p (q c) -> p q c", q=4)[:, :, 0:nb],
                                        reads=[("pt", half)], writes=["hs"])
                            blk += 1
                        store_h(hs, s, t0, n)
                S.barrier()

        def phase_out():
            with ExitStack() as st:
                hs = sb("o_hs", [128, DT, 512], F32, st)
                xn = sb("o_xn", [128, DT, 512], F32, st)
                sqb = [sb("o_sq%d" % i, [128, 512], BF16, st) for i in range(2)]
                rstd = sb("o_rstd", [128, 512], F32, st)
                ot = [sb("o_ot%d" % i, [128, D], F32, st) for i in range(2)]
                pss = ps("o_pss", [128, 512], F32, st)
                pt = [ps("o_pt%d" % i, [128, 512], F32, st) for i in range(2)]
                blk = 0
                for s in range(NS):
                    for (t0, n) in cfg.tiles:
                        load_h(hs, s, t0, n)
                        rmsnorm(hs, xn, sqb, pss, rstd, cF, n)
                        for j in range((n + 127) // 128):
                            tb = t0 + j * 128
                            nb = min(128, t0 + n - tb)
                            ob = ot[blk % 2]
                            orr = ("ot", blk % 2)
                            for half in range(2):
                                pp = pt[half]
                                for q in range(4):
                                    dt = half * 4 + q
                                    S.op("pe", "transpose",
                                        out=pp[0:nb, q * 128:(q + 1) * 128], in_=xn[:, dt, j * 128:j * 128 + nb],
                                        identity=ident_f[:, :],
                                        reads=["xn", "ident_f"], writes=[("pt", half)])
                                if half == 0:
                                    S.op("act", "activation",
                                        out=ob[0:nb, 0:512], in_=pp[0:nb, :], func=AF.Copy,
                                        reads=[("pt", half)], writes=[orr])
                                else:
                                    S.op("dve", "tensor_copy",
                                        out=ob[0:nb, 512:1024], in_=pp[0:nb, :],
                                        reads=[("pt", half)], writes=[(orr, 1)])
                            lo = NMETA if tb == 0 else 0
                            S.dma("sp",
                                out=out[s, tb + lo - NMETA:tb + nb - NMETA, :], in_=ob[lo:nb, :],
                                reads=[orr, (orr, 1)])
                            blk += 1
                S.barrier()

        def phase_ffn(l, which):
            pre = "ffn1" if which == 0 else "ffn2"
            w13 = P[pre + "_w13"][l]
            w2 = P[pre + "_w2"][l]
            c0 = gcol(0 if which == 0 else 2, l)
            with ExitStack() as st:
                w13s = sb("f_w13", [128, DT, 2 * DFF], BF16, st)
                w2s = sb("f_w2", [128, FT, D], BF16, st)
                hsb = [sb("f_hs%d" % i, [128, DT, 512], F32, st) for i in range(2)]
                xn = sb("f_xn", [128, DT, 512], BF16, st)
                sqb = [sb("f_sq%d" % i, [128, 512], BF16, st) for i in range(2)]
                rstd = sb("f_rstd", [128, 512], F32, st)
                gh = sb("f_g", [128, FT, 512], BF16, st)
                sa = [sb("f_sa%d" % i, [128, 512], F32, st) for i in range(2)]
                pss = ps("f_pss", [128, 512], F32, st)
                pa = [ps("f_pa%d" % i, [128, 512], F32, st) for i in range(2)]
                pb = [ps("f_pb%d" % i, [128, 512], F32, st) for i in range(2)]
                po = [ps("f_po%d" % i, [128, 512], F32, st) for i in range(2)]
                for dt in range(DT):
                    load_w_cast(w13s[:, dt, :], w13[dt * 128:(dt + 1) * 128, :], ("w13", dt))
                for f in range(FT):
                    load_w_cast(w2s[:, f, :], w2[f * 128:(f + 1) * 128, :], ("w2", f))
                tiles_all = [(s, t0, n) for s in range(NS) for (t0, n) in cfg.tiles]

                def ld(idx):
                    s_, t0_, n_ = tiles_all[idx]
                    S.dma("sp", out=hsb[idx % 2][:, :, 0:n_],
                          in_=hT[s_, :, t0_:t0_ + n_].rearrange("(t p) n -> p t n", p=128), writes=[("hsb", idx % 2)])

                def nrm(idx):
                    s_, t0_, n_ = tiles_all[idx]
                    rmsnorm(hsb[idx % 2], xn, sqb, pss, rstd, c0, n_, hs_name=("hsb", idx % 2))

                ld(0)
                nrm(0)
                for idx, (s, t0, n) in enumerate(tiles_all):
                    if True:
                        hs = hsb[idx % 2]
                        hsr = ("hsb", idx % 2)
                        if idx + 1 < len(tiles_all):
                            ld(idx + 1)
                        for f in range(FT):
                            b = f % 2
                            for dt in range(DT):
                                S.op("pe", "matmul",
                                    pa[b][:, 0:n], lhsT=w13s[:, dt, f * 128:(f + 1) * 128], rhs=xn[:, dt, 0:n],
                                    start=(dt == 0), stop=(dt == DT - 1),
                                    reads=[("w13", dt), "xn"], writes=[("pa", b)])
                            for dt in range(DT):
                                S.op("pe", "matmul",
                                    pb[b][:, 0:n], lhsT=w13s[:, dt, DFF + f * 128:DFF + (f + 1) * 128],
                                    rhs=xn[:, dt, 0:n], start=(dt == 0), stop=(dt == DT - 1),
                                    reads=[("w13", dt), "xn"], writes=[("pb", b)])
                            S.op("act", "activation", out=sa[b][:, 0:n], in_=pa[b][:, 0:n],
                                                                    func=AF.Silu,
                                 reads=[("pa", b)], writes=[("sa", b)])
                            S.op("dve", "tensor_tensor",
                                out=gh[:, f, 0:n], in0=pb[b][:, 0:n], in1=sa[b][:, 0:n], op=ALU.mult,
                                reads=[("pb", b), ("sa", b)], writes=[("gh", f)])
                        if idx + 1 < len(tiles_all):
                            nrm(idx + 1)
                        for o in range(DT):
                            b = o % 2
                            for f in range(FT):
                                S.op("pe", "matmul",
                                    po[b][:, 0:n], lhsT=w2s[:, f, o * 128:(o + 1) * 128], rhs=gh[:, f, 0:n],
                                    start=(f == 0), stop=(f == FT - 1),
                                    reads=[("w2", f), ("gh", f)], writes=[("po", b)])
                            import os
                            if os.environ.get("DBG") == "po":
                                S.op("dve", "tensor_copy", out=hs[:, o, 0:n], in_=po[b][:, 0:n],
                                     reads=[("po", b), "hs"], writes=["hs"])
                            elif os.environ.get("DBG") == "xn":
                                S.op("dve", "tensor_copy", out=hs[:, o, 0:n], in_=xn[:, o, 0:n],
                                     reads=[("po", b), "hs", "xn"], writes=["hs"])
                            elif os.environ.get("DBG") == "gh":
                                S.op("dve", "tensor_copy", out=hs[:, o, 0:n], in_=gh[:, o, 0:n],
                                     reads=[("po", b), "hs", "xn", ("gh", o)], writes=["hs"])
                            else:
                              S.op("act", "activation", out=sa[b][:, 0:n], in_=po[b][:, 0:n], func=AF.Copy, scale=0.5,
                                   reads=[("po", b)], writes=[("sa", b)])
                              S.op("dve", "tensor_tensor", out=hs[:, o, 0:n], in0=sa[b][:, 0:n], in1=hs[:, o, 0:n],
                                   op=ALU.add, reads=[("sa", b), hsr], writes=[hsr])
                        S.dma("sp", out=hT[s, :, t0:t0 + n].rearrange("(t p) n -> p t n", p=128), in_=hs[:, :, 0:n],
                              reads=[hsr])
                S.barrier()


        def phase_mixA(l):
            w_in = P["w_in"][l]
            c0 = gcol(1, l)
            with ExitStack() as st:
                wA = sb("a_w", [128, DT, 2048], BF16, st)
                hs = sb("a_hs", [128, DT, 512], F32, st)
                xn = sb("a_xn", [128, DT, 512], BF16, st)
                sqb = [sb("a_sq%d" % i, [128, 512], BF16, st) for i in range(2)]
                rstd = sb("a_rstd", [128, 512], F32, st)
                stg = [sb("a_stg%d" % i, [128, 512], BF16, st) for i in range(4)]
                pss = ps("a_pss", [128, 512], F32, st)
                pp = [ps("a_pp%d" % i, [128, 512], F32, st) for i in range(4)]
                for dt in range(DT):
                    load_w_cast(wA[:, dt, 0:512], w_in[dt * 128:(dt + 1) * 128, 0:512], ("wA", dt))
                    load_w_cast(wA[:, dt, 512:2048], w_in[dt * 128:(dt + 1) * 128, 1536:3072], ("wA", dt, 1))
                cnt = 0
                for s in range(NS):
                    for (t0, n) in cfg.tiles:
                        load_h(hs, s, t0, n)
                        rmsnorm(hs, xn, sqb, pss, rstd, c0, n)
                        nk = n // 8
                        k0 = t0 // 8
                        for f in range(12):
                            b = cnt % 4
                            cnt += 1
                            pb = pp[b]
                            sg = stg[b]
                            for dt in range(DT):
                                S.op("pe", "matmul", pb[:, 0:n], lhsT=wA[:, dt, f * 128:(f + 1) * 128],
                                     rhs=xn[:, dt, 0:n], start=(dt == 0), stop=(dt == DT - 1),
                                     reads=[("wA", dt), ("wA", dt, 1), "xn"], writes=[("pp", b)])
                            if f < 4:
                                S.op("act", "activation",
                                     out=sg[:, 0:n].rearrange("p (s k) -> p s k", s=8),
                                     in_=pb[:, 0:n].rearrange("p (k s) -> p s k", s=8), func=AF.Copy,
                                     reads=[("pp", b)], writes=[("stg", b)])
                                for gl in range(8):
                                    S.dma("sp", out=Usc[s, f * 8 + gl, :, :, k0:k0 + nk].rearrange("s c k -> c s k"),
                                          in_=sg[gl * 16:(gl + 1) * 16, 0:n].rearrange("p (s k) -> p s k", s=8),
                                          reads=[("stg", b)])
                            else:
                                if f % 2 == 0:
                                    S.op("act", "activation", out=sg[:, 0:n], in_=pb[:, 0:n], func=AF.Copy,
                                         reads=[("pp", b)], writes=[("stg", b)])
                                else:
                                    S.op("dve", "tensor_copy", out=sg[:, 0:n], in_=pb[:, 0:n],
                                         reads=[("pp", b)], writes=[("stg", b)])
                                dst = qT if f < 8 else kT
                                r0 = (f % 4) * 128
                                S.dma("sp", out=dst[s, r0:r0 + 128, t0:t0 + n], in_=sg[:, 0:n],
                                      reads=[("stg", b)])
                        for j in range((n + 127) // 128):
                            nb = min(128, n - j * 128)
                            b = cnt % 4
                            cnt += 1
                            pb = pp[b]
                            sg = stg[b]
                            for dt in range(DT):
                                S.op("pe", "matmul", pb[0:nb, 0:512], lhsT=xn[:, dt, j * 128:j * 128 + nb],
                                     rhs=wA[:, dt, 1536:2048], start=(dt == 0), stop=(dt == DT - 1),
                                     reads=[("wA", dt), ("wA", dt, 1), "xn"], writes=[("pp", b)])
                            S.op("dve", "tensor_copy", out=sg[0:nb, :], in_=pb[0:nb, :],
                                 reads=[("pp", b)], writes=[("stg", b)])
                            S.dma("sp", out=vS[s, t0 + j * 128:t0 + j * 128 + nb, :], in_=sg[0:nb, :],
                                  reads=[("stg", b)])
                S.barrier()

        def phase_att(l):
            nkb, LP = cfg.nkb, cfg.LP
            tail = L - (nkb - 1) * 128
            with ExitStack() as st:
                triT = sb("t_tri", [128, 128], BF16, st)
                sel0 = sb("t_sel0", [128, 128], BF16, st)
                onec = sb("t_onec", [128, 1], F32, st)
                masks = [sb("t_mask%d" % i, [128, 512], F32, st) for i in range(4)]
                kT2 = [sb("t_k%d" % i, [128, LP], BF16, st) for i in range(2)]
                qz = [[sb("t_q%d_%d" % (i, h), [128, L], BF16, st) for h in range(2)] for i in range(2)]
                vz = [[sb("t_v%d_%d" % (i, h), [128, nkb, 128], BF16, st) for h in range(2)] for i in range(2)]
                NB3 = 3
                e_t = [sb("t_e%d" % i, [128, 2, 512], F32, st) for i in range(NB3)]
                sp_t = [sb("t_sp%d" % i, [128, 2, 512], BF16, st) for i in range(NB3)]
                g_t = [sb("t_g%d" % i, [128, 2, 512], F32, st) for i in range(NB3)]
                w_t = [sb("t_w%d" % i, [128, 2, 512], BF16, st) for i in range(NB3)]
                Rs = sb("t_Rs", [128, 2, 512], BF16, st)
                ob = [sb("t_ob%d" % i, [128, 512], BF16, st) for i in range(2)]
                pz = [ps("t_pz%d" % i, [128, 2, 512], F32, st) for i in range(2)]
                pcs = ps("t_pcs", [128, 2, 512], F32, st)
                po = [ps("t_po%d" % i, [128, 512], F32, st) for i in range(2)]
                S.op("pool", "memset", triT[:], 1.0, writes=["triT"])
                S.op("pool", "affine_select", out=triT[:], in_=triT[:], pattern=[[-1, 128]],
                     compare_op=ALU.is_ge, fill=0.0, base=0, channel_multiplier=1,
                     reads=["triT"], writes=["triT"])
                S.op("pool", "memset", sel0[:], 1.0, writes=["sel0"])
                S.op("pool", "affine_select", out=sel0[:], in_=sel0[:], pattern=[[0, 128]],
                     compare_op=ALU.is_equal, fill=0.0, base=0, channel_multiplier=1,
                     reads=["sel0"], writes=["sel0"])
                S.op("pool", "memset", onec[:], 1.0, writes=["onec"])
                for i in range(4):
                    S.op("pool", "memset", masks[i][:], 1.0, writes=[("mask", i)])
                    S.op("pool", "affine_select", out=masks[i][:], in_=masks[i][:], pattern=[[1, 512]],
                         compare_op=ALU.is_gt, fill=0.0, base=-128 * i, channel_multiplier=-1,
                         reads=[("mask", i)], writes=[("mask", i)])
                for i in range(2):
                    S.op("pool", "memset", kT2[i][:], 0.0, writes=[("kT2", i)])
                    for h in range(2):
                        S.op("pool", "memset", qz[i][h][:], 0.0, writes=[("qz", i, h)])
                        S.op("pool", "memset", vz[i][h][:], 0.0, writes=[("vz", i, h), ("vz", i, h, 1)])
                blocks = []
                it = 0
                qcnt = 0
                for s in range(NS):
                    for hp in range(4):
                        bb = it % 2
                        it += 1
                        first_of_load = True
                        for (q0, nq) in cfg.tiles:
                            qb = qcnt % 2
                            qcnt += 1
                            nblk = max((q0 + nq - 1 + 127) // 128, 1)
                            for bi, kb in enumerate(reversed(range(nblk))):
                                blocks.append(dict(s=s, hp=hp, bb=bb, q0=q0, nq=nq, qb=qb, bi=bi, kb=kb, nblk=nblk,
                                                   load=first_of_load))
                                first_of_load = False
                for i, B_ in enumerate(blocks):
                    B_["i3"] = i % NB3
                    B_["i2"] = i % 2

                def load_inputs(B_):
                    s, hp, bb = B_["s"], B_["hp"], B_["bb"]
                    S.dma("sp", out=kT2[bb][:, 0:L], in_=kT[s, hp * 128:(hp + 1) * 128, :],
                          reads=[], writes=[("kT2", bb)])
                    for h in range(2):
                        r0 = hp * 128 + h * 64
                        S.dma("sp", out=qz[bb][h][h * 64:(h + 1) * 64, :], in_=qT[s, r0:r0 + 64, :],
                              writes=[("qz", bb, h)])
                        if nkb > 1:
                            S.dma("sp", out=vz[bb][h][:, 0:nkb - 1, h * 64:(h + 1) * 64],
                                  in_=vS[s, 0:(nkb - 1) * 128, r0:r0 + 64].rearrange("(b p) d -> p b d", p=128),
                                  writes=[("vz", bb, h)])
                        S.dma("sp", out=vz[bb][h][0:tail, nkb - 1, h * 64:(h + 1) * 64],
                              in_=vS[s, (nkb - 1) * 128:L, r0:r0 + 64],
                              writes=[("vz", bb, h, 1)])

                def stageA(B_):
                    bb, q0, nq, kb, i3, i2 = (B_[k] for k in ("bb", "q0", "nq", "kb", "i3", "i2"))
                    if B_["load"]:
                        load_inputs(B_)
                    diag = (kb * 128 + 128 > q0)
                    for h in range(2):
                        S.op("pe", "matmul", pz[i2][:, h, 0:nq], lhsT=kT2[bb][:, kb * 128:(kb + 1) * 128],
                             rhs=qz[bb][h][:, q0:q0 + nq], start=True, stop=True,
                             reads=[("kT2", bb), ("qz", bb, h)], writes=[("pz", i2)])
                    S.op("act", "activation", out=e_t[i3][:, :, 0:nq], in_=pz[i2][:, :, 0:nq],
                         func=AF.Exp, scale=0.125,
                         reads=[("pz", i2)], writes=[("e", i3)])
                    if diag:
                        mi = (kb * 128 - q0) // 128
                        assert 0 <= mi < 4
                        for h in range(2):
                            S.op("dve", "tensor_tensor", out=e_t[i3][:, h, 0:nq], in0=e_t[i3][:, h, 0:nq],
                                 in1=masks[mi][:, 0:nq], op=ALU.mult,
                                 reads=[("e", i3), ("mask", mi)], writes=[("e", i3)])
                    S.op("act", "activation", out=sp_t[i3][:, :, 0:nq], in_=e_t[i3][:, :, 0:nq],
                         func=AF.Ln, bias=onec[:], scale=1.0,
                         reads=[("e", i3), "onec"], writes=[("sp", i3)])

                def stageB(B_):
                    nq, bi, nblk, i3 = (B_[k] for k in ("nq", "bi", "nblk", "i3"))
                    for h in range(2):
                        S.op("pe", "matmul", pcs[:, h, 0:nq], lhsT=triT[:], rhs=sp_t[i3][:, h, 0:nq],
                             start=True, stop=(bi == 0),
                             reads=["triT", ("sp", i3)], writes=["pcs"])
                        if bi > 0:
                            S.op("pe", "matmul", pcs[:, h, 0:nq], lhsT=ones_bf[:], rhs=Rs[:, h, 0:nq],
                                 start=False, stop=True,
                                 reads=["ones_bf", "Rs"], writes=["pcs"])
                    if bi < nblk - 1:
                        if bi == 0:
                            S.op("pool", "tensor_copy", out=Rs[:, :, 0:nq], in_=sp_t[i3][:, :, 0:nq],
                                 reads=[("sp", i3)], writes=["Rs"])
                        else:
                            S.op("pool", "tensor_tensor", out=Rs[:, :, 0:nq], in0=Rs[:, :, 0:nq],
                                 in1=sp_t[i3][:, :, 0:nq], op=ALU.add,
                                 reads=[("sp", i3), "Rs"], writes=["Rs"])
                    S.op("act", "activation", out=g_t[i3][:, :, 0:nq], in_=pcs[:, :, 0:nq],
                         func=AF.Exp, scale=-1.0,
                         reads=["pcs"], writes=[("g", i3)])
                    S.op("dve", "tensor_tensor", out=w_t[i3][:, :, 0:nq], in0=e_t[i3][:, :, 0:nq],
                         in1=g_t[i3][:, :, 0:nq], op=ALU.mult,
                         reads=[("e", i3), ("g", i3)], writes=[("w", i3)])

                def stageC(B_):
                    s, hp, bb, q0, nq, kb, i3, qb, bi, nblk = (B_[k] for k in ("s", "hp", "bb", "q0", "nq", "kb", "i3", "qb", "bi", "nblk"))
                    for h in range(2):
                        S.op("pe", "matmul", po[qb][:, 0:nq], lhsT=vz[bb][h][:, kb, :],
                             rhs=w_t[i3][:, h, 0:nq], start=(bi == 0 and h == 0), stop=(bi == nblk - 1 and h == 1),
                             reads=[("vz", bb, h), ("vz", bb, h, 1), ("w", i3)], writes=[("po", qb)])
                    if bi == nblk - 1:
                        S.op("act", "activation", out=ob[qb][:, 0:nq], in_=po[qb][:, 0:nq], func=AF.Copy,
                             reads=[("po", qb)], writes=[("ob", qb)])
                        S.dma("sp", out=oattT[s, hp * 128:(hp + 1) * 128, q0:q0 + nq], in_=ob[qb][:, 0:nq],
                              reads=[("ob", qb)])

                NBLK = len(blocks)
                for i in range(NBLK + 2):
                    if i < NBLK:
                        stageA(blocks[i])
                    if 0 <= i - 1 < NBLK:
                        stageB(blocks[i - 1])
                    if 0 <= i - 2 < NBLK:
                        stageC(blocks[i - 2])
                S.barrier()

        def phase_ssm(l):
            NKc = cfg.NK
            if NKc <= 512:
                halves = [(0, NKc)]
            else:
                halves = [(0, NKc // 2), (NKc // 2, NKc - NKc // 2)]
            TWO_PI = 2.0 * math.pi
            with ExitStack() as st:
                stp = ExitStack()

                def t32(name, shape=(128, 32), dt=F32, stack=None):
                    return sb("s_" + name, list(shape), dt, st if stack is None else stack)

                def t32p(name, shape=(128, 32), dt=F32):
                    return t32(name, shape, dt, stp)
                W1, W1s, W2, W2s, W3 = (t32("W%d" % i, (128, 32, 128), BF16) for i in range(5))
                rho8, f8 = t32("rho8"), t32("f8")
                kidx = t32("kidx", (128, NKc))
                ki = t32("ki", (128, NKc), I32)
                kidx_i = ki
                pw = [ps("s_pw%d" % i, [128, 512], F32, st) for i in range(2)]
                pL = ps("s_pL", [128, 2, 512], F32, st)
                pLs = ps("s_pLs", [128, 2, 512], F32, st)
                py = ps("s_py", [128, 2, 512], F32, st)
                lr, li, dtb, ar, ft = t32p("lr"), t32p("li"), t32p("dtb"), t32p("ar"), t32p("ft")
                yp, ti, tf, tt, sn, cs, mag = (t32p("yp"), t32p("ti", dt=I32), t32p("tf"), t32p("tt"),
                                               t32p("sn"), t32p("cs"), t32p("mag"))
                pwr = {p: t32p("pwr%d" % (p + 7)) for p in range(-7, 9)}
                pwi = {p: t32p("pwi%d" % (p + 7)) for p in range(-7, 9)}
                cre, cim, t_a, t_b, dcol = (t32p("cre"), t32p("cim"), t32p("ta"), t32p("tb"), t32p("dcol"))
                Bre, Bim = t32p("Bre", (128, 32, 16)), t32p("Bim", (128, 32, 16))
                bbr, bbi = t32p("bbr", (128, 32, 16)), t32p("bbi", (128, 32, 16))
                Craw = [t32p("Craw%d" % i, (128, 4, 128)) for i in range(2)]
                Cn = [t32p("Cn%d" % i, (128, 32, 16)) for i in range(2)]
                q1, q2, q3, q4 = (t32p("q%d" % i, (128, 32, 16)) for i in range(4))
                X1, X1s, X2, X2s, X1p, X2p = (t32p("X%d" % i, (128, 32, 8, 16)) for i in range(6))
                maskBL = t32p("maskBL", (128, 128))
                w3t = t32p("w3t", (128, 128))
                w3r = [t32p("w3r%d" % i, (128, 128)) for i in range(2)]

                for hf in (0, 64):
                    S.dma("sp", out=lr[hf:hf + 64, :], in_=P["ssm_lam_re"][l].rearrange("g n -> n g"),
                          allow_slow_non_contiguous=True, writes=["lr"])
                    S.dma("sp", out=li[hf:hf + 64, :], in_=P["ssm_lam_im"][l].rearrange("g n -> n g"),
                          allow_slow_non_contiguous=True, writes=["li"])
                    S.dma("sp", out=dtb[hf:hf + 64, :], in_=P["ssm_log_dt"][l].partition_broadcast(64),
                          writes=["dtb"])
                    S.dma("sp", out=Bre[hf:hf + 64, :, :], in_=P["ssm_b_re"][l].rearrange("g n c -> n g c"),
                          writes=["Bre"])
                    S.dma("sp", out=Bim[hf:hf + 64, :, :], in_=P["ssm_b_im"][l].rearrange("g n c -> n g c"),
                          writes=["Bim"])
                for i, nm in enumerate(["ssm_c_re", "ssm_c_im"]):
                    for dup in range(2):
                        S.dma("sp", out=Craw[i][:, :, dup * 64:(dup + 1) * 64],
                              in_=P[nm][l].rearrange("g c n -> (g c) n").rearrange("(t p) n -> p t n", p=128),
                              writes=[("Craw", i)])
                for s8 in range(8):
                    S.dma("sp", out=dcol[s8 * 16:(s8 + 1) * 16, :], in_=P["ssm_d"][l].rearrange("(g c) -> c g", c=16),
                          allow_slow_non_contiguous=True, writes=["dcol"])
                S.op("pool", "memset", maskBL[:], 1.0, writes=["maskBL"])
                S.op("pool", "affine_select", out=maskBL[:].rearrange("p (j c) -> p j c", c=16),
                     in_=maskBL[:].rearrange("p (j c) -> p j c", c=16), pattern=[[16, 8], [0, 16]],
                     compare_op=ALU.is_ge, fill=0.0, base=15, channel_multiplier=-1,
                     reads=["maskBL"], writes=["maskBL"])
                S.op("pool", "iota", kidx_i[:], pattern=[[1, NKc]], base=0, channel_multiplier=0,
                     writes=["kti"])
                S.op("pool", "tensor_copy", out=kidx[:], in_=kidx_i[:], reads=["kti"], writes=["kidx"])

                if int(os.environ.get("SSMSTOP", "9")) <= 1:
                    S.barrier()
                    return
                def V(eng, meth, *a, r=(), w=(), **kw):
                    S.op(eng, meth, *a, reads=list(r), writes=list(w), **kw)

                def sin_turns(dst, dname, y, yname, ti_, tf_, tt_, pre):
                    V("dve", "tensor_copy", out=ti_, in_=y, r=[yname], w=[pre + "ti"])
                    V("dve", "tensor_copy", out=tf_, in_=ti_, r=[pre + "ti"], w=[pre + "tf"])
                    V("dve", "tensor_tensor", out=tf_, in0=y, in1=tf_, op=ALU.subtract, r=[yname, pre + "tf"], w=[pre + "tf"])
                    V("dve", "tensor_single_scalar", out=tt_, in_=tf_, scalar=0.5, op=ALU.is_gt, r=[pre + "tf"], w=[pre + "tt"])
                    V("dve", "tensor_tensor", out=tf_, in0=tf_, in1=tt_, op=ALU.subtract, r=[pre + "tf", pre + "tt"], w=[pre + "tf"])
                    V("dve", "tensor_single_scalar", out=tt_, in_=tf_, scalar=-0.5, op=ALU.is_lt, r=[pre + "tf"], w=[pre + "tt"])
                    V("dve", "tensor_tensor", out=tf_, in0=tf_, in1=tt_, op=ALU.add, r=[pre + "tf", pre + "tt"], w=[pre + "tf"])
                    if dst is not None:
                        V("act", "activation", out=dst, in_=tf_, func=AF.Sin, scale=6.283185, r=[pre + "tf"], w=[dname])

                def cos_from_reduced(dst, dname, tf_, tt_, pre):
                    V("dve", "tensor_scalar_add", out=tf_, in0=tf_, scalar1=0.25, r=[pre + "tf"], w=[pre + "tf"])
                    V("dve", "tensor_single_scalar", out=tt_, in_=tf_, scalar=0.5, op=ALU.is_gt, r=[pre + "tf"], w=[pre + "tt"])
                    V("dve", "tensor_tensor", out=tf_, in0=tf_, in1=tt_, op=ALU.subtract, r=[pre + "tf", pre + "tt"], w=[pre + "tf"])
                    V("act", "activation", out=dst, in_=tf_, func=AF.Sin, scale=6.283185, r=[pre + "tf"], w=[dname])

                V("act", "activation", out=dtb[:], in_=dtb[:], func=AF.Exp, r=["dtb"], w=["dtb"])
                V("dve", "tensor_tensor", out=ar[:], in0=lr[:], in1=dtb[:], op=ALU.mult, r=["lr", "dtb"], w=["ar"])
                V("dve", "tensor_tensor", out=ft[:], in0=li[:], in1=dtb[:], op=ALU.mult, r=["li", "dtb"], w=["ft"])
                V("dve", "tensor_scalar_mul", out=ft[:], in0=ft[:], scalar1=1.0 / TWO_PI, r=["ft"], w=["ft"])
                for p in range(-7, 9):
                    V("act", "activation", out=mag[:], in_=ar[:], func=AF.Exp, scale=float(p), r=["ar"], w=["mag"])
                    if p == 8:
                        V("dve", "tensor_copy", out=rho8[:], in_=mag[:], r=["mag"], w=["rho8"])
                    V("dve", "tensor_scalar_mul", out=yp[:], in0=ft[:], scalar1=float(p), r=["ft"], w=["yp"])
                    sin_turns(sn[:], "sn", yp[:], "yp", ti[:], tf[:], tt[:], "a")
                    if p == 8:
                        V("dve", "tensor_copy", out=f8[:], in_=tf[:], r=["atf"], w=["f8"])
                    cos_from_reduced(cs[:], "cs", tf[:], tt[:], "a")
                    V("dve", "tensor_tensor", out=pwr[p][:], in0=mag[:], in1=cs[:], op=ALU.mult, r=["mag", "cs"], w=[("pwr", p)])
                    V("dve", "tensor_tensor", out=pwi[p][:], in0=mag[:], in1=sn[:], op=ALU.mult, r=["mag", "sn"], w=[("pwi", p)])
                if int(os.environ.get("SSMSTOP", "9")) <= 2:
                    S.barrier()
                    return
                V("dve", "tensor_scalar_add", out=t_a[:], in0=pwr[1][:], scalar1=-1.0, r=[("pwr", 1)], w=["ta"])
                V("dve", "tensor_tensor", out=cre[:], in0=t_a[:], in1=lr[:], op=ALU.mult, r=["ta", "lr"], w=["cre"])
                V("dve", "tensor_tensor", out=t_b[:], in0=pwi[1][:], in1=li[:], op=ALU.mult, r=[("pwi", 1), "li"], w=["tb"])
                V("dve", "tensor_tensor", out=cre[:], in0=cre[:], in1=t_b[:], op=ALU.add, r=["cre", "tb"], w=["cre"])
                V("dve", "tensor_tensor", out=cim[:], in0=pwi[1][:], in1=lr[:], op=ALU.mult, r=[("pwi", 1), "lr"], w=["cim"])
                V("dve", "tensor_tensor", out=t_b[:], in0=t_a[:], in1=li[:], op=ALU.mult, r=["ta", "li"], w=["tb"])
                V("dve", "tensor_tensor", out=cim[:], in0=cim[:], in1=t_b[:], op=ALU.subtract, r=["cim", "tb"], w=["cim"])
                V("dve", "tensor_tensor", out=t_a[:], in0=lr[:], in1=lr[:], op=ALU.mult, r=["lr"], w=["ta"])
                V("dve", "tensor_tensor", out=t_b[:], in0=li[:], in1=li[:], op=ALU.mult, r=["li"], w=["tb"])
                V("dve", "tensor_tensor", out=t_a[:], in0=t_a[:], in1=t_b[:], op=ALU.add, r=["ta", "tb"], w=["ta"])
                V("dve", "reciprocal", out=t_a[:], in_=t_a[:], r=["ta"], w=["ta"])
                V("dve", "tensor_tensor", out=cre[:], in0=cre[:], in1=t_a[:], op=ALU.mult, r=["cre", "ta"], w=["cre"])
                V("dve", "tensor_tensor", out=cim[:], in0=cim[:], in1=t_a[:], op=ALU.mult, r=["cim", "ta"], w=["cim"])

                def bc(t):
                    return t[:, :].unsqueeze(2).broadcast_to([128, 32, 16])

                def cplx(fr, fi, frn, fin, xr, xi, xrn, xin):
                    V("dve", "tensor_tensor", out=q1[:], in0=xr[:], in1=bc(fr), op=ALU.mult, r=[xrn, frn, "q1"], w=["q1"])
                    V("dve", "tensor_tensor", out=q3[:], in0=xi[:], in1=bc(fi), op=ALU.mult, r=[xin, fin, "q3"], w=["q3"])
                    V("dve", "tensor_tensor", out=q1[:], in0=q1[:], in1=q3[:], op=ALU.subtract, r=["q1", "q3"], w=["q1"])
                    V("dve", "tensor_tensor", out=q2[:], in0=xi[:], in1=bc(fr), op=ALU.mult, r=[xin, frn, "q2"], w=["q2"])
                    V("dve", "tensor_tensor", out=q4[:], in0=xr[:], in1=bc(fi), op=ALU.mult, r=[xrn, fin, "q4"], w=["q4"])
                    V("dve", "tensor_tensor", out=q2[:], in0=q2[:], in1=q4[:], op=ALU.add, r=["q2", "q4"], w=["q2"])

                def put(dst, dname, blk, lo_src, lo_sign, up_src, up_sign):
                    for (rows, src, sign, eng) in ((slice(0, 64), lo_src, lo_sign, "act"),
                                                   (slice(64, 128), up_src, up_sign, "pool")):
                        srcn = "q1" if src is q1 else "q2"
                        if eng == "act":
                            V("act", "activation", out=dst[rows, :, blk, :], in_=src[rows, :, :], func=AF.Copy,
                              scale=float(sign), r=[srcn], w=[dname])
                        else:
                            V("pool", "tensor_scalar", out=dst[rows, :, blk, :], in0=src[rows, :, :],
                              scalar1=float(sign), scalar2=None, op0=ALU.mult, r=[srcn], w=[dname])

                if int(os.environ.get("SSMSTOP", "9")) <= 3:
                    S.barrier()
                    return
                cplx(cre, cim, "cre", "cim", Bre, Bim, "Bre", "Bim")
                V("dve", "tensor_copy", out=bbr[:], in_=q1[:], r=["q1"], w=["bbr"])
                V("dve", "tensor_copy", out=bbi[:], in_=q2[:], r=["q2"], w=["bbi"])
                for i in range(2):
                    for t4 in range(4):
                        pb = pw[(i * 4 + t4) % 2]
                        V("pe", "transpose", out=pb[:, 0:128], in_=Craw[i][:, t4, :], identity=ident_f[:, :],
                          r=[("Craw", i), "ident_f"], w=[("pw", (i * 4 + t4) % 2)])
                        V("act", "activation", out=Cn[i][:, t4 * 8:(t4 + 1) * 8, :],
                          in_=pb[:, 0:128].rearrange("p (g c) -> p g c", c=16), func=AF.Copy,
                          r=[("pw", (i * 4 + t4) % 2)], w=[("Cn", i)])
                for s8 in range(8):
                    p = 7 - s8
                    cplx(pwr[p], pwi[p], ("pwr", p), ("pwi", p), bbr, bbi, "bbr", "bbi")
                    put(X1, "X1", s8, q1, 1, q2, 1)
                    put(X1s, "X1s", s8, q2, 1, q1, -1)
                    p = -s8
                    cplx(pwr[p], pwi[p], ("pwr", p), ("pwi", p), bbr, bbi, "bbr", "bbi")
                    put(X1p, "X1p", s8, q1, 1, q2, 1)
                    p = s8 + 1
                    cplx(pwr[p], pwi[p], ("pwr", p), ("pwi", p), Cn[0], Cn[1], ("Cn", 0), ("Cn", 1))
                    put(X2, "X2", s8, q1, 1, q2, -1)
                    put(X2s, "X2s", s8, q2, -1, q1, -1)
                    p = s8
                    cplx(pwr[p], pwi[p], ("pwr", p), ("pwi", p), Cn[0], Cn[1], ("Cn", 0), ("Cn", 1))
                    put(X2p, "X2p", s8, q1, 1, q2, -1)
                if int(os.environ.get("SSMSTOP", "9")) <= 4:
                    S.barrier()
                    return
                V("act", "activation", out=W2[:].rearrange("p g x -> p (g x)"),
                  in_=X2[:].rearrange("p g s c -> p (g s c)"), func=AF.Copy, r=["X2"], w=["W2"])
                V("dve", "tensor_copy", out=W2s[:].rearrange("p g x -> p (g x)"),
                  in_=X2s[:].rearrange("p g s c -> p (g s c)"), r=["X2s"], w=["W2s"])
                for g in range(NG):
                    b = g % 2
                    V("pe", "transpose", out=pw[b][:, 0:128], in_=X1[:, g].rearrange("p s c -> p (s c)"),
                      identity=ident_f[:, :], r=["X1", "ident_f"], w=[("pw", b)])
                    V("pe", "transpose", out=pw[b][:, 128:256], in_=X1s[:, g].rearrange("p s c -> p (s c)"),
                      identity=ident_f[:, :], r=["X1s", "ident_f"], w=[("pw", b)])
                    V("pe", "matmul", pw[b][:, 256:384], lhsT=X1p[:, g].rearrange("p s c -> p (s c)"),
                      rhs=X2p[:, g].rearrange("p s c -> p (s c)"), start=True, stop=True,
                      r=["X1p", "X2p"], w=[("pw", b)])
                    V("act", "activation", out=W1[:, g, :], in_=pw[b][:, 0:128], func=AF.Copy,
                      r=[("pw", b)], w=["W1"])
                    V("act", "activation", out=W1s[:, g, :], in_=pw[b][:, 128:256], func=AF.Copy,
                      r=[("pw", b)], w=["W1s"])
                    V("act", "activation", out=w3r[b][:], in_=pw[b][:, 256:384], func=AF.Copy,
                      r=[("pw", b)], w=[("w3r", b)])
                    V("dve", "tensor_tensor", out=w3t[:], in0=w3r[b][:], in1=maskBL[:], op=ALU.mult,
                      r=[("w3r", b), "maskBL"], w=["w3t"])
                    V("dve", "scalar_tensor_tensor", out=W3[:, g, :], in0=ident_f[:], scalar=dcol[:, g:g + 1],
                      in1=w3t[:], op0=ALU.mult, op1=ALU.add, r=["ident_f", "dcol", "w3t"], w=["W3"])

                if int(os.environ.get("SSMSTOP", "9")) <= 5:
                    S.barrier()
                    return
                S.barrier()
                stp.close()
                cT = [t32("cT%d" % i, (128, NKc)) for i in range(2)]
                sT = [t32("sT%d" % i, (128, NKc)) for i in range(2)]
                rtab = [t32("rtab%d" % i, (128, NKc)) for i in range(2)]
                yk = t32("yk", (128, NKc))
                kf = t32("kf", (128, NKc))
                kt = t32("kt", (128, NKc))
                Ut = [t32("U%d" % i, (128, NKc), BF16) for i in range(2)]
                m1, m2, Tt = t32("m1", (128, NKc)), t32("m2", (128, NKc)), t32("Tt", (128, NKc))
                Pm = [t32("Pm%d" % i, (128, NKc + 2), BF16) for i in range(2)]
                Qm = [t32("Qm%d" % i, (128, NKc + 2), BF16) for i in range(2)]
                yo = [t32("yo%d" % i, (128, NKc)) for i in range(2)]
                for i in range(2):
                    S.op("pool", "memset", Pm[i][:], 0.0, writes=[("Pm", i)])
                    S.op("pool", "memset", Qm[i][:], 0.0, writes=[("Qm", i)])
                it = 0
                for g in range(NG):
                    gb = g % 2
                    V("dve", "tensor_scalar_mul", out=yk[:], in0=kidx[:], scalar1=f8[:, g:g + 1],
                      r=["kidx", "f8"], w=["yk"])
                    sin_turns(sT[gb][:], ("sT", gb), yk[:], "yk", ki[:], kf[:], kt[:], "k")
                    cos_from_reduced(cT[gb][:], ("cT", gb), kf[:], kt[:], "k")
                    V("act", "activation", out=rtab[gb][:], in_=kidx[:], func=AF.Identity, scale=0.0,
                      bias=rho8[:, g:g + 1], r=["kidx", "rho8"], w=[("rtab", gb)])
                    RL = int(os.environ.get("RUNLVL", "9"))
                    for s in range(NS):
                        if RL < 2:
                            continue
                        ub = it % 2
                        it += 1
                        S.dma("sp", out=Ut[ub][:], in_=Usc[s, g].rearrange("s c k -> (s c) k"),
                              writes=[("U", ub)])
                        for hi, (lo, wd) in enumerate(halves):
                            V("pe", "matmul", pL[:, hi, 0:wd], lhsT=W1[:, g, :], rhs=Ut[ub][:, lo:lo + wd],
                              start=True, stop=True, r=["W1", ("U", ub)], w=["pL"])
                            V("pe", "matmul", pLs[:, hi, 0:wd], lhsT=W1s[:, g, :], rhs=Ut[ub][:, lo:lo + wd],
                              start=True, stop=True, r=["W1s", ("U", ub)], w=["pLs"])
                        if RL < 3:
                            continue
                        for hi, (lo, wd) in enumerate(halves):
                            V("dve", "tensor_tensor", out=m1[:, lo:lo + wd], in0=pL[:, hi, 0:wd],
                              in1=cT[gb][:, lo:lo + wd], op=ALU.mult, r=["pL", ("cT", gb)], w=["m1"])
                            V("dve", "tensor_tensor", out=m2[:, lo:lo + wd], in0=pLs[:, hi, 0:wd],
                              in1=sT[gb][:, lo:lo + wd], op=ALU.mult, r=["pLs", ("sT", gb)], w=["m2"])
                        V("dve", "tensor_tensor", out=m1[:], in0=m1[:], in1=m2[:], op=ALU.add,
                          r=["m1", "m2"], w=["m1"])
                        V("dve", "tensor_tensor_scan", out=Tt[:], data0=rtab[gb][:], data1=m1[:], initial=0.0,
                          op0=ALU.mult, op1=ALU.add, r=[("rtab", gb), "m1"], w=["Tt"])
                        V("dve", "tensor_tensor", out=Pm[ub][:, 1:NKc + 1], in0=Tt[:], in1=cT[gb][:], op=ALU.mult,
                          r=["Tt", ("cT", gb)], w=[("Pm", ub)])
                        V("pool", "tensor_tensor", out=Qm[ub][:, 1:NKc + 1], in0=Tt[:], in1=sT[gb][:], op=ALU.mult,
                          r=["Tt", ("sT", gb)], w=[("Qm", ub)])
                        if RL < 4:
                            continue
                        for hi, (lo, wd) in enumerate(halves):
                            V("pe", "matmul", py[:, hi, 0:wd], lhsT=W3[:, g, :], rhs=Ut[ub][:, lo:lo + wd],
                              start=True, stop=False, r=["W3", ("U", ub)], w=["py"])
                            V("pe", "matmul", py[:, hi, 0:wd], lhsT=W2[:, g, :],
                              rhs=Pm[ub][:, lo:lo + wd], start=False, stop=False,
                              r=["W2", ("Pm", ub)], w=["py"])
                            V("pe", "matmul", py[:, hi, 0:wd], lhsT=W2s[:, g, :],
                              rhs=Qm[ub][:, lo:lo + wd], start=False, stop=True,
                              r=["W2s", ("Qm", ub)], w=["py"])
                        if RL < 5:
                            continue
                        for hi, (lo, wd) in enumerate(halves):
                            V("act", "activation", out=yo[ub][:, lo:lo + wd], in_=py[:, hi, 0:wd], func=AF.Copy,
                              r=["py"], w=[("yo", ub)])
                        S.dma("sp", out=ysS[s, g].rearrange("s c k -> (s c) k"), in_=yo[ub][:],
                              reads=[("yo", ub)])
                S.barrier()

        def phase_mixB1(l):
            w_in = P["w_in"][l]
            c0 = gcol(1, l)
            HALO = CW - 1
            with ExitStack() as st:
                wC = sb("c_w", [128, DT, 1024], BF16, st)
                dg = sb("c_dg", [128, 4 * CW, 128], BF16, st)
                cw = sb("c_cw", [128, 4, CW], F32, st)
                cb = sb("c_cb", [128, 4], F32, st)
                lng = sb("c_lng", [128, 4], F32, st)
                lnb = sb("c_lnb", [128, 4], F32, st)
                eps_ln = sb("c_epsln", [128, 1], F32, st)
                hsb = [sb("c_hs%d" % i, [128, DT, 512], F32, st) for i in range(2)]
                xn = sb("c_xn", [128, DT, 512], BF16, st)
                sqb = [sb("c_sq%d" % i, [128, 512], BF16, st) for i in range(2)]
                rstd = sb("c_rstd", [128, 512], F32, st)
                hc = sb("c_hc", [128, 4, HALO + 512], BF16, st)
                cacc = sb("c_cacc", [128, 4, 512], F32, st)
                hcvb = [sb("c_hcv%d" % i, [128, 4, 512], BF16, st) for i in range(2)]
                sg = [sb("c_sg%d" % i, [128, 512], F32, st) for i in range(2)]
                tmq = sb("c_tm", [128, 512], F32, st)
                mu = sb("c_mu", [128, 512], F32, st)
                lrs = sb("c_lrs", [128, 512], F32, st)
                pss = ps("c_pss", [128, 512], F32, st)
                NPB = 6
                pp = [ps("c_pp%d" % i, [128, 512], F32, st) for i in range(NPB)]
                pcnt = [0]

                def nxt():
                    b = pcnt[0] % NPB
                    pcnt[0] += 1
                    return b

                for dt in range(DT):
                    load_w_cast(wC[:, dt, :], w_in[dt * 128:(dt + 1) * 128, 512:1536], ("wC", dt))
                wCr = [("wC", dt) for dt in range(DT)]
                for ct in range(4):
                    S.dma("sp", out=cw[:, ct, :], in_=P["conv_w"][l][:, ct * 128:(ct + 1) * 128].rearrange("j p -> p j"),
                          allow_slow_non_contiguous=True, writes=["cw"])
                for nm, tl in (("conv_b", cb), ("conv_ln_g", lng), ("conv_ln_b", lnb)):
                    S.dma("sp", out=tl[:], in_=P[nm][l].rearrange("(t p) -> p t", p=128),
                          allow_slow_non_contiguous=True, writes=[nm])
                S.op("pool", "memset", eps_ln[:], LN_EPS, writes=["eps_ln"])
                for ct in range(4):
                    for j in range(CW):
                        eng = "dve" if (j % 2 == 0) else "pool"
                        S.op(eng, "tensor_scalar", out=dg[:, ct * CW + j, :], in0=ident_f[:], scalar1=cw[:, ct, j:j + 1],
                             scalar2=None, op0=ALU.mult, reads=["ident_f", "cw"], writes=[("dg", ct)])
                tcnt = 0
                for s in range(NS):
                    for (t0, n) in cfg.tiles:
                        hs = hsb[tcnt % 2]
                        hcv = hcvb[tcnt % 2]
                        hsr, hcvr = ("hsb", tcnt % 2), ("hcvb", tcnt % 2)
                        tcnt += 1
                        S.dma("sp", out=hs[:, :, 0:n], in_=hT[s, :, t0:t0 + n].rearrange("(t p) n -> p t n", p=128),
                              writes=[hsr])
                        rmsnorm(hs, xn, sqb, pss, rstd, c0, n, hs_name=hsr)
                        if t0 == 0:
                            S.op("pool", "memset", hc[:, :, 0:HALO], 0.0, writes=["hc"])
                        for ct in range(4):
                            ba, bg = nxt(), nxt()
                            for dt in range(DT):
                                S.op("pe", "matmul", pp[ba][:, 0:n], lhsT=wC[:, dt, ct * 128:(ct + 1) * 128],
                                     rhs=xn[:, dt, 0:n], start=(dt == 0), stop=(dt == DT - 1),
                                     reads=wCr + ["xn"], writes=[("pp", ba)])
                            for dt in range(DT):
                                S.op("pe", "matmul", pp[bg][:, 0:n], lhsT=wC[:, dt, 512 + ct * 128:512 + (ct + 1) * 128],
                                     rhs=xn[:, dt, 0:n], start=(dt == 0), stop=(dt == DT - 1),
                                     reads=wCr + ["xn"], writes=[("pp", bg)])
                            S.op("act", "activation", out=sg[ct % 2][:, 0:n], in_=pp[bg][:, 0:n], func=AF.Sigmoid,
                                 reads=[("pp", bg)], writes=[("sg", ct % 2)])
                            S.op("dve", "tensor_tensor", out=hc[:, ct, HALO:HALO + n], in0=pp[ba][:, 0:n],
                                 in1=sg[ct % 2][:, 0:n], op=ALU.mult,
                                 reads=[("pp", ba), ("sg", ct % 2)], writes=["hc"])
                        for ct in range(4):
                            bc_ = nxt()
                            for j in range(CW):
                                S.op("pe", "matmul", pp[bc_][:, 0:n], lhsT=dg[:, ct * CW + j, :], rhs=hc[:, ct, j:j + n],
                                     start=(j == 0), stop=(j == CW - 1),
                                     reads=[("dg", ct), "hc"], writes=[("pp", bc_)])
                            S.op("act", "activation", out=cacc[:, ct, 0:n], in_=pp[bc_][:, 0:n], func=AF.Identity,
                                 bias=cb[:, ct:ct + 1], scale=1.0,
                                 reads=[("pp", bc_), "conv_b"], writes=[("cacc", ct)])
                        if n >= HALO:
                            S.op("pool", "tensor_copy", out=hc[:, :, 0:HALO], in_=hc[:, :, n:n + HALO],
                                 reads=["hc"], writes=["hc"])
                        bmu, bvar = nxt(), nxt()
                        for ct in range(4):
                            b2 = ct % 2
                            S.op("act", "activation", out=sqb[b2][:, 0:n], in_=cacc[:, ct, 0:n], func=AF.Copy,
                                 reads=[("cacc", ct)], writes=[("sqb", b2)])
                            S.op("pe", "matmul", pp[bmu][:, 0:n], lhsT=ones_bf[:], rhs=sqb[b2][:, 0:n],
                                 start=(ct == 0), stop=(ct == 3), reads=[("sqb", b2), "ones_bf"], writes=[("pp", bmu)])
                        for ct in range(4):
                            b2 = ct % 2
                            S.op("act", "activation", out=sqb[b2][:, 0:n], in_=cacc[:, ct, 0:n], func=AF.Square,
                                 reads=[("cacc", ct)], writes=[("sqb", b2)])
                            S.op("pe", "matmul", pp[bvar][:, 0:n], lhsT=ones_bf[:], rhs=sqb[b2][:, 0:n],
                                 start=(ct == 0), stop=(ct == 3), reads=[("sqb", b2), "ones_bf"], writes=[("pp", bvar)])
                        S.op("act", "activation", out=mu[:, 0:n], in_=pp[bmu][:, 0:n], func=AF.Copy, scale=1.0 / 512,
                             reads=[("pp", bmu)], writes=["mu"])
                        S.op("dve", "tensor_tensor", out=tmq[:, 0:n], in0=mu[:, 0:n], in1=mu[:, 0:n], op=ALU.mult,
                             reads=["mu"], writes=["tmq"])
                        S.op("dve", "scalar_tensor_tensor", out=lrs[:, 0:n], in0=pp[bvar][:, 0:n], scalar=1.0 / 512,
                             in1=tmq[:, 0:n], op0=ALU.mult, op1=ALU.subtract,
                             reads=[("pp", bvar), "tmq"], writes=["lrs"])
                        S.op("act", "activation", out=lrs[:, 0:n], in_=lrs[:, 0:n], func=AF.Sqrt, bias=eps_ln[:], scale=1.0,
                             reads=["lrs", "eps_ln"], writes=["lrs"])
                        S.op("dve", "reciprocal", out=lrs[:, 0:n], in_=lrs[:, 0:n], reads=["lrs"], writes=["lrs"])
                        for ct in range(4):
                            S.op("dve", "tensor_tensor", out=tmq[:, 0:n], in0=cacc[:, ct, 0:n], in1=mu[:, 0:n],
                                 op=ALU.subtract, reads=[("cacc", ct), "mu"], writes=["tmq"])
                            S.op("dve", "tensor_tensor", out=tmq[:, 0:n], in0=tmq[:, 0:n], in1=lrs[:, 0:n],
                                 op=ALU.mult, reads=["tmq", "lrs"], writes=["tmq"])
                            S.op("act", "activation", out=hcv[:, ct, 0:n], in_=tmq[:, 0:n], func=AF.Silu,
                                 scale=lng[:, ct:ct + 1], bias=lnb[:, ct:ct + 1],
                                 reads=["tmq", "conv_ln_g", "conv_ln_b"], writes=[hcvr])
                        S.dma("sp", out=hcvS[s, :, t0:t0 + n].rearrange("(t p) n -> p t n", p=128), in_=hcv[:, :, 0:n],
                              reads=[hcvr])
                S.barrier()

        def phase_mixB2(l):
            w_in = P["w_in"][l]
            c0 = gcol(1, l)
            HALO = CW - 1
            with ExitStack() as st:
                wB = sb("b_w", [128, DT, 3072], BF16, st)
                wglu = sb("b_wglu", [128, 4, 2048], BF16, st)
                wpw = sb("b_wpw", [128, 4, D], BF16, st)
                wo = sb("b_wo", [128, 4, D], BF16, st)
                wout = sb("b_wout", [128, DT, D], BF16, st)
                hsb = [sb("b_hs%d" % i, [128, DT, 512], F32, st) for i in range(2)]
                xnb = [sb("b_xn%d" % i, [128, DT, 512], BF16, st) for i in range(2)]
                sqb = [sb("b_sq%d" % i, [128, 512], BF16, st) for i in range(2)]
                rstd = sb("b_rstd", [128, 512], F32, st)
                hcvb = [sb("b_hcv%d" % i, [128, 4, 512], BF16, st) for i in range(2)]
                yst = [sb("b_yst%d" % i, [128, 8, 64], F32, st) for i in range(4)]
                gyb = [sb("b_gy%d" % i, [128, 4, 512], BF16, st) for i in range(2)]
                oatb = [sb("b_oat%d" % i, [128, 4, 512], BF16, st) for i in range(2)]
                tg = [sb("b_tg%d" % i, [128, 512], F32, st) for i in range(2)]
                mrg = sb("b_mrg", [128, DT, 512], BF16, st)
                sg = [sb("b_sg%d" % i, [128, 512], F32, st) for i in range(2)]
                tm = [sb("b_tm%d" % i, [128, 512], F32, st) for i in range(2)]
                macc = sb("b_macc", [128, 512], F32, st)
                pss = ps("b_pss", [128, 512], F32, st)
                NPB = 6
                pp = [ps("b_pp%d" % i, [128, 512], F32, st) for i in range(NPB)]
                pcnt = [0]

                def nxt():
                    b = pcnt[0] % NPB
                    pcnt[0] += 1
                    return b

                for dt in range(DT):
                    load_w_cast(wB[:, dt, 0:1536], w_in[dt * 128:(dt + 1) * 128, 3072:4608], ("wB", dt))
                    load_w_cast(wB[:, dt, 1536:3072], w_in[dt * 128:(dt + 1) * 128, 4608:6144], ("wB", dt, 1))
                    load_w_cast(wout[:, dt, :], P["w_out"][l][dt * 128:(dt + 1) * 128, :], ("wout", dt))
                for ct in range(4):
                    load_w_cast(wglu[:, ct, :], P["ssm_w_glu"][l][ct * 128:(ct + 1) * 128, :], ("wglu", ct))
                    load_w_cast(wpw[:, ct, :], P["conv_w_out"][l][ct * 128:(ct + 1) * 128, :], ("wpw", ct))
                    load_w_cast(wo[:, ct, :], P["attn_w_o"][l][ct * 128:(ct + 1) * 128, :], ("wo", ct))
                wBr = [("wB", dt) for dt in range(DT)] + [("wB", dt, 1) for dt in range(DT)]
                tiles_all = [(s, t0, n) for s in range(NS) for (t0, n) in cfg.tiles]

                def pro_loads(idx):
                    s, t0, n = tiles_all[idx]
                    ib = idx % 2
                    hs, xn, oat, hcv, gy = hsb[ib], xnb[ib], oatb[ib], hcvb[ib], gyb[ib]
                    hsr, xnr, oatr, hcvr, gyr = ("hs", ib), ("xn", ib), ("oat", ib), ("hcv", ib), ("gy", ib)
                    nk = n // 8
                    k0 = t0 // 8
                    S.dma("sp", out=hs[:, :, 0:n], in_=hT[s, :, t0:t0 + n].rearrange("(t p) n -> p t n", p=128),
                          writes=[hsr])
                    S.dma("sp", out=oat[:, :, 0:n],
                          in_=oattT[s, :, t0:t0 + n].rearrange("(t p) n -> p t n", p=128), writes=[oatr])
                    S.dma("sp", out=hcv[:, :, 0:n],
                          in_=hcvS[s, :, t0:t0 + n].rearrange("(t p) n -> p t n", p=128), writes=[hcvr])
                    for ct in range(4):
                        yt = yst[ct]
                        for gl in range(8):
                            S.dma("sp", out=yt[gl * 16:(gl + 1) * 16, :, 0:nk],
                                  in_=ysS[s, ct * 8 + gl, :, :, k0:k0 + nk].rearrange("j c k -> c j k"),
                                  writes=[("yst", ct)])

                def pro_compute(idx):
                    s, t0, n = tiles_all[idx]
                    ib = idx % 2
                    hs, xn, oat, hcv, gy = hsb[ib], xnb[ib], oatb[ib], hcvb[ib], gyb[ib]
                    hsr, xnr, oatr, hcvr, gyr = ("hs", ib), ("xn", ib), ("oat", ib), ("hcv", ib), ("gy", ib)
                    nk = n // 8
                    rmsnorm(hs, xn, sqb, pss, rstd, c0, n, xn_name=xnr, hs_name=hsr)
                    for ct in range(4):
                        yb = ct
                        yt = yst[yb]
                        yv = yt[:, :, 0:nk]
                        t1 = tg[0][:, 0:n].rearrange("p (j k) -> p j k", j=8)
                        t2 = tg[1][:, 0:n].rearrange("p (j k) -> p j k", j=8)
                        S.op("act", "activation", out=t1, in_=yv, func=AF.Square,
                             reads=[("yst", yb)], writes=[("tg", 0)])
                        S.op("dve", "tensor_scalar", out=t1, in0=t1, scalar1=0.044715, scalar2=1.0,
                             op0=ALU.mult, op1=ALU.add, reads=[("tg", 0)], writes=[("tg", 0)])
                        S.op("dve", "tensor_tensor", out=t1, in0=t1, in1=yv, op=ALU.mult,
                             reads=[("tg", 0), ("yst", yb)], writes=[("tg", 0)])
                        S.op("act", "activation", out=t2, in_=t1, func=AF.Tanh, scale=0.7978845608,
                             reads=[("tg", 0)], writes=[("tg", 1)])
                        S.op("dve", "tensor_scalar", out=t2, in0=t2, scalar1=0.5, scalar2=0.5,
                             op0=ALU.mult, op1=ALU.add, reads=[("tg", 1)], writes=[("tg", 1)])
                        S.op("dve", "tensor_tensor", out=gy[:, ct, 0:n].rearrange("p (k j) -> p j k", j=8),
                             in0=t2, in1=yv, op=ALU.mult,
                             reads=[("tg", 1), ("yst", yb)], writes=[gyr])

                def main_tile(idx):
                    s, t0, n = tiles_all[idx]
                    ib = idx % 2
                    hs, xn, oat, hcv, gy = hsb[ib], xnb[ib], oatb[ib], hcvb[ib], gyb[ib]
                    hsr, xnr, oatr, hcvr, gyr = ("hs", ib), ("xn", ib), ("oat", ib), ("hcv", ib), ("gy", ib)
                    if idx + 1 < len(tiles_all):
                        pro_loads(idx + 1)
                    for f in range(DT):
                        if f == 2 and idx + 1 < len(tiles_all):
                            pro_compute(idx + 1)
                        def gate(br):
                            bgt = nxt()
                            c1 = br * 1024 + f * 128
                            for dt in range(DT):
                                S.op("pe", "matmul", pp[bgt][:, 0:n], lhsT=wB[:, dt, c1:c1 + 128],
                                     rhs=xn[:, dt, 0:n], start=(dt == 0), stop=(dt == DT - 1),
                                     reads=wBr + [xnr], writes=[("pp", bgt)])
                            S.op("act", "activation", out=sg[br % 2][:, 0:n], in_=pp[bgt][:, 0:n], func=AF.Sigmoid,
                                 reads=[("pp", bgt)], writes=[("sg", br % 2)])
                            return sg[br % 2], ("sg", br % 2)
                        ba, bg = nxt(), nxt()
                        for ct in range(4):
                            S.op("pe", "matmul", pp[ba][:, 0:n], lhsT=wglu[:, ct, f * 128:(f + 1) * 128],
                                 rhs=gy[:, ct, 0:n], start=(ct == 0), stop=(ct == 3),
                                 reads=[("wglu", c) for c in range(4)] + [gyr], writes=[("pp", ba)])
                        for ct in range(4):
                            S.op("pe", "matmul", pp[bg][:, 0:n], lhsT=wglu[:, ct, 1024 + f * 128:1024 + (f + 1) * 128],
                                 rhs=gy[:, ct, 0:n], start=(ct == 0), stop=(ct == 3),
                                 reads=[("wglu", c) for c in range(4)] + [gyr], writes=[("pp", bg)])
                        S.op("act", "activation", out=tm[1][:, 0:n], in_=pp[bg][:, 0:n], func=AF.Sigmoid,
                             reads=[("pp", bg)], writes=[("tm", 1)])
                        S.op("dve", "tensor_tensor", out=tm[0][:, 0:n], in0=pp[ba][:, 0:n], in1=tm[1][:, 0:n],
                             op=ALU.mult, reads=[("pp", ba), ("tm", 1)], writes=[("tm", 0)])
                        g0, g0r = gate(0)
                        S.op("dve", "tensor_tensor", out=macc[:, 0:n], in0=tm[0][:, 0:n], in1=g0[:, 0:n],
                             op=ALU.mult, reads=[("tm", 0), g0r], writes=["macc"])
                        bc_ = nxt()
                        for ct in range(4):
                            S.op("pe", "matmul", pp[bc_][:, 0:n], lhsT=wpw[:, ct, f * 128:(f + 1) * 128],
                                 rhs=hcv[:, ct, 0:n], start=(ct == 0), stop=(ct == 3),
                                 reads=[("wpw", c) for c in range(4)] + [hcvr], writes=[("pp", bc_)])
                        g1, g1r = gate(1)
                        S.op("dve", "tensor_tensor", out=tm[0][:, 0:n], in0=pp[bc_][:, 0:n], in1=g1[:, 0:n],
                             op=ALU.mult, reads=[("pp", bc_), g1r], writes=[("tm", 0)])
                        S.op("dve", "tensor_tensor", out=macc[:, 0:n], in0=macc[:, 0:n], in1=tm[0][:, 0:n],
                             op=ALU.add, reads=["macc", ("tm", 0)], writes=["macc"])
                        bo_ = nxt()
                        for ct in range(4):
                            S.op("pe", "matmul", pp[bo_][:, 0:n], lhsT=wo[:, ct, f * 128:(f + 1) * 128],
                                 rhs=oat[:, ct, 0:n], start=(ct == 0), stop=(ct == 3),
                                 reads=[("wo", c) for c in range(4)] + [oatr], writes=[("pp", bo_)])
                        g2, g2r = gate(2)
                        S.op("dve", "tensor_tensor", out=tm[0][:, 0:n], in0=pp[bo_][:, 0:n], in1=g2[:, 0:n],
                             op=ALU.mult, reads=[("pp", bo_), g2r], writes=[("tm", 0)])
                        S.op("dve", "tensor_tensor", out=mrg[:, f, 0:n], in0=macc[:, 0:n], in1=tm[0][:, 0:n],
                             op=ALU.add, reads=["macc", ("tm", 0)], writes=[("mrg", f)])
                    for o in range(DT):
                        bo_ = nxt()
                        for f in range(DT):
                            S.op("pe", "matmul", pp[bo_][:, 0:n], lhsT=wout[:, f, o * 128:(o + 1) * 128],
                                 rhs=mrg[:, f, 0:n], start=(f == 0), stop=(f == DT - 1),
                                 reads=[("wout", f), ("mrg", f)], writes=[("pp", bo_)])
                        S.op("dve", "tensor_tensor", out=hs[:, o, 0:n], in0=pp[bo_][:, 0:n], in1=hs[:, o, 0:n],
                             op=ALU.add, reads=[("pp", bo_), hsr], writes=[hsr])
                    S.dma("sp", out=hT[s, :, t0:t0 + n].rearrange("(t p) n -> p t n", p=128), in_=hs[:, :, 0:n],
                          reads=[hsr])

                pro_loads(0)
                pro_compute(0)
                for idx in range(len(tiles_all)):
                    main_tile(idx)
                S.barrier()

        phases = cfg.phases

        def on(name):
            return phases is None or name in phases

        phase_in()
        for l in range(DEPTH):
            if on("ffn1"):
                phase_ffn(l, 0)
            if on("mixa"):
                phase_mixA(l)
            if on("att"):
                phase_att(l)
            if on("ssm"):
                phase_ssm(l)
            if on("mixb"):
                phase_mixB1(l)
                phase_mixB2(l)
            if on("ffn2"):
                phase_ffn(l, 1)
        phase_out()

        with nc.Block() as block:
            S.emit(block)
    return nc


def kernel(**inputs):
    cfg = Cfg()
    nc = build(cfg)
    n = 8
    in_maps = []
    for c in range(n):
        m = {}
        for k, v in inputs.items():
            a = np.asarray(v)
            if k == "x":
                a = a[c * cfg.nseq:(c + 1) * cfg.nseq]
            m[k] = np.ascontiguousarray(a, dtype=np.float32)
        in_maps.append(m)
    res = run_bass_kernel_spmd(nc, in_maps, core_ids=list(range(n)))
    return np.concatenate([np.asarray(r["out"]) for r in res.results], axis=0).astype(np.float32)
```
